# Optimizing a Trainium2 kernel written in Bass

```python
import math
import jax, jax.numpy as jnp
from jax import lax
import numpy as np

D_MODEL = 1024
BATCH = 4
SEQ = 8192
DEPTH = 2
DEC_BATCH = 8
DEC_SEQ = 32
PAST_LEN = 1024

CHUNK = 64
Q_BLOCK = 128
HEAD_DIM = 64
H_A = 4
H_B = 4
H_C = 4
H_X = 4
C_PREV = 8
C_PAST = C_PREV * CHUNK
REL_CLIP = 128
T5_BUCKETS = 32
T5_MAX_DIST = 128
N_MEM = 256
D_FF = 4 * D_MODEL
WA = H_A * 2 * HEAD_DIM
WB = H_B * HEAD_DIM
WC = H_C * HEAD_DIM
WX = H_X * HEAD_DIM
N_BRANCH = 3
IN_SPLITS = (WA, WA, WA, WB, WB, WB, WC, WC, WC, N_BRANCH * D_MODEL)
IN_COLS = sum(IN_SPLITS)
EPS = 1e-6

kernel_name = 'hybrid_chunk_streaming_encoder_step'


def rmsnorm(x, g):
    xf = x.astype(jnp.float32)
    y = xf * lax.rsqrt(jnp.mean(xf * xf, axis=-1, keepdims=True) + EPS)
    return (y * g.astype(jnp.float32)).astype(x.dtype)


def half_ffn(x, g, wg, wu, wd):
    h = rmsnorm(x, g)
    return 0.5 * ((jax.nn.silu(h @ wg) * (h @ wu)) @ wd)


def lambda_init(layer):
    return 0.8 - 0.6 * math.exp(-0.3 * layer)


def diff_lambda(lq1, lk1, lq2, lk2, layer):
    f = lambda a, b: jnp.exp(jnp.sum(a.astype(jnp.float32) * b.astype(jnp.float32)))
    return f(lq1, lk1) - f(lq2, lk2) + lambda_init(layer)


def t5_bucket(rel):
    half = T5_BUCKETS // 2
    max_exact = half // 2
    n = jnp.abs(rel)
    nf = jnp.maximum(n, 1).astype(jnp.float32)
    large = max_exact + (jnp.log(nf / max_exact) / math.log(T5_MAX_DIST / max_exact) * (half - max_exact)).astype(jnp.int32)
    large = jnp.minimum(large, half - 1)
    return jnp.where(rel > 0, half, 0) + jnp.where(n < max_exact, n, large)


def diff_attend(q, k, v, qpos, kpos, t5_bias, lam):
    bias = jnp.transpose(t5_bias[t5_bucket(kpos[None, :] - qpos[:, None])], (2, 0, 1)).astype(jnp.float32)
    mask = (kpos[None, :] // CHUNK) <= (qpos[:, None] // CHUNK)
    s = jnp.einsum('bqhmd,bkhmd->bmhqk', q, k).astype(jnp.float32) * (HEAD_DIM ** -0.5) + bias
    p = jax.nn.softmax(jnp.where(mask, s, -jnp.inf), axis=-1)
    a = p[:, 0] - lam * p[:, 1]
    return jnp.einsum('bhqk,bkhe->bqhe', a.astype(v.dtype), v)


def stick_attend(q, k, v, qpos, kpos):
    z = jnp.einsum('bqhd,bkhd->bhqk', q, k).astype(jnp.float32) * (HEAD_DIM ** -0.5)
    valid = kpos[None, :] < qpos[:, None]
    log_1m = jnp.where(valid, jax.nn.log_sigmoid(-z), 0.0)
    after = lax.cumsum(log_1m, axis=3, reverse=True) - log_1m
    w = jnp.where(valid, jnp.exp(jax.nn.log_sigmoid(z) + after), 0.0)
    return jnp.einsum('bhqk,bkhd->bqhd', w.astype(v.dtype), v)


def band_attend(q, k, v, qpos, kpos, rel_table):
    rel = kpos[:, None, :] - qpos[:, :, None]
    bias = jnp.moveaxis(rel_table[:, jnp.clip(rel, -REL_CLIP, REL_CLIP) + REL_CLIP], 0, 1).astype(jnp.float32)
    qc = qpos[:, :, None] // CHUNK
    kc = kpos[:, None, :] // CHUNK
    mask = (kpos[:, None, :] >= 0) & (kc <= qc) & (kc >= qc - C_PREV)
    s = jnp.einsum('bcqhd,bckhd->bchqk', q, k).astype(jnp.float32) * (HEAD_DIM ** -0.5) + bias
    p = jax.nn.softmax(jnp.where(mask[:, None], s, -jnp.inf), axis=-1)
    return jnp.einsum('bchqk,bckhd->bcqhd', p.astype(v.dtype), v)


def band_gather(t):
    b, s, h, d = t.shape
    nc = s // CHUNK
    tp = jnp.pad(t.reshape(b, nc, CHUNK, h, d), ((0, 0), (C_PREV, 0), (0, 0), (0, 0), (0, 0)))
    return jnp.concatenate([tp[:, j:j + nc] for j in range(C_PREV + 1)], axis=2)


def map_query_blocks(attend, q):
    b, s = q.shape[0], q.shape[1]
    nb = s // Q_BLOCK
    qb = jnp.moveaxis(q.reshape((b, nb, Q_BLOCK) + q.shape[2:]), 1, 0)

    def body(args):
        qi, bi = args
        return attend(qi, bi * Q_BLOCK + jnp.arange(Q_BLOCK))

    o = lax.map(body, (qb, jnp.arange(nb)))
    return jnp.moveaxis(o, 0, 1).reshape((b, s) + o.shape[3:])


def mixer_project(x, g, w_in, a_qn, a_kn, c_qn, c_kn):
    h = rmsnorm(x, g)
    lead = h.shape[:-1]
    cuts = [int(c) for c in np.cumsum(IN_SPLITS)[:-1]]
    qa, ka, va, qb, kb, vb, qc, kc, vc, gz = jnp.split(h @ w_in, cuts, axis=-1)
    qa = rmsnorm(qa.reshape(lead + (H_A, 2, HEAD_DIM)), a_qn)
    ka = rmsnorm(ka.reshape(lead + (H_A, 2, HEAD_DIM)), a_kn)
    va = va.reshape(lead + (H_A, 2 * HEAD_DIM))
    qb = qb.reshape(lead + (H_B, HEAD_DIM))
    kb = kb.reshape(lead + (H_B, HEAD_DIM))
    vb = vb.reshape(lead + (H_B, HEAD_DIM))
    qc = rmsnorm(qc.reshape(lead + (H_C, HEAD_DIM)), c_qn)
    kc = rmsnorm(kc.reshape(lead + (H_C, HEAD_DIM)), c_kn)
    vc = vc.reshape(lead + (H_C, HEAD_DIM))
    gates = jax.nn.sigmoid(gz.reshape(lead + (N_BRANCH, D_MODEL)))
    return qa, ka, va, qb, kb, vb, qc, kc, vc, gates


def mixer_merge(oa, ob, oc, gates, subln, layer, w_br_a, w_br_b, w_br_c, w_out):
    lead = oa.shape[:-2]
    oa = rmsnorm(oa, subln) * (1.0 - lambda_init(layer))
    ba = oa.reshape(lead + (WA,)) @ w_br_a
    bb = ob.reshape(lead + (WB,)) @ w_br_b
    bc = oc.reshape(lead + (WC,)) @ w_br_c
    merged = gates[..., 0, :] * ba + gates[..., 1, :] * bb + gates[..., 2, :] * bc
    return merged @ w_out


def memory_kv(mem, g, w_kv, kn):
    lead = mem.shape[:-1]
    mk, mv = jnp.split(rmsnorm(mem, g) @ w_kv, 2, axis=-1)
    return rmsnorm(mk.reshape(lead + (H_X, HEAD_DIM)), kn), mv.reshape(lead + (H_X, HEAD_DIM))


def cross_attend(x, mk, mv, g, w_q, qn, w_o):
    lead = x.shape[:-1]
    q = rmsnorm((rmsnorm(x, g) @ w_q).reshape(lead + (H_X, HEAD_DIM)), qn)
    s = jnp.einsum('bqhd,bmhd->bhqm', q, mk).astype(jnp.float32) * (HEAD_DIM ** -0.5)
    p = jax.nn.softmax(s, axis=-1)
    o = jnp.einsum('bhqm,bmhd->bqhd', p.astype(mv.dtype), mv)
    return o.reshape(lead + (WX,)) @ w_o


def setup_inputs(seed: int = 0) -> dict:
    key = jax.random.key(seed)
    ks = iter(jax.random.split(key, 64))
    nrm = lambda shape, scale: jax.random.normal(next(ks), shape, jnp.float32) * scale
    gain = lambda shape: 1.0 + 0.05 * jax.random.normal(next(ks), shape, jnp.float32)
    c_buf = min(C_PAST, PAST_LEN)
    L = DEPTH
    return {
        'x_prompt': nrm((BATCH, SEQ, D_MODEL), 1.0),
        'x_sample': nrm((DEC_BATCH, DEC_SEQ, D_MODEL), 1.0),
        'mem_prompt': nrm((BATCH, N_MEM, D_MODEL), 1.0),
        'cache_a_k': nrm((L, DEC_BATCH, PAST_LEN, H_A, 2 * HEAD_DIM), 1.0),
        'cache_a_v': nrm((L, DEC_BATCH, PAST_LEN, H_A, 2 * HEAD_DIM), 1.0),
        'cache_b_k': nrm((L, DEC_BATCH, PAST_LEN, H_B, HEAD_DIM), 1.0),
        'cache_b_v': nrm((L, DEC_BATCH, PAST_LEN, H_B, HEAD_DIM), 1.0),
        'cache_c_k': nrm((L, DEC_BATCH, c_buf, H_C, HEAD_DIM), 1.0),
        'cache_c_v': nrm((L, DEC_BATCH, c_buf, H_C, HEAD_DIM), 1.0),
        'cache_mem_k': nrm((L, DEC_BATCH, N_MEM, H_X, HEAD_DIM), 1.0),
        'cache_mem_v': nrm((L, DEC_BATCH, N_MEM, H_X, HEAD_DIM), 1.0),
        't5_bias': nrm((T5_BUCKETS, H_A), 0.5),
        'ffn1_norm': gain((L, D_MODEL)),
        'ffn1_wg': nrm((L, D_MODEL, D_FF), D_MODEL ** -0.5),
        'ffn1_wu': nrm((L, D_MODEL, D_FF), D_MODEL ** -0.5),
        'ffn1_wd': nrm((L, D_FF, D_MODEL), D_FF ** -0.5),
        'mix_norm': gain((L, D_MODEL)),
        'w_in': nrm((L, D_MODEL, IN_COLS), D_MODEL ** -0.5),
        'a_qnorm': gain((L, HEAD_DIM)),
        'a_knorm': gain((L, HEAD_DIM)),
        'a_lq1': nrm((L, HEAD_DIM), 0.1),
        'a_lk1': nrm((L, HEAD_DIM), 0.1),
        'a_lq2': nrm((L, HEAD_DIM), 0.1),
        'a_lk2': nrm((L, HEAD_DIM), 0.1),
        'a_subln': gain((L, 2 * HEAD_DIM)),
        'c_qnorm': gain((L, HEAD_DIM)),
        'c_knorm': gain((L, HEAD_DIM)),
        'c_rel_bias': nrm((L, H_C, 2 * REL_CLIP + 1), 0.5),
        'w_br_a': nrm((L, WA, D_MODEL), WA ** -0.5),
        'w_br_b': nrm((L, WB, D_MODEL), WB ** -0.5),
        'w_br_c': nrm((L, WC, D_MODEL), WC ** -0.5),
        'w_out': nrm((L, D_MODEL, D_MODEL), D_MODEL ** -0.5),
        'x_norm': gain((L, D_MODEL)),
        'mem_norm': gain((L, D_MODEL)),
        'x_wq': nrm((L, D_MODEL, WX), D_MODEL ** -0.5),
        'x_wkv': nrm((L, D_MODEL, 2 * WX), D_MODEL ** -0.5),
        'x_qnorm': gain((L, HEAD_DIM)),
        'x_knorm': gain((L, HEAD_DIM)),
        'x_wo': nrm((L, WX, D_MODEL), WX ** -0.5),
        'ffn2_norm': gain((L, D_MODEL)),
        'ffn2_wg': nrm((L, D_MODEL, D_FF), D_MODEL ** -0.5),
        'ffn2_wu': nrm((L, D_MODEL, D_FF), D_MODEL ** -0.5),
        'ffn2_wd': nrm((L, D_FF, D_MODEL), D_FF ** -0.5),
    }


def reference(x_prompt, x_sample, mem_prompt, cache_a_k, cache_a_v, cache_b_k, cache_b_v,
              cache_c_k, cache_c_v, cache_mem_k, cache_mem_v, t5_bias,
              ffn1_norm, ffn1_wg, ffn1_wu, ffn1_wd, mix_norm, w_in,
              a_qnorm, a_knorm, a_lq1, a_lk1, a_lq2, a_lk2, a_subln,
              c_qnorm, c_knorm, c_rel_bias, w_br_a, w_br_b, w_br_c, w_out,
              x_norm, mem_norm, x_wq, x_wkv, x_qnorm, x_knorm, x_wo,
              ffn2_norm, ffn2_wg, ffn2_wu, ffn2_wd):
    xp = x_prompt
    bp, sp = xp.shape[0], xp.shape[1]
    nc = sp // CHUNK
    pos_p = jnp.arange(sp)
    qpos_band = pos_p.reshape(nc, CHUNK)
    kpos_band = (jnp.arange(nc)[:, None] - C_PREV) * CHUNK + jnp.arange((C_PREV + 1) * CHUNK)[None, :]
    w_keep = min(C_PAST, sp)
    ak_p, av_p, bk_p, bv_p, ck_p, cv_p, mk_p, mv_p = [], [], [], [], [], [], [], []
    for i in range(DEPTH):
        xp = xp + half_ffn(xp, ffn1_norm[i], ffn1_wg[i], ffn1_wu[i], ffn1_wd[i])
        qa, ka, va, qb, kb, vb, qc, kc, vc, gates = mixer_project(
            xp, mix_norm[i], w_in[i], a_qnorm[i], a_knorm[i], c_qnorm[i], c_knorm[i])
        lam = diff_lambda(a_lq1[i], a_lk1[i], a_lq2[i], a_lk2[i], i)
        oa = map_query_blocks(lambda qi, qpos: diff_attend(qi, ka, va, qpos, pos_p, t5_bias, lam), qa)
        ob = map_query_blocks(lambda qi, qpos: stick_attend(qi, kb, vb, qpos, pos_p), qb)
        oc = band_attend(qc.reshape(bp, nc, CHUNK, H_C, HEAD_DIM), band_gather(kc), band_gather(vc),
                         qpos_band, kpos_band, c_rel_bias[i]).reshape(bp, sp, H_C, HEAD_DIM)
        xp = xp + mixer_merge(oa, ob, oc, gates, a_subln[i], i, w_br_a[i], w_br_b[i], w_br_c[i], w_out[i])
        mk, mv = memory_kv(mem_prompt, mem_norm[i], x_wkv[i], x_knorm[i])
        xp = xp + cross_attend(xp, mk, mv, x_norm[i], x_wq[i], x_qnorm[i], x_wo[i])
        xp = xp + half_ffn(xp, ffn2_norm[i], ffn2_wg[i], ffn2_wu[i], ffn2_wd[i])
        ak_p.append(ka.reshape(bp, sp, H_A, 2 * HEAD_DIM))
        av_p.append(va)
        bk_p.append(kb)
        bv_p.append(vb)
        ck_p.append(kc[:, sp - w_keep:])
        cv_p.append(vc[:, sp - w_keep:])
        mk_p.append(mk)
        mv_p.append(mv)

    xs = x_sample
    bs, ns = xs.shape[0], xs.shape[1]
    past = cache_a_k.shape[2]
    w_buf = cache_c_k.shape[2]
    qpos_s = past + jnp.arange(ns)
    kpos_s = jnp.arange(past + ns)
    kpos_sb = (past - w_buf + jnp.arange(w_buf + ns))[None, :]
    ak_s, av_s, bk_s, bv_s, ck_s, cv_s = [], [], [], [], [], []
    for i in range(DEPTH):
        xs = xs + half_ffn(xs, ffn1_norm[i], ffn1_wg[i], ffn1_wu[i], ffn1_wd[i])
        qa, ka, va, qb, kb, vb, qc, kc, vc, gates = mixer_project(
            xs, mix_norm[i], w_in[i], a_qnorm[i], a_knorm[i], c_qnorm[i], c_knorm[i])
        lam = diff_lambda(a_lq1[i], a_lk1[i], a_lq2[i], a_lk2[i], i)
        ka_all = jnp.concatenate([cache_a_k[i].reshape(bs, past, H_A, 2, HEAD_DIM), ka], axis=1)
        va_all = jnp.concatenate([cache_a_v[i], va], axis=1)
        oa = diff_attend(qa, ka_all, va_all, qpos_s, kpos_s, t5_bias, lam)
        kb_all = jnp.concatenate([cache_b_k[i], kb], axis=1)
        vb_all = jnp.concatenate([cache_b_v[i], vb], axis=1)
        ob = stick_attend(qb, kb_all, vb_all, qpos_s, kpos_s)
        kc_all = jnp.concatenate([cache_c_k[i], kc], axis=1)
        vc_all = jnp.concatenate([cache_c_v[i], vc], axis=1)
        oc = band_attend(qc[:, None], kc_all[:, None], vc_all[:, None], qpos_s[None, :], kpos_sb, c_rel_bias[i])[:, 0]
        xs = xs + mixer_merge(oa, ob, oc, gates, a_subln[i], i, w_br_a[i], w_br_b[i], w_br_c[i], w_out[i])
        xs = xs + cross_attend(xs, cache_mem_k[i], cache_mem_v[i], x_norm[i], x_wq[i], x_qnorm[i], x_wo[i])
        xs = xs + half_ffn(xs, ffn2_norm[i], ffn2_wg[i], ffn2_wu[i], ffn2_wd[i])
        ak_s.append(ka.reshape(bs, ns, H_A, 2 * HEAD_DIM))
        av_s.append(va)
        bk_s.append(kb)
        bv_s.append(vb)
        ck_s.append(kc_all[:, ns:])
        cv_s.append(vc_all[:, ns:])

    return (xp, xs,
            jnp.stack(ak_p), jnp.stack(av_p), jnp.stack(bk_p), jnp.stack(bv_p),
            jnp.stack(ck_p), jnp.stack(cv_p), jnp.stack(mk_p), jnp.stack(mv_p),
            jnp.stack(ak_s), jnp.stack(av_s), jnp.stack(bk_s), jnp.stack(bv_s),
            jnp.stack(ck_s), jnp.stack(cv_s))
```

```python
import math
import contextlib
import numpy as np
import concourse.bass as bass
import concourse.mybir as mybir
from concourse.bass_utils import run_bass_kernel_spmd

F32 = mybir.dt.float32
BF16 = mybir.dt.bfloat16
ALU = mybir.AluOpType
AF = mybir.ActivationFunctionType
AX = mybir.AxisListType

D = 1024
DFF = 4096
HD = 64
NL = 2
NMEM = 256
CHUNK = 64
C_PREV = 8
REL_CLIP = 128
EPS = 1e-6
NEG = -30000.0
WA_MAIN, WA_S = 1536, 160
WB_MAIN, WB_S = 1408, 32
WC_MAIN, WC_S = 1920, 544
WA_STRIP = WA_MAIN + WA_S
WB_STRIP = WB_MAIN + WB_S
WC_STRIP = WC_MAIN + WC_S
GA_W = 2048
GC_W = 2048

import os
STAGE = int(os.environ.get("KSTAGE", "8"))
KGROUP = os.environ.get("KGROUP", "ps")
POOL_OFF = os.environ.get("KPOOL", "pool")
ENGS = ("pe", "act", "dve", "pool", "sp")
NDMASEM = 24


class Buf:
    __slots__ = ("w", "r", "name", "x")

    def __init__(self, name=""):
        self.w = None
        self.r = []
        self.name = name
        self.x = False


class TT:
    def __init__(self, t, name):
        self.t = t
        self.b = Buf(name)


class Sched:
    def __init__(self, nc):
        self.nc = nc
        self.acts = {e: [] for e in ENGS}
        self.ninst = {e: 0 for e in ENGS}
        self.flag = {e: set() for e in ENGS}
        self.seen = {e: {s: 0 for s in ENGS} for e in ENGS}
        self.dseen = {e: {} for e in ENGS}
        self.dma_n = {e: 0 for e in ENGS}
        self.dma_val = {}

    def _wait(self, e, tok):
        if tok is None:
            return
        if tok[0] == "e":
            _, s, idx = tok
            if self.seen[e][s] >= idx:
                return
            self.flag[s].add(idx)
            self.acts[e].append(("we", s, idx))
            self.seen[e][s] = idx
        else:
            _, q, slot, val = tok
            if self.dseen[e].get((q, slot), 0) >= val:
                return
            self.acts[e].append(("wd", q, slot, val))
            self.dseen[e][(q, slot)] = val

    def _deps(self, e, reads, writes):
        for b in reads:
            self._wait(e, b.w)
            if b.x:
                for t in b.r:
                    if not (t[0] == "e" and t[1] == e):
                        self._wait(e, t)
        for b in writes:
            for t in [b.w] + b.r:
                if t is not None and t[0] == "e" and t[1] == e:
                    continue
                self._wait(e, t)

    def _commit(self, tok, reads, writes):
        for b in reads:
            if tok[0] == "e":
                b.r = [t for t in b.r if not (t[0] == "e" and t[1] == tok[1])]
            else:
                b.r = [t for t in b.r if not (t[0] == "d" and t[1] == tok[1] and t[2] == tok[2])]
            b.r.append(tok)
        for b in writes:
            b.w = tok
            b.r = []

    def op(self, e, fn, reads=(), writes=()):
        self._deps(e, reads, writes)
        self.ninst[e] += 1
        idx = self.ninst[e]
        self.acts[e].append(("i", fn, idx))
        self._commit(("e", e, idx), reads, writes)

    def dma(self, q, out, in_, reads=(), writes=(), **kw):
        self._deps(q, reads, writes)
        n = self.dma_n[q]
        self.dma_n[q] += 1
        slot = n % NDMASEM
        prev = self.dma_val.get((q, slot), 0)
        if prev:
            self._wait(q, ("d", q, slot, prev))
        val = prev + 16
        self.dma_val[(q, slot)] = val
        self.acts[q].append(("d", out, in_, slot, kw))
        self._commit(("d", q, slot, val), reads, writes)

    def cc(self, fn, reads=(), writes=()):
        self._deps("pool", reads, writes)
        self.ncc = getattr(self, "ncc", 0) + 1
        self.acts["pool"].append(("c", fn))
        self._commit(("d", "cc", 0, self.ncc), reads, writes)

    def emit(self):
        nc = self.nc
        for (q, slot), val in sorted(self.dma_val.items()):
            self._wait("sp", ("d", q, slot, val))
        if getattr(self, "ncc", 0):
            self._wait("sp", ("d", "cc", 0, self.ncc))
        with contextlib.ExitStack() as st:
            esem = {e: st.enter_context(nc.semaphore("es_" + e)) for e in ENGS}
            dsem = {}
            for q in ENGS:
                for s in range(min(NDMASEM, self.dma_n[q])):
                    dsem[(q, s)] = st.enter_context(nc.semaphore("ds_%s_%d" % (q, s)))
            if getattr(self, "ncc", 0):
                dsem[("cc", 0)] = st.enter_context(nc.semaphore("ccsem"))
            block = st.enter_context(nc.Block())
            cum = {}
            for e in ENGS:
                cum[e] = {idx: i + 1 for i, idx in enumerate(sorted(self.flag[e]))}

            def run(e, eng):
                for a in self.acts[e]:
                    k = a[0]
                    if k == "we":
                        eng.wait_ge(esem[a[1]], cum[a[1]][a[2]])
                    elif k == "wd":
                        eng.wait_ge(dsem[(a[1], a[2])], a[3])
                    elif k == "c":
                        a[1](eng).then_inc(dsem[("cc", 0)], 1)
                    elif k == "i":
                        ins = a[1](eng)
                        if a[2] in cum[e]:
                            ins.then_inc(esem[e], 1)
                    else:
                        eng.dma_start(out=a[1], in_=a[2], **a[4]).then_inc(dsem[(e, a[3])], 16)

            @block.tensor
            def _(eng):
                run("pe", eng)

            @block.scalar
            def _(eng):
                run("act", eng)

            @block.vector
            def _(eng):
                run("dve", eng)

            @block.gpsimd
            def _(eng):
                run("pool", eng)

            @block.sync
            def _(eng):
                run("sp", eng)


def _t5_bucket_np(rel):
    import jax
    import jax.numpy as jnp
    with jax.default_device(jax.devices("cpu")[0]):
        rel = jnp.asarray(rel, dtype=jnp.int32)
        half = 16
        max_exact = 8
        n = jnp.abs(rel)
        nf = jnp.maximum(n, 1).astype(jnp.float32)
        large = max_exact + (jnp.log(nf / max_exact) / math.log(128 / max_exact) * (half - max_exact)).astype(jnp.int32)
        large = jnp.minimum(large, half - 1)
        out = jnp.where(rel > 0, half, 0) + jnp.where(n < max_exact, n, large)
        return np.asarray(out)


def lambda_init(layer):
    return 0.8 - 0.6 * math.exp(-0.3 * layer)


def host_consts(e=0):
    c = {}
    def oh_a_for(ee):
        s_ = np.arange(GA_W)
        bk = _t5_bucket_np(511 + 512 * ee - s_)
        oh = np.zeros((32, GA_W), np.float32)
        oh[bk, s_] = 1.0
        return oh
    def oh_c_for(ee):
        s_ = np.arange(GC_W)
        idx = np.clip(511 + 512 * ee - s_, -REL_CLIP, REL_CLIP) + REL_CLIP
        oh = np.zeros((384, GC_W), np.float32)
        oh[idx, s_] = 1.0
        return oh
    c["oh_a"] = np.stack([oh_a_for(e), oh_a_for(0)])
    c["oh_c"] = np.stack([oh_c_for(e), oh_c_for(0)])
    c["far_bucket"] = int(_t5_bucket_np(np.array([-4000]))[0])
    k = np.arange(128)[:, None]
    def masks(ee, jj):
        rel = k - jj + 384 + 512 * ee
        dch = k // 64 - jj // 64 + 6 + 8 * ee
        ma = np.where(dch <= 0, 0.0, NEG).astype(np.float32)
        mb = (rel < 0).astype(np.float32)
        mc = np.where((dch <= 0) & (dch >= -C_PREV), 0.0, NEG).astype(np.float32)
        return ma, mb, mc
    ma, _, _ = masks(e, np.arange(WA_MAIN)[None, :])
    _, mb, _ = masks(e, np.arange(WB_MAIN)[None, :])
    _, _, mc = masks(e, np.arange(WC_MAIN)[None, :])
    mas, _, _ = masks(0, np.arange(384, 384 + WA_S)[None, :])
    _, mbs, _ = masks(0, np.arange(384, 384 + WB_S)[None, :])
    _, _, mcs = masks(0, np.arange(384, 384 + WC_S)[None, :])
    c["mask_a"] = np.concatenate([ma, mas], axis=1)
    c["mask_b"] = np.concatenate([mb, mbs], axis=1)
    c["mask_c"] = np.concatenate([mc, mcs], axis=1)
    return c


class Group:
    pass


class K:
    def __init__(self, SEQ, NS, PAST, CB, far_bucket, NPAIR=4):
        self.SEQ, self.NS, self.PAST, self.CB = SEQ, NS, PAST, CB
        self.NPAIR = NPAIR
        self.far_bucket = far_bucket
        self.nc = bass.Bass("TRN2", target_bir_lowering=False)
        self.S = Sched(self.nc)
        self.din = {}
        self.dout = {}

    def dram_in(self, name, shape):
        self.din[name] = self.nc.dram_tensor(name, list(shape), F32, kind="ExternalInput").ap()
        return self.din[name]

    def dram_out(self, name, shape):
        self.dout[name] = self.nc.dram_tensor(name, list(shape), F32, kind="ExternalOutput").ap()
        return self.dout[name]

    def dram_tmp(self, name, shape, dt):
        return self.nc.dram_tensor(name, list(shape), dt).ap()

    def sb(self, name, shape, dt):
        return TT(self.st.enter_context(self.nc.sbuf_tensor(name, list(shape), dt)), name)

    def mm(self, out, lhsT, rhs, start, stop, reads, writes):
        self.S.op("pe", lambda e: e.matmul(out, lhsT=lhsT, rhs=rhs, start=start, stop=stop), reads, writes)

    def act(self, out, in_, func, reads, writes, **kw):
        self.S.op("act", lambda e: e.activation(out=out, in_=in_, func=func, **kw), reads, writes)

    def tt(self, out, in0, in1, op, reads, writes, eng="dve"):
        self.S.op(eng, lambda e: e.tensor_tensor(out=out, in0=in0, in1=in1, op=op), reads, writes)

    def ts(self, out, in0, s1, s2, op0, op1, reads, writes, eng="dve"):
        if s2 is None:
            self.S.op(eng, lambda e: e.tensor_scalar(out=out, in0=in0, scalar1=s1, scalar2=None, op0=op0), reads, writes)
        else:
            self.S.op(eng, lambda e: e.tensor_scalar(out=out, in0=in0, scalar1=s1, scalar2=s2, op0=op0, op1=op1), reads, writes)

    def stt(self, out, in0, scalar, in1, op0, op1, reads, writes):
        self.S.op("dve", lambda e: e.scalar_tensor_tensor(out=out, in0=in0, scalar=scalar, in1=in1, op0=op0, op1=op1), reads, writes)

    def cp(self, out, in_, reads, writes, eng="dve"):
        self.S.op(eng, lambda e: e.tensor_copy(out=out, in_=in_), reads, writes)

    def recip(self, out, in_, reads, writes):
        self.S.op("dve", lambda e: e.reciprocal(out=out, in_=in_), reads, writes)

    def memset(self, ap, val, writes, eng="pool"):
        self.S.op(eng, lambda e: e.memset(ap, val), (), writes)

    def dma(self, out, in_, reads=(), writes=(), q="sp", **kw):
        self.S.dma(q, out, in_, reads, writes, **kw)

    def ring(self, key):
        lst, i = self.rings[key]
        self.rings[key][1] = (i + 1) % len(lst)
        return lst[i]

    def build(self):
        nc = self.nc
        SEQ, NS, PAST, CB = self.SEQ, self.NS, self.PAST, self.CB
        di, do, dt_ = self.dram_in, self.dram_out, self.dram_tmp
        x_p = di("x_p", [SEQ, D]); x_s = di("x_s", [NS, D]); mem_p = di("mem_p", [NMEM, D])
        ca_k = di("ca_k", [NL, PAST, 512]); ca_v = di("ca_v", [NL, PAST, 512])
        cb_k = di("cb_k", [NL, PAST, 256]); cb_v = di("cb_v", [NL, PAST, 256])
        cc_k = di("cc_k", [NL, CB, 256]); cc_v = di("cc_v", [NL, CB, 256])
        cm_k = di("cm_k", [NL, NMEM, 256]); cm_v = di("cm_v", [NL, NMEM, 256])
        t5 = di("t5", [32, 4])
        gains = di("gains", [NL, 5, D])
        hgains = di("hgains", [NL, 6, HD])
        lvec = di("lvec", [NL, 4, HD])
        subln = di("subln", [NL, 128])
        crel = di("crel", [NL, 4, 257])
        W = {}
        wshapes = {"wg1": (D, DFF), "wu1": (D, DFF), "wd1": (DFF, D), "w_in": (D, 6144), "w_br_a": (512, D),
                   "w_br_b": (256, D), "w_br_c": (256, D), "w_out": (D, D), "x_wq": (D, 256), "x_wkv": (D, 512),
                   "x_wo": (256, D), "wg2": (D, DFF), "wu2": (D, DFF), "wd2": (DFF, D)}
        for n_, shp in wshapes.items():
            W[n_] = di(n_, [NL, shp[0], shp[1]])
        oh_a = di("oh_a", [2, 32, GA_W]); oh_c = di("oh_c", [2, 384, GC_W])
        mask_a = di("mask_a", [128, WA_STRIP]); mask_b = di("mask_b", [128, WB_STRIP]); mask_c = di("mask_c", [128, WC_STRIP])

        y_p = do("y_p", [SEQ, D]); y_s = do("y_s", [NS, D])
        o_akp = do("a_k_p", [NL, SEQ, 512]); o_avp = do("a_v_p", [NL, SEQ, 512])
        o_bkp = do("b_k_p", [NL, SEQ, 256]); o_bvp = do("b_v_p", [NL, SEQ, 256])
        CKP = 512
        o_ckp = do("c_k_p", [NL, CKP, 256]); o_cvp = do("c_v_p", [NL, CKP, 256])
        o_mkp = do("m_k_p", [NL, NMEM, 256]); o_mvp = do("m_v_p", [NL, NMEM, 256])
        o_aks = do("a_k_s", [NL, NS, 512]); o_avs = do("a_v_s", [NL, NS, 512])
        o_bks = do("b_k_s", [NL, NS, 256]); o_bvs = do("b_v_s", [NL, NS, 256])
        o_cks = do("c_k_s", [NL, CB, 256]); o_cvs = do("c_v_s", [NL, CB, 256])

        Wb = {}
        Wbuf = {}
        for n_, shp in wshapes.items():
            Wb[n_] = dt_("wb_" + n_, [NL, shp[0], shp[1]], BF16)
        xcur = dt_("xcur", [SEQ, D], F32)
        xcur_s = dt_("xcur_s", [NS, D], F32)
        ga_d = dt_("ga_d", [2, 4, GA_W], F32)
        gc_d = dt_("gc_d", [2, 4, GC_W], F32)
        NSLOT = SEQ // 512
        xkv = [dt_("xkv%d" % i, [2048, 512], BF16) for i in range(2)]
        xkvb = [Buf("xkv%d" % i) for i in range(2)]
        gout = [dt_("gout%d" % i, [4096, 512], BF16) for i in range(NSLOT)]
        gbuf = [Buf("gout%d" % i) for i in range(NSLOT)]

        def kvscr(tag, n):
            g = {}
            g["kta"] = dt_("kta_" + tag, [4, 128, n], BF16); g["va"] = dt_("va_" + tag, [n, 512], BF16)
            g["ktb"] = dt_("ktb_" + tag, [2, 128, n], BF16); g["vb"] = dt_("vb_" + tag, [n, 256], BF16)
            g["ktc"] = dt_("ktc_" + tag, [2, 128, n], BF16); g["vc"] = dt_("vc_" + tag, [n, 256], BF16)
            return g

        with contextlib.ExitStack() as st:
            self.st = st
            sb = self.sb
            S = self.S
            self.bank = [TT(st.enter_context(nc.psum_tensor("bank%d" % i, [128, 512], F32)), "bank%d" % i) for i in range(8)]
            bank = self.bank
            for b_ in bank:
                b_.b.x = True
            ident = sb("ident", [128, 128], BF16); onesb = sb("onesb", [128, 128], BF16)
            onesf = sb("onesf", [128, 128], F32); Jf = sb("Jf", [128, 128], F32)
            negU8 = sb("negU8", [128, 128], BF16); epsc = sb("epsc", [128, 1], F32)
            tmpc = sb("tmpc", [128, 128], F32)
            self.ident, self.onesb, self.onesf, self.negU8, self.epsc = ident, onesb, onesf, negU8, epsc
            self.memset(tmpc.t[:], 0.0, [tmpc.b])
            S.op("pool", lambda e: e.affine_select(out=tmpc.t[:], in_=tmpc.t[:], pattern=[[-1, 128]], compare_op=ALU.not_equal, fill=1.0, base=0, channel_multiplier=1), [tmpc.b], [tmpc.b])
            self.cp(ident.t[:], tmpc.t[:], [tmpc.b], [ident.b])
            identF = sb("identF", [128, 128], F32)
            self.cp(identF.t[:], tmpc.t[:], [tmpc.b], [identF.b])
            self.memset(Jf.t[:], 0.0, [Jf.b])
            S.op("pool", lambda e: e.affine_select(out=Jf.t[:], in_=Jf.t[:], pattern=[[1, 128]], compare_op=ALU.not_equal, fill=1.0, base=-127, channel_multiplier=1), [Jf.b], [Jf.b])
            self.memset(onesf.t[:], 1.0, [onesf.b])
            self.cp(onesb.t[:], onesf.t[:], [onesf.b], [onesb.b])
            tmpc2 = tmpc
            self.memset(tmpc2.t[:], -8.0, [tmpc2.b])
            S.op("pool", lambda e: e.affine_select(out=tmpc2.t[:], in_=tmpc2.t[:], pattern=[[-1, 128]], compare_op=ALU.is_ge, fill=0.0, base=0, channel_multiplier=1), [tmpc2.b], [tmpc2.b])
            self.cp(negU8.t[:], tmpc2.t[:], [tmpc2.b], [negU8.b])
            self.memset(epsc.t[:], EPS, [epsc.b])
            onec = sb("onec", [128, 1], F32)
            self.memset(onec.t[:], 1.0, [onec.b])

            def bc_ap(src, off, n):
                return bass.AP(tensor=src.tensor, offset=off, ap=[[0, 128], [1, n]])

            t5bc = sb("t5bc", [128, 128], F32)
            self.dma(t5bc.t[:], bc_ap(t5, 0, 128), (), [t5bc.b])
            gcol = sb("gcol", [128, NL * 5, 8], F32)
            NG = NL * 5 * 8
            self.dma(tmpc.t[:NG, :], gains.rearrange("l w (c p) -> (l w c) p", p=128), (), [tmpc.b])
            self.mm(bank[1].t[:, :NG], tmpc.t[:NG, :], identF.t[:NG, :NG], True, True, [tmpc.b, identF.b], [bank[1].b])
            self.cp(gcol.t[:].rearrange("p a b -> p (a b)"), bank[1].t[:, :NG], [bank[1].b], [gcol.b])
            hg = sb("hg", [128, NL * 6, HD], F32)
            self.dma(hg.t[:].rearrange("p a b -> p (a b)"), bc_ap(hgains, 0, NL * 6 * HD), (), [hg.b])
            lv = sb("lv", [128, NL * 4, HD], F32)
            self.dma(lv.t[:].rearrange("p a b -> p (a b)"), bc_ap(lvec, 0, NL * 4 * HD), (), [lv.b])
            subcol = sb("subcol", [128, NL], F32)
            self.dma(tmpc.t[:NL, :], subln[:, :], (), [tmpc.b])
            self.mm(bank[1].t[:, :NL], tmpc.t[:NL, :], identF.t[:NL, :NL], True, True, [tmpc.b, identF.b], [bank[1].b])
            self.cp(subcol.t[:], bank[1].t[:, :NL], [bank[1].b], [subcol.b])
            neglam = sb("neglam", [128, NL], F32)
            lt = sb("lt", [128, HD], F32); le = sb("le", [128, 4], F32)
            for l in range(NL):
                for i in range(2):
                    self.tt(lt.t[:], lv.t[:, l * 4 + 2 * i, :], lv.t[:, l * 4 + 2 * i + 1, :], ALU.mult, [lv.b], [lt.b])
                    S.op("dve", lambda e, i=i: e.tensor_reduce(out=le.t[:, i:i + 1], in_=lt.t[:], axis=AX.X, op=ALU.add), [lt.b], [le.b])
                self.act(le.t[:, 2:4], le.t[:, 0:2], AF.Exp, [le.b], [le.b])
                self.tt(neglam.t[:, l:l + 1], le.t[:, 3:4], le.t[:, 2:3], ALU.subtract, [le.b], [neglam.b])
                self.ts(neglam.t[:, l:l + 1], neglam.t[:, l:l + 1], -lambda_init(l), None, ALU.add, ALU.bypass, [neglam.b], [neglam.b])
                self.ts(subcol.t[:, l:l + 1], subcol.t[:, l:l + 1], 1.0 - lambda_init(l), None, ALU.mult, ALU.bypass, [subcol.b], [subcol.b])
            self.neglam, self.subcol, self.hg, self.gcol, self.t5bc = neglam, subcol, hg, gcol, t5bc

            xin = sb("xin", [128, 4, D], F32)
            hT = sb("hT", [128, 8, 512], BF16)
            hidT = sb("hidT", [128, 32, 512], BF16)
            self.rings = {
                "xn": [[sb("xn%d" % i, [128, D], BF16) for i in range(2)], 0],
                "w": [[sb("wslot%d" % i, [128, 8, 512], BF16) for i in range(4)], 0],
                "gt": [[sb("gt%d" % i, [128, 3, 512], BF16) for i in range(2)], 0],
                "tf": [[sb("tf%d" % i, [128, 512], F32) for i in range(4)], 0],
                "tb": [[sb("tb%d" % i, [128, 512], BF16) for i in range(5)], 0],
                "of": [[sb("of%d" % i, [128, 512], F32) for i in range(3)], 0],
                "ob": [[sb("ob%d" % i, [128, 4, 128], BF16) for i in range(2)], 0],
                "kt": [[sb("ktl%d" % i, [128, 1024], BF16) for i in range(2)], 0],
                "v": [[sb("vl%d" % i, [128, 8, 128], BF16) for i in range(2)], 0],
                "sc": [[bank[0], bank[1], bank[6], bank[7]], 0],
                "gu": [[(bank[0], bank[1]), (bank[2], bank[3]), (bank[4], bank[5])], 0],
            }
            QTA = [sb("QTA%d" % i, [128, 4, 512], BF16) for i in range(2)]
            QTB = [sb("QTB%d" % i, [128, 2, 512], BF16) for i in range(2)]
            QTC = [sb("QTC%d" % i, [128, 2, 512], BF16) for i in range(2)]
            QTX = QTC
            for q_ in QTA + QTB + QTC:
                self.memset(q_.t[:], 0.0, [q_.b])
            gates_d = {"p": [dt_("gates_p%d" % i, [4, 128, 3072], BF16) for i in range(2)], "s": [dt_("gates_s", [4, 128, 3072], BF16)]}
            gates_b = {"p": [Buf("gp0"), Buf("gp1")], "s": [Buf("gs")]}
            oaT = sb("oaT", [128, 4, 512], BF16); obT = sb("obT", [128, 2, 512], BF16)
            ocT = sb("ocT", [128, 2, 512], BF16); oxT = ocT
            Rt = sb("Rt", [128, 512], F32)
            mrgb = sb("mrgb", [128, D], BF16); junk = mrgb
            stripA = sb("stripA", [128, 4, WA_STRIP], BF16)
            stripC = sb("stripC", [128, 4, WC_STRIP], BF16)
            maskB = sb("maskB", [128, WB_STRIP], BF16)
            mkT = {g: sb("mkT_" + g, [128, 2, NMEM], BF16) for g in "ps"}
            mvb = {g: sb("mvb_" + g, [128, 2, 256], BF16) for g in "ps"}
            self.rings["hn"] = [[(sb("ssg%d" % i, [128, 8], F32), sb("sdg%d" % i, [128, 8], F32), sb("rsg%d" % i, [128, 8], F32)) for i in range(4)], 0]
            self.rings["rn"] = [[(sb("ss%d" % i, [128, 1], F32), sb("sd%d" % i, [128, 1], F32), sb("rs%d" % i, [128, 1], F32)) for i in range(4)], 0]
            self.xin, self.hT, self.hidT = xin, hT, hidT

            use_order = ["wg1", "wu1", "wd1", "w_in", "w_br_a", "w_br_b", "w_br_c", "w_out", "x_wkv", "x_wq", "x_wo", "wg2", "wu2", "wd2"]
            for n_ in use_order:
                Wbuf[n_] = [[], []]

            def cast_weights(l, names):
                for n_ in names:
                    shp = wshapes[n_]
                    rows = max(1, (1 << 20) // shp[1])
                    for r0 in range(0, shp[0], rows):
                        r1 = min(shp[0], r0 + rows)
                        b_ = Buf("wb_%s_%d_%d" % (n_, l, r0))
                        self.dma(Wb[n_][l, r0:r1, :], W[n_][l, r0:r1, :], (), [b_], q="pool")
                        Wbuf[n_][l].append(b_)
            cast_weights(0, use_order)

            stg_t = xin.t[:, 0:2, :].rearrange("p a b -> p (a b)")
            stg2_t = xin.t[:, 2:4, :].rearrange("p a b -> p (a b)")
            self.dma(stg_t[:, :WB_STRIP], mask_b[:, :], (), [xin.b])
            self.cp(maskB.t[:], stg_t[:, :WB_STRIP], [xin.b], [maskB.b])
            t5_sb = sb("t5_sb", [32, 4], F32)
            self.dma(t5_sb.t[:], t5[:, :], (), [t5_sb.b])
            gsb_t = stg2_t[:4, :GC_W]
            gab = Buf("ga_d")
            for v_ in range(2):
                self.dma(stg_t[:32, :GA_W], oh_a[v_], (), [xin.b])
                for c0 in range(0, GA_W, 512):
                    self.mm(bank[0].t[:4, :512], t5_sb.t[:, :], stg_t[:32, c0:c0 + 512], True, True, [t5_sb.b, xin.b], [bank[0].b])
                    self.cp(gsb_t[:, c0:c0 + 512], bank[0].t[:4, :512], [bank[0].b], [xin.b])
                self.dma(ga_d[v_], gsb_t[:, :GA_W], [xin.b], [gab])

            def build_strip(gd, gbuf_, gw, goff, jj_start, mask_d, mcol0, width, dst, dcol0):
                self.dma(stg2_t[:, :width], mask_d[:, mcol0:mcol0 + width], (), [xin.b])
                for h in range(4):
                    src = bass.AP(tensor=gd.tensor, offset=goff + h * gw + jj_start, ap=[[1, 128], [1, width]])
                    self.dma(stg_t[:, :width], src, [gbuf_], [xin.b])
                    for c0 in range(0, width, 512):
                        cw = min(512, width - c0)
                        self.mm(bank[0].t[:, :cw], Jf.t[:, :], stg_t[:, c0:c0 + cw], True, True, [Jf.b, xin.b], [bank[0].b])
                        self.tt(dst.t[:, h, dcol0 + c0:dcol0 + c0 + cw], bank[0].t[:, :cw], stg2_t[:, c0:c0 + cw], ALU.add, [bank[0].b, xin.b], [dst.b])

            build_strip(ga_d, gab, GA_W, 0, 0, mask_a, 0, WA_MAIN, stripA, 0)
            build_strip(ga_d, gab, GA_W, 4 * GA_W, 384, mask_a, WA_MAIN, WA_S, stripA, WA_MAIN)
            crT = sb("crT", [128, 3, 4], F32)
            cr_in = sb("cr_in", [4, 264], F32)
            gcb = Buf("gc_d")

            def layer_setup(l):
                self.memset(crT.t[:], 0.0, [crT.b])
                self.dma(cr_in.t[:4, :257], crel[l], (), [cr_in.b])
                for c in range(3):
                    n = min(128, 257 - c * 128)
                    self.mm(bank[1].t[:n, :4], cr_in.t[:4, c * 128:c * 128 + n], identF.t[:4, :4], True, True, [cr_in.b, identF.b], [bank[1].b])
                    self.cp(crT.t[:n, c, :], bank[1].t[:n, :4], [bank[1].b], [crT.b])
                ohv = stg_t[:, :1536].rearrange("p (c s) -> p c s", s=512)
                for v_ in range(2):
                    for c0 in range(0, GC_W, 512):
                        self.dma(ohv, oh_c[v_, :, c0:c0 + 512].rearrange("(c p) s -> p c s", p=128), (), [xin.b])
                        for c in range(3):
                            self.mm(bank[0].t[:4, :512], crT.t[:, c, :], ohv[:, c, :], c == 0, c == 2, [crT.b, xin.b], [bank[0].b])
                        self.cp(gsb_t[:, c0:c0 + 512], bank[0].t[:4, :512], [bank[0].b], [xin.b])
                    self.dma(gc_d[v_], gsb_t[:, :], [xin.b], [gcb])
                build_strip(gc_d, gcb, GC_W, 0, 0, mask_c, 0, WC_MAIN, stripC, 0)
                build_strip(gc_d, gcb, GC_W, 4 * GC_W, 384, mask_c, WC_MAIN, WC_S, stripC, WC_MAIN)

            def load_w(name, l, r0, c0, ncols, kc=8, pk=128):
                slot = self.ring("w")
                src = Wb[name][l, r0:r0 + kc * pk, c0:c0 + ncols].rearrange("(c p) n -> p c n", p=pk)
                self.dma(slot.t[:pk, :kc, :ncols], src, Wbuf[name][l], [slot.b])
                return slot

            def rms_T(src, np_, nsub, gi, dst):
                sc_ = []
                for s in range(nsub):
                    ss_, sd_, rs_ = self.ring("rn")
                    self.memset(ss_.t[:], 0.0, [ss_.b], eng="dve")
                    self.act(junk.t[:np_, :], src.t[:np_, s, :], AF.Square, [src.b], [junk.b, ss_.b], accum_out=ss_.t[:np_, 0:1])
                    sc_.append((ss_, sd_, rs_))
                for s in range(nsub):
                    ss_, sd_, rs_ = sc_[s]
                    self.act(sd_.t[:np_, :], ss_.t[:np_, :], AF.Sqrt, [ss_.b, epsc.b], [sd_.b], scale=1.0 / D, bias=epsc.t[:np_, :])
                    self.recip(rs_.t[:np_, :], sd_.t[:np_, :], [sd_.b], [rs_.b])
                    xn = self.ring("xn")
                    self.ts(xn.t[:np_, :], src.t[:np_, s, :], rs_.t[:np_, 0:1], None, ALU.mult, ALU.bypass, [src.b, rs_.b], [xn.b])
                    self.transpose_to(lambda c, xn=xn: xn.t[:np_, c * 128:(c + 1) * 128], [xn.b], 8, np_,
                                      dst.t[:, :, s * 128:s * 128 + np_], [dst.b],
                                      mul=gcol.t[:, gi, :].unsqueeze(2).to_broadcast([128, 8, np_]), mulb=[gcol.b])

            self.trp_i = 0
            self.acc_i = 0

            def transpose_to(src_fn, srcb, nblk, np_, dst_ap, dstb, mul=None, mulb=()):
                bk = bank[6] if self.trp_i % 2 == 0 else bank[7]
                self.trp_i += 1
                tv = bk.t[:, :].bitcast(BF16).rearrange("p (c n) -> p c n", n=128)
                for c in range(nblk):
                    S.op("pe", lambda e, c=c: e.transpose(out=tv[:, c, :np_], in_=src_fn(c), identity=ident.t[:np_, :np_]), list(srcb) + [ident.b], [bk.b])
                if mul is None:
                    self.act(dst_ap, tv[:, :nblk, :np_], AF.Copy, [bk.b], dstb)
                else:
                    self.tt(dst_ap, tv[:, :nblk, :np_], mul, ALU.mult, [bk.b] + list(mulb), dstb)
            self.transpose_to = transpose_to

            def transpose_split(src_fn, srcb, nblk, np_, q2, cs_):
                bk = bank[6] if self.trp_i % 2 == 0 else bank[7]
                self.trp_i += 1
                tv = bk.t[:, :].bitcast(BF16).rearrange("p (c n) -> p c n", n=128)
                for c in range(nblk):
                    S.op("pe", lambda e, c=c: e.transpose(out=tv[:, c, :np_], in_=src_fn(c), identity=ident.t[:np_, :np_]), list(srcb) + [ident.b], [bk.b])
                self.act(q2[0].t[0:64, :nblk, cs_], tv[0:64, :nblk, :np_], AF.Copy, [bk.b], [q2[0].b])
                self.cp(q2[1].t[64:128, :nblk, cs_], tv[64:128, :nblk, :np_], [bk.b], [q2[1].b])

            def linear_tm(name, l, KC, N, lhs_fn, lhsb, np_, nsub, evac, pieces=None):
                npieces = (N + 511) // 512
                accring = [bank[2], bank[3], bank[4], bank[5], bank[0], bank[1]]
                for p in (pieces if pieces is not None else range(npieces)):
                    ncols = min(512, N - p * 512)
                    if KC <= 8:
                        slot = load_w(name, l, 0, p * 512, ncols, KC)
                        for s in range(nsub):
                            acc = accring[self.acc_i % 6]
                            self.acc_i += 1
                            for k in range(KC):
                                self.mm(acc.t[:np_, :ncols], lhs_fn(k, s), slot.t[:, k, :ncols], k == 0, k == KC - 1,
                                        list(lhsb) + [slot.b], [acc.b])
                            evac(p, s, acc, ncols)
                        continue
                    for kg in range(0, KC, 8):
                        kc = min(8, KC - kg)
                        slot = load_w(name, l, kg * 128, p * 512, ncols, kc)
                        for s in range(nsub):
                            acc = bank[2 + s]
                            for k in range(kc):
                                self.mm(acc.t[:np_, :ncols], lhs_fn(kg + k, s), slot.t[:, k, :ncols], kg + k == 0, kg + k == KC - 1,
                                        list(lhsb) + [slot.b], [acc.b])
                    for s in range(nsub):
                        evac(p, s, bank[2 + s], ncols)

            def headnorm(pb_ap, pbb, np_, G, gi, out_ap, outb):
                tf = self.ring("tf")
                ssg, sdg, rsg = self.ring("hn")
                self.act(tf.t[:np_, :G * 64], pb_ap, AF.Square, pbb, [tf.b])
                S.op("dve", lambda e: e.tensor_reduce(out=ssg.t[:np_, :G], in_=tf.t[:np_, :G * 64].rearrange("p (g d) -> p g d", d=64), axis=AX.X, op=ALU.add), [tf.b], [ssg.b])
                self.act(sdg.t[:np_, :G], ssg.t[:np_, :G], AF.Sqrt, [ssg.b, epsc.b], [sdg.b], scale=1.0 / HD, bias=epsc.t[:np_, :])
                self.recip(rsg.t[:np_, :G], sdg.t[:np_, :G], [sdg.b], [rsg.b])
                tf2 = self.ring("tf")
                self.tt(tf2.t[:np_, :G * 64].rearrange("p (g d) -> p g d", d=64), pb_ap.rearrange("p (g d) -> p g d", d=64),
                        rsg.t[:np_, :G].unsqueeze(2).to_broadcast([np_, G, 64]), ALU.mult, list(pbb) + [rsg.b], [tf2.b])
                self.tt(out_ap.rearrange("p (g d) -> p g d", d=64), tf2.t[:np_, :G * 64].rearrange("p (g d) -> p g d", d=64),
                        hg.t[:np_, gi, :].unsqueeze(1).to_broadcast([np_, G, 64]), ALU.mult, [tf2.b, hg.b], outb, eng=POOL_OFF)

            def ffn(l, which, G):
                np_, nsub, T = G.np, G.nsub, G.T
                sfx = "1" if which == 0 else "2"
                rms_T(xin, np_, nsub, l * 5 + (0 if which == 0 else 4), hT)
                for fg in range(8):
                    wgs = load_w("wg" + sfx, l, 0, fg * 512, 512)
                    wus = load_w("wu" + sfx, l, 0, fg * 512, 512)
                    for fc in range(4):
                        bg, bu = self.ring("gu")
                        for k in range(8):
                            self.mm(bg.t[:, :T], wgs.t[:, k, fc * 128:(fc + 1) * 128], hT.t[:, k, :T], k == 0, k == 7, [wgs.b, hT.b], [bg.b])
                        for k in range(8):
                            self.mm(bu.t[:, :T], wus.t[:, k, fc * 128:(fc + 1) * 128], hT.t[:, k, :T], k == 0, k == 7, [wus.b, hT.b], [bu.b])
                        tf = self.ring("tf")
                        self.act(tf.t[:, :T], bg.t[:, :T], AF.Silu, [bg.b], [tf.b])
                        self.tt(hidT.t[:, fg * 4 + fc, :T], tf.t[:, :T], bu.t[:, :T], ALU.mult, [tf.b, bu.b], [hidT.b])

                def evac(p, s, acc, ncols):
                    self.stt(xin.t[:np_, s, p * 512:(p + 1) * 512], acc.t[:np_, :512], 0.5, xin.t[:np_, s, p * 512:(p + 1) * 512],
                             ALU.mult, ALU.add, [acc.b, xin.b], [xin.b])
                linear_tm("wd" + sfx, l, 32, D, lambda k, s: hidT.t[:, k, s * 128:s * 128 + np_], [hidT.b], np_, nsub, evac)

            def proj(l, G, t):
                np_, nsub, T = G.np, G.nsub, G.T
                i0 = G.widx0(t)
                i0c = G.widx0c(t)
                scr_w = G.scrw(t)
                r0 = G.orow(t)
                rms_T(xin, np_, nsub, l * 5 + 1, hT)

                def out_f(dst, of, w):
                    self.dma(dst, of.t[:np_, :w], [of.b], (), q="pool")

                def kt_out(kname, tb, nb, idx0, s):
                    ob = self.ring("ob")
                    transpose_to(lambda c, tb=tb: tb.t[:np_, c * 128:(c + 1) * 128], [tb.b], nb, np_, ob.t[:, :nb, :np_], [ob.b])
                    self.dma(scr_w[kname][:, :, idx0 + s * 128:idx0 + s * 128 + np_].rearrange("h p s -> p h s"), ob.t[:, :nb, :np_], [ob.b], [G.wbuf(kname, t)], q="pool")

                def v_out(vname, tb, w, idx0, s):
                    if os.environ.get("KNOV") == "1":
                        return
                    self.dma(scr_w[vname][idx0 + s * 128:idx0 + s * 128 + np_, :], tb.t[:np_, :w], [tb.b], [G.wbuf(vname, t)], q="pool")

                def evac(p, s, acc, ncols):
                    pb = acc.t[:np_, :512]
                    rs_ = slice(r0 + s * 128, r0 + s * 128 + np_)
                    if p == 0:
                        tb = self.ring("tb")
                        headnorm(pb, [acc.b], np_, 8, l * 6 + 0, tb.t[:np_, :], [tb.b])
                        transpose_split(lambda c, tb=tb: tb.t[:np_, c * 128:(c + 1) * 128], [tb.b], 4, np_, QTA, slice(s * 128, s * 128 + np_))
                    elif p == 1:
                        of = self.ring("of")
                        headnorm(pb, [acc.b], np_, 8, l * 6 + 1, of.t[:np_, :], [of.b])
                        out_f(G.o_ak[l, rs_, :], of, 512)
                        tb = self.ring("tb")
                        self.cp(tb.t[:np_, :], of.t[:np_, :], [of.b], [tb.b], eng=POOL_OFF)
                        kt_out("kta", tb, 4, i0, s)
                    elif p == 2:
                        of = self.ring("of")
                        self.act(of.t[:np_, :], pb, AF.Copy, [acc.b], [of.b])
                        out_f(G.o_av[l, rs_, :], of, 512)
                        tb = self.ring("tb")
                        self.cp(tb.t[:np_, :], pb, [acc.b], [tb.b])
                        v_out("va", tb, 512, i0, s)
                    elif p == 3:
                        tb = self.ring("tb")
                        self.cp(tb.t[:np_, :256], pb[:, 0:256], [acc.b], [tb.b])
                        transpose_split(lambda c, tb=tb: tb.t[:np_, c * 128:(c + 1) * 128], [tb.b], 2, np_, QTB, slice(s * 128, s * 128 + np_))
                        of = self.ring("of")
                        self.act(of.t[:np_, :256], pb[:, 256:512], AF.Copy, [acc.b], [of.b])
                        out_f(G.o_bk[l, rs_, :], of, 256)
                        tb2 = self.ring("tb")
                        self.cp(tb2.t[:np_, :256], pb[:, 256:512], [acc.b], [tb2.b])
                        kt_out("ktb", tb2, 2, i0, s)
                    elif p == 4:
                        of = self.ring("of")
                        self.act(of.t[:np_, :256], pb[:, 0:256], AF.Copy, [acc.b], [of.b])
                        out_f(G.o_bv[l, rs_, :], of, 256)
                        tb = self.ring("tb")
                        self.cp(tb.t[:np_, :256], pb[:, 0:256], [acc.b], [tb.b])
                        v_out("vb", tb, 256, i0, s)
                        tb2 = self.ring("tb")
                        headnorm(pb[:, 256:512], [acc.b], np_, 4, l * 6 + 2, tb2.t[:np_, :256], [tb2.b])
                        transpose_split(lambda c, tb2=tb2: tb2.t[:np_, c * 128:(c + 1) * 128], [tb2.b], 2, np_, QTC, slice(s * 128, s * 128 + np_))
                    elif p == 5:
                        of = self.ring("of")
                        headnorm(pb[:, 0:256], [acc.b], np_, 4, l * 6 + 3, of.t[:np_, 0:256], [of.b])
                        self.act(of.t[:np_, 256:512], pb[:, 256:512], AF.Copy, [acc.b], [of.b])
                        G.store_c(l, t, s, of)
                        tb = self.ring("tb")
                        self.cp(tb.t[:np_, :512], of.t[:np_, :512], [of.b], [tb.b], eng=POOL_OFF)
                        kt_out("ktc", tb, 2, i0c, s)
                        tb3 = self.ring("tb")
                        self.cp(tb3.t[:np_, :256], pb[:, 256:512], [acc.b], [tb3.b])
                        v_out("vc", tb3, 256, i0c, s)
                    else:
                        tbg = self.ring("tb")
                        self.act(tbg.t[:np_, :], pb, AF.Sigmoid, [acc.b], [tbg.b])
                        gi_ = t % len(gates_d[G.name])
                        self.dma(gates_d[G.name][gi_][s, :np_, (p - 6) * 512:(p - 5) * 512], tbg.t[:np_, :], [tbg.b], [gates_b[G.name][gi_]], q="pool")
                linear_tm("w_in", l, 8, 6144, lambda k, s: hT.t[:, k, s * 128:s * 128 + np_], [hT.b], np_, nsub, evac,
                          pieces=[int(x) for x in os.environ["KPIECES"].split(",")] if "KPIECES" in os.environ else None)

            def key_chunks(lo, hi, csz=1024):
                out = []
                c0 = lo
                while c0 < hi:
                    n = min(csz, hi - c0)
                    out.append((c0, n))
                    c0 += n
                return out

            KBASE = {"kta": 0, "ktb": 1024, "ktc": 1536}
            VBASE = {"va": 512, "vb": 1280, "vc": 1792}

            def load_kv(G, kname, vname, hp_or_h, prow0, nrow, vcol0, dv, c0, n):
                kt = self.ring("kt")
                v = self.ring("v")
                if G.name == "p":
                    slot = c0 // 1024
                    assert c0 % 1024 == 0 and n == 1024
                    g, deps = gout[slot], [gbuf[slot]]
                    for r in range(2):
                        base = r * 2048
                        krow = base + KBASE[kname] + hp_or_h * 128 + prow0
                        self.dma(kt.t[prow0:prow0 + nrow, r * 512:(r + 1) * 512], g[krow:krow + nrow, :], deps, [kt.b])
                        if vname == "va":
                            vsrc = g[base + 512:base + 1024, vcol0:vcol0 + dv]
                        else:
                            vsrc = g[base + VBASE[vname]:base + VBASE[vname] + 256, :].rearrange("r (a c) -> (r a) c", c=256)[:, vcol0:vcol0 + dv]
                        self.dma(v.t[:, r * 4:(r + 1) * 4, :dv], vsrc.rearrange("(c p) f -> p c f", p=128), deps, [v.b])
                    return kt, v
                deps_k = G.kdeps(kname, c0, n)
                deps_v = G.kdeps(vname, c0, n)
                self.dma(kt.t[prow0:prow0 + nrow, :n], G.scr[kname][hp_or_h, prow0:prow0 + nrow, c0:c0 + n], deps_k, [kt.b])
                nfull = n // 128
                if nfull:
                    self.dma(v.t[:, :nfull, :dv], G.scr[vname][c0:c0 + nfull * 128, vcol0:vcol0 + dv].rearrange("(c p) f -> p c f", p=128), deps_v, [v.b])
                rem = n - nfull * 128
                if rem:
                    self.dma(v.t[:rem, nfull, :dv], G.scr[vname][c0 + nfull * 128:c0 + n, vcol0:vcol0 + dv], deps_v, [v.b])
                return kt, v

            def attn_A(l, G, t):
                T = G.T
                q0 = G.kidx0(t)
                hi = q0 + T
                LA = 2
                for h in range(4):
                    O = [bank[2], bank[4]]
                    Sm = [bank[3], bank[5]]
                    chunks = key_chunks(0, hi)
                    items = []
                    for ci, (c0, n) in enumerate(chunks):
                        nt = (n + 127) // 128
                        for i in range(nt):
                            for m in range(2):
                                items.append((ci, c0, n, i, m))
                    nit = len(items)
                    loaded = {}
                    pend = []

                    def stage1(j):
                        ci, c0, n, i, m = items[j]
                        if ci not in loaded:
                            loaded[ci] = load_kv(G, "kta", "va", h, 0, 128, h * 128, 128, c0, n)
                        kt, v = loaded[ci]
                        k0 = c0 + i * 128
                        nk = min(128, c0 + n - k0)
                        delta = k0 - q0
                        sc = self.ring("sc")
                        self.mm(sc.t[:nk, :T], kt.t[:, i * 128:i * 128 + nk], QTA[m].t[:, h, :T], True, True, [kt.b, QTA[m].b], [sc.b])
                        P = self.ring("tb")
                        if delta >= G.sthr:
                            j0 = G.jcol(384 - delta, WA_MAIN)
                            tf = self.ring("tf")
                            self.stt(tf.t[:nk, :T], sc.t[:nk, :T], 0.125, stripA.t[:nk, h, j0:j0 + T], ALU.mult, ALU.add, [sc.b, stripA.b], [tf.b])
                            self.act(P.t[:nk, :T], tf.t[:nk, :T], AF.Exp, [tf.b], [P.b])
                        else:
                            fb = self.far_bucket * 4 + h
                            self.act(P.t[:nk, :T], sc.t[:nk, :T], AF.Exp, [sc.b, t5bc.b], [P.b], scale=0.125, bias=t5bc.t[:nk, fb:fb + 1])
                        pend.append((P, v, i, nk, m, j < 2, j >= nit - 2))

                    def stage2(j):
                        P, v, i, nk, m, first, last = pend[j]
                        self.mm(O[m].t[:, :T], v.t[:nk, i, :128], P.t[:nk, :T], first, last, [v.b, P.b], [O[m].b])
                        self.mm(Sm[m].t[:, :T], onesb.t[:nk, :], P.t[:nk, :T], first, last, [onesb.b, P.b], [Sm[m].b])

                    for j in range(nit + LA):
                        if j < nit:
                            stage1(j)
                        if j >= LA:
                            stage2(j - LA)
                    r1 = self.ring("tf"); t1 = self.ring("tf"); r2 = self.ring("tf"); t2 = self.ring("tf")
                    self.act(r1.t[:, :T], Sm[0].t[:, :T], AF.Ln, [Sm[0].b], [r1.b])
                    self.act(r1.t[:, :T], r1.t[:, :T], AF.Exp, [r1.b], [r1.b], scale=-1.0)
                    self.tt(t1.t[:, :T], O[0].t[:, :T], r1.t[:, :T], ALU.mult, [O[0].b, r1.b], [t1.b])
                    self.act(r2.t[:, :T], Sm[1].t[:, :T], AF.Ln, [Sm[1].b], [r2.b])
                    self.act(r2.t[:, :T], r2.t[:, :T], AF.Exp, [r2.b], [r2.b], scale=-1.0)
                    self.tt(t2.t[:, :T], O[1].t[:, :T], r2.t[:, :T], ALU.mult, [O[1].b, r2.b], [t2.b])
                    self.stt(r1.t[:, :T], t2.t[:, :T], neglam.t[:, l:l + 1], t1.t[:, :T], ALU.mult, ALU.add, [t2.b, t1.b, neglam.b], [r1.b])
                    self.act(r2.t[:, :T], r1.t[:, :T], AF.Square, [r1.b], [r2.b])
                    self.mm(bank[2].t[:, :T], onesf.t[:, :], r2.t[:, :T], True, True, [onesf.b, r2.b], [bank[2].b])
                    self.act(t1.t[:, :T], bank[2].t[:, :T], AF.Ln, [bank[2].b, epsc.b], [t1.b], scale=1.0 / 128, bias=epsc.t[:, :])
                    self.act(t2.t[:, :T], t1.t[:, :T], AF.Exp, [t1.b], [t2.b], scale=-0.5)
                    self.tt(r2.t[:, :T], r1.t[:, :T], t2.t[:, :T], ALU.mult, [r1.b, t2.b], [r2.b])
                    self.ts(oaT.t[:, h, :T], r2.t[:, :T], subcol.t[:, l:l + 1], None, ALU.mult, ALU.bypass, [r2.b, subcol.b], [oaT.b])

            def attn_B(l, G, t):
                T = G.T
                q0 = G.kidx0(t)
                hi = q0 + T
                zring = [bank[0], bank[1], bank[6], bank[4]]
                cring = [bank[7], bank[5], bank[3]]
                for h in range(4):
                    hp, ho = h // 2, (h % 2) * 64
                    self.memset(Rt.t[:, :], 0.0, [Rt.b], eng="dve")
                    chunks = key_chunks(0, hi)[::-1]
                    items = []
                    for ci, (c0, n) in enumerate(chunks):
                        nt = (n + 127) // 128
                        for i in range(nt - 1, -1, -1):
                            items.append((ci, c0, n, i))
                    nit = len(items)
                    loaded = {}
                    st = {}

                    def s12(j):
                        ci, c0, n, i = items[j]
                        if ci not in loaded:
                            loaded[ci] = load_kv(G, "ktb", "vb", hp, 0, 128, hp * 128, 128, c0, n)
                        kt, v = loaded[ci]
                        k0 = c0 + i * 128
                        nk = min(128, c0 + n - k0)
                        delta = k0 - q0
                        msk = delta >= G.bthr
                        j0 = G.jcol(384 - delta, WB_MAIN) if msk else 0
                        z = zring[j % 4]
                        cs = cring[j % 3]
                        self.mm(z.t[:nk, :T], kt.t[:, i * 128:i * 128 + nk], QTB[h % 2].t[:, hp, :T], True, True, [kt.b, QTB[h % 2].b], [z.b])
                        e_ = self.ring("tf")
                        self.act(e_.t[:nk, :T], z.t[:nk, :T], AF.Exp, [z.b], [e_.b], scale=0.125)
                        sp = self.ring("tb")
                        self.act(sp.t[:nk, :T], e_.t[:nk, :T], AF.Ln, [e_.b, onec.b], [sp.b], bias=onec.t[:nk, :])
                        if msk:
                            self.tt(sp.t[:nk, :T], sp.t[:nk, :T], maskB.t[:nk, j0:j0 + T], ALU.mult, [sp.b, maskB.b], [sp.b])
                        st[j] = dict(v=v, i=i, nk=nk, msk=msk, j0=j0, z=z, cs=cs, sp=sp)

                    def s345(j):
                        d = st[j]
                        z, cs, sp, nk = d["z"], d["cs"], d["sp"], d["nk"]
                        S.op("pe", lambda e, z=z, sp=sp, nk=nk: e.matmul(z.t[:nk, :T], lhsT=negU8.t[:nk, :nk], rhs=sp.t[:nk, :T], start=False, stop=True, skip_group_check=True),
                             [negU8.b, sp.b], [z.b])
                        self.mm(cs.t[:, :T], onesb.t[:nk, :], sp.t[:nk, :T], True, True, [onesb.b, sp.b], [cs.b])
                        arg = self.ring("tf")
                        self.stt(arg.t[:nk, :T], z.t[:nk, :T], 0.125, Rt.t[:nk, :T], ALU.mult, ALU.subtract, [z.b, Rt.b], [arg.b])
                        self.tt(Rt.t[:, :T], cs.t[:, :T], Rt.t[:, :T], ALU.add, [cs.b, Rt.b], [Rt.b])
                        wb_ = self.ring("tb")
                        self.act(wb_.t[:nk, :T], arg.t[:nk, :T], AF.Exp, [arg.b], [wb_.b])
                        if d["msk"]:
                            self.tt(wb_.t[:nk, :T], wb_.t[:nk, :T], maskB.t[:nk, d["j0"]:d["j0"] + T], ALU.mult, [wb_.b, maskB.b], [wb_.b])
                        d["w"] = wb_

                    def s6(j):
                        d = st.pop(j)
                        self.mm(bank[2].t[:, :T], d["v"].t[:d["nk"], d["i"], :128], d["w"].t[:d["nk"], :T], j == 0, j == nit - 1, [d["v"].b, d["w"].b], [bank[2].b])

                    for j in range(nit + 2):
                        if j < nit:
                            s12(j)
                        if 1 <= j <= nit:
                            s345(j - 1)
                        if j >= 2:
                            s6(j - 2)
                    self.act(obT.t[ho:ho + 64, hp, :T], bank[2].t[ho:ho + 64, :T], AF.Copy, [bank[2].b], [obT.b])

            def attn_soft(G, T, QT, nkeys_tiles, dst, h, hp, ho, strip=None):
                n_ = len(nkeys_tiles)
                pend = []
                LA = 2

                def stage1(i):
                    kap, vap, nk, j0, deps = nkeys_tiles[i]
                    sc = self.ring("sc")
                    self.mm(sc.t[:nk, :T], kap, QT[h % 2].t[:, hp, :T], True, True, deps + [QT[h % 2].b], [sc.b])
                    P = self.ring("tb")
                    if j0 is not None:
                        tf = self.ring("tf")
                        self.stt(tf.t[:nk, :T], sc.t[:nk, :T], 0.125, strip.t[:nk, h, j0:j0 + T], ALU.mult, ALU.add, [sc.b, strip.b], [tf.b])
                        self.act(P.t[:nk, :T], tf.t[:nk, :T], AF.Exp, [tf.b], [P.b])
                    else:
                        self.act(P.t[:nk, :T], sc.t[:nk, :T], AF.Exp, [sc.b], [P.b], scale=0.125)
                    pend.append(P)

                def stage2(i):
                    kap, vap, nk, j0, deps = nkeys_tiles[i]
                    P = pend[i]
                    self.mm(bank[2].t[:, :T], vap, P.t[:nk, :T], i == 0, i == n_ - 1, deps + [P.b], [bank[2].b])
                    self.mm(bank[3].t[:, :T], onesb.t[:nk, :], P.t[:nk, :T], i == 0, i == n_ - 1, [onesb.b, P.b], [bank[3].b])

                for i in range(n_ + LA):
                    if i < n_:
                        stage1(i)
                    if i >= LA:
                        stage2(i - LA)
                r = self.ring("tf")
                self.act(r.t[ho:ho + 64, :T], bank[3].t[ho:ho + 64, :T], AF.Ln, [bank[3].b], [r.b])
                self.act(r.t[ho:ho + 64, :T], r.t[ho:ho + 64, :T], AF.Exp, [r.b], [r.b], scale=-1.0)
                self.tt(dst.t[ho:ho + 64, hp, :T], bank[2].t[ho:ho + 64, :T], r.t[ho:ho + 64, :T], ALU.mult, [bank[2].b, r.b], [dst.b])

            def attn_C(l, G, t):
                T = G.T
                q0 = G.kidx0c(t)
                hi = q0 + T
                lo = max(0, q0 - G.cback)
                for h in range(4):
                    hp, ho = h // 2, (h % 2) * 64
                    tl = []
                    for (c0, n) in key_chunks((lo // 1024) * 1024 if G.name == "p" else lo, hi, 1024 if G.name == "p" else 2048):
                        kt, v = load_kv(G, "ktc", "vc", hp, 0, 128, hp * 128, 128, c0, n)
                        for i in range((n + 127) // 128):
                            k0 = c0 + i * 128
                            if k0 < lo:
                                continue
                            nk = min(128, c0 + n - k0)
                            delta = k0 - q0
                            tl.append((kt.t[:, i * 128:i * 128 + nk], v.t[:nk, i, :128], nk, G.jcol(384 - delta, WC_MAIN), [kt.b, v.b]))
                    attn_soft(G, T, QTC, tl, ocT, h, hp, ho, strip=stripC)

            def attn_X(l, G, t):
                T = G.T
                mk_, mv_ = mkT[G.name], mvb[G.name]
                for h in range(4):
                    hp, ho = h // 2, (h % 2) * 64
                    tl = []
                    for i in range(2):
                        tl.append((mk_.t[:, hp, i * 128:(i + 1) * 128], mv_.t[:, i, hp * 128:(hp + 1) * 128], 128, None, [mk_.b, mv_.b]))
                    attn_soft(G, T, QTX, tl, oxT, h, hp, ho)

            def merge(l, G):
                np_, nsub, T = G.np, G.nsub, G.T
                gi_ = G.gidx
                mrgs_t = hidT.t[:, 0:8, :].rearrange("p (s a) c -> p s (a c)", s=4)
                for n in range(2):
                    cs_ = slice(n * 512, (n + 1) * 512)
                    wa = load_w("w_br_a", l, 0, n * 512, 512, kc=4)
                    wb_ = load_w("w_br_b", l, 0, n * 512, 512, kc=2)
                    wc = load_w("w_br_c", l, 0, n * 512, 512, kc=2)
                    for s in range(nsub):
                        sl = slice(s * 128, s * 128 + np_)
                        gt = self.ring("gt")
                        self.dma(gt.t[:np_, :, :], gates_d[G.name][gi_][s, :np_, :].rearrange("p (g c) -> p g c", g=3)[:, :, n * 512:(n + 1) * 512], [gates_b[G.name][gi_]], [gt.b])
                        ba, bb, bc = (bank[0], bank[1], bank[6]) if s % 2 == 0 else (bank[2], bank[3], bank[7])
                        for h in range(4):
                            self.mm(ba.t[:np_, :], oaT.t[:, h, sl], wa.t[:, h, :], h == 0, h == 3, [oaT.b, wa.b], [ba.b])
                        for h in range(2):
                            self.mm(bb.t[:np_, :], obT.t[:, h, sl], wb_.t[:, h, :], h == 0, h == 1, [obT.b, wb_.b], [bb.b])
                        for h in range(2):
                            self.mm(bc.t[:np_, :], ocT.t[:, h, sl], wc.t[:, h, :], h == 0, h == 1, [ocT.b, wc.b], [bc.b])
                        m1 = self.ring("tf"); m2 = self.ring("tf")
                        self.tt(m1.t[:np_, :], ba.t[:np_, :], gt.t[:np_, 0, :], ALU.mult, [ba.b, gt.b], [m1.b])
                        self.tt(m2.t[:np_, :], bb.t[:np_, :], gt.t[:np_, 1, :], ALU.mult, [bb.b, gt.b], [m2.b])
                        self.tt(m1.t[:np_, :], m1.t[:np_, :], m2.t[:np_, :], ALU.add, [m1.b, m2.b], [m1.b])
                        self.tt(m2.t[:np_, :], bc.t[:np_, :], gt.t[:np_, 2, :], ALU.mult, [bc.b, gt.b], [m2.b])
                        self.tt(mrgs_t[:np_, s, cs_], m1.t[:np_, :], m2.t[:np_, :], ALU.add, [m1.b, m2.b], [hidT.b])
                for s in range(nsub):
                    sl = slice(s * 128, s * 128 + np_)
                    transpose_to(lambda c, s=s: mrgs_t[:np_, s, c * 128:(c + 1) * 128], [hidT.b], 8, np_, hT.t[:, :, sl], [hT.b])

                def evac(p, s, acc, ncols):
                    self.tt(xin.t[:np_, s, p * 512:(p + 1) * 512], acc.t[:np_, :512], xin.t[:np_, s, p * 512:(p + 1) * 512], ALU.add, [acc.b, xin.b], [xin.b])
                linear_tm("w_out", l, 8, D, lambda k, s: hT.t[:, k, s * 128:s * 128 + np_], [hT.b], np_, nsub, evac)

            def cross(l, G, t):
                np_, nsub, T = G.np, G.nsub, G.T
                rms_T(xin, np_, nsub, l * 5 + 2, hT)

                def evq(p, s, acc, ncols):
                    tb = self.ring("tb")
                    headnorm(acc.t[:np_, :256], [acc.b], np_, 4, l * 6 + 4, tb.t[:np_, :256], [tb.b])
                    transpose_split(lambda c, tb=tb: tb.t[:np_, c * 128:(c + 1) * 128], [tb.b], 2, np_, QTX, slice(s * 128, s * 128 + np_))
                linear_tm("x_wq", l, 8, 256, lambda k, s: hT.t[:, k, s * 128:s * 128 + np_], [hT.b], np_, nsub, evq)
                attn_X(l, G, t)
                for n in range(2):
                    cs_ = slice(n * 512, (n + 1) * 512)
                    wo = load_w("x_wo", l, 0, n * 512, 512, kc=2)
                    for s in range(nsub):
                        sl = slice(s * 128, s * 128 + np_)
                        acc = bank[4 + (s % 2)]
                        for h in range(2):
                            self.mm(acc.t[:np_, :], oxT.t[:, h, sl], wo.t[:, h, :], h == 0, h == 1, [oxT.b, wo.b], [acc.b])
                        self.tt(xin.t[:np_, s, cs_], acc.t[:np_, :], xin.t[:np_, s, cs_], ALU.add, [acc.b, xin.b], [xin.b])

            def mem_kv_prompt(l):
                self.dma(xin.t[:, 0:2, :], mem_p.rearrange("(c p) d -> p c d", p=128), (), [xin.b])
                rms_T(xin, 128, 2, l * 5 + 3, hT)
                mk_, mv_ = mkT["p"], mvb["p"]

                def ev(p, s, acc, ncols):
                    of = self.ring("of")
                    headnorm(acc.t[:, 0:256], [acc.b], 128, 4, l * 6 + 5, of.t[:, 0:256], [of.b])
                    self.act(of.t[:, 256:512], acc.t[:, 256:512], AF.Copy, [acc.b], [of.b])
                    tb = self.ring("tb")
                    self.cp(tb.t[:, :256], of.t[:, 0:256], [of.b], [tb.b])
                    transpose_to(lambda c, tb=tb: tb.t[:, c * 128:(c + 1) * 128], [tb.b], 2, 128, mk_.t[:, :, s * 128:(s + 1) * 128], [mk_.b])
                    self.cp(mv_.t[:, s, :], acc.t[:, 256:512], [acc.b], [mv_.b])
                    self.dma(o_mkp[l, s * 128:(s + 1) * 128, :], of.t[:, 0:256], [of.b], (), q="pool")
                    self.dma(o_mvp[l, s * 128:(s + 1) * 128, :], of.t[:, 256:512], [of.b], (), q="pool")
                linear_tm("x_wkv", l, 8, 512, lambda k, s: hT.t[:, k, s * 128:(s + 1) * 128], [hT.b], 128, 2, ev)

            def prep_kT(src, nrow, F, dst_fn):
                for kt_ in range(nrow // 128):
                    of = self.ring("of")
                    self.dma(of.t[:, :F], src[kt_ * 128:(kt_ + 1) * 128, :], (), [of.b])
                    tb = self.ring("tb")
                    self.cp(tb.t[:, :F], of.t[:, :F], [of.b], [tb.b])
                    dst_fn(kt_, tb)

            def sample_prep(l, G):
                scr, sbf = G.scr, G.sb_
                self.dma(scr["va"][0:PAST, :], ca_v[l], (), [sbf["va"]["cache"][0]], q="pool")
                self.dma(scr["vb"][0:PAST, :], cb_v[l], (), [sbf["vb"]["cache"][0]], q="pool")
                self.dma(scr["vc"][0:CB, :], cc_v[l], (), [sbf["vc"]["cache"][0]], q="pool")
                for (src, nrow, F, kname) in ((ca_k[l], PAST, 512, "kta"), (cb_k[l], PAST, 256, "ktb"), (cc_k[l], CB, 256, "ktc")):
                    def dst_fn(kt_, tb, kname=kname, F=F):
                        ob = self.ring("ob")
                        nb = F // 128
                        transpose_to(lambda c, tb=tb: tb.t[:, c * 128:(c + 1) * 128], [tb.b], nb, 128, ob.t[:, :nb, :128], [ob.b])
                        self.dma(scr[kname][:, :, kt_ * 128:(kt_ + 1) * 128].rearrange("h p s -> p h s"), ob.t[:, :nb, :128], [ob.b], [sbf[kname]["cache"][kt_]], q="pool")
                    prep_kT(src, nrow, F, dst_fn)
                mk_, mv_ = mkT["s"], mvb["s"]

                def dst_m(kt_, tb):
                    transpose_to(lambda c, tb=tb: tb.t[:, c * 128:(c + 1) * 128], [tb.b], 2, 128, mk_.t[:, :, kt_ * 128:(kt_ + 1) * 128], [mk_.b])
                prep_kT(cm_k[l], NMEM, 256, dst_m)

                def dst_v(kt_, tb):
                    self.cp(mv_.t[:, kt_, :], tb.t[:, :256], [tb.b], [mv_.b])
                prep_kT(cm_v[l], NMEM, 256, dst_v)
                self.dma(o_cks[l, 0:CB - NS, :], cc_k[l, NS:CB, :], (), (), q="pool")
                self.dma(o_cvs[l, 0:CB - NS, :], cc_v[l, NS:CB, :], (), (), q="pool")

            NT = NSLOT
            Gp = Group()
            Gp.name, Gp.T, Gp.np, Gp.nsub = "p", 512, 128, 4

            def scrw_p(t):
                x_ = xkv[t % 2]
                return {"kta": x_[0:512, :].rearrange("(h p) s -> h p s", p=128), "va": x_[512:1024, :],
                        "ktb": x_[1024:1280, :].rearrange("(h p) s -> h p s", p=128),
                        "vb": x_[1280:1536, :].rearrange("r (a c) -> (r a) c", c=256),
                        "ktc": x_[1536:1792, :].rearrange("(h p) s -> h p s", p=128),
                        "vc": x_[1792:2048, :].rearrange("r (a c) -> (r a) c", c=256)}
            Gp.scrw = scrw_p
            Gp.wbuf = lambda name, t: xkvb[t % 2]
            Gp.widx0 = lambda t: 0
            Gp.widx0c = lambda t: 0
            Gp.kidx0 = lambda t: (2 * t + 1) * 512
            Gp.kidx0c = lambda t: (2 * t + 1) * 512
            Gp.orow = lambda t: t * 512
            Gp.sthr, Gp.bthr, Gp.cback = -640, -512, 1024
            Gp.jcol = lambda j0, wmain: j0
            Gp.o_ak, Gp.o_av, Gp.o_bk, Gp.o_bv = o_akp, o_avp, o_bkp, o_bvp

            def store_c_p(l, t, s, of):
                if t == NT - 1:
                    r = s * 128
                    self.dma(o_ckp[l, r:r + 128, :], of.t[:, 0:256], [of.b], (), q="pool")
                    self.dma(o_cvp[l, r:r + 128, :], of.t[:, 256:512], [of.b], (), q="pool")
            Gp.store_c = store_c_p

            groups_cc = [[b_, b_ + self.NPAIR] for b_ in range(self.NPAIR)]

            def exchange(l, t):
                src, dst = xkv[t % 2], gout[t]
                self.S.cc(lambda e: e.collective_compute("AllGather", ALU.bypass, replica_groups=groups_cc, ins=[src[:, :]], outs=[dst[:, :]]),
                          [xkvb[t % 2]], [gbuf[t]])

            Gs = Group()
            Gs.name, Gs.T, Gs.np, Gs.nsub = "s", NS, NS, 1
            Gs.scr = kvscr("s", PAST + NS)
            Gs.sb_ = {k_: {"cache": [Buf("%s_s_c%d" % (k_, i)) for i in range(PAST // 128)], 0: Buf(k_ + "_s_n")} for k_ in Gs.scr}
            Gs.scrw = lambda t: Gs.scr
            Gs.wbuf = lambda name, t: Gs.sb_[name][0]
            Gs.widx0 = lambda t: PAST
            Gs.widx0c = lambda t: CB
            Gs.kidx0 = lambda t: PAST
            Gs.kidx0c = lambda t: CB
            Gs.orow = lambda t: 0
            Gs.sthr, Gs.bthr, Gs.cback = -128, -127, 512
            Gs.jcol = lambda j0, wmain: wmain + (j0 - 384)
            Gs.o_ak, Gs.o_av, Gs.o_bk, Gs.o_bv = o_aks, o_avs, o_bks, o_bvs
            Gs.kdeps = lambda name, c0, n: list(Gs.sb_[name]["cache"]) + [Gs.sb_[name][0]]

            def store_c_s(l, t, s, of):
                self.dma(o_cks[l, CB - NS:CB, :], of.t[:NS, 0:256], [of.b], (), q="pool")
                self.dma(o_cvs[l, CB - NS:CB, :], of.t[:NS, 256:512], [of.b], (), q="pool")
            Gs.store_c = store_c_s

            def run_tile(l, G, t, xsrc, xdst, xsrc_deps, xdst_buf):
                np_, nsub, T = G.np, G.nsub, G.T
                self.dma(xin.t[:np_, :nsub, :], xsrc.rearrange("(c p) d -> p c d", p=np_), xsrc_deps, [xin.b])
                if STAGE >= 1:
                    ffn(l, 0, G)
                if STAGE >= 2:
                    proj(l, G, t)
                    if G.name == "p":
                        exchange(l, t)
                        if l == 0 and NL > 1:
                            nsl = min(3, NT)
                            per = (len(use_order) + nsl - 1) // nsl
                            if t < nsl:
                                cast_weights(1, use_order[t * per:(t + 1) * per])
                if STAGE >= 3:
                    attn_A(l, G, t)
                if STAGE >= 4:
                    attn_B(l, G, t)
                if STAGE >= 5:
                    attn_C(l, G, t)
                G.gidx = t % len(gates_d[G.name])
                if STAGE >= 6:
                    merge(l, G)
                if STAGE >= 7:
                    cross(l, G, t)
                if STAGE >= 8:
                    ffn(l, 1, G)
                self.dma(xdst.rearrange("(c p) d -> p c d", p=np_), xin.t[:np_, :nsub, :], [xin.b], xdst_buf, q="pool")

            xb_p = [Buf("xcur_%d" % t) for t in range(NT)]
            xb_s = Buf("xcur_s")
            for l in range(NL):
                layer_setup(l)
                mem_kv_prompt(l)
                for t in range(NT if "p" in KGROUP else 0):
                    src = x_p if l == 0 else xcur
                    dst = xcur if l < NL - 1 else y_p
                    run_tile(l, Gp, t, src[t * 512:(t + 1) * 512, :], dst[t * 512:(t + 1) * 512, :],
                             [xb_p[t]] if l > 0 else [], [xb_p[t]] if l < NL - 1 else [])
                if "s" not in KGROUP:
                    continue
                sample_prep(l, Gs)
                src = x_s if l == 0 else xcur_s
                dst = xcur_s if l < NL - 1 else y_s
                run_tile(l, Gs, 0, src[:, :], dst[:, :], [xb_s] if l > 0 else [], [xb_s] if l < NL - 1 else [])
            S.emit()
        return nc


_CACHE = {}


def _prep_inputs(inp, c, consts2, npair):
    b = c % npair
    hf = c // npair
    consts = consts2[1 - hf]
    f = lambda a: np.ascontiguousarray(a, dtype=np.float32)
    nsb = inp["x_sample"].shape[0]
    cs = c % nsb
    xb = inp["x_prompt"][b]
    SEQ = xb.shape[0]
    m = {
        "x_p": f(xb.reshape(SEQ // 1024, 2, 512, D)[:, hf].reshape(SEQ // 2, D)), "x_s": f(inp["x_sample"][cs]), "mem_p": f(inp["mem_prompt"][b]),
        "ca_k": f(inp["cache_a_k"][:, cs].reshape(NL, -1, 512)), "ca_v": f(inp["cache_a_v"][:, cs].reshape(NL, -1, 512)),
        "cb_k": f(inp["cache_b_k"][:, cs].reshape(NL, -1, 256)), "cb_v": f(inp["cache_b_v"][:, cs].reshape(NL, -1, 256)),
        "cc_k": f(inp["cache_c_k"][:, cs].reshape(NL, -1, 256)), "cc_v": f(inp["cache_c_v"][:, cs].reshape(NL, -1, 256)),
        "cm_k": f(inp["cache_mem_k"][:, cs].reshape(NL, -1, 256)), "cm_v": f(inp["cache_mem_v"][:, cs].reshape(NL, -1, 256)),
        "t5": f(inp["t5_bias"]),
        "gains": f(np.stack([inp["ffn1_norm"], inp["mix_norm"], inp["x_norm"], inp["mem_norm"], inp["ffn2_norm"]], axis=1)),
        "hgains": f(np.stack([inp["a_qnorm"], inp["a_knorm"], inp["c_qnorm"], inp["c_knorm"], inp["x_qnorm"], inp["x_knorm"]], axis=1)),
        "lvec": f(np.stack([inp["a_lq1"], inp["a_lk1"], inp["a_lq2"], inp["a_lk2"]], axis=1)),
        "subln": f(inp["a_subln"]), "crel": f(inp["c_rel_bias"]),
        "wg1": f(inp["ffn1_wg"]), "wu1": f(inp["ffn1_wu"]), "wd1": f(inp["ffn1_wd"]), "w_in": f(inp["w_in"]),
        "w_br_a": f(inp["w_br_a"]), "w_br_b": f(inp["w_br_b"]), "w_br_c": f(inp["w_br_c"]), "w_out": f(inp["w_out"]),
        "x_wq": f(inp["x_wq"]), "x_wkv": f(inp["x_wkv"]), "x_wo": f(inp["x_wo"]),
        "wg2": f(inp["ffn2_wg"]), "wu2": f(inp["ffn2_wu"]), "wd2": f(inp["ffn2_wd"]),
        "oh_a": consts["oh_a"], "oh_c": consts["oh_c"], "mask_a": consts["mask_a"], "mask_b": consts["mask_b"], "mask_c": consts["mask_c"],
    }
    return m


def kernel(**inp):
    inp = {k: np.asarray(v) for k, v in inp.items()}
    B, SEQ = inp["x_prompt"].shape[0], inp["x_prompt"].shape[1]
    NB, NS = inp["x_sample"].shape[0], inp["x_sample"].shape[1]
    PAST = inp["cache_a_k"].shape[2]
    CB = inp["cache_c_k"].shape[2]
    consts2 = [host_consts(0), host_consts(1)]
    npair = B
    ncores = 2 * npair
    key = (SEQ, NS, PAST, CB, npair)
    if key not in _CACHE:
        _CACHE[key] = K(SEQ // 2, NS, PAST, CB, consts2[0]["far_bucket"], NPAIR=npair).build()
    nc = _CACHE[key]
    in_maps = [_prep_inputs(inp, c, consts2, npair) for c in range(ncores)]
    res = run_bass_kernel_spmd(nc, in_maps, core_ids=list(range(ncores)))
    R = res.results
    g = lambda c, name: np.asarray(R[c][name], dtype=np.float32)

    def inter(name, lead):
        outs = []
        for b in range(B):
            a0, a1 = g(b, name), g(b + npair, name)
            sh = a0.shape
            if lead:
                a0 = a0.reshape(sh[0], -1, 512, sh[-1]); a1 = a1.reshape(sh[0], -1, 512, sh[-1])
                outs.append(np.stack([a0, a1], axis=2).reshape(sh[0], SEQ, sh[-1]))
            else:
                a0 = a0.reshape(-1, 512, sh[-1]); a1 = a1.reshape(-1, 512, sh[-1])
                outs.append(np.stack([a0, a1], axis=1).reshape(SEQ, sh[-1]))
        return np.stack(outs, axis=1 if lead else 0)

    stl = lambda name, cores: np.stack([g(c, name) for c in cores], axis=1)
    sc = [c % ncores for c in range(NB)]
    late = [b + npair for b in range(B)]
    early = list(range(B))
    y_p = inter("y_p", False)
    y_s = np.stack([g(c, "y_s") for c in sc])
    outs = [y_p, y_s,
            inter("a_k_p", True).reshape(NL, B, SEQ, 4, 128), inter("a_v_p", True).reshape(NL, B, SEQ, 4, 128),
            inter("b_k_p", True).reshape(NL, B, SEQ, 4, 64), inter("b_v_p", True).reshape(NL, B, SEQ, 4, 64),
            stl("c_k_p", late).reshape(NL, B, 512, 4, 64), stl("c_v_p", late).reshape(NL, B, 512, 4, 64),
            stl("m_k_p", early).reshape(NL, B, NMEM, 4, 64), stl("m_v_p", early).reshape(NL, B, NMEM, 4, 64),
            stl("a_k_s", sc).reshape(NL, NB, NS, 4, 128), stl("a_v_s", sc).reshape(NL, NB, NS, 4, 128),
            stl("b_k_s", sc).reshape(NL, NB, NS, 4, 64), stl("b_v_s", sc).reshape(NL, NB, NS, 4, 64),
            stl("c_k_s", sc).reshape(NL, NB, CB, 4, 64), stl("c_v_s", sc).reshape(NL, NB, CB, 4, 64)]
    return tuple(np.ascontiguousarray(o) for o in outs)
```

```python
import math
import contextlib
import numpy as np
import concourse.bass as bass
import concourse.mybir as mybir
from concourse.bass_utils import run_bass_kernel_spmd

F32 = mybir.dt.float32
BF16 = mybir.dt.bfloat16
ALU = mybir.AluOpType
AF = mybir.ActivationFunctionType
AX = mybir.AxisListType

D = 1024
DFF = 4096
HD = 64
NL = 2
NMEM = 256
CHUNK = 64
C_PREV = 8
REL_CLIP = 128
EPS = 1e-6
NEG = -30000.0
WA_MAIN, WA_S = 1536, 160
WB_MAIN, WB_S = 1408, 32
WC_MAIN, WC_S = 1920, 544
WA_STRIP = WA_MAIN + WA_S
WB_STRIP = WB_MAIN + WB_S
WC_STRIP = WC_MAIN + WC_S
GA_W = 2048
GC_W = 2048

import os
STAGE = int(os.environ.get("KSTAGE", "8"))
KGROUP = os.environ.get("KGROUP", "ps")
POOL_OFF = os.environ.get("KPOOL", "pool")
ENGS = ("pe", "act", "dve", "pool", "sp")
NDMASEM = 24


class Buf:
    __slots__ = ("w", "r", "name", "x")

    def __init__(self, name=""):
        self.w = None
        self.r = []
        self.name = name
        self.x = False


class TT:
    def __init__(self, t, name):
        self.t = t
        self.b = Buf(name)


class Sched:
    def __init__(self, nc):
        self.nc = nc
        self.acts = {e: [] for e in ENGS}
        self.ninst = {e: 0 for e in ENGS}
        self.flag = {e: set() for e in ENGS}
        self.seen = {e: {s: 0 for s in ENGS} for e in ENGS}
        self.dseen = {e: {} for e in ENGS}
        self.dma_n = {e: 0 for e in ENGS}
        self.dma_val = {}

    def _wait(self, e, tok):
        if tok is None:
            return
        if tok[0] == "e":
            _, s, idx = tok
            if self.seen[e][s] >= idx:
                return
            self.flag[s].add(idx)
            self.acts[e].append(("we", s, idx))
            self.seen[e][s] = idx
        else:
            _, q, slot, val = tok
            if self.dseen[e].get((q, slot), 0) >= val:
                return
            self.acts[e].append(("wd", q, slot, val))
            self.dseen[e][(q, slot)] = val

    def _deps(self, e, reads, writes):
        for b in reads:
            self._wait(e, b.w)
            if b.x:
                for t in b.r:
                    if not (t[0] == "e" and t[1] == e):
                        self._wait(e, t)
        for b in writes:
            for t in [b.w] + b.r:
                if t is not None and t[0] == "e" and t[1] == e:
                    continue
                self._wait(e, t)

    def _commit(self, tok, reads, writes):
        for b in reads:
            if tok[0] == "e":
                b.r = [t for t in b.r if not (t[0] == "e" and t[1] == tok[1])]
            else:
                b.r = [t for t in b.r if not (t[0] == "d" and t[1] == tok[1] and t[2] == tok[2])]
            b.r.append(tok)
        for b in writes:
            b.w = tok
            b.r = []

    def op(self, e, fn, reads=(), writes=()):
        self._deps(e, reads, writes)
        self.ninst[e] += 1
        idx = self.ninst[e]
        self.acts[e].append(("i", fn, idx))
        self._commit(("e", e, idx), reads, writes)

    def dma(self, q, out, in_, reads=(), writes=(), **kw):
        self._deps(q, reads, writes)
        n = self.dma_n[q]
        self.dma_n[q] += 1
        slot = n % NDMASEM
        prev = self.dma_val.get((q, slot), 0)
        if prev:
            self._wait(q, ("d", q, slot, prev))
        val = prev + 16
        self.dma_val[(q, slot)] = val
        self.acts[q].append(("d", out, in_, slot, kw))
        self._commit(("d", q, slot, val), reads, writes)

    def cc(self, fn, reads=(), writes=()):
        self._deps("pool", reads, writes)
        self.ncc = getattr(self, "ncc", 0) + 1
        self.acts["pool"].append(("c", fn))
        self._commit(("d", "cc", 0, self.ncc), reads, writes)

    def emit(self):
        nc = self.nc
        for (q, slot), val in sorted(self.dma_val.items()):
            self._wait("sp", ("d", q, slot, val))
        if getattr(self, "ncc", 0):
            self._wait("sp", ("d", "cc", 0, self.ncc))
        with contextlib.ExitStack() as st:
            esem = {e: st.enter_context(nc.semaphore("es_" + e)) for e in ENGS}
            dsem = {}
            for q in ENGS:
                for s in range(min(NDMASEM, self.dma_n[q])):
                    dsem[(q, s)] = st.enter_context(nc.semaphore("ds_%s_%d" % (q, s)))
            if getattr(self, "ncc", 0):
                dsem[("cc", 0)] = st.enter_context(nc.semaphore("ccsem"))
            block = st.enter_context(nc.Block())
            cum = {}
            for e in ENGS:
                cum[e] = {idx: i + 1 for i, idx in enumerate(sorted(self.flag[e]))}

            def run(e, eng):
                for a in self.acts[e]:
                    k = a[0]
                    if k == "we":
                        eng.wait_ge(esem[a[1]], cum[a[1]][a[2]])
                    elif k == "wd":
                        eng.wait_ge(dsem[(a[1], a[2])], a[3])
                    elif k == "c":
                        a[1](eng).then_inc(dsem[("cc", 0)], 1)
                    elif k == "i":
                        ins = a[1](eng)
                        if a[2] in cum[e]:
                            ins.then_inc(esem[e], 1)
                    else:
                        eng.dma_start(out=a[1], in_=a[2], **a[4]).then_inc(dsem[(e, a[3])], 16)

            @block.tensor
            def _(eng):
                run("pe", eng)

            @block.scalar
            def _(eng):
                run("act", eng)

            @block.vector
            def _(eng):
                run("dve", eng)

            @block.gpsimd
            def _(eng):
                run("pool", eng)

            @block.sync
            def _(eng):
                run("sp", eng)


def _t5_bucket_np(rel):
    import jax
    import jax.numpy as jnp
    with jax.default_device(jax.devices("cpu")[0]):
        rel = jnp.asarray(rel, dtype=jnp.int32)
        half = 16
        max_exact = 8
        n = jnp.abs(rel)
        nf = jnp.maximum(n, 1).astype(jnp.float32)
        large = max_exact + (jnp.log(nf / max_exact) / math.log(128 / max_exact) * (half - max_exact)).astype(jnp.int32)
        large = jnp.minimum(large, half - 1)
        out = jnp.where(rel > 0, half, 0) + jnp.where(n < max_exact, n, large)
        return np.asarray(out)


def lambda_init(layer):
    return 0.8 - 0.6 * math.exp(-0.3 * layer)


def host_consts(e=0):
    c = {}
    def oh_a_for(ee):
        s_ = np.arange(GA_W)
        bk = _t5_bucket_np(511 + 512 * ee - s_)
        oh = np.zeros((32, GA_W), np.float32)
        oh[bk, s_] = 1.0
        return oh
    def oh_c_for(ee):
        s_ = np.arange(GC_W)
        idx = np.clip(511 + 512 * ee - s_, -REL_CLIP, REL_CLIP) + REL_CLIP
        oh = np.zeros((384, GC_W), np.float32)
        oh[idx, s_] = 1.0
        return oh
    c["oh_a"] = np.stack([oh_a_for(e), oh_a_for(0)])
    c["oh_c"] = np.stack([oh_c_for(e), oh_c_for(0)])
    c["far_bucket"] = int(_t5_bucket_np(np.array([-4000]))[0])
    k = np.arange(128)[:, None]
    def masks(ee, jj):
        rel = k - jj + 384 + 512 * ee
        dch = k // 64 - jj // 64 + 6 + 8 * ee
        ma = np.where(dch <= 0, 0.0, NEG).astype(np.float32)
        mb = (rel < 0).astype(np.float32)
        mc = np.where((dch <= 0) & (dch >= -C_PREV), 0.0, NEG).astype(np.float32)
        return ma, mb, mc
    ma, _, _ = masks(e, np.arange(WA_MAIN)[None, :])
    _, mb, _ = masks(e, np.arange(WB_MAIN)[None, :])
    _, _, mc = masks(e, np.arange(WC_MAIN)[None, :])
    mas, _, _ = masks(0, np.arange(384, 384 + WA_S)[None, :])
    _, mbs, _ = masks(0, np.arange(384, 384 + WB_S)[None, :])
    _, _, mcs = masks(0, np.arange(384, 384 + WC_S)[None, :])
    c["mask_a"] = np.concatenate([ma, mas], axis=1)
    c["mask_b"] = np.concatenate([mb, mbs], axis=1)
    c["mask_c"] = np.concatenate([mc, mcs], axis=1)
    return c


class Group:
    pass


class K:
    def __init__(self, SEQ, NS, PAST, CB, far_bucket, NPAIR=4):
        self.SEQ, self.NS, self.PAST, self.CB = SEQ, NS, PAST, CB
        self.NPAIR = NPAIR
        self.far_bucket = far_bucket
        self.nc = bass.Bass("TRN2", target_bir_lowering=False)
        self.S = Sched(self.nc)
        self.din = {}
        self.dout = {}

    def dram_in(self, name, shape):
        self.din[name] = self.nc.dram_tensor(name, list(shape), F32, kind="ExternalInput").ap()
        return self.din[name]

    def dram_out(self, name, shape):
        self.dout[name] = self.nc.dram_tensor(name, list(shape), F32, kind="ExternalOutput").ap()
        return self.dout[name]

    def dram_tmp(self, name, shape, dt):
        return self.nc.dram_tensor(name, list(shape), dt).ap()

    def sb(self, name, shape, dt):
        return TT(self.st.enter_context(self.nc.sbuf_tensor(name, list(shape), dt)), name)

    def mm(self, out, lhsT, rhs, start, stop, reads, writes):
        self.S.op("pe", lambda e: e.matmul(out, lhsT=lhsT, rhs=rhs, start=start, stop=stop), reads, writes)

    def act(self, out, in_, func, reads, writes, **kw):
        self.S.op("act", lambda e: e.activation(out=out, in_=in_, func=func, **kw), reads, writes)

    def tt(self, out, in0, in1, op, reads, writes, eng="dve"):
        self.S.op(eng, lambda e: e.tensor_tensor(out=out, in0=in0, in1=in1, op=op), reads, writes)

    def ts(self, out, in0, s1, s2, op0, op1, reads, writes, eng="dve"):
        if s2 is None:
            self.S.op(eng, lambda e: e.tensor_scalar(out=out, in0=in0, scalar1=s1, scalar2=None, op0=op0), reads, writes)
        else:
            self.S.op(eng, lambda e: e.tensor_scalar(out=out, in0=in0, scalar1=s1, scalar2=s2, op0=op0, op1=op1), reads, writes)

    def stt(self, out, in0, scalar, in1, op0, op1, reads, writes):
        self.S.op("dve", lambda e: e.scalar_tensor_tensor(out=out, in0=in0, scalar=scalar, in1=in1, op0=op0, op1=op1), reads, writes)

    def cp(self, out, in_, reads, writes, eng="dve"):
        self.S.op(eng, lambda e: e.tensor_copy(out=out, in_=in_), reads, writes)

    def recip(self, out, in_, reads, writes):
        self.S.op("dve", lambda e: e.reciprocal(out=out, in_=in_), reads, writes)

    def memset(self, ap, val, writes, eng="pool"):
        self.S.op(eng, lambda e: e.memset(ap, val), (), writes)

    def dma(self, out, in_, reads=(), writes=(), q="sp", **kw):
        self.S.dma(q, out, in_, reads, writes, **kw)

    def ring(self, key):
        lst, i = self.rings[key]
        self.rings[key][1] = (i + 1) % len(lst)
        return lst[i]

    def build(self):
        nc = self.nc
        SEQ, NS, PAST, CB = self.SEQ, self.NS, self.PAST, self.CB
        di, do, dt_ = self.dram_in, self.dram_out, self.dram_tmp
        x_p = di("x_p", [SEQ, D]); x_s = di("x_s", [NS, D]); mem_p = di("mem_p", [NMEM, D])
        ca_k = di("ca_k", [NL, PAST, 512]); ca_v = di("ca_v", [NL, PAST, 512])
        cb_k = di("cb_k", [NL, PAST, 256]); cb_v = di("cb_v", [NL, PAST, 256])
        cc_k = di("cc_k", [NL, CB, 256]); cc_v = di("cc_v", [NL, CB, 256])
        cm_k = di("cm_k", [NL, NMEM, 256]); cm_v = di("cm_v", [NL, NMEM, 256])
        t5 = di("t5", [32, 4])
        gains = di("gains", [NL, 5, D])
        hgains = di("hgains", [NL, 6, HD])
        lvec = di("lvec", [NL, 4, HD])
        subln = di("subln", [NL, 128])
        crel = di("crel", [NL, 4, 257])
        W = {}
        wshapes = {"wg1": (D, DFF), "wu1": (D, DFF), "wd1": (DFF, D), "w_in": (D, 6144), "w_br_a": (512, D),
                   "w_br_b": (256, D), "w_br_c": (256, D), "w_out": (D, D), "x_wq": (D, 256), "x_wkv": (D, 512),
                   "x_wo": (256, D), "wg2": (D, DFF), "wu2": (D, DFF), "wd2": (DFF, D)}
        for n_, shp in wshapes.items():
            W[n_] = di(n_, [NL, shp[0], shp[1]])
        oh_a = di("oh_a", [2, 32, GA_W]); oh_c = di("oh_c", [2, 384, GC_W])
        mask_a = di("mask_a", [128, WA_STRIP]); mask_b = di("mask_b", [128, WB_STRIP]); mask_c = di("mask_c", [128, WC_STRIP])

        y_p = do("y_p", [SEQ, D]); y_s = do("y_s", [NS, D])
        o_akp = do("a_k_p", [NL, SEQ, 512]); o_avp = do("a_v_p", [NL, SEQ, 512])
        o_bkp = do("b_k_p", [NL, SEQ, 256]); o_bvp = do("b_v_p", [NL, SEQ, 256])
        CKP = 512
        o_ckp = do("c_k_p", [NL, CKP, 256]); o_cvp = do("c_v_p", [NL, CKP, 256])
        o_mkp = do("m_k_p", [NL, NMEM, 256]); o_mvp = do("m_v_p", [NL, NMEM, 256])
        o_aks = do("a_k_s", [NL, NS, 512]); o_avs = do("a_v_s", [NL, NS, 512])
        o_bks = do("b_k_s", [NL, NS, 256]); o_bvs = do("b_v_s", [NL, NS, 256])
        o_cks = do("c_k_s", [NL, CB, 256]); o_cvs = do("c_v_s", [NL, CB, 256])

        Wb = {}
        Wbuf = {}
        for n_, shp in wshapes.items():
            Wb[n_] = dt_("wb_" + n_, [NL, shp[0], shp[1]], BF16)
        xcur = dt_("xcur", [SEQ, D], F32)
        xcur_s = dt_("xcur_s", [NS, D], F32)
        ga_d = dt_("ga_d", [2, 4, GA_W], F32)
        gc_d = dt_("gc_d", [2, 4, GC_W], F32)
        NSLOT = SEQ // 512
        xkv = [dt_("xkv%d" % i, [2048, 512], BF16) for i in range(2)]
        xkvb = [Buf("xkv%d" % i) for i in range(2)]
        gout = [dt_("gout%d" % i, [4096, 512], BF16) for i in range(NSLOT)]
        gbuf = [Buf("gout%d" % i) for i in range(NSLOT)]

        def kvscr(tag, n):
            g = {}
            g["kta"] = dt_("kta_" + tag, [4, 128, n], BF16); g["va"] = dt_("va_" + tag, [n, 512], BF16)
            g["ktb"] = dt_("ktb_" + tag, [2, 128, n], BF16); g["vb"] = dt_("vb_" + tag, [n, 256], BF16)
            g["ktc"] = dt_("ktc_" + tag, [2, 128, n], BF16); g["vc"] = dt_("vc_" + tag, [n, 256], BF16)
            return g

        with contextlib.ExitStack() as st:
            self.st = st
            sb = self.sb
            S = self.S
            self.bank = [TT(st.enter_context(nc.psum_tensor("bank%d" % i, [128, 512], F32)), "bank%d" % i) for i in range(8)]
            bank = self.bank
            for b_ in bank:
                b_.b.x = True
            ident = sb("ident", [128, 128], BF16); onesb = sb("onesb", [128, 128], BF16)
            onesf = sb("onesf", [128, 128], F32); Jf = sb("Jf", [128, 128], F32)
            negU8 = sb("negU8", [128, 128], BF16); epsc = sb("epsc", [128, 1], F32)
            tmpc = sb("tmpc", [128, 128], F32)
            self.ident, self.onesb, self.onesf, self.negU8, self.epsc = ident, onesb, onesf, negU8, epsc
            self.memset(tmpc.t[:], 0.0, [tmpc.b])
            S.op("pool", lambda e: e.affine_select(out=tmpc.t[:], in_=tmpc.t[:], pattern=[[-1, 128]], compare_op=ALU.not_equal, fill=1.0, base=0, channel_multiplier=1), [tmpc.b], [tmpc.b])
            self.cp(ident.t[:], tmpc.t[:], [tmpc.b], [ident.b])
            identF = sb("identF", [128, 128], F32)
            self.cp(identF.t[:], tmpc.t[:], [tmpc.b], [identF.b])
            self.memset(Jf.t[:], 0.0, [Jf.b])
            S.op("pool", lambda e: e.affine_select(out=Jf.t[:], in_=Jf.t[:], pattern=[[1, 128]], compare_op=ALU.not_equal, fill=1.0, base=-127, channel_multiplier=1), [Jf.b], [Jf.b])
            self.memset(onesf.t[:], 1.0, [onesf.b])
            self.cp(onesb.t[:], onesf.t[:], [onesf.b], [onesb.b])
            tmpc2 = tmpc
            self.memset(tmpc2.t[:], -8.0, [tmpc2.b])
            S.op("pool", lambda e: e.affine_select(out=tmpc2.t[:], in_=tmpc2.t[:], pattern=[[-1, 128]], compare_op=ALU.is_ge, fill=0.0, base=0, channel_multiplier=1), [tmpc2.b], [tmpc2.b])
            self.cp(negU8.t[:], tmpc2.t[:], [tmpc2.b], [negU8.b])
            self.memset(epsc.t[:], EPS, [epsc.b])
            onec = sb("onec", [128, 1], F32)
            self.memset(onec.t[:], 1.0, [onec.b])

            def bc_ap(src, off, n):
                return bass.AP(tensor=src.tensor, offset=off, ap=[[0, 128], [1, n]])

            t5bc = sb("t5bc", [128, 128], F32)
            self.dma(t5bc.t[:], bc_ap(t5, 0, 128), (), [t5bc.b])
            gcol = sb("gcol", [128, NL * 5, 8], F32)
            NG = NL * 5 * 8
            self.dma(tmpc.t[:NG, :], gains.rearrange("l w (c p) -> (l w c) p", p=128), (), [tmpc.b])
            self.mm(bank[1].t[:, :NG], tmpc.t[:NG, :], identF.t[:NG, :NG], True, True, [tmpc.b, identF.b], [bank[1].b])
            self.cp(gcol.t[:].rearrange("p a b -> p (a b)"), bank[1].t[:, :NG], [bank[1].b], [gcol.b])
            hg = sb("hg", [128, NL * 6, HD], F32)
            self.dma(hg.t[:].rearrange("p a b -> p (a b)"), bc_ap(hgains, 0, NL * 6 * HD), (), [hg.b])
            lv = sb("lv", [128, NL * 4, HD], F32)
            self.dma(lv.t[:].rearrange("p a b -> p (a b)"), bc_ap(lvec, 0, NL * 4 * HD), (), [lv.b])
            subcol = sb("subcol", [128, NL], F32)
            self.dma(tmpc.t[:NL, :], subln[:, :], (), [tmpc.b])
            self.mm(bank[1].t[:, :NL], tmpc.t[:NL, :], identF.t[:NL, :NL], True, True, [tmpc.b, identF.b], [bank[1].b])
            self.cp(subcol.t[:], bank[1].t[:, :NL], [bank[1].b], [subcol.b])
            neglam = sb("neglam", [128, NL], F32)
            lt = sb("lt", [128, HD], F32); le = sb("le", [128, 4], F32)
            for l in range(NL):
                for i in range(2):
                    self.tt(lt.t[:], lv.t[:, l * 4 + 2 * i, :], lv.t[:, l * 4 + 2 * i + 1, :], ALU.mult, [lv.b], [lt.b])
                    S.op("dve", lambda e, i=i: e.tensor_reduce(out=le.t[:, i:i + 1], in_=lt.t[:], axis=AX.X, op=ALU.add), [lt.b], [le.b])
                self.act(le.t[:, 2:4], le.t[:, 0:2], AF.Exp, [le.b], [le.b])
                self.tt(neglam.t[:, l:l + 1], le.t[:, 3:4], le.t[:, 2:3], ALU.subtract, [le.b], [neglam.b])
                self.ts(neglam.t[:, l:l + 1], neglam.t[:, l:l + 1], -lambda_init(l), None, ALU.add, ALU.bypass, [neglam.b], [neglam.b])
                self.ts(subcol.t[:, l:l + 1], subcol.t[:, l:l + 1], 1.0 - lambda_init(l), None, ALU.mult, ALU.bypass, [subcol.b], [subcol.b])
            self.neglam, self.subcol, self.hg, self.gcol, self.t5bc = neglam, subcol, hg, gcol, t5bc

            xin = sb("xin", [128, 4, D], F32)
            hT = sb("hT", [128, 8, 512], BF16)
            hidT = sb("hidT", [128, 32, 512], BF16)
            self.rings = {
                "xn": [[sb("xn%d" % i, [128, D], BF16) for i in range(2)], 0],
                "w": [[sb("wslot%d" % i, [128, 8, 512], BF16) for i in range(4)], 0],
                "gt": [[sb("gt%d" % i, [128, 3, 512], BF16) for i in range(2)], 0],
                "tf": [[sb("tf%d" % i, [128, 512], F32) for i in range(4)], 0],
                "tb": [[sb("tb%d" % i, [128, 512], BF16) for i in range(5)], 0],
                "of": [[sb("of%d" % i, [128, 512], F32) for i in range(3)], 0],
                "ob": [[sb("ob%d" % i, [128, 4, 128], BF16) for i in range(2)], 0],
                "kt": [[sb("ktl%d" % i, [128, 1024], BF16) for i in range(2)], 0],
                "v": [[sb("vl%d" % i, [128, 8, 128], BF16) for i in range(2)], 0],
                "sc": [[bank[0], bank[1], bank[6], bank[7]], 0],
                "gu": [[(bank[0], bank[1]), (bank[2], bank[3]), (bank[4], bank[5])], 0],
            }
            QTA = [sb("QTA%d" % i, [128, 4, 512], BF16) for i in range(2)]
            QTB = [sb("QTB%d" % i, [128, 2, 512], BF16) for i in range(2)]
            QTC = [sb("QTC%d" % i, [128, 2, 512], BF16) for i in range(2)]
            QTX = QTC
            for q_ in QTA + QTB + QTC:
                self.memset(q_.t[:], 0.0, [q_.b])
            gates_d = {"p": [dt_("gates_p%d" % i, [4, 128, 3072], BF16) for i in range(2)], "s": [dt_("gates_s", [4, 128, 3072], BF16)]}
            gates_b = {"p": [Buf("gp0"), Buf("gp1")], "s": [Buf("gs")]}
            oaT = sb("oaT", [128, 4, 512], BF16); obT = sb("obT", [128, 2, 512], BF16)
            ocT = sb("ocT", [128, 2, 512], BF16); oxT = ocT
            Rt = sb("Rt", [128, 512], F32)
            mrgb = sb("mrgb", [128, D], BF16); junk = mrgb
            stripA = sb("stripA", [128, 4, WA_STRIP], BF16)
            stripC = sb("stripC", [128, 4, WC_STRIP], BF16)
            maskB = sb("maskB", [128, WB_STRIP], BF16)
            mkT = {g: sb("mkT_" + g, [128, 2, NMEM], BF16) for g in "ps"}
            mvb = {g: sb("mvb_" + g, [128, 2, 256], BF16) for g in "ps"}
            self.rings["hn"] = [[(sb("ssg%d" % i, [128, 8], F32), sb("sdg%d" % i, [128, 8], F32), sb("rsg%d" % i, [128, 8], F32)) for i in range(4)], 0]
            self.rings["rn"] = [[(sb("ss%d" % i, [128, 1], F32), sb("sd%d" % i, [128, 1], F32), sb("rs%d" % i, [128, 1], F32)) for i in range(4)], 0]
            self.xin, self.hT, self.hidT = xin, hT, hidT

            use_order = ["wg1", "wu1", "wd1", "w_in", "w_br_a", "w_br_b", "w_br_c", "w_out", "x_wkv", "x_wq", "x_wo", "wg2", "wu2", "wd2"]
            for n_ in use_order:
                Wbuf[n_] = [[], []]

            def cast_weights(l, names):
                for n_ in names:
                    shp = wshapes[n_]
                    rows = max(1, (1 << 20) // shp[1])
                    for r0 in range(0, shp[0], rows):
                        r1 = min(shp[0], r0 + rows)
                        b_ = Buf("wb_%s_%d_%d" % (n_, l, r0))
                        self.dma(Wb[n_][l, r0:r1, :], W[n_][l, r0:r1, :], (), [b_], q="pool")
                        Wbuf[n_][l].append(b_)
            for l_ in range(NL):
                cast_weights(l_, use_order)

            stg_t = xin.t[:, 0:2, :].rearrange("p a b -> p (a b)")
            stg2_t = xin.t[:, 2:4, :].rearrange("p a b -> p (a b)")
            self.dma(stg_t[:, :WB_STRIP], mask_b[:, :], (), [xin.b])
            self.cp(maskB.t[:], stg_t[:, :WB_STRIP], [xin.b], [maskB.b])
            t5_sb = sb("t5_sb", [32, 4], F32)
            self.dma(t5_sb.t[:], t5[:, :], (), [t5_sb.b])
            gsb_t = stg2_t[:4, :GC_W]
            gab = Buf("ga_d")
            for v_ in range(2):
                self.dma(stg_t[:32, :GA_W], oh_a[v_], (), [xin.b])
                for c0 in range(0, GA_W, 512):
                    self.mm(bank[0].t[:4, :512], t5_sb.t[:, :], stg_t[:32, c0:c0 + 512], True, True, [t5_sb.b, xin.b], [bank[0].b])
                    self.cp(gsb_t[:, c0:c0 + 512], bank[0].t[:4, :512], [bank[0].b], [xin.b])
                self.dma(ga_d[v_], gsb_t[:, :GA_W], [xin.b], [gab])

            def build_strip(gd, gbuf_, gw, goff, jj_start, mask_d, mcol0, width, dst, dcol0):
                self.dma(stg2_t[:, :width], mask_d[:, mcol0:mcol0 + width], (), [xin.b])
                for h in range(4):
                    src = bass.AP(tensor=gd.tensor, offset=goff + h * gw + jj_start, ap=[[1, 128], [1, width]])
                    self.dma(stg_t[:, :width], src, [gbuf_], [xin.b])
                    for c0 in range(0, width, 512):
                        cw = min(512, width - c0)
                        self.mm(bank[0].t[:, :cw], Jf.t[:, :], stg_t[:, c0:c0 + cw], True, True, [Jf.b, xin.b], [bank[0].b])
                        self.tt(dst.t[:, h, dcol0 + c0:dcol0 + c0 + cw], bank[0].t[:, :cw], stg2_t[:, c0:c0 + cw], ALU.add, [bank[0].b, xin.b], [dst.b])

            build_strip(ga_d, gab, GA_W, 0, 0, mask_a, 0, WA_MAIN, stripA, 0)
            build_strip(ga_d, gab, GA_W, 4 * GA_W, 384, mask_a, WA_MAIN, WA_S, stripA, WA_MAIN)
            crT = sb("crT", [128, 3, 4], F32)
            cr_in = sb("cr_in", [4, 264], F32)
            gcb = Buf("gc_d")

            def layer_setup(l):
                self.memset(crT.t[:], 0.0, [crT.b])
                self.dma(cr_in.t[:4, :257], crel[l], (), [cr_in.b])
                for c in range(3):
                    n = min(128, 257 - c * 128)
                    self.mm(bank[1].t[:n, :4], cr_in.t[:4, c * 128:c * 128 + n], identF.t[:4, :4], True, True, [cr_in.b, identF.b], [bank[1].b])
                    self.cp(crT.t[:n, c, :], bank[1].t[:n, :4], [bank[1].b], [crT.b])
                ohv = stg_t[:, :1536].rearrange("p (c s) -> p c s", s=512)
                for v_ in range(2):
                    for c0 in range(0, GC_W, 512):
                        self.dma(ohv, oh_c[v_, :, c0:c0 + 512].rearrange("(c p) s -> p c s", p=128), (), [xin.b])
                        for c in range(3):
                            self.mm(bank[0].t[:4, :512], crT.t[:, c, :], ohv[:, c, :], c == 0, c == 2, [crT.b, xin.b], [bank[0].b])
                        self.cp(gsb_t[:, c0:c0 + 512], bank[0].t[:4, :512], [bank[0].b], [xin.b])
                    self.dma(gc_d[v_], gsb_t[:, :], [xin.b], [gcb])
                build_strip(gc_d, gcb, GC_W, 0, 0, mask_c, 0, WC_MAIN, stripC, 0)
                build_strip(gc_d, gcb, GC_W, 4 * GC_W, 384, mask_c, WC_MAIN, WC_S, stripC, WC_MAIN)

            def load_w(name, l, r0, c0, ncols, kc=8, pk=128):
                slot = self.ring("w")
                src = Wb[name][l, r0:r0 + kc * pk, c0:c0 + ncols].rearrange("(c p) n -> p c n", p=pk)
                self.dma(slot.t[:pk, :kc, :ncols], src, Wbuf[name][l], [slot.b])
                return slot

            def rms_T(src, np_, nsub, gi, dst):
                sc_ = []
                for s in range(nsub):
                    ss_, sd_, rs_ = self.ring("rn")
                    self.memset(ss_.t[:], 0.0, [ss_.b], eng="dve")
                    self.act(junk.t[:np_, :], src.t[:np_, s, :], AF.Square, [src.b], [junk.b, ss_.b], accum_out=ss_.t[:np_, 0:1])
                    sc_.append((ss_, sd_, rs_))
                for s in range(nsub):
                    ss_, sd_, rs_ = sc_[s]
                    self.act(sd_.t[:np_, :], ss_.t[:np_, :], AF.Sqrt, [ss_.b, epsc.b], [sd_.b], scale=1.0 / D, bias=epsc.t[:np_, :])
                    self.recip(rs_.t[:np_, :], sd_.t[:np_, :], [sd_.b], [rs_.b])
                    xn = self.ring("xn")
                    self.ts(xn.t[:np_, :], src.t[:np_, s, :], rs_.t[:np_, 0:1], None, ALU.mult, ALU.bypass, [src.b, rs_.b], [xn.b])
                    self.transpose_to(lambda c, xn=xn: xn.t[:np_, c * 128:(c + 1) * 128], [xn.b], 8, np_,
                                      dst.t[:, :, s * 128:s * 128 + np_], [dst.b],
                                      mul=gcol.t[:, gi, :].unsqueeze(2).to_broadcast([128, 8, np_]), mulb=[gcol.b])

            self.trp_i = 0
            self.acc_i = 0

            def transpose_to(src_fn, srcb, nblk, np_, dst_ap, dstb, mul=None, mulb=()):
                bk = bank[6] if self.trp_i % 2 == 0 else bank[7]
                self.trp_i += 1
                tv = bk.t[:, :].bitcast(BF16).rearrange("p (c n) -> p c n", n=128)
                for c in range(nblk):
                    S.op("pe", lambda e, c=c: e.transpose(out=tv[:, c, :np_], in_=src_fn(c), identity=ident.t[:np_, :np_]), list(srcb) + [ident.b], [bk.b])
                if mul is None:
                    self.act(dst_ap, tv[:, :nblk, :np_], AF.Copy, [bk.b], dstb)
                else:
                    self.tt(dst_ap, tv[:, :nblk, :np_], mul, ALU.mult, [bk.b] + list(mulb), dstb)
            self.transpose_to = transpose_to

            def transpose_split(src_fn, srcb, nblk, np_, q2, cs_):
                bk = bank[6] if self.trp_i % 2 == 0 else bank[7]
                self.trp_i += 1
                tv = bk.t[:, :].bitcast(BF16).rearrange("p (c n) -> p c n", n=128)
                for c in range(nblk):
                    S.op("pe", lambda e, c=c: e.transpose(out=tv[:, c, :np_], in_=src_fn(c), identity=ident.t[:np_, :np_]), list(srcb) + [ident.b], [bk.b])
                self.act(q2[0].t[0:64, :nblk, cs_], tv[0:64, :nblk, :np_], AF.Copy, [bk.b], [q2[0].b])
                self.cp(q2[1].t[64:128, :nblk, cs_], tv[64:128, :nblk, :np_], [bk.b], [q2[1].b])

            def linear_tm(name, l, KC, N, lhs_fn, lhsb, np_, nsub, evac, pieces=None):
                npieces = (N + 511) // 512
                accring = [bank[2], bank[3], bank[4], bank[5], bank[0], bank[1]]
                for p in (pieces if pieces is not None else range(npieces)):
                    ncols = min(512, N - p * 512)
                    if KC <= 8:
                        slot = load_w(name, l, 0, p * 512, ncols, KC)
                        for s in range(nsub):
                            acc = accring[self.acc_i % 6]
                            self.acc_i += 1
                            for k in range(KC):
                                self.mm(acc.t[:np_, :ncols], lhs_fn(k, s), slot.t[:, k, :ncols], k == 0, k == KC - 1,
                                        list(lhsb) + [slot.b], [acc.b])
                            evac(p, s, acc, ncols)
                        continue
                    for kg in range(0, KC, 8):
                        kc = min(8, KC - kg)
                        slot = load_w(name, l, kg * 128, p * 512, ncols, kc)
                        for s in range(nsub):
                            acc = bank[2 + s]
                            for k in range(kc):
                                self.mm(acc.t[:np_, :ncols], lhs_fn(kg + k, s), slot.t[:, k, :ncols], kg + k == 0, kg + k == KC - 1,
                                        list(lhsb) + [slot.b], [acc.b])
                    for s in range(nsub):
                        evac(p, s, bank[2 + s], ncols)

            def headnorm(pb_ap, pbb, np_, G, gi, out_ap, outb):
                tf = self.ring("tf")
                ssg, sdg, rsg = self.ring("hn")
                self.act(tf.t[:np_, :G * 64], pb_ap, AF.Square, pbb, [tf.b])
                S.op("dve", lambda e: e.tensor_reduce(out=ssg.t[:np_, :G], in_=tf.t[:np_, :G * 64].rearrange("p (g d) -> p g d", d=64), axis=AX.X, op=ALU.add), [tf.b], [ssg.b])
                self.act(sdg.t[:np_, :G], ssg.t[:np_, :G], AF.Sqrt, [ssg.b, epsc.b], [sdg.b], scale=1.0 / HD, bias=epsc.t[:np_, :])
                self.recip(rsg.t[:np_, :G], sdg.t[:np_, :G], [sdg.b], [rsg.b])
                tf2 = self.ring("tf")
                self.tt(tf2.t[:np_, :G * 64].rearrange("p (g d) -> p g d", d=64), pb_ap.rearrange("p (g d) -> p g d", d=64),
                        rsg.t[:np_, :G].unsqueeze(2).to_broadcast([np_, G, 64]), ALU.mult, list(pbb) + [rsg.b], [tf2.b])
                self.tt(out_ap.rearrange("p (g d) -> p g d", d=64), tf2.t[:np_, :G * 64].rearrange("p (g d) -> p g d", d=64),
                        hg.t[:np_, gi, :].unsqueeze(1).to_broadcast([np_, G, 64]), ALU.mult, [tf2.b, hg.b], outb, eng=POOL_OFF)

            def ffn(l, which, G):
                np_, nsub, T = G.np, G.nsub, G.T
                sfx = "1" if which == 0 else "2"
                rms_T(xin, np_, nsub, l * 5 + (0 if which == 0 else 4), hT)
                for fg in range(8):
                    wgs = load_w("wg" + sfx, l, 0, fg * 512, 512)
                    wus = load_w("wu" + sfx, l, 0, fg * 512, 512)
                    for fc in range(4):
                        bg, bu = self.ring("gu")
                        for k in range(8):
                            self.mm(bg.t[:, :T], wgs.t[:, k, fc * 128:(fc + 1) * 128], hT.t[:, k, :T], k == 0, k == 7, [wgs.b, hT.b], [bg.b])
                        for k in range(8):
                            self.mm(bu.t[:, :T], wus.t[:, k, fc * 128:(fc + 1) * 128], hT.t[:, k, :T], k == 0, k == 7, [wus.b, hT.b], [bu.b])
                        tf = self.ring("tf")
                        self.act(tf.t[:, :T], bg.t[:, :T], AF.Silu, [bg.b], [tf.b])
                        self.tt(hidT.t[:, fg * 4 + fc, :T], tf.t[:, :T], bu.t[:, :T], ALU.mult, [tf.b, bu.b], [hidT.b])

                def evac(p, s, acc, ncols):
                    self.stt(xin.t[:np_, s, p * 512:(p + 1) * 512], acc.t[:np_, :512], 0.5, xin.t[:np_, s, p * 512:(p + 1) * 512],
                             ALU.mult, ALU.add, [acc.b, xin.b], [xin.b])
                linear_tm("wd" + sfx, l, 32, D, lambda k, s: hidT.t[:, k, s * 128:s * 128 + np_], [hidT.b], np_, nsub, evac)

            def proj(l, G, t):
                np_, nsub, T = G.np, G.nsub, G.T
                i0 = G.widx0(t)
                i0c = G.widx0c(t)
                scr_w = G.scrw(t)
                r0 = G.orow(t)
                rms_T(xin, np_, nsub, l * 5 + 1, hT)

                def out_f(dst, of, w):
                    self.dma(dst, of.t[:np_, :w], [of.b], (), q="pool")

                def kt_out(kname, tb, nb, idx0, s):
                    ob = self.ring("ob")
                    transpose_to(lambda c, tb=tb: tb.t[:np_, c * 128:(c + 1) * 128], [tb.b], nb, np_, ob.t[:, :nb, :np_], [ob.b])
                    self.dma(scr_w[kname][:, :, idx0 + s * 128:idx0 + s * 128 + np_].rearrange("h p s -> p h s"), ob.t[:, :nb, :np_], [ob.b], [G.wbuf(kname, t)], q="pool")

                def v_out(vname, tb, w, idx0, s):
                    if os.environ.get("KNOV") == "1":
                        return
                    self.dma(scr_w[vname][idx0 + s * 128:idx0 + s * 128 + np_, :], tb.t[:np_, :w], [tb.b], [G.wbuf(vname, t)], q="pool")

                def evac(p, s, acc, ncols):
                    pb = acc.t[:np_, :512]
                    rs_ = slice(r0 + s * 128, r0 + s * 128 + np_)
                    if p == 0:
                        tb = self.ring("tb")
                        headnorm(pb, [acc.b], np_, 8, l * 6 + 0, tb.t[:np_, :], [tb.b])
                        transpose_split(lambda c, tb=tb: tb.t[:np_, c * 128:(c + 1) * 128], [tb.b], 4, np_, QTA, slice(s * 128, s * 128 + np_))
                    elif p == 1:
                        of = self.ring("of")
                        headnorm(pb, [acc.b], np_, 8, l * 6 + 1, of.t[:np_, :], [of.b])
                        out_f(G.o_ak[l, rs_, :], of, 512)
                        tb = self.ring("tb")
                        self.cp(tb.t[:np_, :], of.t[:np_, :], [of.b], [tb.b], eng=POOL_OFF)
                        kt_out("kta", tb, 4, i0, s)
                    elif p == 2:
                        of = self.ring("of")
                        self.act(of.t[:np_, :], pb, AF.Copy, [acc.b], [of.b])
                        out_f(G.o_av[l, rs_, :], of, 512)
                        tb = self.ring("tb")
                        self.cp(tb.t[:np_, :], pb, [acc.b], [tb.b])
                        v_out("va", tb, 512, i0, s)
                    elif p == 3:
                        tb = self.ring("tb")
                        self.cp(tb.t[:np_, :256], pb[:, 0:256], [acc.b], [tb.b])
                        transpose_split(lambda c, tb=tb: tb.t[:np_, c * 128:(c + 1) * 128], [tb.b], 2, np_, QTB, slice(s * 128, s * 128 + np_))
                        of = self.ring("of")
                        self.act(of.t[:np_, :256], pb[:, 256:512], AF.Copy, [acc.b], [of.b])
                        out_f(G.o_bk[l, rs_, :], of, 256)
                        tb2 = self.ring("tb")
                        self.cp(tb2.t[:np_, :256], pb[:, 256:512], [acc.b], [tb2.b])
                        kt_out("ktb", tb2, 2, i0, s)
                    elif p == 4:
                        of = self.ring("of")
                        self.act(of.t[:np_, :256], pb[:, 0:256], AF.Copy, [acc.b], [of.b])
                        out_f(G.o_bv[l, rs_, :], of, 256)
                        tb = self.ring("tb")
                        self.cp(tb.t[:np_, :256], pb[:, 0:256], [acc.b], [tb.b])
                        v_out("vb", tb, 256, i0, s)
                        tb2 = self.ring("tb")
                        headnorm(pb[:, 256:512], [acc.b], np_, 4, l * 6 + 2, tb2.t[:np_, :256], [tb2.b])
                        transpose_split(lambda c, tb2=tb2: tb2.t[:np_, c * 128:(c + 1) * 128], [tb2.b], 2, np_, QTC, slice(s * 128, s * 128 + np_))
                    elif p == 5:
                        of = self.ring("of")
                        headnorm(pb[:, 0:256], [acc.b], np_, 4, l * 6 + 3, of.t[:np_, 0:256], [of.b])
                        self.act(of.t[:np_, 256:512], pb[:, 256:512], AF.Copy, [acc.b], [of.b])
                        G.store_c(l, t, s, of)
                        tb = self.ring("tb")
                        self.cp(tb.t[:np_, :512], of.t[:np_, :512], [of.b], [tb.b], eng=POOL_OFF)
                        kt_out("ktc", tb, 2, i0c, s)
                        tb3 = self.ring("tb")
                        self.cp(tb3.t[:np_, :256], pb[:, 256:512], [acc.b], [tb3.b])
                        v_out("vc", tb3, 256, i0c, s)
                    else:
                        tbg = self.ring("tb")
                        self.act(tbg.t[:np_, :], pb, AF.Sigmoid, [acc.b], [tbg.b])
                        gi_ = t % len(gates_d[G.name])
                        self.dma(gates_d[G.name][gi_][s, :np_, (p - 6) * 512:(p - 5) * 512], tbg.t[:np_, :], [tbg.b], [gates_b[G.name][gi_]], q="pool")
                linear_tm("w_in", l, 8, 6144, lambda k, s: hT.t[:, k, s * 128:s * 128 + np_], [hT.b], np_, nsub, evac,
                          pieces=[int(x) for x in os.environ["KPIECES"].split(",")] if "KPIECES" in os.environ else None)

            def key_chunks(lo, hi, csz=1024):
                out = []
                c0 = lo
                while c0 < hi:
                    n = min(csz, hi - c0)
                    out.append((c0, n))
                    c0 += n
                return out

            KBASE = {"kta": 0, "ktb": 1024, "ktc": 1536}
            VBASE = {"va": 512, "vb": 1280, "vc": 1792}

            def load_kv(G, kname, vname, hp_or_h, prow0, nrow, vcol0, dv, c0, n):
                kt = self.ring("kt")
                v = self.ring("v")
                if G.name == "p":
                    slot = c0 // 1024
                    assert c0 % 1024 == 0 and n == 1024
                    g, deps = gout[slot], [gbuf[slot]]
                    for r in range(2):
                        base = r * 2048
                        krow = base + KBASE[kname] + hp_or_h * 128 + prow0
                        self.dma(kt.t[prow0:prow0 + nrow, r * 512:(r + 1) * 512], g[krow:krow + nrow, :], deps, [kt.b])
                        if vname == "va":
                            vsrc = g[base + 512:base + 1024, vcol0:vcol0 + dv]
                        else:
                            vsrc = g[base + VBASE[vname]:base + VBASE[vname] + 256, :].rearrange("r (a c) -> (r a) c", c=256)[:, vcol0:vcol0 + dv]
                        self.dma(v.t[:, r * 4:(r + 1) * 4, :dv], vsrc.rearrange("(c p) f -> p c f", p=128), deps, [v.b])
                    return kt, v
                deps_k = G.kdeps(kname, c0, n)
                deps_v = G.kdeps(vname, c0, n)
                self.dma(kt.t[prow0:prow0 + nrow, :n], G.scr[kname][hp_or_h, prow0:prow0 + nrow, c0:c0 + n], deps_k, [kt.b])
                nfull = n // 128
                if nfull:
                    self.dma(v.t[:, :nfull, :dv], G.scr[vname][c0:c0 + nfull * 128, vcol0:vcol0 + dv].rearrange("(c p) f -> p c f", p=128), deps_v, [v.b])
                rem = n - nfull * 128
                if rem:
                    self.dma(v.t[:rem, nfull, :dv], G.scr[vname][c0 + nfull * 128:c0 + n, vcol0:vcol0 + dv], deps_v, [v.b])
                return kt, v

            def attn_A(l, G, t):
                T = G.T
                q0 = G.kidx0(t)
                hi = q0 + T
                LA = 2
                for h in range(4):
                    O = [bank[2], bank[4]]
                    Sm = [bank[3], bank[5]]
                    chunks = key_chunks(0, hi)
                    items = []
                    for ci, (c0, n) in enumerate(chunks):
                        nt = (n + 127) // 128
                        for i in range(nt):
                            for m in range(2):
                                items.append((ci, c0, n, i, m))
                    nit = len(items)
                    loaded = {}
                    pend = []

                    def stage1(j):
                        ci, c0, n, i, m = items[j]
                        if ci not in loaded:
                            loaded[ci] = load_kv(G, "kta", "va", h, 0, 128, h * 128, 128, c0, n)
                        kt, v = loaded[ci]
                        k0 = c0 + i * 128
                        nk = min(128, c0 + n - k0)
                        delta = k0 - q0
                        sc = self.ring("sc")
                        self.mm(sc.t[:nk, :T], kt.t[:, i * 128:i * 128 + nk], QTA[m].t[:, h, :T], True, True, [kt.b, QTA[m].b], [sc.b])
                        P = self.ring("tb")
                        if delta >= G.sthr:
                            j0 = G.jcol(384 - delta, WA_MAIN)
                            tf = self.ring("tf")
                            self.stt(tf.t[:nk, :T], sc.t[:nk, :T], 0.125, stripA.t[:nk, h, j0:j0 + T], ALU.mult, ALU.add, [sc.b, stripA.b], [tf.b])
                            self.act(P.t[:nk, :T], tf.t[:nk, :T], AF.Exp, [tf.b], [P.b])
                        else:
                            fb = self.far_bucket * 4 + h
                            self.act(P.t[:nk, :T], sc.t[:nk, :T], AF.Exp, [sc.b, t5bc.b], [P.b], scale=0.125, bias=t5bc.t[:nk, fb:fb + 1])
                        pend.append((P, v, i, nk, m, j < 2, j >= nit - 2))

                    def stage2(j):
                        P, v, i, nk, m, first, last = pend[j]
                        self.mm(O[m].t[:, :T], v.t[:nk, i, :128], P.t[:nk, :T], first, last, [v.b, P.b], [O[m].b])
                        self.mm(Sm[m].t[:, :T], onesb.t[:nk, :], P.t[:nk, :T], first, last, [onesb.b, P.b], [Sm[m].b])

                    for j in range(nit + LA):
                        if j < nit:
                            stage1(j)
                        if j >= LA:
                            stage2(j - LA)
                    r1 = self.ring("tf"); t1 = self.ring("tf"); r2 = self.ring("tf"); t2 = self.ring("tf")
                    self.act(r1.t[:, :T], Sm[0].t[:, :T], AF.Ln, [Sm[0].b], [r1.b])
                    self.act(r1.t[:, :T], r1.t[:, :T], AF.Exp, [r1.b], [r1.b], scale=-1.0)
                    self.tt(t1.t[:, :T], O[0].t[:, :T], r1.t[:, :T], ALU.mult, [O[0].b, r1.b], [t1.b])
                    self.act(r2.t[:, :T], Sm[1].t[:, :T], AF.Ln, [Sm[1].b], [r2.b])
                    self.act(r2.t[:, :T], r2.t[:, :T], AF.Exp, [r2.b], [r2.b], scale=-1.0)
                    self.tt(t2.t[:, :T], O[1].t[:, :T], r2.t[:, :T], ALU.mult, [O[1].b, r2.b], [t2.b])
                    self.stt(r1.t[:, :T], t2.t[:, :T], neglam.t[:, l:l + 1], t1.t[:, :T], ALU.mult, ALU.add, [t2.b, t1.b, neglam.b], [r1.b])
                    self.act(r2.t[:, :T], r1.t[:, :T], AF.Square, [r1.b], [r2.b])
                    self.mm(bank[2].t[:, :T], onesf.t[:, :], r2.t[:, :T], True, True, [onesf.b, r2.b], [bank[2].b])
                    self.act(t1.t[:, :T], bank[2].t[:, :T], AF.Ln, [bank[2].b, epsc.b], [t1.b], scale=1.0 / 128, bias=epsc.t[:, :])
                    self.act(t2.t[:, :T], t1.t[:, :T], AF.Exp, [t1.b], [t2.b], scale=-0.5)
                    self.tt(r2.t[:, :T], r1.t[:, :T], t2.t[:, :T], ALU.mult, [r1.b, t2.b], [r2.b])
                    self.ts(oaT.t[:, h, :T], r2.t[:, :T], subcol.t[:, l:l + 1], None, ALU.mult, ALU.bypass, [r2.b, subcol.b], [oaT.b])

            def attn_B(l, G, t):
                T = G.T
                q0 = G.kidx0(t)
                hi = q0 + T
                zring = [bank[0], bank[1], bank[6], bank[4]]
                cring = [bank[7], bank[5], bank[3]]
                for h in range(4):
                    hp, ho = h // 2, (h % 2) * 64
                    self.memset(Rt.t[:, :], 0.0, [Rt.b], eng="dve")
                    chunks = key_chunks(0, hi)[::-1]
                    items = []
                    for ci, (c0, n) in enumerate(chunks):
                        nt = (n + 127) // 128
                        for i in range(nt - 1, -1, -1):
                            items.append((ci, c0, n, i))
                    nit = len(items)
                    loaded = {}
                    st = {}

                    def s12(j):
                        ci, c0, n, i = items[j]
                        if ci not in loaded:
                            loaded[ci] = load_kv(G, "ktb", "vb", hp, 0, 128, hp * 128, 128, c0, n)
                        kt, v = loaded[ci]
                        k0 = c0 + i * 128
                        nk = min(128, c0 + n - k0)
                        delta = k0 - q0
                        msk = delta >= G.bthr
                        j0 = G.jcol(384 - delta, WB_MAIN) if msk else 0
                        z = zring[j % 4]
                        cs = cring[j % 3]
                        self.mm(z.t[:nk, :T], kt.t[:, i * 128:i * 128 + nk], QTB[h % 2].t[:, hp, :T], True, True, [kt.b, QTB[h % 2].b], [z.b])
                        e_ = self.ring("tf")
                        self.act(e_.t[:nk, :T], z.t[:nk, :T], AF.Exp, [z.b], [e_.b], scale=0.125)
                        sp = self.ring("tb")
                        self.act(sp.t[:nk, :T], e_.t[:nk, :T], AF.Ln, [e_.b, onec.b], [sp.b], bias=onec.t[:nk, :])
                        if msk:
                            self.tt(sp.t[:nk, :T], sp.t[:nk, :T], maskB.t[:nk, j0:j0 + T], ALU.mult, [sp.b, maskB.b], [sp.b])
                        st[j] = dict(v=v, i=i, nk=nk, msk=msk, j0=j0, z=z, cs=cs, sp=sp)

                    def s345(j):
                        d = st[j]
                        z, cs, sp, nk = d["z"], d["cs"], d["sp"], d["nk"]
                        S.op("pe", lambda e, z=z, sp=sp, nk=nk: e.matmul(z.t[:nk, :T], lhsT=negU8.t[:nk, :nk], rhs=sp.t[:nk, :T], start=False, stop=True, skip_group_check=True),
                             [negU8.b, sp.b], [z.b])
                        self.mm(cs.t[:, :T], onesb.t[:nk, :], sp.t[:nk, :T], True, True, [onesb.b, sp.b], [cs.b])
                        arg = self.ring("tf")
                        self.stt(arg.t[:nk, :T], z.t[:nk, :T], 0.125, Rt.t[:nk, :T], ALU.mult, ALU.subtract, [z.b, Rt.b], [arg.b])
                        self.tt(Rt.t[:, :T], cs.t[:, :T], Rt.t[:, :T], ALU.add, [cs.b, Rt.b], [Rt.b])
                        wb_ = self.ring("tb")
                        self.act(wb_.t[:nk, :T], arg.t[:nk, :T], AF.Exp, [arg.b], [wb_.b])
                        if d["msk"]:
                            self.tt(wb_.t[:nk, :T], wb_.t[:nk, :T], maskB.t[:nk, d["j0"]:d["j0"] + T], ALU.mult, [wb_.b, maskB.b], [wb_.b])
                        d["w"] = wb_

                    def s6(j):
                        d = st.pop(j)
                        self.mm(bank[2].t[:, :T], d["v"].t[:d["nk"], d["i"], :128], d["w"].t[:d["nk"], :T], j == 0, j == nit - 1, [d["v"].b, d["w"].b], [bank[2].b])

                    for j in range(nit + 2):
                        if j < nit:
                            s12(j)
                        if 1 <= j <= nit:
                            s345(j - 1)
                        if j >= 2:
                            s6(j - 2)
                    self.act(obT.t[ho:ho + 64, hp, :T], bank[2].t[ho:ho + 64, :T], AF.Copy, [bank[2].b], [obT.b])

            def attn_soft(G, T, QT, nkeys_tiles, dst, h, hp, ho, strip=None):
                n_ = len(nkeys_tiles)
                pend = []
                LA = 2

                def stage1(i):
                    kap, vap, nk, j0, deps = nkeys_tiles[i]
                    sc = self.ring("sc")
                    self.mm(sc.t[:nk, :T], kap, QT[h % 2].t[:, hp, :T], True, True, deps + [QT[h % 2].b], [sc.b])
                    P = self.ring("tb")
                    if j0 is not None:
                        tf = self.ring("tf")
                        self.stt(tf.t[:nk, :T], sc.t[:nk, :T], 0.125, strip.t[:nk, h, j0:j0 + T], ALU.mult, ALU.add, [sc.b, strip.b], [tf.b])
                        self.act(P.t[:nk, :T], tf.t[:nk, :T], AF.Exp, [tf.b], [P.b])
                    else:
                        self.act(P.t[:nk, :T], sc.t[:nk, :T], AF.Exp, [sc.b], [P.b], scale=0.125)
                    pend.append(P)

                def stage2(i):
                    kap, vap, nk, j0, deps = nkeys_tiles[i]
                    P = pend[i]
                    self.mm(bank[2].t[:, :T], vap, P.t[:nk, :T], i == 0, i == n_ - 1, deps + [P.b], [bank[2].b])
                    self.mm(bank[3].t[:, :T], onesb.t[:nk, :], P.t[:nk, :T], i == 0, i == n_ - 1, [onesb.b, P.b], [bank[3].b])

                for i in range(n_ + LA):
                    if i < n_:
                        stage1(i)
                    if i >= LA:
                        stage2(i - LA)
                r = self.ring("tf")
                self.act(r.t[ho:ho + 64, :T], bank[3].t[ho:ho + 64, :T], AF.Ln, [bank[3].b], [r.b])
                self.act(r.t[ho:ho + 64, :T], r.t[ho:ho + 64, :T], AF.Exp, [r.b], [r.b], scale=-1.0)
                self.tt(dst.t[ho:ho + 64, hp, :T], bank[2].t[ho:ho + 64, :T], r.t[ho:ho + 64, :T], ALU.mult, [bank[2].b, r.b], [dst.b])

            def attn_C(l, G, t):
                T = G.T
                q0 = G.kidx0c(t)
                hi = q0 + T
                lo = max(0, q0 - G.cback)
                for h in range(4):
                    hp, ho = h // 2, (h % 2) * 64
                    tl = []
                    for (c0, n) in key_chunks((lo // 1024) * 1024 if G.name == "p" else lo, hi, 1024 if G.name == "p" else 2048):
                        kt, v = load_kv(G, "ktc", "vc", hp, 0, 128, hp * 128, 128, c0, n)
                        for i in range((n + 127) // 128):
                            k0 = c0 + i * 128
                            if k0 < lo:
                                continue
                            nk = min(128, c0 + n - k0)
                            delta = k0 - q0
                            tl.append((kt.t[:, i * 128:i * 128 + nk], v.t[:nk, i, :128], nk, G.jcol(384 - delta, WC_MAIN), [kt.b, v.b]))
                    attn_soft(G, T, QTC, tl, ocT, h, hp, ho, strip=stripC)

            def attn_X(l, G, t):
                T = G.T
                mk_, mv_ = mkT[G.name], mvb[G.name]
                for h in range(4):
                    hp, ho = h // 2, (h % 2) * 64
                    tl = []
                    for i in range(2):
                        tl.append((mk_.t[:, hp, i * 128:(i + 1) * 128], mv_.t[:, i, hp * 128:(hp + 1) * 128], 128, None, [mk_.b, mv_.b]))
                    attn_soft(G, T, QTX, tl, oxT, h, hp, ho)

            def merge(l, G):
                np_, nsub, T = G.np, G.nsub, G.T
                gi_ = G.gidx
                mrgs_t = hidT.t[:, 0:8, :].rearrange("p (s a) c -> p s (a c)", s=4)
                for n in range(2):
                    cs_ = slice(n * 512, (n + 1) * 512)
                    wa = load_w("w_br_a", l, 0, n * 512, 512, kc=4)
                    wb_ = load_w("w_br_b", l, 0, n * 512, 512, kc=2)
                    wc = load_w("w_br_c", l, 0, n * 512, 512, kc=2)
                    for s in range(nsub):
                        sl = slice(s * 128, s * 128 + np_)
                        gt = self.ring("gt")
                        self.dma(gt.t[:np_, :, :], gates_d[G.name][gi_][s, :np_, :].rearrange("p (g c) -> p g c", g=3)[:, :, n * 512:(n + 1) * 512], [gates_b[G.name][gi_]], [gt.b])
                        ba, bb, bc = (bank[0], bank[1], bank[6]) if s % 2 == 0 else (bank[2], bank[3], bank[7])
                        for h in range(4):
                            self.mm(ba.t[:np_, :], oaT.t[:, h, sl], wa.t[:, h, :], h == 0, h == 3, [oaT.b, wa.b], [ba.b])
                        for h in range(2):
                            self.mm(bb.t[:np_, :], obT.t[:, h, sl], wb_.t[:, h, :], h == 0, h == 1, [obT.b, wb_.b], [bb.b])
                        for h in range(2):
                            self.mm(bc.t[:np_, :], ocT.t[:, h, sl], wc.t[:, h, :], h == 0, h == 1, [ocT.b, wc.b], [bc.b])
                        m1 = self.ring("tf"); m2 = self.ring("tf")
                        self.tt(m1.t[:np_, :], ba.t[:np_, :], gt.t[:np_, 0, :], ALU.mult, [ba.b, gt.b], [m1.b])
                        self.tt(m2.t[:np_, :], bb.t[:np_, :], gt.t[:np_, 1, :], ALU.mult, [bb.b, gt.b], [m2.b])
                        self.tt(m1.t[:np_, :], m1.t[:np_, :], m2.t[:np_, :], ALU.add, [m1.b, m2.b], [m1.b])
                        self.tt(m2.t[:np_, :], bc.t[:np_, :], gt.t[:np_, 2, :], ALU.mult, [bc.b, gt.b], [m2.b])
                        self.tt(mrgs_t[:np_, s, cs_], m1.t[:np_, :], m2.t[:np_, :], ALU.add, [m1.b, m2.b], [hidT.b])
                for s in range(nsub):
                    sl = slice(s * 128, s * 128 + np_)
                    transpose_to(lambda c, s=s: mrgs_t[:np_, s, c * 128:(c + 1) * 128], [hidT.b], 8, np_, hT.t[:, :, sl], [hT.b])

                def evac(p, s, acc, ncols):
                    self.tt(xin.t[:np_, s, p * 512:(p + 1) * 512], acc.t[:np_, :512], xin.t[:np_, s, p * 512:(p + 1) * 512], ALU.add, [acc.b, xin.b], [xin.b])
                linear_tm("w_out", l, 8, D, lambda k, s: hT.t[:, k, s * 128:s * 128 + np_], [hT.b], np_, nsub, evac)

            def cross(l, G, t):
                np_, nsub, T = G.np, G.nsub, G.T
                rms_T(xin, np_, nsub, l * 5 + 2, hT)

                def evq(p, s, acc, ncols):
                    tb = self.ring("tb")
                    headnorm(acc.t[:np_, :256], [acc.b], np_, 4, l * 6 + 4, tb.t[:np_, :256], [tb.b])
                    transpose_split(lambda c, tb=tb: tb.t[:np_, c * 128:(c + 1) * 128], [tb.b], 2, np_, QTX, slice(s * 128, s * 128 + np_))
                linear_tm("x_wq", l, 8, 256, lambda k, s: hT.t[:, k, s * 128:s * 128 + np_], [hT.b], np_, nsub, evq)
                attn_X(l, G, t)
                for n in range(2):
                    cs_ = slice(n * 512, (n + 1) * 512)
                    wo = load_w("x_wo", l, 0, n * 512, 512, kc=2)
                    for s in range(nsub):
                        sl = slice(s * 128, s * 128 + np_)
                        acc = bank[4 + (s % 2)]
                        for h in range(2):
                            self.mm(acc.t[:np_, :], oxT.t[:, h, sl], wo.t[:, h, :], h == 0, h == 1, [oxT.b, wo.b], [acc.b])
                        self.tt(xin.t[:np_, s, cs_], acc.t[:np_, :], xin.t[:np_, s, cs_], ALU.add, [acc.b, xin.b], [xin.b])

            def mem_kv_prompt(l):
                self.dma(xin.t[:, 0:2, :], mem_p.rearrange("(c p) d -> p c d", p=128), (), [xin.b])
                rms_T(xin, 128, 2, l * 5 + 3, hT)
                mk_, mv_ = mkT["p"], mvb["p"]

                def ev(p, s, acc, ncols):
                    of = self.ring("of")
                    headnorm(acc.t[:, 0:256], [acc.b], 128, 4, l * 6 + 5, of.t[:, 0:256], [of.b])
                    self.act(of.t[:, 256:512], acc.t[:, 256:512], AF.Copy, [acc.b], [of.b])
                    tb = self.ring("tb")
                    self.cp(tb.t[:, :256], of.t[:, 0:256], [of.b], [tb.b])
                    transpose_to(lambda c, tb=tb: tb.t[:, c * 128:(c + 1) * 128], [tb.b], 2, 128, mk_.t[:, :, s * 128:(s + 1) * 128], [mk_.b])
                    self.cp(mv_.t[:, s, :], acc.t[:, 256:512], [acc.b], [mv_.b])
                    self.dma(o_mkp[l, s * 128:(s + 1) * 128, :], of.t[:, 0:256], [of.b], (), q="pool")
                    self.dma(o_mvp[l, s * 128:(s + 1) * 128, :], of.t[:, 256:512], [of.b], (), q="pool")
                linear_tm("x_wkv", l, 8, 512, lambda k, s: hT.t[:, k, s * 128:(s + 1) * 128], [hT.b], 128, 2, ev)

            def prep_kT(src, nrow, F, dst_fn):
                for kt_ in range(nrow // 128):
                    of = self.ring("of")
                    self.dma(of.t[:, :F], src[kt_ * 128:(kt_ + 1) * 128, :], (), [of.b])
                    tb = self.ring("tb")
                    self.cp(tb.t[:, :F], of.t[:, :F], [of.b], [tb.b])
                    dst_fn(kt_, tb)

            def sample_prep(l, G):
                scr, sbf = G.scr, G.sb_
                self.dma(scr["va"][0:PAST, :], ca_v[l], (), [sbf["va"]["cache"][0]], q="pool")
                self.dma(scr["vb"][0:PAST, :], cb_v[l], (), [sbf["vb"]["cache"][0]], q="pool")
                self.dma(scr["vc"][0:CB, :], cc_v[l], (), [sbf["vc"]["cache"][0]], q="pool")
                for (src, nrow, F, kname) in ((ca_k[l], PAST, 512, "kta"), (cb_k[l], PAST, 256, "ktb"), (cc_k[l], CB, 256, "ktc")):
                    def dst_fn(kt_, tb, kname=kname, F=F):
                        ob = self.ring("ob")
                        nb = F // 128
                        transpose_to(lambda c, tb=tb: tb.t[:, c * 128:(c + 1) * 128], [tb.b], nb, 128, ob.t[:, :nb, :128], [ob.b])
                        self.dma(scr[kname][:, :, kt_ * 128:(kt_ + 1) * 128].rearrange("h p s -> p h s"), ob.t[:, :nb, :128], [ob.b], [sbf[kname]["cache"][kt_]], q="pool")
                    prep_kT(src, nrow, F, dst_fn)
                mk_, mv_ = mkT["s"], mvb["s"]

                def dst_m(kt_, tb):
                    transpose_to(lambda c, tb=tb: tb.t[:, c * 128:(c + 1) * 128], [tb.b], 2, 128, mk_.t[:, :, kt_ * 128:(kt_ + 1) * 128], [mk_.b])
                prep_kT(cm_k[l], NMEM, 256, dst_m)

                def dst_v(kt_, tb):
                    self.cp(mv_.t[:, kt_, :], tb.t[:, :256], [tb.b], [mv_.b])
                prep_kT(cm_v[l], NMEM, 256, dst_v)
                self.dma(o_cks[l, 0:CB - NS, :], cc_k[l, NS:CB, :], (), (), q="pool")
                self.dma(o_cvs[l, 0:CB - NS, :], cc_v[l, NS:CB, :], (), (), q="pool")

            NT = NSLOT
            Gp = Group()
            Gp.name, Gp.T, Gp.np, Gp.nsub = "p", 512, 128, 4

            def scrw_p(t):
                x_ = xkv[t % 2]
                return {"kta": x_[0:512, :].rearrange("(h p) s -> h p s", p=128), "va": x_[512:1024, :],
                        "ktb": x_[1024:1280, :].rearrange("(h p) s -> h p s", p=128),
                        "vb": x_[1280:1536, :].rearrange("r (a c) -> (r a) c", c=256),
                        "ktc": x_[1536:1792, :].rearrange("(h p) s -> h p s", p=128),
                        "vc": x_[1792:2048, :].rearrange("r (a c) -> (r a) c", c=256)}
            Gp.scrw = scrw_p
            Gp.wbuf = lambda name, t: xkvb[t % 2]
            Gp.widx0 = lambda t: 0
            Gp.widx0c = lambda t: 0
            Gp.kidx0 = lambda t: (2 * t + 1) * 512
            Gp.kidx0c = lambda t: (2 * t + 1) * 512
            Gp.orow = lambda t: t * 512
            Gp.sthr, Gp.bthr, Gp.cback = -640, -512, 1024
            Gp.jcol = lambda j0, wmain: j0
            Gp.o_ak, Gp.o_av, Gp.o_bk, Gp.o_bv = o_akp, o_avp, o_bkp, o_bvp

            def store_c_p(l, t, s, of):
                if t == NT - 1:
                    r = s * 128
                    self.dma(o_ckp[l, r:r + 128, :], of.t[:, 0:256], [of.b], (), q="pool")
                    self.dma(o_cvp[l, r:r + 128, :], of.t[:, 256:512], [of.b], (), q="pool")
            Gp.store_c = store_c_p

            groups_cc = [[b_, b_ + self.NPAIR] for b_ in range(self.NPAIR)]

            def exchange(l, t):
                src, dst = xkv[t % 2], gout[t]
                self.S.cc(lambda e: e.collective_compute("AllGather", ALU.bypass, replica_groups=groups_cc, ins=[src[:, :]], outs=[dst[:, :]]),
                          [xkvb[t % 2]], [gbuf[t]])

            Gs = Group()
            Gs.name, Gs.T, Gs.np, Gs.nsub = "s", NS, NS, 1
            Gs.scr = kvscr("s", PAST + NS)
            Gs.sb_ = {k_: {"cache": [Buf("%s_s_c%d" % (k_, i)) for i in range(PAST // 128)], 0: Buf(k_ + "_s_n")} for k_ in Gs.scr}
            Gs.scrw = lambda t: Gs.scr
            Gs.wbuf = lambda name, t: Gs.sb_[name][0]
            Gs.widx0 = lambda t: PAST
            Gs.widx0c = lambda t: CB
            Gs.kidx0 = lambda t: PAST
            Gs.kidx0c = lambda t: CB
            Gs.orow = lambda t: 0
            Gs.sthr, Gs.bthr, Gs.cback = -128, -127, 512
            Gs.jcol = lambda j0, wmain: wmain + (j0 - 384)
            Gs.o_ak, Gs.o_av, Gs.o_bk, Gs.o_bv = o_aks, o_avs, o_bks, o_bvs
            Gs.kdeps = lambda name, c0, n: list(Gs.sb_[name]["cache"]) + [Gs.sb_[name][0]]

            def store_c_s(l, t, s, of):
                self.dma(o_cks[l, CB - NS:CB, :], of.t[:NS, 0:256], [of.b], (), q="pool")
                self.dma(o_cvs[l, CB - NS:CB, :], of.t[:NS, 256:512], [of.b], (), q="pool")
            Gs.store_c = store_c_s

            def run_tile(l, G, t, xsrc, xdst, xsrc_deps, xdst_buf):
                np_, nsub, T = G.np, G.nsub, G.T
                self.dma(xin.t[:np_, :nsub, :], xsrc.rearrange("(c p) d -> p c d", p=np_), xsrc_deps, [xin.b])
                if STAGE >= 1:
                    ffn(l, 0, G)
                if STAGE >= 2:
                    proj(l, G, t)
                    if G.name == "p":
                        exchange(l, t)
                if STAGE >= 3:
                    attn_A(l, G, t)
                if STAGE >= 4:
                    attn_B(l, G, t)
                if STAGE >= 5:
                    attn_C(l, G, t)
                G.gidx = t % len(gates_d[G.name])
                if STAGE >= 6:
                    merge(l, G)
                if STAGE >= 7:
                    cross(l, G, t)
                if STAGE >= 8:
                    ffn(l, 1, G)
                self.dma(xdst.rearrange("(c p) d -> p c d", p=np_), xin.t[:np_, :nsub, :], [xin.b], xdst_buf, q="pool")

            xb_p = [Buf("xcur_%d" % t) for t in range(NT)]
            xb_s = Buf("xcur_s")
            for l in range(NL):
                layer_setup(l)
                mem_kv_prompt(l)
                for t in range(NT if "p" in KGROUP else 0):
                    src = x_p if l == 0 else xcur
                    dst = xcur if l < NL - 1 else y_p
                    run_tile(l, Gp, t, src[t * 512:(t + 1) * 512, :], dst[t * 512:(t + 1) * 512, :],
                             [xb_p[t]] if l > 0 else [], [xb_p[t]] if l < NL - 1 else [])
                if "s" not in KGROUP:
                    continue
                sample_prep(l, Gs)
                src = x_s if l == 0 else xcur_s
                dst = xcur_s if l < NL - 1 else y_s
                run_tile(l, Gs, 0, src[:, :], dst[:, :], [xb_s] if l > 0 else [], [xb_s] if l < NL - 1 else [])
            S.emit()
        return nc


_CACHE = {}


def _prep_inputs(inp, c, consts2, npair):
    b = c % npair
    hf = c // npair
    consts = consts2[1 - hf]
    f = lambda a: np.ascontiguousarray(a, dtype=np.float32)
    nsb = inp["x_sample"].shape[0]
    cs = c % nsb
    xb = inp["x_prompt"][b]
    SEQ = xb.shape[0]
    m = {
        "x_p": f(xb.reshape(SEQ // 1024, 2, 512, D)[:, hf].reshape(SEQ // 2, D)), "x_s": f(inp["x_sample"][cs]), "mem_p": f(inp["mem_prompt"][b]),
        "ca_k": f(inp["cache_a_k"][:, cs].reshape(NL, -1, 512)), "ca_v": f(inp["cache_a_v"][:, cs].reshape(NL, -1, 512)),
        "cb_k": f(inp["cache_b_k"][:, cs].reshape(NL, -1, 256)), "cb_v": f(inp["cache_b_v"][:, cs].reshape(NL, -1, 256)),
        "cc_k": f(inp["cache_c_k"][:, cs].reshape(NL, -1, 256)), "cc_v": f(inp["cache_c_v"][:, cs].reshape(NL, -1, 256)),
        "cm_k": f(inp["cache_mem_k"][:, cs].reshape(NL, -1, 256)), "cm_v": f(inp["cache_mem_v"][:, cs].reshape(NL, -1, 256)),
        "t5": f(inp["t5_bias"]),
        "gains": f(np.stack([inp["ffn1_norm"], inp["mix_norm"], inp["x_norm"], inp["mem_norm"], inp["ffn2_norm"]], axis=1)),
        "hgains": f(np.stack([inp["a_qnorm"], inp["a_knorm"], inp["c_qnorm"], inp["c_knorm"], inp["x_qnorm"], inp["x_knorm"]], axis=1)),
        "lvec": f(np.stack([inp["a_lq1"], inp["a_lk1"], inp["a_lq2"], inp["a_lk2"]], axis=1)),
        "subln": f(inp["a_subln"]), "crel": f(inp["c_rel_bias"]),
        "wg1": f(inp["ffn1_wg"]), "wu1": f(inp["ffn1_wu"]), "wd1": f(inp["ffn1_wd"]), "w_in": f(inp["w_in"]),
        "w_br_a": f(inp["w_br_a"]), "w_br_b": f(inp["w_br_b"]), "w_br_c": f(inp["w_br_c"]), "w_out": f(inp["w_out"]),
        "x_wq": f(inp["x_wq"]), "x_wkv": f(inp["x_wkv"]), "x_wo": f(inp["x_wo"]),
        "wg2": f(inp["ffn2_wg"]), "wu2": f(inp["ffn2_wu"]), "wd2": f(inp["ffn2_wd"]),
        "oh_a": consts["oh_a"], "oh_c": consts["oh_c"], "mask_a": consts["mask_a"], "mask_b": consts["mask_b"], "mask_c": consts["mask_c"],
    }
    return m


def kernel(**inp):
    inp = {k: np.asarray(v) for k, v in inp.items()}
    B, SEQ = inp["x_prompt"].shape[0], inp["x_prompt"].shape[1]
    NB, NS = inp["x_sample"].shape[0], inp["x_sample"].shape[1]
    PAST = inp["cache_a_k"].shape[2]
    CB = inp["cache_c_k"].shape[2]
    consts2 = [host_consts(0), host_consts(1)]
    npair = B
    ncores = 2 * npair
    key = (SEQ, NS, PAST, CB, npair)
    if key not in _CACHE:
        _CACHE[key] = K(SEQ // 2, NS, PAST, CB, consts2[0]["far_bucket"], NPAIR=npair).build()
    nc = _CACHE[key]
    in_maps = [_prep_inputs(inp, c, consts2, npair) for c in range(ncores)]
    res = run_bass_kernel_spmd(nc, in_maps, core_ids=list(range(ncores)))
    R = res.results
    g = lambda c, name: np.asarray(R[c][name], dtype=np.float32)

    def inter(name, lead):
        outs = []
        for b in range(B):
            a0, a1 = g(b, name), g(b + npair, name)
            sh = a0.shape
            if lead:
                a0 = a0.reshape(sh[0], -1, 512, sh[-1]); a1 = a1.reshape(sh[0], -1, 512, sh[-1])
                outs.append(np.stack([a0, a1], axis=2).reshape(sh[0], SEQ, sh[-1]))
            else:
                a0 = a0.reshape(-1, 512, sh[-1]); a1 = a1.reshape(-1, 512, sh[-1])
                outs.append(np.stack([a0, a1], axis=1).reshape(SEQ, sh[-1]))
        return np.stack(outs, axis=1 if lead else 0)

    stl = lambda name, cores: np.stack([g(c, name) for c in cores], axis=1)
    sc = [c % ncores for c in range(NB)]
    late = [b + npair for b in range(B)]
    early = list(range(B))
    y_p = inter("y_p", False)
    y_s = np.stack([g(c, "y_s") for c in sc])
    outs = [y_p, y_s,
            inter("a_k_p", True).reshape(NL, B, SEQ, 4, 128), inter("a_v_p", True).reshape(NL, B, SEQ, 4, 128),
            inter("b_k_p", True).reshape(NL, B, SEQ, 4, 64), inter("b_v_p", True).reshape(NL, B, SEQ, 4, 64),
            stl("c_k_p", late).reshape(NL, B, 512, 4, 64), stl("c_v_p", late).reshape(NL, B, 512, 4, 64),
            stl("m_k_p", early).reshape(NL, B, NMEM, 4, 64), stl("m_v_p", early).reshape(NL, B, NMEM, 4, 64),
            stl("a_k_s", sc).reshape(NL, NB, NS, 4, 128), stl("a_v_s", sc).reshape(NL, NB, NS, 4, 128),
            stl("b_k_s", sc).reshape(NL, NB, NS, 4, 64), stl("b_v_s", sc).reshape(NL, NB, NS, 4, 64),
            stl("c_k_s", sc).reshape(NL, NB, CB, 4, 64), stl("c_v_s", sc).reshape(NL, NB, CB, 4, 64)]
    return tuple(np.ascontiguousarray(o) for o in outs)
```

```python
import math
import contextlib
import numpy as np
import concourse.bass as bass
import concourse.mybir as mybir
from concourse.bass_utils import run_bass_kernel_spmd

F32 = mybir.dt.float32
BF16 = mybir.dt.bfloat16
ALU = mybir.AluOpType
AF = mybir.ActivationFunctionType
AX = mybir.AxisListType

D = 1024
DFF = 4096
HD = 64
NL = 2
NMEM = 256
CHUNK = 64
C_PREV = 8
REL_CLIP = 128
EPS = 1e-6
NEG = -30000.0
WA_MAIN, WA_S = 1536, 160
WB_MAIN, WB_S = 1408, 32
WC_MAIN, WC_S = 1920, 544
WA_STRIP = WA_MAIN + WA_S
WB_STRIP = WB_MAIN + WB_S
WC_STRIP = WC_MAIN + WC_S
GA_W = 2048
GC_W = 2048

import os
STAGE = int(os.environ.get("KSTAGE", "8"))
KGROUP = os.environ.get("KGROUP", "ps")
POOL_OFF = os.environ.get("KPOOL", "dve")
ENGS = ("pe", "act", "dve", "pool", "sp")
NDMASEM = 24


class Buf:
    __slots__ = ("w", "r", "name", "x")

    def __init__(self, name=""):
        self.w = None
        self.r = []
        self.name = name
        self.x = False


class TT:
    def __init__(self, t, name):
        self.t = t
        self.b = Buf(name)


class Sched:
    def __init__(self, nc):
        self.nc = nc
        self.acts = {e: [] for e in ENGS}
        self.ninst = {e: 0 for e in ENGS}
        self.flag = {e: set() for e in ENGS}
        self.seen = {e: {s: 0 for s in ENGS} for e in ENGS}
        self.dseen = {e: {} for e in ENGS}
        self.dma_n = {e: 0 for e in ENGS}
        self.dma_val = {}

    def _wait(self, e, tok):
        if tok is None:
            return
        if tok[0] == "e":
            _, s, idx = tok
            if self.seen[e][s] >= idx:
                return
            self.flag[s].add(idx)
            self.acts[e].append(("we", s, idx))
            self.seen[e][s] = idx
        else:
            _, q, slot, val = tok
            if self.dseen[e].get((q, slot), 0) >= val:
                return
            self.acts[e].append(("wd", q, slot, val))
            self.dseen[e][(q, slot)] = val

    def _deps(self, e, reads, writes):
        for b in reads:
            self._wait(e, b.w)
            if b.x:
                for t in b.r:
                    if not (t[0] == "e" and t[1] == e):
                        self._wait(e, t)
        for b in writes:
            for t in [b.w] + b.r:
                if t is not None and t[0] == "e" and t[1] == e:
                    continue
                self._wait(e, t)

    def _commit(self, tok, reads, writes):
        for b in reads:
            if tok[0] == "e":
                b.r = [t for t in b.r if not (t[0] == "e" and t[1] == tok[1])]
            else:
                b.r = [t for t in b.r if not (t[0] == "d" and t[1] == tok[1] and t[2] == tok[2])]
            b.r.append(tok)
        for b in writes:
            b.w = tok
            b.r = []

    def op(self, e, fn, reads=(), writes=()):
        self._deps(e, reads, writes)
        self.ninst[e] += 1
        idx = self.ninst[e]
        self.acts[e].append(("i", fn, idx))
        self._commit(("e", e, idx), reads, writes)

    def dma(self, q, out, in_, reads=(), writes=(), **kw):
        self._deps(q, reads, writes)
        n = self.dma_n[q]
        self.dma_n[q] += 1
        slot = n % NDMASEM
        prev = self.dma_val.get((q, slot), 0)
        if prev:
            self._wait(q, ("d", q, slot, prev))
        val = prev + 16
        self.dma_val[(q, slot)] = val
        self.acts[q].append(("d", out, in_, slot, kw))
        self._commit(("d", q, slot, val), reads, writes)

    def cc(self, fn, reads=(), writes=()):
        self._deps("pool", reads, writes)
        self.ncc = getattr(self, "ncc", 0) + 1
        self.acts["pool"].append(("c", fn))
        self._commit(("d", "cc", 0, self.ncc), reads, writes)

    def emit(self):
        nc = self.nc
        for (q, slot), val in sorted(self.dma_val.items()):
            self._wait("sp", ("d", q, slot, val))
        if getattr(self, "ncc", 0):
            self._wait("sp", ("d", "cc", 0, self.ncc))
        with contextlib.ExitStack() as st:
            esem = {e: st.enter_context(nc.semaphore("es_" + e)) for e in ENGS}
            dsem = {}
            for q in ENGS:
                for s in range(min(NDMASEM, self.dma_n[q])):
                    dsem[(q, s)] = st.enter_context(nc.semaphore("ds_%s_%d" % (q, s)))
            if getattr(self, "ncc", 0):
                dsem[("cc", 0)] = st.enter_context(nc.semaphore("ccsem"))
            block = st.enter_context(nc.Block())
            cum = {}
            for e in ENGS:
                cum[e] = {idx: i + 1 for i, idx in enumerate(sorted(self.flag[e]))}

            def run(e, eng):
                for a in self.acts[e]:
                    k = a[0]
                    if k == "we":
                        eng.wait_ge(esem[a[1]], cum[a[1]][a[2]])
                    elif k == "wd":
                        eng.wait_ge(dsem[(a[1], a[2])], a[3])
                    elif k == "c":
                        a[1](eng).then_inc(dsem[("cc", 0)], 1)
                    elif k == "i":
                        ins = a[1](eng)
                        if a[2] in cum[e]:
                            ins.then_inc(esem[e], 1)
                    else:
                        eng.dma_start(out=a[1], in_=a[2], **a[4]).then_inc(dsem[(e, a[3])], 16)

            @block.tensor
            def _(eng):
                run("pe", eng)

            @block.scalar
            def _(eng):
                run("act", eng)

            @block.vector
            def _(eng):
                run("dve", eng)

            @block.gpsimd
            def _(eng):
                run("pool", eng)

            @block.sync
            def _(eng):
                run("sp", eng)


def _t5_bucket_np(rel):
    import jax
    import jax.numpy as jnp
    with jax.default_device(jax.devices("cpu")[0]):
        rel = jnp.asarray(rel, dtype=jnp.int32)
        half = 16
        max_exact = 8
        n = jnp.abs(rel)
        nf = jnp.maximum(n, 1).astype(jnp.float32)
        large = max_exact + (jnp.log(nf / max_exact) / math.log(128 / max_exact) * (half - max_exact)).astype(jnp.int32)
        large = jnp.minimum(large, half - 1)
        out = jnp.where(rel > 0, half, 0) + jnp.where(n < max_exact, n, large)
        return np.asarray(out)


def lambda_init(layer):
    return 0.8 - 0.6 * math.exp(-0.3 * layer)


def host_consts(e=0):
    c = {}
    def oh_a_for(ee):
        s_ = np.arange(GA_W)
        bk = _t5_bucket_np(511 + 512 * ee - s_)
        oh = np.zeros((32, GA_W), np.float32)
        oh[bk, s_] = 1.0
        return oh
    def oh_c_for(ee):
        s_ = np.arange(GC_W)
        idx = np.clip(511 + 512 * ee - s_, -REL_CLIP, REL_CLIP) + REL_CLIP
        oh = np.zeros((384, GC_W), np.float32)
        oh[idx, s_] = 1.0
        return oh
    c["oh_a"] = np.stack([oh_a_for(e), oh_a_for(0)])
    c["oh_c"] = np.stack([oh_c_for(e), oh_c_for(0)])
    c["far_bucket"] = int(_t5_bucket_np(np.array([-4000]))[0])
    k = np.arange(128)[:, None]
    def masks(ee, jj):
        rel = k - jj + 384 + 512 * ee
        dch = k // 64 - jj // 64 + 6 + 8 * ee
        ma = np.where(dch <= 0, 0.0, NEG).astype(np.float32)
        mb = (rel < 0).astype(np.float32)
        mc = np.where((dch <= 0) & (dch >= -C_PREV), 0.0, NEG).astype(np.float32)
        return ma, mb, mc
    ma, _, _ = masks(e, np.arange(WA_MAIN)[None, :])
    _, mb, _ = masks(e, np.arange(WB_MAIN)[None, :])
    _, _, mc = masks(e, np.arange(WC_MAIN)[None, :])
    mas, _, _ = masks(0, np.arange(384, 384 + WA_S)[None, :])
    _, mbs, _ = masks(0, np.arange(384, 384 + WB_S)[None, :])
    _, _, mcs = masks(0, np.arange(384, 384 + WC_S)[None, :])
    c["mask_a"] = np.concatenate([ma, mas], axis=1)
    c["mask_b"] = np.concatenate([mb, mbs], axis=1)
    c["mask_c"] = np.concatenate([mc, mcs], axis=1)
    return c


class Group:
    pass


class K:
    def __init__(self, SEQ, NS, PAST, CB, far_bucket, NPAIR=4):
        self.SEQ, self.NS, self.PAST, self.CB = SEQ, NS, PAST, CB
        self.NPAIR = NPAIR
        self.far_bucket = far_bucket
        self.nc = bass.Bass("TRN2", target_bir_lowering=False)
        self.S = Sched(self.nc)
        self.din = {}
        self.dout = {}

    def dram_in(self, name, shape):
        self.din[name] = self.nc.dram_tensor(name, list(shape), F32, kind="ExternalInput").ap()
        return self.din[name]

    def dram_out(self, name, shape):
        self.dout[name] = self.nc.dram_tensor(name, list(shape), F32, kind="ExternalOutput").ap()
        return self.dout[name]

    def dram_tmp(self, name, shape, dt):
        return self.nc.dram_tensor(name, list(shape), dt).ap()

    def sb(self, name, shape, dt):
        return TT(self.st.enter_context(self.nc.sbuf_tensor(name, list(shape), dt)), name)

    def mm(self, out, lhsT, rhs, start, stop, reads, writes):
        self.S.op("pe", lambda e: e.matmul(out, lhsT=lhsT, rhs=rhs, start=start, stop=stop), reads, writes)

    def act(self, out, in_, func, reads, writes, **kw):
        self.S.op("act", lambda e: e.activation(out=out, in_=in_, func=func, **kw), reads, writes)

    def tt(self, out, in0, in1, op, reads, writes, eng="dve"):
        self.S.op(eng, lambda e: e.tensor_tensor(out=out, in0=in0, in1=in1, op=op), reads, writes)

    def ts(self, out, in0, s1, s2, op0, op1, reads, writes, eng="dve"):
        if s2 is None:
            self.S.op(eng, lambda e: e.tensor_scalar(out=out, in0=in0, scalar1=s1, scalar2=None, op0=op0), reads, writes)
        else:
            self.S.op(eng, lambda e: e.tensor_scalar(out=out, in0=in0, scalar1=s1, scalar2=s2, op0=op0, op1=op1), reads, writes)

    def stt(self, out, in0, scalar, in1, op0, op1, reads, writes):
        self.S.op("dve", lambda e: e.scalar_tensor_tensor(out=out, in0=in0, scalar=scalar, in1=in1, op0=op0, op1=op1), reads, writes)

    def cp(self, out, in_, reads, writes, eng="dve"):
        self.S.op(eng, lambda e: e.tensor_copy(out=out, in_=in_), reads, writes)

    def recip(self, out, in_, reads, writes):
        self.S.op("dve", lambda e: e.reciprocal(out=out, in_=in_), reads, writes)

    def memset(self, ap, val, writes, eng="pool"):
        self.S.op(eng, lambda e: e.memset(ap, val), (), writes)

    def dma(self, out, in_, reads=(), writes=(), q="sp", **kw):
        self.S.dma(q, out, in_, reads, writes, **kw)

    def ring(self, key):
        lst, i = self.rings[key]
        self.rings[key][1] = (i + 1) % len(lst)
        return lst[i]

    def build(self):
        nc = self.nc
        SEQ, NS, PAST, CB = self.SEQ, self.NS, self.PAST, self.CB
        di, do, dt_ = self.dram_in, self.dram_out, self.dram_tmp
        x_p = di("x_p", [SEQ, D]); x_s = di("x_s", [NS, D]); mem_p = di("mem_p", [NMEM, D])
        ca_k = di("ca_k", [NL, PAST, 512]); ca_v = di("ca_v", [NL, PAST, 512])
        cb_k = di("cb_k", [NL, PAST, 256]); cb_v = di("cb_v", [NL, PAST, 256])
        cc_k = di("cc_k", [NL, CB, 256]); cc_v = di("cc_v", [NL, CB, 256])
        cm_k = di("cm_k", [NL, NMEM, 256]); cm_v = di("cm_v", [NL, NMEM, 256])
        t5 = di("t5", [32, 4])
        gains = di("gains", [NL, 5, D])
        hgains = di("hgains", [NL, 6, HD])
        lvec = di("lvec", [NL, 4, HD])
        subln = di("subln", [NL, 128])
        crel = di("crel", [NL, 4, 257])
        W = {}
        wshapes = {"wg1": (D, DFF), "wu1": (D, DFF), "wd1": (DFF, D), "w_in": (D, 6144), "w_br_a": (512, D),
                   "w_br_b": (256, D), "w_br_c": (256, D), "w_out": (D, D), "x_wq": (D, 256), "x_wkv": (D, 512),
                   "x_wo": (256, D), "wg2": (D, DFF), "wu2": (D, DFF), "wd2": (DFF, D)}
        for n_, shp in wshapes.items():
            W[n_] = di(n_, [NL, shp[0], shp[1]])
        oh_a = di("oh_a", [2, 32, GA_W]); oh_c = di("oh_c", [2, 384, GC_W])
        mask_a = di("mask_a", [128, WA_STRIP]); mask_b = di("mask_b", [128, WB_STRIP]); mask_c = di("mask_c", [128, WC_STRIP])

        y_p = do("y_p", [SEQ, D]); y_s = do("y_s", [NS, D])
        o_akp = do("a_k_p", [NL, SEQ, 512]); o_avp = do("a_v_p", [NL, SEQ, 512])
        o_bkp = do("b_k_p", [NL, SEQ, 256]); o_bvp = do("b_v_p", [NL, SEQ, 256])
        CKP = 512
        o_ckp = do("c_k_p", [NL, CKP, 256]); o_cvp = do("c_v_p", [NL, CKP, 256])
        o_mkp = do("m_k_p", [NL, NMEM, 256]); o_mvp = do("m_v_p", [NL, NMEM, 256])
        o_aks = do("a_k_s", [NL, NS, 512]); o_avs = do("a_v_s", [NL, NS, 512])
        o_bks = do("b_k_s", [NL, NS, 256]); o_bvs = do("b_v_s", [NL, NS, 256])
        o_cks = do("c_k_s", [NL, CB, 256]); o_cvs = do("c_v_s", [NL, CB, 256])

        Wb = {}
        Wbuf = {}
        for n_, shp in wshapes.items():
            Wb[n_] = dt_("wb_" + n_, [NL, shp[0], shp[1]], BF16)
        xcur = dt_("xcur", [SEQ, D], F32)
        xcur_s = dt_("xcur_s", [NS, D], F32)
        ga_d = dt_("ga_d", [2, 4, GA_W], F32)
        gc_d = dt_("gc_d", [2, 4, GC_W], F32)
        NSLOT = SEQ // 512
        xkv = [dt_("xkv%d" % i, [2048, 512], BF16) for i in range(2)]
        xkvb = [Buf("xkv%d" % i) for i in range(2)]
        gout = [dt_("gout%d" % i, [4096, 512], BF16) for i in range(NSLOT)]
        gbuf = [Buf("gout%d" % i) for i in range(NSLOT)]

        def kvscr(tag, n):
            g = {}
            g["kta"] = dt_("kta_" + tag, [4, 128, n], BF16); g["va"] = dt_("va_" + tag, [n, 512], BF16)
            g["ktb"] = dt_("ktb_" + tag, [2, 128, n], BF16); g["vb"] = dt_("vb_" + tag, [n, 256], BF16)
            g["ktc"] = dt_("ktc_" + tag, [2, 128, n], BF16); g["vc"] = dt_("vc_" + tag, [n, 256], BF16)
            return g

        with contextlib.ExitStack() as st:
            self.st = st
            sb = self.sb
            S = self.S
            self.bank = [TT(st.enter_context(nc.psum_tensor("bank%d" % i, [128, 512], F32)), "bank%d" % i) for i in range(8)]
            bank = self.bank
            for b_ in bank:
                b_.b.x = True
            ident = sb("ident", [128, 128], BF16); onesb = sb("onesb", [128, 128], BF16)
            onesf = sb("onesf", [128, 128], F32); Jf = sb("Jf", [128, 128], F32)
            negU8 = sb("negU8", [128, 128], BF16); epsc = sb("epsc", [128, 1], F32)
            tmpc = sb("tmpc", [128, 128], F32)
            self.ident, self.onesb, self.onesf, self.negU8, self.epsc = ident, onesb, onesf, negU8, epsc
            self.memset(tmpc.t[:], 0.0, [tmpc.b])
            S.op("pool", lambda e: e.affine_select(out=tmpc.t[:], in_=tmpc.t[:], pattern=[[-1, 128]], compare_op=ALU.not_equal, fill=1.0, base=0, channel_multiplier=1), [tmpc.b], [tmpc.b])
            self.cp(ident.t[:], tmpc.t[:], [tmpc.b], [ident.b])
            identF = sb("identF", [128, 128], F32)
            self.cp(identF.t[:], tmpc.t[:], [tmpc.b], [identF.b])
            self.memset(Jf.t[:], 0.0, [Jf.b])
            S.op("pool", lambda e: e.affine_select(out=Jf.t[:], in_=Jf.t[:], pattern=[[1, 128]], compare_op=ALU.not_equal, fill=1.0, base=-127, channel_multiplier=1), [Jf.b], [Jf.b])
            self.memset(onesf.t[:], 1.0, [onesf.b])
            self.cp(onesb.t[:], onesf.t[:], [onesf.b], [onesb.b])
            tmpc2 = tmpc
            self.memset(tmpc2.t[:], -8.0, [tmpc2.b])
            S.op("pool", lambda e: e.affine_select(out=tmpc2.t[:], in_=tmpc2.t[:], pattern=[[-1, 128]], compare_op=ALU.is_ge, fill=0.0, base=0, channel_multiplier=1), [tmpc2.b], [tmpc2.b])
            self.cp(negU8.t[:], tmpc2.t[:], [tmpc2.b], [negU8.b])
            self.memset(epsc.t[:], EPS, [epsc.b])
            onec = sb("onec", [128, 1], F32)
            self.memset(onec.t[:], 1.0, [onec.b])

            def bc_ap(src, off, n):
                return bass.AP(tensor=src.tensor, offset=off, ap=[[0, 128], [1, n]])

            t5bc = sb("t5bc", [128, 128], F32)
            self.dma(t5bc.t[:], bc_ap(t5, 0, 128), (), [t5bc.b])
            gcol = sb("gcol", [128, NL * 5, 8], F32)
            NG = NL * 5 * 8
            self.dma(tmpc.t[:NG, :], gains.rearrange("l w (c p) -> (l w c) p", p=128), (), [tmpc.b])
            self.mm(bank[1].t[:, :NG], tmpc.t[:NG, :], identF.t[:NG, :NG], True, True, [tmpc.b, identF.b], [bank[1].b])
            self.cp(gcol.t[:].rearrange("p a b -> p (a b)"), bank[1].t[:, :NG], [bank[1].b], [gcol.b])
            hg = sb("hg", [128, NL * 6, HD], F32)
            self.dma(hg.t[:].rearrange("p a b -> p (a b)"), bc_ap(hgains, 0, NL * 6 * HD), (), [hg.b])
            lv = sb("lv", [128, NL * 4, HD], F32)
            self.dma(lv.t[:].rearrange("p a b -> p (a b)"), bc_ap(lvec, 0, NL * 4 * HD), (), [lv.b])
            subcol = sb("subcol", [128, NL], F32)
            self.dma(tmpc.t[:NL, :], subln[:, :], (), [tmpc.b])
            self.mm(bank[1].t[:, :NL], tmpc.t[:NL, :], identF.t[:NL, :NL], True, True, [tmpc.b, identF.b], [bank[1].b])
            self.cp(subcol.t[:], bank[1].t[:, :NL], [bank[1].b], [subcol.b])
            neglam = sb("neglam", [128, NL], F32)
            lt = sb("lt", [128, HD], F32); le = sb("le", [128, 4], F32)
            for l in range(NL):
                for i in range(2):
                    self.tt(lt.t[:], lv.t[:, l * 4 + 2 * i, :], lv.t[:, l * 4 + 2 * i + 1, :], ALU.mult, [lv.b], [lt.b])
                    S.op("dve", lambda e, i=i: e.tensor_reduce(out=le.t[:, i:i + 1], in_=lt.t[:], axis=AX.X, op=ALU.add), [lt.b], [le.b])
                self.act(le.t[:, 2:4], le.t[:, 0:2], AF.Exp, [le.b], [le.b])
                self.tt(neglam.t[:, l:l + 1], le.t[:, 3:4], le.t[:, 2:3], ALU.subtract, [le.b], [neglam.b])
                self.ts(neglam.t[:, l:l + 1], neglam.t[:, l:l + 1], -lambda_init(l), None, ALU.add, ALU.bypass, [neglam.b], [neglam.b])
                self.ts(subcol.t[:, l:l + 1], subcol.t[:, l:l + 1], 1.0 - lambda_init(l), None, ALU.mult, ALU.bypass, [subcol.b], [subcol.b])
            self.neglam, self.subcol, self.hg, self.gcol, self.t5bc = neglam, subcol, hg, gcol, t5bc

            xin = sb("xin", [128, 4, D], F32)
            hT = sb("hT", [128, 8, 512], BF16)
            hidT = sb("hidT", [128, 32, 512], BF16)
            self.rings = {
                "xn": [[sb("xn%d" % i, [128, D], BF16) for i in range(2)], 0],
                "w": [[sb("wslot%d" % i, [128, 8, 512], BF16) for i in range(4)], 0],
                "gt": [[sb("gt%d" % i, [128, 3, 512], BF16) for i in range(2)], 0],
                "tf": [[sb("tf%d" % i, [128, 512], F32) for i in range(4)], 0],
                "tb": [[sb("tb%d" % i, [128, 512], BF16) for i in range(5)], 0],
                "of": [[sb("of%d" % i, [128, 512], F32) for i in range(3)], 0],
                "ob": [[sb("ob%d" % i, [128, 4, 128], BF16) for i in range(2)], 0],
                "kt": [[sb("ktl%d" % i, [128, 1024], BF16) for i in range(2)], 0],
                "v": [[sb("vl%d" % i, [128, 8, 128], BF16) for i in range(2)], 0],
                "sc": [[bank[0], bank[1], bank[6], bank[7]], 0],
                "gu": [[(bank[0], bank[1]), (bank[2], bank[3]), (bank[4], bank[5])], 0],
            }
            QTA = [sb("QTA%d" % i, [128, 4, 512], BF16) for i in range(2)]
            QTB = [sb("QTB%d" % i, [128, 2, 512], BF16) for i in range(2)]
            QTC = [sb("QTC%d" % i, [128, 2, 512], BF16) for i in range(2)]
            QTX = QTC
            for q_ in QTA + QTB + QTC:
                self.memset(q_.t[:], 0.0, [q_.b])
            gates_d = {"p": [dt_("gates_p%d" % i, [4, 128, 3072], BF16) for i in range(2)], "s": [dt_("gates_s", [4, 128, 3072], BF16)]}
            gates_b = {"p": [Buf("gp0"), Buf("gp1")], "s": [Buf("gs")]}
            oaT = sb("oaT", [128, 4, 512], BF16); obT = sb("obT", [128, 2, 512], BF16)
            ocT = sb("ocT", [128, 2, 512], BF16); oxT = ocT
            Rt = sb("Rt", [128, 512], F32)
            mrgb = sb("mrgb", [128, D], BF16); junk = mrgb
            stripA = sb("stripA", [128, 4, WA_STRIP], BF16)
            stripC = sb("stripC", [128, 4, WC_STRIP], BF16)
            maskB = sb("maskB", [128, WB_STRIP], BF16)
            mkT = {g: sb("mkT_" + g, [128, 2, NMEM], BF16) for g in "ps"}
            mvb = {g: sb("mvb_" + g, [128, 2, 256], BF16) for g in "ps"}
            self.rings["hn"] = [[(sb("ssg%d" % i, [128, 8], F32), sb("sdg%d" % i, [128, 8], F32), sb("rsg%d" % i, [128, 8], F32)) for i in range(4)], 0]
            self.rings["rn"] = [[(sb("ss%d" % i, [128, 1], F32), sb("sd%d" % i, [128, 1], F32), sb("rs%d" % i, [128, 1], F32)) for i in range(4)], 0]
            self.xin, self.hT, self.hidT = xin, hT, hidT

            use_order = ["wg1", "wu1", "wd1", "w_in", "w_br_a", "w_br_b", "w_br_c", "w_out", "x_wkv", "x_wq", "x_wo", "wg2", "wu2", "wd2"]
            for n_ in use_order:
                Wbuf[n_] = [[], []]

            def cast_weights(l, names):
                for n_ in names:
                    shp = wshapes[n_]
                    rows = max(1, (1 << 20) // shp[1])
                    for r0 in range(0, shp[0], rows):
                        r1 = min(shp[0], r0 + rows)
                        b_ = Buf("wb_%s_%d_%d" % (n_, l, r0))
                        self.dma(Wb[n_][l, r0:r1, :], W[n_][l, r0:r1, :], (), [b_], q="pool")
                        Wbuf[n_][l].append(b_)
            for l_ in range(NL):
                cast_weights(l_, use_order)

            stg_t = xin.t[:, 0:2, :].rearrange("p a b -> p (a b)")
            stg2_t = xin.t[:, 2:4, :].rearrange("p a b -> p (a b)")
            self.dma(stg_t[:, :WB_STRIP], mask_b[:, :], (), [xin.b])
            self.cp(maskB.t[:], stg_t[:, :WB_STRIP], [xin.b], [maskB.b])
            t5_sb = sb("t5_sb", [32, 4], F32)
            self.dma(t5_sb.t[:], t5[:, :], (), [t5_sb.b])
            gsb_t = stg2_t[:4, :GC_W]
            gab = Buf("ga_d")
            for v_ in range(2):
                self.dma(stg_t[:32, :GA_W], oh_a[v_], (), [xin.b])
                for c0 in range(0, GA_W, 512):
                    self.mm(bank[0].t[:4, :512], t5_sb.t[:, :], stg_t[:32, c0:c0 + 512], True, True, [t5_sb.b, xin.b], [bank[0].b])
                    self.cp(gsb_t[:, c0:c0 + 512], bank[0].t[:4, :512], [bank[0].b], [xin.b])
                self.dma(ga_d[v_], gsb_t[:, :GA_W], [xin.b], [gab])

            def build_strip(gd, gbuf_, gw, goff, jj_start, mask_d, mcol0, width, dst, dcol0):
                self.dma(stg2_t[:, :width], mask_d[:, mcol0:mcol0 + width], (), [xin.b])
                for h in range(4):
                    src = bass.AP(tensor=gd.tensor, offset=goff + h * gw + jj_start, ap=[[1, 128], [1, width]])
                    self.dma(stg_t[:, :width], src, [gbuf_], [xin.b])
                    for c0 in range(0, width, 512):
                        cw = min(512, width - c0)
                        self.mm(bank[0].t[:, :cw], Jf.t[:, :], stg_t[:, c0:c0 + cw], True, True, [Jf.b, xin.b], [bank[0].b])
                        self.tt(dst.t[:, h, dcol0 + c0:dcol0 + c0 + cw], bank[0].t[:, :cw], stg2_t[:, c0:c0 + cw], ALU.add, [bank[0].b, xin.b], [dst.b])

            build_strip(ga_d, gab, GA_W, 0, 0, mask_a, 0, WA_MAIN, stripA, 0)
            build_strip(ga_d, gab, GA_W, 4 * GA_W, 384, mask_a, WA_MAIN, WA_S, stripA, WA_MAIN)
            crT = sb("crT", [128, 3, 4], F32)
            cr_in = sb("cr_in", [4, 264], F32)
            gcb = Buf("gc_d")

            def layer_setup(l):
                self.memset(crT.t[:], 0.0, [crT.b])
                self.dma(cr_in.t[:4, :257], crel[l], (), [cr_in.b])
                for c in range(3):
                    n = min(128, 257 - c * 128)
                    self.mm(bank[1].t[:n, :4], cr_in.t[:4, c * 128:c * 128 + n], identF.t[:4, :4], True, True, [cr_in.b, identF.b], [bank[1].b])
                    self.cp(crT.t[:n, c, :], bank[1].t[:n, :4], [bank[1].b], [crT.b])
                ohv = stg_t[:, :1536].rearrange("p (c s) -> p c s", s=512)
                for v_ in range(2):
                    for c0 in range(0, GC_W, 512):
                        self.dma(ohv, oh_c[v_, :, c0:c0 + 512].rearrange("(c p) s -> p c s", p=128), (), [xin.b])
                        for c in range(3):
                            self.mm(bank[0].t[:4, :512], crT.t[:, c, :], ohv[:, c, :], c == 0, c == 2, [crT.b, xin.b], [bank[0].b])
                        self.cp(gsb_t[:, c0:c0 + 512], bank[0].t[:4, :512], [bank[0].b], [xin.b])
                    self.dma(gc_d[v_], gsb_t[:, :], [xin.b], [gcb])
                build_strip(gc_d, gcb, GC_W, 0, 0, mask_c, 0, WC_MAIN, stripC, 0)
                build_strip(gc_d, gcb, GC_W, 4 * GC_W, 384, mask_c, WC_MAIN, WC_S, stripC, WC_MAIN)

            def load_w(name, l, r0, c0, ncols, kc=8, pk=128):
                slot = self.ring("w")
                src = Wb[name][l, r0:r0 + kc * pk, c0:c0 + ncols].rearrange("(c p) n -> p c n", p=pk)
                self.dma(slot.t[:pk, :kc, :ncols], src, Wbuf[name][l], [slot.b])
                return slot

            def rms_T(src, np_, nsub, gi, dst):
                sc_ = []
                for s in range(nsub):
                    ss_, sd_, rs_ = self.ring("rn")
                    self.memset(ss_.t[:], 0.0, [ss_.b], eng="dve")
                    self.act(junk.t[:np_, :], src.t[:np_, s, :], AF.Square, [src.b], [junk.b, ss_.b], accum_out=ss_.t[:np_, 0:1])
                    sc_.append((ss_, sd_, rs_))
                for s in range(nsub):
                    ss_, sd_, rs_ = sc_[s]
                    self.act(sd_.t[:np_, :], ss_.t[:np_, :], AF.Sqrt, [ss_.b, epsc.b], [sd_.b], scale=1.0 / D, bias=epsc.t[:np_, :])
                    self.recip(rs_.t[:np_, :], sd_.t[:np_, :], [sd_.b], [rs_.b])
                    xn = self.ring("xn")
                    self.ts(xn.t[:np_, :], src.t[:np_, s, :], rs_.t[:np_, 0:1], None, ALU.mult, ALU.bypass, [src.b, rs_.b], [xn.b])
                    self.transpose_to(lambda c, xn=xn: xn.t[:np_, c * 128:(c + 1) * 128], [xn.b], 8, np_,
                                      dst.t[:, :, s * 128:s * 128 + np_], [dst.b],
                                      mul=gcol.t[:, gi, :].unsqueeze(2).to_broadcast([128, 8, np_]), mulb=[gcol.b])

            self.trp_i = 0
            self.acc_i = 0

            def transpose_to(src_fn, srcb, nblk, np_, dst_ap, dstb, mul=None, mulb=()):
                bk = bank[6] if self.trp_i % 2 == 0 else bank[7]
                self.trp_i += 1
                tv = bk.t[:, :].bitcast(BF16).rearrange("p (c n) -> p c n", n=128)
                for c in range(nblk):
                    S.op("pe", lambda e, c=c: e.transpose(out=tv[:, c, :np_], in_=src_fn(c), identity=ident.t[:np_, :np_]), list(srcb) + [ident.b], [bk.b])
                if mul is None:
                    self.act(dst_ap, tv[:, :nblk, :np_], AF.Copy, [bk.b], dstb)
                else:
                    self.tt(dst_ap, tv[:, :nblk, :np_], mul, ALU.mult, [bk.b] + list(mulb), dstb)
            self.transpose_to = transpose_to

            def transpose_split(src_fn, srcb, nblk, np_, q2, cs_):
                bk = bank[6] if self.trp_i % 2 == 0 else bank[7]
                self.trp_i += 1
                tv = bk.t[:, :].bitcast(BF16).rearrange("p (c n) -> p c n", n=128)
                for c in range(nblk):
                    S.op("pe", lambda e, c=c: e.transpose(out=tv[:, c, :np_], in_=src_fn(c), identity=ident.t[:np_, :np_]), list(srcb) + [ident.b], [bk.b])
                self.act(q2[0].t[0:64, :nblk, cs_], tv[0:64, :nblk, :np_], AF.Copy, [bk.b], [q2[0].b])
                self.cp(q2[1].t[64:128, :nblk, cs_], tv[64:128, :nblk, :np_], [bk.b], [q2[1].b])

            def linear_tm(name, l, KC, N, lhs_fn, lhsb, np_, nsub, evac, pieces=None):
                npieces = (N + 511) // 512
                accring = [bank[2], bank[3], bank[4], bank[5], bank[0], bank[1]]
                for p in (pieces if pieces is not None else range(npieces)):
                    ncols = min(512, N - p * 512)
                    if KC <= 8:
                        slot = load_w(name, l, 0, p * 512, ncols, KC)
                        for s in range(nsub):
                            acc = accring[self.acc_i % 6]
                            self.acc_i += 1
                            for k in range(KC):
                                self.mm(acc.t[:np_, :ncols], lhs_fn(k, s), slot.t[:, k, :ncols], k == 0, k == KC - 1,
                                        list(lhsb) + [slot.b], [acc.b])
                            evac(p, s, acc, ncols)
                        continue
                    for kg in range(0, KC, 8):
                        kc = min(8, KC - kg)
                        slot = load_w(name, l, kg * 128, p * 512, ncols, kc)
                        for s in range(nsub):
                            acc = bank[2 + s]
                            for k in range(kc):
                                self.mm(acc.t[:np_, :ncols], lhs_fn(kg + k, s), slot.t[:, k, :ncols], kg + k == 0, kg + k == KC - 1,
                                        list(lhsb) + [slot.b], [acc.b])
                    for s in range(nsub):
                        evac(p, s, bank[2 + s], ncols)

            def headnorm(pb_ap, pbb, np_, G, gi, out_ap, outb):
                tf = self.ring("tf")
                ssg, sdg, rsg = self.ring("hn")
                self.act(tf.t[:np_, :G * 64], pb_ap, AF.Square, pbb, [tf.b])
                S.op("dve", lambda e: e.tensor_reduce(out=ssg.t[:np_, :G], in_=tf.t[:np_, :G * 64].rearrange("p (g d) -> p g d", d=64), axis=AX.X, op=ALU.add), [tf.b], [ssg.b])
                self.act(sdg.t[:np_, :G], ssg.t[:np_, :G], AF.Sqrt, [ssg.b, epsc.b], [sdg.b], scale=1.0 / HD, bias=epsc.t[:np_, :])
                self.recip(rsg.t[:np_, :G], sdg.t[:np_, :G], [sdg.b], [rsg.b])
                tf2 = self.ring("tf")
                self.tt(tf2.t[:np_, :G * 64].rearrange("p (g d) -> p g d", d=64), pb_ap.rearrange("p (g d) -> p g d", d=64),
                        rsg.t[:np_, :G].unsqueeze(2).to_broadcast([np_, G, 64]), ALU.mult, list(pbb) + [rsg.b], [tf2.b])
                self.tt(out_ap.rearrange("p (g d) -> p g d", d=64), tf2.t[:np_, :G * 64].rearrange("p (g d) -> p g d", d=64),
                        hg.t[:np_, gi, :].unsqueeze(1).to_broadcast([np_, G, 64]), ALU.mult, [tf2.b, hg.b], outb, eng=POOL_OFF)

            def ffn(l, which, G):
                np_, nsub, T = G.np, G.nsub, G.T
                sfx = "1" if which == 0 else "2"
                rms_T(xin, np_, nsub, l * 5 + (0 if which == 0 else 4), hT)
                for fg in range(8):
                    wgs = load_w("wg" + sfx, l, 0, fg * 512, 512)
                    wus = load_w("wu" + sfx, l, 0, fg * 512, 512)
                    for fc in range(4):
                        bg, bu = self.ring("gu")
                        for k in range(8):
                            self.mm(bg.t[:, :T], wgs.t[:, k, fc * 128:(fc + 1) * 128], hT.t[:, k, :T], k == 0, k == 7, [wgs.b, hT.b], [bg.b])
                        for k in range(8):
                            self.mm(bu.t[:, :T], wus.t[:, k, fc * 128:(fc + 1) * 128], hT.t[:, k, :T], k == 0, k == 7, [wus.b, hT.b], [bu.b])
                        tf = self.ring("tf")
                        self.act(tf.t[:, :T], bg.t[:, :T], AF.Silu, [bg.b], [tf.b])
                        self.tt(hidT.t[:, fg * 4 + fc, :T], tf.t[:, :T], bu.t[:, :T], ALU.mult, [tf.b, bu.b], [hidT.b])

                def evac(p, s, acc, ncols):
                    self.stt(xin.t[:np_, s, p * 512:(p + 1) * 512], acc.t[:np_, :512], 0.5, xin.t[:np_, s, p * 512:(p + 1) * 512],
                             ALU.mult, ALU.add, [acc.b, xin.b], [xin.b])
                linear_tm("wd" + sfx, l, 32, D, lambda k, s: hidT.t[:, k, s * 128:s * 128 + np_], [hidT.b], np_, nsub, evac)

            def proj(l, G, t):
                np_, nsub, T = G.np, G.nsub, G.T
                i0 = G.widx0(t)
                i0c = G.widx0c(t)
                scr_w = G.scrw(t)
                r0 = G.orow(t)
                rms_T(xin, np_, nsub, l * 5 + 1, hT)

                def out_f(dst, of, w):
                    self.dma(dst, of.t[:np_, :w], [of.b], (), q="pool")

                def kt_out(kname, tb, nb, idx0, s):
                    ob = self.ring("ob")
                    transpose_to(lambda c, tb=tb: tb.t[:np_, c * 128:(c + 1) * 128], [tb.b], nb, np_, ob.t[:, :nb, :np_], [ob.b])
                    self.dma(scr_w[kname][:, :, idx0 + s * 128:idx0 + s * 128 + np_].rearrange("h p s -> p h s"), ob.t[:, :nb, :np_], [ob.b], [G.wbuf(kname, t)], q="pool")

                def v_out(vname, tb, w, idx0, s):
                    if os.environ.get("KNOV") == "1":
                        return
                    self.dma(scr_w[vname][idx0 + s * 128:idx0 + s * 128 + np_, :], tb.t[:np_, :w], [tb.b], [G.wbuf(vname, t)], q="pool")

                def evac(p, s, acc, ncols):
                    pb = acc.t[:np_, :512]
                    rs_ = slice(r0 + s * 128, r0 + s * 128 + np_)
                    if p == 0:
                        tb = self.ring("tb")
                        headnorm(pb, [acc.b], np_, 8, l * 6 + 0, tb.t[:np_, :], [tb.b])
                        transpose_split(lambda c, tb=tb: tb.t[:np_, c * 128:(c + 1) * 128], [tb.b], 4, np_, QTA, slice(s * 128, s * 128 + np_))
                    elif p == 1:
                        of = self.ring("of")
                        headnorm(pb, [acc.b], np_, 8, l * 6 + 1, of.t[:np_, :], [of.b])
                        out_f(G.o_ak[l, rs_, :], of, 512)
                        tb = self.ring("tb")
                        self.cp(tb.t[:np_, :], of.t[:np_, :], [of.b], [tb.b], eng=POOL_OFF)
                        kt_out("kta", tb, 4, i0, s)
                    elif p == 2:
                        of = self.ring("of")
                        self.act(of.t[:np_, :], pb, AF.Copy, [acc.b], [of.b])
                        out_f(G.o_av[l, rs_, :], of, 512)
                        tb = self.ring("tb")
                        self.cp(tb.t[:np_, :], pb, [acc.b], [tb.b])
                        v_out("va", tb, 512, i0, s)
                    elif p == 3:
                        tb = self.ring("tb")
                        self.cp(tb.t[:np_, :256], pb[:, 0:256], [acc.b], [tb.b])
                        transpose_split(lambda c, tb=tb: tb.t[:np_, c * 128:(c + 1) * 128], [tb.b], 2, np_, QTB, slice(s * 128, s * 128 + np_))
                        of = self.ring("of")
                        self.act(of.t[:np_, :256], pb[:, 256:512], AF.Copy, [acc.b], [of.b])
                        out_f(G.o_bk[l, rs_, :], of, 256)
                        tb2 = self.ring("tb")
                        self.cp(tb2.t[:np_, :256], pb[:, 256:512], [acc.b], [tb2.b])
                        kt_out("ktb", tb2, 2, i0, s)
                    elif p == 4:
                        of = self.ring("of")
                        self.act(of.t[:np_, :256], pb[:, 0:256], AF.Copy, [acc.b], [of.b])
                        out_f(G.o_bv[l, rs_, :], of, 256)
                        tb = self.ring("tb")
                        self.cp(tb.t[:np_, :256], pb[:, 0:256], [acc.b], [tb.b])
                        v_out("vb", tb, 256, i0, s)
                        tb2 = self.ring("tb")
                        headnorm(pb[:, 256:512], [acc.b], np_, 4, l * 6 + 2, tb2.t[:np_, :256], [tb2.b])
                        transpose_split(lambda c, tb2=tb2: tb2.t[:np_, c * 128:(c + 1) * 128], [tb2.b], 2, np_, QTC, slice(s * 128, s * 128 + np_))
                    elif p == 5:
                        of = self.ring("of")
                        headnorm(pb[:, 0:256], [acc.b], np_, 4, l * 6 + 3, of.t[:np_, 0:256], [of.b])
                        self.act(of.t[:np_, 256:512], pb[:, 256:512], AF.Copy, [acc.b], [of.b])
                        G.store_c(l, t, s, of)
                        tb = self.ring("tb")
                        self.cp(tb.t[:np_, :512], of.t[:np_, :512], [of.b], [tb.b], eng=POOL_OFF)
                        kt_out("ktc", tb, 2, i0c, s)
                        tb3 = self.ring("tb")
                        self.cp(tb3.t[:np_, :256], pb[:, 256:512], [acc.b], [tb3.b])
                        v_out("vc", tb3, 256, i0c, s)
                    else:
                        tbg = self.ring("tb")
                        self.act(tbg.t[:np_, :], pb, AF.Sigmoid, [acc.b], [tbg.b])
                        gi_ = t % len(gates_d[G.name])
                        self.dma(gates_d[G.name][gi_][s, :np_, (p - 6) * 512:(p - 5) * 512], tbg.t[:np_, :], [tbg.b], [gates_b[G.name][gi_]], q="pool")
                linear_tm("w_in", l, 8, 6144, lambda k, s: hT.t[:, k, s * 128:s * 128 + np_], [hT.b], np_, nsub, evac,
                          pieces=[int(x) for x in os.environ["KPIECES"].split(",")] if "KPIECES" in os.environ else None)

            def key_chunks(lo, hi, csz=1024):
                out = []
                c0 = lo
                while c0 < hi:
                    n = min(csz, hi - c0)
                    out.append((c0, n))
                    c0 += n
                return out

            KBASE = {"kta": 0, "ktb": 1024, "ktc": 1536}
            VBASE = {"va": 512, "vb": 1280, "vc": 1792}

            def load_kv(G, kname, vname, hp_or_h, prow0, nrow, vcol0, dv, c0, n):
                kt = self.ring("kt")
                v = self.ring("v")
                if G.name == "p":
                    slot = c0 // 1024
                    assert c0 % 1024 == 0 and n == 1024
                    g, deps = gout[slot], [gbuf[slot]]
                    for r in range(2):
                        base = r * 2048
                        krow = base + KBASE[kname] + hp_or_h * 128 + prow0
                        self.dma(kt.t[prow0:prow0 + nrow, r * 512:(r + 1) * 512], g[krow:krow + nrow, :], deps, [kt.b])
                        if vname == "va":
                            vsrc = g[base + 512:base + 1024, vcol0:vcol0 + dv]
                        else:
                            vsrc = g[base + VBASE[vname]:base + VBASE[vname] + 256, :].rearrange("r (a c) -> (r a) c", c=256)[:, vcol0:vcol0 + dv]
                        self.dma(v.t[:, r * 4:(r + 1) * 4, :dv], vsrc.rearrange("(c p) f -> p c f", p=128), deps, [v.b])
                    return kt, v
                deps_k = G.kdeps(kname, c0, n)
                deps_v = G.kdeps(vname, c0, n)
                self.dma(kt.t[prow0:prow0 + nrow, :n], G.scr[kname][hp_or_h, prow0:prow0 + nrow, c0:c0 + n], deps_k, [kt.b])
                nfull = n // 128
                if nfull:
                    self.dma(v.t[:, :nfull, :dv], G.scr[vname][c0:c0 + nfull * 128, vcol0:vcol0 + dv].rearrange("(c p) f -> p c f", p=128), deps_v, [v.b])
                rem = n - nfull * 128
                if rem:
                    self.dma(v.t[:rem, nfull, :dv], G.scr[vname][c0 + nfull * 128:c0 + n, vcol0:vcol0 + dv], deps_v, [v.b])
                return kt, v

            def attn_A(l, G, t):
                T = G.T
                q0 = G.kidx0(t)
                hi = q0 + T
                LA = 2
                for h in range(4):
                    O = [bank[2], bank[4]]
                    Sm = [bank[3], bank[5]]
                    chunks = key_chunks(0, hi)
                    items = []
                    for ci, (c0, n) in enumerate(chunks):
                        nt = (n + 127) // 128
                        for i in range(nt):
                            for m in range(2):
                                items.append((ci, c0, n, i, m))
                    nit = len(items)
                    loaded = {}
                    pend = []

                    def stage1(j):
                        ci, c0, n, i, m = items[j]
                        if ci not in loaded:
                            loaded[ci] = load_kv(G, "kta", "va", h, 0, 128, h * 128, 128, c0, n)
                        kt, v = loaded[ci]
                        k0 = c0 + i * 128
                        nk = min(128, c0 + n - k0)
                        delta = k0 - q0
                        sc = self.ring("sc")
                        self.mm(sc.t[:nk, :T], kt.t[:, i * 128:i * 128 + nk], QTA[m].t[:, h, :T], True, True, [kt.b, QTA[m].b], [sc.b])
                        P = self.ring("tb")
                        if delta >= G.sthr:
                            j0 = G.jcol(384 - delta, WA_MAIN)
                            tf = self.ring("tf")
                            self.stt(tf.t[:nk, :T], sc.t[:nk, :T], 0.125, stripA.t[:nk, h, j0:j0 + T], ALU.mult, ALU.add, [sc.b, stripA.b], [tf.b])
                            self.act(P.t[:nk, :T], tf.t[:nk, :T], AF.Exp, [tf.b], [P.b])
                        else:
                            fb = self.far_bucket * 4 + h
                            self.act(P.t[:nk, :T], sc.t[:nk, :T], AF.Exp, [sc.b, t5bc.b], [P.b], scale=0.125, bias=t5bc.t[:nk, fb:fb + 1])
                        pend.append((P, v, i, nk, m, j < 2, j >= nit - 2))

                    def stage2(j):
                        P, v, i, nk, m, first, last = pend[j]
                        self.mm(O[m].t[:, :T], v.t[:nk, i, :128], P.t[:nk, :T], first, last, [v.b, P.b], [O[m].b])
                        self.mm(Sm[m].t[:, :T], onesb.t[:nk, :], P.t[:nk, :T], first, last, [onesb.b, P.b], [Sm[m].b])

                    for j in range(nit + LA):
                        if j < nit:
                            stage1(j)
                        if j >= LA:
                            stage2(j - LA)
                    r1 = self.ring("tf"); t1 = self.ring("tf"); r2 = self.ring("tf"); t2 = self.ring("tf")
                    self.act(r1.t[:, :T], Sm[0].t[:, :T], AF.Ln, [Sm[0].b], [r1.b])
                    self.act(r1.t[:, :T], r1.t[:, :T], AF.Exp, [r1.b], [r1.b], scale=-1.0)
                    self.tt(t1.t[:, :T], O[0].t[:, :T], r1.t[:, :T], ALU.mult, [O[0].b, r1.b], [t1.b])
                    self.act(r2.t[:, :T], Sm[1].t[:, :T], AF.Ln, [Sm[1].b], [r2.b])
                    self.act(r2.t[:, :T], r2.t[:, :T], AF.Exp, [r2.b], [r2.b], scale=-1.0)
                    self.tt(t2.t[:, :T], O[1].t[:, :T], r2.t[:, :T], ALU.mult, [O[1].b, r2.b], [t2.b])
                    self.stt(r1.t[:, :T], t2.t[:, :T], neglam.t[:, l:l + 1], t1.t[:, :T], ALU.mult, ALU.add, [t2.b, t1.b, neglam.b], [r1.b])
                    self.act(r2.t[:, :T], r1.t[:, :T], AF.Square, [r1.b], [r2.b])
                    self.mm(bank[2].t[:, :T], onesf.t[:, :], r2.t[:, :T], True, True, [onesf.b, r2.b], [bank[2].b])
                    self.act(t1.t[:, :T], bank[2].t[:, :T], AF.Ln, [bank[2].b, epsc.b], [t1.b], scale=1.0 / 128, bias=epsc.t[:, :])
                    self.act(t2.t[:, :T], t1.t[:, :T], AF.Exp, [t1.b], [t2.b], scale=-0.5)
                    self.tt(r2.t[:, :T], r1.t[:, :T], t2.t[:, :T], ALU.mult, [r1.b, t2.b], [r2.b])
                    self.ts(oaT.t[:, h, :T], r2.t[:, :T], subcol.t[:, l:l + 1], None, ALU.mult, ALU.bypass, [r2.b, subcol.b], [oaT.b])

            def attn_B(l, G, t):
                T = G.T
                q0 = G.kidx0(t)
                hi = q0 + T
                zring = [bank[0], bank[1], bank[6], bank[4]]
                cring = [bank[7], bank[5], bank[3]]
                for h in range(4):
                    hp, ho = h // 2, (h % 2) * 64
                    self.memset(Rt.t[:, :], 0.0, [Rt.b], eng="dve")
                    chunks = key_chunks(0, hi)[::-1]
                    items = []
                    for ci, (c0, n) in enumerate(chunks):
                        nt = (n + 127) // 128
                        for i in range(nt - 1, -1, -1):
                            items.append((ci, c0, n, i))
                    nit = len(items)
                    loaded = {}
                    st = {}

                    def s12(j):
                        ci, c0, n, i = items[j]
                        if ci not in loaded:
                            loaded[ci] = load_kv(G, "ktb", "vb", hp, 0, 128, hp * 128, 128, c0, n)
                        kt, v = loaded[ci]
                        k0 = c0 + i * 128
                        nk = min(128, c0 + n - k0)
                        delta = k0 - q0
                        msk = delta >= G.bthr
                        j0 = G.jcol(384 - delta, WB_MAIN) if msk else 0
                        z = zring[j % 4]
                        cs = cring[j % 3]
                        self.mm(z.t[:nk, :T], kt.t[:, i * 128:i * 128 + nk], QTB[h % 2].t[:, hp, :T], True, True, [kt.b, QTB[h % 2].b], [z.b])
                        e_ = self.ring("tf")
                        self.act(e_.t[:nk, :T], z.t[:nk, :T], AF.Exp, [z.b], [e_.b], scale=0.125)
                        sp = self.ring("tb")
                        self.act(sp.t[:nk, :T], e_.t[:nk, :T], AF.Ln, [e_.b, onec.b], [sp.b], bias=onec.t[:nk, :])
                        if msk:
                            self.tt(sp.t[:nk, :T], sp.t[:nk, :T], maskB.t[:nk, j0:j0 + T], ALU.mult, [sp.b, maskB.b], [sp.b])
                        st[j] = dict(v=v, i=i, nk=nk, msk=msk, j0=j0, z=z, cs=cs, sp=sp)

                    def s345(j):
                        d = st[j]
                        z, cs, sp, nk = d["z"], d["cs"], d["sp"], d["nk"]
                        S.op("pe", lambda e, z=z, sp=sp, nk=nk: e.matmul(z.t[:nk, :T], lhsT=negU8.t[:nk, :nk], rhs=sp.t[:nk, :T], start=False, stop=True, skip_group_check=True),
                             [negU8.b, sp.b], [z.b])
                        self.mm(cs.t[:, :T], onesb.t[:nk, :], sp.t[:nk, :T], True, True, [onesb.b, sp.b], [cs.b])
                        arg = self.ring("tf")
                        self.stt(arg.t[:nk, :T], z.t[:nk, :T], 0.125, Rt.t[:nk, :T], ALU.mult, ALU.subtract, [z.b, Rt.b], [arg.b])
                        self.tt(Rt.t[:, :T], cs.t[:, :T], Rt.t[:, :T], ALU.add, [cs.b, Rt.b], [Rt.b])
                        wb_ = self.ring("tb")
                        self.act(wb_.t[:nk, :T], arg.t[:nk, :T], AF.Exp, [arg.b], [wb_.b])
                        if d["msk"]:
                            self.tt(wb_.t[:nk, :T], wb_.t[:nk, :T], maskB.t[:nk, d["j0"]:d["j0"] + T], ALU.mult, [wb_.b, maskB.b], [wb_.b])
                        d["w"] = wb_

                    def s6(j):
                        d = st.pop(j)
                        self.mm(bank[2].t[:, :T], d["v"].t[:d["nk"], d["i"], :128], d["w"].t[:d["nk"], :T], j == 0, j == nit - 1, [d["v"].b, d["w"].b], [bank[2].b])

                    for j in range(nit + 2):
                        if j < nit:
                            s12(j)
                        if 1 <= j <= nit:
                            s345(j - 1)
                        if j >= 2:
                            s6(j - 2)
                    self.act(obT.t[ho:ho + 64, hp, :T], bank[2].t[ho:ho + 64, :T], AF.Copy, [bank[2].b], [obT.b])

            def attn_soft(G, T, QT, nkeys_tiles, dst, h, hp, ho, strip=None):
                n_ = len(nkeys_tiles)
                pend = []
                LA = 2

                def stage1(i):
                    kap, vap, nk, j0, deps = nkeys_tiles[i]
                    sc = self.ring("sc")
                    self.mm(sc.t[:nk, :T], kap, QT[h % 2].t[:, hp, :T], True, True, deps + [QT[h % 2].b], [sc.b])
                    P = self.ring("tb")
                    if j0 is not None:
                        tf = self.ring("tf")
                        self.stt(tf.t[:nk, :T], sc.t[:nk, :T], 0.125, strip.t[:nk, h, j0:j0 + T], ALU.mult, ALU.add, [sc.b, strip.b], [tf.b])
                        self.act(P.t[:nk, :T], tf.t[:nk, :T], AF.Exp, [tf.b], [P.b])
                    else:
                        self.act(P.t[:nk, :T], sc.t[:nk, :T], AF.Exp, [sc.b], [P.b], scale=0.125)
                    pend.append(P)

                def stage2(i):
                    kap, vap, nk, j0, deps = nkeys_tiles[i]
                    P = pend[i]
                    self.mm(bank[2].t[:, :T], vap, P.t[:nk, :T], i == 0, i == n_ - 1, deps + [P.b], [bank[2].b])
                    self.mm(bank[3].t[:, :T], onesb.t[:nk, :], P.t[:nk, :T], i == 0, i == n_ - 1, [onesb.b, P.b], [bank[3].b])

                for i in range(n_ + LA):
                    if i < n_:
                        stage1(i)
                    if i >= LA:
                        stage2(i - LA)
                r = self.ring("tf")
                self.act(r.t[ho:ho + 64, :T], bank[3].t[ho:ho + 64, :T], AF.Ln, [bank[3].b], [r.b])
                self.act(r.t[ho:ho + 64, :T], r.t[ho:ho + 64, :T], AF.Exp, [r.b], [r.b], scale=-1.0)
                self.tt(dst.t[ho:ho + 64, hp, :T], bank[2].t[ho:ho + 64, :T], r.t[ho:ho + 64, :T], ALU.mult, [bank[2].b, r.b], [dst.b])

            def attn_C(l, G, t):
                T = G.T
                q0 = G.kidx0c(t)
                hi = q0 + T
                lo = max(0, q0 - G.cback)
                for h in range(4):
                    hp, ho = h // 2, (h % 2) * 64
                    tl = []
                    for (c0, n) in key_chunks((lo // 1024) * 1024 if G.name == "p" else lo, hi, 1024 if G.name == "p" else 2048):
                        kt, v = load_kv(G, "ktc", "vc", hp, 0, 128, hp * 128, 128, c0, n)
                        for i in range((n + 127) // 128):
                            k0 = c0 + i * 128
                            if k0 < lo:
                                continue
                            nk = min(128, c0 + n - k0)
                            delta = k0 - q0
                            tl.append((kt.t[:, i * 128:i * 128 + nk], v.t[:nk, i, :128], nk, G.jcol(384 - delta, WC_MAIN), [kt.b, v.b]))
                    attn_soft(G, T, QTC, tl, ocT, h, hp, ho, strip=stripC)

            def attn_X(l, G, t):
                T = G.T
                mk_, mv_ = mkT[G.name], mvb[G.name]
                for h in range(4):
                    hp, ho = h // 2, (h % 2) * 64
                    tl = []
                    for i in range(2):
                        tl.append((mk_.t[:, hp, i * 128:(i + 1) * 128], mv_.t[:, i, hp * 128:(hp + 1) * 128], 128, None, [mk_.b, mv_.b]))
                    attn_soft(G, T, QTX, tl, oxT, h, hp, ho)

            def merge(l, G):
                np_, nsub, T = G.np, G.nsub, G.T
                gi_ = G.gidx
                mrgs_t = hidT.t[:, 0:8, :].rearrange("p (s a) c -> p s (a c)", s=4)
                for n in range(2):
                    cs_ = slice(n * 512, (n + 1) * 512)
                    wa = load_w("w_br_a", l, 0, n * 512, 512, kc=4)
                    wb_ = load_w("w_br_b", l, 0, n * 512, 512, kc=2)
                    wc = load_w("w_br_c", l, 0, n * 512, 512, kc=2)
                    for s in range(nsub):
                        sl = slice(s * 128, s * 128 + np_)
                        gt = self.ring("gt")
                        self.dma(gt.t[:np_, :, :], gates_d[G.name][gi_][s, :np_, :].rearrange("p (g c) -> p g c", g=3)[:, :, n * 512:(n + 1) * 512], [gates_b[G.name][gi_]], [gt.b])
                        ba, bb, bc = (bank[0], bank[1], bank[6]) if s % 2 == 0 else (bank[2], bank[3], bank[7])
                        for h in range(4):
                            self.mm(ba.t[:np_, :], oaT.t[:, h, sl], wa.t[:, h, :], h == 0, h == 3, [oaT.b, wa.b], [ba.b])
                        for h in range(2):
                            self.mm(bb.t[:np_, :], obT.t[:, h, sl], wb_.t[:, h, :], h == 0, h == 1, [obT.b, wb_.b], [bb.b])
                        for h in range(2):
                            self.mm(bc.t[:np_, :], ocT.t[:, h, sl], wc.t[:, h, :], h == 0, h == 1, [ocT.b, wc.b], [bc.b])
                        m1 = self.ring("tf"); m2 = self.ring("tf")
                        self.tt(m1.t[:np_, :], ba.t[:np_, :], gt.t[:np_, 0, :], ALU.mult, [ba.b, gt.b], [m1.b])
                        self.tt(m2.t[:np_, :], bb.t[:np_, :], gt.t[:np_, 1, :], ALU.mult, [bb.b, gt.b], [m2.b])
                        self.tt(m1.t[:np_, :], m1.t[:np_, :], m2.t[:np_, :], ALU.add, [m1.b, m2.b], [m1.b])
                        self.tt(m2.t[:np_, :], bc.t[:np_, :], gt.t[:np_, 2, :], ALU.mult, [bc.b, gt.b], [m2.b])
                        self.tt(mrgs_t[:np_, s, cs_], m1.t[:np_, :], m2.t[:np_, :], ALU.add, [m1.b, m2.b], [hidT.b])
                for s in range(nsub):
                    sl = slice(s * 128, s * 128 + np_)
                    transpose_to(lambda c, s=s: mrgs_t[:np_, s, c * 128:(c + 1) * 128], [hidT.b], 8, np_, hT.t[:, :, sl], [hT.b])

                def evac(p, s, acc, ncols):
                    self.tt(xin.t[:np_, s, p * 512:(p + 1) * 512], acc.t[:np_, :512], xin.t[:np_, s, p * 512:(p + 1) * 512], ALU.add, [acc.b, xin.b], [xin.b])
                linear_tm("w_out", l, 8, D, lambda k, s: hT.t[:, k, s * 128:s * 128 + np_], [hT.b], np_, nsub, evac)

            def cross(l, G, t):
                np_, nsub, T = G.np, G.nsub, G.T
                rms_T(xin, np_, nsub, l * 5 + 2, hT)

                def evq(p, s, acc, ncols):
                    tb = self.ring("tb")
                    headnorm(acc.t[:np_, :256], [acc.b], np_, 4, l * 6 + 4, tb.t[:np_, :256], [tb.b])
                    transpose_split(lambda c, tb=tb: tb.t[:np_, c * 128:(c + 1) * 128], [tb.b], 2, np_, QTX, slice(s * 128, s * 128 + np_))
                linear_tm("x_wq", l, 8, 256, lambda k, s: hT.t[:, k, s * 128:s * 128 + np_], [hT.b], np_, nsub, evq)
                attn_X(l, G, t)
                for n in range(2):
                    cs_ = slice(n * 512, (n + 1) * 512)
                    wo = load_w("x_wo", l, 0, n * 512, 512, kc=2)
                    for s in range(nsub):
                        sl = slice(s * 128, s * 128 + np_)
                        acc = bank[4 + (s % 2)]
                        for h in range(2):
                            self.mm(acc.t[:np_, :], oxT.t[:, h, sl], wo.t[:, h, :], h == 0, h == 1, [oxT.b, wo.b], [acc.b])
                        self.tt(xin.t[:np_, s, cs_], acc.t[:np_, :], xin.t[:np_, s, cs_], ALU.add, [acc.b, xin.b], [xin.b])

            def mem_kv_prompt(l):
                self.dma(xin.t[:, 0:2, :], mem_p.rearrange("(c p) d -> p c d", p=128), (), [xin.b])
                rms_T(xin, 128, 2, l * 5 + 3, hT)
                mk_, mv_ = mkT["p"], mvb["p"]

                def ev(p, s, acc, ncols):
                    of = self.ring("of")
                    headnorm(acc.t[:, 0:256], [acc.b], 128, 4, l * 6 + 5, of.t[:, 0:256], [of.b])
                    self.act(of.t[:, 256:512], acc.t[:, 256:512], AF.Copy, [acc.b], [of.b])
                    tb = self.ring("tb")
                    self.cp(tb.t[:, :256], of.t[:, 0:256], [of.b], [tb.b])
                    transpose_to(lambda c, tb=tb: tb.t[:, c * 128:(c + 1) * 128], [tb.b], 2, 128, mk_.t[:, :, s * 128:(s + 1) * 128], [mk_.b])
                    self.cp(mv_.t[:, s, :], acc.t[:, 256:512], [acc.b], [mv_.b])
                    self.dma(o_mkp[l, s * 128:(s + 1) * 128, :], of.t[:, 0:256], [of.b], (), q="pool")
                    self.dma(o_mvp[l, s * 128:(s + 1) * 128, :], of.t[:, 256:512], [of.b], (), q="pool")
                linear_tm("x_wkv", l, 8, 512, lambda k, s: hT.t[:, k, s * 128:(s + 1) * 128], [hT.b], 128, 2, ev)

            def prep_kT(src, nrow, F, dst_fn):
                for kt_ in range(nrow // 128):
                    of = self.ring("of")
                    self.dma(of.t[:, :F], src[kt_ * 128:(kt_ + 1) * 128, :], (), [of.b])
                    tb = self.ring("tb")
                    self.cp(tb.t[:, :F], of.t[:, :F], [of.b], [tb.b])
                    dst_fn(kt_, tb)

            def sample_prep(l, G):
                scr, sbf = G.scr, G.sb_
                self.dma(scr["va"][0:PAST, :], ca_v[l], (), [sbf["va"]["cache"][0]], q="pool")
                self.dma(scr["vb"][0:PAST, :], cb_v[l], (), [sbf["vb"]["cache"][0]], q="pool")
                self.dma(scr["vc"][0:CB, :], cc_v[l], (), [sbf["vc"]["cache"][0]], q="pool")
                for (src, nrow, F, kname) in ((ca_k[l], PAST, 512, "kta"), (cb_k[l], PAST, 256, "ktb"), (cc_k[l], CB, 256, "ktc")):
                    def dst_fn(kt_, tb, kname=kname, F=F):
                        ob = self.ring("ob")
                        nb = F // 128
                        transpose_to(lambda c, tb=tb: tb.t[:, c * 128:(c + 1) * 128], [tb.b], nb, 128, ob.t[:, :nb, :128], [ob.b])
                        self.dma(scr[kname][:, :, kt_ * 128:(kt_ + 1) * 128].rearrange("h p s -> p h s"), ob.t[:, :nb, :128], [ob.b], [sbf[kname]["cache"][kt_]], q="pool")
                    prep_kT(src, nrow, F, dst_fn)
                mk_, mv_ = mkT["s"], mvb["s"]

                def dst_m(kt_, tb):
                    transpose_to(lambda c, tb=tb: tb.t[:, c * 128:(c + 1) * 128], [tb.b], 2, 128, mk_.t[:, :, kt_ * 128:(kt_ + 1) * 128], [mk_.b])
                prep_kT(cm_k[l], NMEM, 256, dst_m)

                def dst_v(kt_, tb):
                    self.cp(mv_.t[:, kt_, :], tb.t[:, :256], [tb.b], [mv_.b])
                prep_kT(cm_v[l], NMEM, 256, dst_v)
                self.dma(o_cks[l, 0:CB - NS, :], cc_k[l, NS:CB, :], (), (), q="pool")
                self.dma(o_cvs[l, 0:CB - NS, :], cc_v[l, NS:CB, :], (), (), q="pool")

            NT = NSLOT
            Gp = Group()
            Gp.name, Gp.T, Gp.np, Gp.nsub = "p", 512, 128, 4

            def scrw_p(t):
                x_ = xkv[t % 2]
                return {"kta": x_[0:512, :].rearrange("(h p) s -> h p s", p=128), "va": x_[512:1024, :],
                        "ktb": x_[1024:1280, :].rearrange("(h p) s -> h p s", p=128),
                        "vb": x_[1280:1536, :].rearrange("r (a c) -> (r a) c", c=256),
                        "ktc": x_[1536:1792, :].rearrange("(h p) s -> h p s", p=128),
                        "vc": x_[1792:2048, :].rearrange("r (a c) -> (r a) c", c=256)}
            Gp.scrw = scrw_p
            Gp.wbuf = lambda name, t: xkvb[t % 2]
            Gp.widx0 = lambda t: 0
            Gp.widx0c = lambda t: 0
            Gp.kidx0 = lambda t: (2 * t + 1) * 512
            Gp.kidx0c = lambda t: (2 * t + 1) * 512
            Gp.orow = lambda t: t * 512
            Gp.sthr, Gp.bthr, Gp.cback = -640, -512, 1024
            Gp.jcol = lambda j0, wmain: j0
            Gp.o_ak, Gp.o_av, Gp.o_bk, Gp.o_bv = o_akp, o_avp, o_bkp, o_bvp

            def store_c_p(l, t, s, of):
                if t == NT - 1:
                    r = s * 128
                    self.dma(o_ckp[l, r:r + 128, :], of.t[:, 0:256], [of.b], (), q="pool")
                    self.dma(o_cvp[l, r:r + 128, :], of.t[:, 256:512], [of.b], (), q="pool")
            Gp.store_c = store_c_p

            groups_cc = [[b_, b_ + self.NPAIR] for b_ in range(self.NPAIR)]

            def exchange(l, t):
                src, dst = xkv[t % 2], gout[t]
                self.S.cc(lambda e: e.collective_compute("AllGather", ALU.bypass, replica_groups=groups_cc, ins=[src[:, :]], outs=[dst[:, :]]),
                          [xkvb[t % 2]], [gbuf[t]])

            Gs = Group()
            Gs.name, Gs.T, Gs.np, Gs.nsub = "s", NS, NS, 1
            Gs.scr = kvscr("s", PAST + NS)
            Gs.sb_ = {k_: {"cache": [Buf("%s_s_c%d" % (k_, i)) for i in range(PAST // 128)], 0: Buf(k_ + "_s_n")} for k_ in Gs.scr}
            Gs.scrw = lambda t: Gs.scr
            Gs.wbuf = lambda name, t: Gs.sb_[name][0]
            Gs.widx0 = lambda t: PAST
            Gs.widx0c = lambda t: CB
            Gs.kidx0 = lambda t: PAST
            Gs.kidx0c = lambda t: CB
            Gs.orow = lambda t: 0
            Gs.sthr, Gs.bthr, Gs.cback = -128, -127, 512
            Gs.jcol = lambda j0, wmain: wmain + (j0 - 384)
            Gs.o_ak, Gs.o_av, Gs.o_bk, Gs.o_bv = o_aks, o_avs, o_bks, o_bvs
            Gs.kdeps = lambda name, c0, n: list(Gs.sb_[name]["cache"]) + [Gs.sb_[name][0]]

            def store_c_s(l, t, s, of):
                self.dma(o_cks[l, CB - NS:CB, :], of.t[:NS, 0:256], [of.b], (), q="pool")
                self.dma(o_cvs[l, CB - NS:CB, :], of.t[:NS, 256:512], [of.b], (), q="pool")
            Gs.store_c = store_c_s

            def run_tile(l, G, t, xsrc, xdst, xsrc_deps, xdst_buf):
                np_, nsub, T = G.np, G.nsub, G.T
                self.dma(xin.t[:np_, :nsub, :], xsrc.rearrange("(c p) d -> p c d", p=np_), xsrc_deps, [xin.b])
                if STAGE >= 1:
                    ffn(l, 0, G)
                if STAGE >= 2:
                    proj(l, G, t)
                    if G.name == "p":
                        exchange(l, t)
                if STAGE >= 3:
                    attn_A(l, G, t)
                if STAGE >= 4:
                    attn_B(l, G, t)
                if STAGE >= 5:
                    attn_C(l, G, t)
                G.gidx = t % len(gates_d[G.name])
                if STAGE >= 6:
                    merge(l, G)
                if STAGE >= 7:
                    cross(l, G, t)
                if STAGE >= 8:
                    ffn(l, 1, G)
                self.dma(xdst.rearrange("(c p) d -> p c d", p=np_), xin.t[:np_, :nsub, :], [xin.b], xdst_buf, q="pool")

            xb_p = [Buf("xcur_%d" % t) for t in range(NT)]
            xb_s = Buf("xcur_s")
            for l in range(NL):
                layer_setup(l)
                mem_kv_prompt(l)
                for t in range(NT if "p" in KGROUP else 0):
                    src = x_p if l == 0 else xcur
                    dst = xcur if l < NL - 1 else y_p
                    run_tile(l, Gp, t, src[t * 512:(t + 1) * 512, :], dst[t * 512:(t + 1) * 512, :],
                             [xb_p[t]] if l > 0 else [], [xb_p[t]] if l < NL - 1 else [])
                if "s" not in KGROUP:
                    continue
                sample_prep(l, Gs)
                src = x_s if l == 0 else xcur_s
                dst = xcur_s if l < NL - 1 else y_s
                run_tile(l, Gs, 0, src[:, :], dst[:, :], [xb_s] if l > 0 else [], [xb_s] if l < NL - 1 else [])
            S.emit()
        return nc


_CACHE = {}


def _prep_inputs(inp, c, consts2, npair):
    b = c % npair
    hf = c // npair
    consts = consts2[1 - hf]
    f = lambda a: np.ascontiguousarray(a, dtype=np.float32)
    nsb = inp["x_sample"].shape[0]
    cs = c % nsb
    xb = inp["x_prompt"][b]
    SEQ = xb.shape[0]
    m = {
        "x_p": f(xb.reshape(SEQ // 1024, 2, 512, D)[:, hf].reshape(SEQ // 2, D)), "x_s": f(inp["x_sample"][cs]), "mem_p": f(inp["mem_prompt"][b]),
        "ca_k": f(inp["cache_a_k"][:, cs].reshape(NL, -1, 512)), "ca_v": f(inp["cache_a_v"][:, cs].reshape(NL, -1, 512)),
        "cb_k": f(inp["cache_b_k"][:, cs].reshape(NL, -1, 256)), "cb_v": f(inp["cache_b_v"][:, cs].reshape(NL, -1, 256)),
        "cc_k": f(inp["cache_c_k"][:, cs].reshape(NL, -1, 256)), "cc_v": f(inp["cache_c_v"][:, cs].reshape(NL, -1, 256)),
        "cm_k": f(inp["cache_mem_k"][:, cs].reshape(NL, -1, 256)), "cm_v": f(inp["cache_mem_v"][:, cs].reshape(NL, -1, 256)),
        "t5": f(inp["t5_bias"]),
        "gains": f(np.stack([inp["ffn1_norm"], inp["mix_norm"], inp["x_norm"], inp["mem_norm"], inp["ffn2_norm"]], axis=1)),
        "hgains": f(np.stack([inp["a_qnorm"], inp["a_knorm"], inp["c_qnorm"], inp["c_knorm"], inp["x_qnorm"], inp["x_knorm"]], axis=1)),
        "lvec": f(np.stack([inp["a_lq1"], inp["a_lk1"], inp["a_lq2"], inp["a_lk2"]], axis=1)),
        "subln": f(inp["a_subln"]), "crel": f(inp["c_rel_bias"]),
        "wg1": f(inp["ffn1_wg"]), "wu1": f(inp["ffn1_wu"]), "wd1": f(inp["ffn1_wd"]), "w_in": f(inp["w_in"]),
        "w_br_a": f(inp["w_br_a"]), "w_br_b": f(inp["w_br_b"]), "w_br_c": f(inp["w_br_c"]), "w_out": f(inp["w_out"]),
        "x_wq": f(inp["x_wq"]), "x_wkv": f(inp["x_wkv"]), "x_wo": f(inp["x_wo"]),
        "wg2": f(inp["ffn2_wg"]), "wu2": f(inp["ffn2_wu"]), "wd2": f(inp["ffn2_wd"]),
        "oh_a": consts["oh_a"], "oh_c": consts["oh_c"], "mask_a": consts["mask_a"], "mask_b": consts["mask_b"], "mask_c": consts["mask_c"],
    }
    return m


def kernel(**inp):
    inp = {k: np.asarray(v) for k, v in inp.items()}
    B, SEQ = inp["x_prompt"].shape[0], inp["x_prompt"].shape[1]
    NB, NS = inp["x_sample"].shape[0], inp["x_sample"].shape[1]
    PAST = inp["cache_a_k"].shape[2]
    CB = inp["cache_c_k"].shape[2]
    consts2 = [host_consts(0), host_consts(1)]
    npair = B
    ncores = 2 * npair
    key = (SEQ, NS, PAST, CB, npair)
    if key not in _CACHE:
        _CACHE[key] = K(SEQ // 2, NS, PAST, CB, consts2[0]["far_bucket"], NPAIR=npair).build()
    nc = _CACHE[key]
    in_maps = [_prep_inputs(inp, c, consts2, npair) for c in range(ncores)]
    res = run_bass_kernel_spmd(nc, in_maps, core_ids=list(range(ncores)))
    R = res.results
    g = lambda c, name: np.asarray(R[c][name], dtype=np.float32)

    def inter(name, lead):
        outs = []
        for b in range(B):
            a0, a1 = g(b, name), g(b + npair, name)
            sh = a0.shape
            if lead:
                a0 = a0.reshape(sh[0], -1, 512, sh[-1]); a1 = a1.reshape(sh[0], -1, 512, sh[-1])
                outs.append(np.stack([a0, a1], axis=2).reshape(sh[0], SEQ, sh[-1]))
            else:
                a0 = a0.reshape(-1, 512, sh[-1]); a1 = a1.reshape(-1, 512, sh[-1])
                outs.append(np.stack([a0, a1], axis=1).reshape(SEQ, sh[-1]))
        return np.stack(outs, axis=1 if lead else 0)

    stl = lambda name, cores: np.stack([g(c, name) for c in cores], axis=1)
    sc = [c % ncores for c in range(NB)]
    late = [b + npair for b in range(B)]
    early = list(range(B))
    y_p = inter("y_p", False)
    y_s = np.stack([g(c, "y_s") for c in sc])
    outs = [y_p, y_s,
            inter("a_k_p", True).reshape(NL, B, SEQ, 4, 128), inter("a_v_p", True).reshape(NL, B, SEQ, 4, 128),
            inter("b_k_p", True).reshape(NL, B, SEQ, 4, 64), inter("b_v_p", True).reshape(NL, B, SEQ, 4, 64),
            stl("c_k_p", late).reshape(NL, B, 512, 4, 64), stl("c_v_p", late).reshape(NL, B, 512, 4, 64),
            stl("m_k_p", early).reshape(NL, B, NMEM, 4, 64), stl("m_v_p", early).reshape(NL, B, NMEM, 4, 64),
            stl("a_k_s", sc).reshape(NL, NB, NS, 4, 128), stl("a_v_s", sc).reshape(NL, NB, NS, 4, 128),
            stl("b_k_s", sc).reshape(NL, NB, NS, 4, 64), stl("b_v_s", sc).reshape(NL, NB, NS, 4, 64),
            stl("c_k_s", sc).reshape(NL, NB, CB, 4, 64), stl("c_v_s", sc).reshape(NL, NB, CB, 4, 64)]
    return tuple(np.ascontiguousarray(o) for o in outs)
```

```python
import math
import contextlib
import numpy as np
import concourse.bass as bass
import concourse.mybir as mybir
from concourse.bass_utils import run_bass_kernel_spmd

F32 = mybir.dt.float32
BF16 = mybir.dt.bfloat16
ALU = mybir.AluOpType
AF = mybir.ActivationFunctionType
AX = mybir.AxisListType

D = 1024
DFF = 4096
HD = 64
NL = 2
NMEM = 256
CHUNK = 64
C_PREV = 8
REL_CLIP = 128
EPS = 1e-6
NEG = -30000.0
WA_MAIN, WA_S = 1536, 160
WB_MAIN, WB_S = 1408, 32
WC_MAIN, WC_S = 1920, 544
WA_STRIP = WA_MAIN + WA_S
WB_STRIP = WB_MAIN + WB_S
WC_STRIP = WC_MAIN + WC_S
GA_W = 2048
GC_W = 2048

import os
STAGE = int(os.environ.get("KSTAGE", "8"))
KGROUP = os.environ.get("KGROUP", "ps")
ENGS = ("pe", "act", "dve", "pool", "sp")
NDMASEM = 24


class Buf:
    __slots__ = ("w", "r", "name", "x")

    def __init__(self, name=""):
        self.w = None
        self.r = []
        self.name = name
        self.x = False


class TT:
    def __init__(self, t, name):
        self.t = t
        self.b = Buf(name)


class Sched:
    def __init__(self, nc):
        self.nc = nc
        self.acts = {e: [] for e in ENGS}
        self.ninst = {e: 0 for e in ENGS}
        self.flag = {e: set() for e in ENGS}
        self.seen = {e: {s: 0 for s in ENGS} for e in ENGS}
        self.dseen = {e: {} for e in ENGS}
        self.dma_n = {e: 0 for e in ENGS}
        self.dma_val = {}

    def _wait(self, e, tok):
        if tok is None:
            return
        if tok[0] == "e":
            _, s, idx = tok
            if self.seen[e][s] >= idx:
                return
            self.flag[s].add(idx)
            self.acts[e].append(("we", s, idx))
            self.seen[e][s] = idx
        else:
            _, q, slot, val = tok
            if self.dseen[e].get((q, slot), 0) >= val:
                return
            self.acts[e].append(("wd", q, slot, val))
            self.dseen[e][(q, slot)] = val

    def _deps(self, e, reads, writes):
        for b in reads:
            self._wait(e, b.w)
            if b.x:
                for t in b.r:
                    if not (t[0] == "e" and t[1] == e):
                        self._wait(e, t)
        for b in writes:
            for t in [b.w] + b.r:
                if t is not None and t[0] == "e" and t[1] == e:
                    continue
                self._wait(e, t)

    def _commit(self, tok, reads, writes):
        for b in reads:
            if tok[0] == "e":
                b.r = [t for t in b.r if not (t[0] == "e" and t[1] == tok[1])]
            else:
                b.r = [t for t in b.r if not (t[0] == "d" and t[1] == tok[1] and t[2] == tok[2])]
            b.r.append(tok)
        for b in writes:
            b.w = tok
            b.r = []

    def op(self, e, fn, reads=(), writes=()):
        self._deps(e, reads, writes)
        self.ninst[e] += 1
        idx = self.ninst[e]
        self.acts[e].append(("i", fn, idx))
        self._commit(("e", e, idx), reads, writes)

    def dma(self, q, out, in_, reads=(), writes=(), **kw):
        self._deps(q, reads, writes)
        n = self.dma_n[q]
        self.dma_n[q] += 1
        slot = n % NDMASEM
        prev = self.dma_val.get((q, slot), 0)
        if prev:
            self._wait(q, ("d", q, slot, prev))
        val = prev + 16
        self.dma_val[(q, slot)] = val
        self.acts[q].append(("d", out, in_, slot, kw))
        self._commit(("d", q, slot, val), reads, writes)

    def cc(self, fn, reads=(), writes=()):
        self._deps("pool", reads, writes)
        self.ncc = getattr(self, "ncc", 0) + 1
        self.acts["pool"].append(("c", fn))
        self._commit(("d", "cc", 0, self.ncc), reads, writes)

    def emit(self):
        nc = self.nc
        for (q, slot), val in sorted(self.dma_val.items()):
            self._wait("sp", ("d", q, slot, val))
        if getattr(self, "ncc", 0):
            self._wait("sp", ("d", "cc", 0, self.ncc))
        with contextlib.ExitStack() as st:
            esem = {e: st.enter_context(nc.semaphore("es_" + e)) for e in ENGS}
            dsem = {}
            for q in ENGS:
                for s in range(min(NDMASEM, self.dma_n[q])):
                    dsem[(q, s)] = st.enter_context(nc.semaphore("ds_%s_%d" % (q, s)))
            if getattr(self, "ncc", 0):
                dsem[("cc", 0)] = st.enter_context(nc.semaphore("ccsem"))
            block = st.enter_context(nc.Block())
            cum = {}
            for e in ENGS:
                cum[e] = {idx: i + 1 for i, idx in enumerate(sorted(self.flag[e]))}

            def run(e, eng):
                for a in self.acts[e]:
                    k = a[0]
                    if k == "we":
                        eng.wait_ge(esem[a[1]], cum[a[1]][a[2]])
                    elif k == "wd":
                        eng.wait_ge(dsem[(a[1], a[2])], a[3])
                    elif k == "c":
                        a[1](eng).then_inc(dsem[("cc", 0)], 1)
                    elif k == "i":
                        ins = a[1](eng)
                        if a[2] in cum[e]:
                            ins.then_inc(esem[e], 1)
                    else:
                        eng.dma_start(out=a[1], in_=a[2], **a[4]).then_inc(dsem[(e, a[3])], 16)

            @block.tensor
            def _(eng):
                run("pe", eng)

            @block.scalar
            def _(eng):
                run("act", eng)

            @block.vector
            def _(eng):
                run("dve", eng)

            @block.gpsimd
            def _(eng):
                run("pool", eng)

            @block.sync
            def _(eng):
                run("sp", eng)


def _t5_bucket_np(rel):
    import jax
    import jax.numpy as jnp
    with jax.default_device(jax.devices("cpu")[0]):
        rel = jnp.asarray(rel, dtype=jnp.int32)
        half = 16
        max_exact = 8
        n = jnp.abs(rel)
        nf = jnp.maximum(n, 1).astype(jnp.float32)
        large = max_exact + (jnp.log(nf / max_exact) / math.log(128 / max_exact) * (half - max_exact)).astype(jnp.int32)
        large = jnp.minimum(large, half - 1)
        out = jnp.where(rel > 0, half, 0) + jnp.where(n < max_exact, n, large)
        return np.asarray(out)


def lambda_init(layer):
    return 0.8 - 0.6 * math.exp(-0.3 * layer)


def host_consts(e=0):
    c = {}
    def oh_a_for(ee):
        s_ = np.arange(GA_W)
        bk = _t5_bucket_np(511 + 512 * ee - s_)
        oh = np.zeros((32, GA_W), np.float32)
        oh[bk, s_] = 1.0
        return oh
    def oh_c_for(ee):
        s_ = np.arange(GC_W)
        idx = np.clip(511 + 512 * ee - s_, -REL_CLIP, REL_CLIP) + REL_CLIP
        oh = np.zeros((384, GC_W), np.float32)
        oh[idx, s_] = 1.0
        return oh
    c["oh_a"] = np.stack([oh_a_for(e), oh_a_for(0)])
    c["oh_c"] = np.stack([oh_c_for(e), oh_c_for(0)])
    c["far_bucket"] = int(_t5_bucket_np(np.array([-4000]))[0])
    k = np.arange(128)[:, None]
    def masks(ee, jj):
        rel = k - jj + 384 + 512 * ee
        dch = k // 64 - jj // 64 + 6 + 8 * ee
        ma = np.where(dch <= 0, 0.0, NEG).astype(np.float32)
        mb = (rel < 0).astype(np.float32)
        mc = np.where((dch <= 0) & (dch >= -C_PREV), 0.0, NEG).astype(np.float32)
        return ma, mb, mc
    ma, _, _ = masks(e, np.arange(WA_MAIN)[None, :])
    _, mb, _ = masks(e, np.arange(WB_MAIN)[None, :])
    _, _, mc = masks(e, np.arange(WC_MAIN)[None, :])
    mas, _, _ = masks(0, np.arange(384, 384 + WA_S)[None, :])
    _, mbs, _ = masks(0, np.arange(384, 384 + WB_S)[None, :])
    _, _, mcs = masks(0, np.arange(384, 384 + WC_S)[None, :])
    c["mask_a"] = np.concatenate([ma, mas], axis=1)
    c["mask_b"] = np.concatenate([mb, mbs], axis=1)
    c["mask_c"] = np.concatenate([mc, mcs], axis=1)
    return c


class Group:
    pass


class K:
    def __init__(self, SEQ, NS, PAST, CB, far_bucket, NPAIR=4):
        self.SEQ, self.NS, self.PAST, self.CB = SEQ, NS, PAST, CB
        self.NPAIR = NPAIR
        self.far_bucket = far_bucket
        self.nc = bass.Bass("TRN2", target_bir_lowering=False)
        self.S = Sched(self.nc)
        self.din = {}
        self.dout = {}

    def dram_in(self, name, shape):
        self.din[name] = self.nc.dram_tensor(name, list(shape), F32, kind="ExternalInput").ap()
        return self.din[name]

    def dram_out(self, name, shape):
        self.dout[name] = self.nc.dram_tensor(name, list(shape), F32, kind="ExternalOutput").ap()
        return self.dout[name]

    def dram_tmp(self, name, shape, dt):
        return self.nc.dram_tensor(name, list(shape), dt).ap()

    def sb(self, name, shape, dt):
        return TT(self.st.enter_context(self.nc.sbuf_tensor(name, list(shape), dt)), name)

    def mm(self, out, lhsT, rhs, start, stop, reads, writes):
        self.S.op("pe", lambda e: e.matmul(out, lhsT=lhsT, rhs=rhs, start=start, stop=stop), reads, writes)

    def act(self, out, in_, func, reads, writes, **kw):
        self.S.op("act", lambda e: e.activation(out=out, in_=in_, func=func, **kw), reads, writes)

    def tt(self, out, in0, in1, op, reads, writes, eng="dve"):
        self.S.op(eng, lambda e: e.tensor_tensor(out=out, in0=in0, in1=in1, op=op), reads, writes)

    def ts(self, out, in0, s1, s2, op0, op1, reads, writes, eng="dve"):
        if s2 is None:
            self.S.op(eng, lambda e: e.tensor_scalar(out=out, in0=in0, scalar1=s1, scalar2=None, op0=op0), reads, writes)
        else:
            self.S.op(eng, lambda e: e.tensor_scalar(out=out, in0=in0, scalar1=s1, scalar2=s2, op0=op0, op1=op1), reads, writes)

    def stt(self, out, in0, scalar, in1, op0, op1, reads, writes):
        self.S.op("dve", lambda e: e.scalar_tensor_tensor(out=out, in0=in0, scalar=scalar, in1=in1, op0=op0, op1=op1), reads, writes)

    def cp(self, out, in_, reads, writes, eng="dve"):
        self.S.op(eng, lambda e: e.tensor_copy(out=out, in_=in_), reads, writes)

    def recip(self, out, in_, reads, writes):
        self.S.op("dve", lambda e: e.reciprocal(out=out, in_=in_), reads, writes)

    def memset(self, ap, val, writes, eng="pool"):
        self.S.op(eng, lambda e: e.memset(ap, val), (), writes)

    def dma(self, out, in_, reads=(), writes=(), q="sp", **kw):
        self.S.dma(q, out, in_, reads, writes, **kw)

    def ring(self, key):
        lst, i = self.rings[key]
        self.rings[key][1] = (i + 1) % len(lst)
        return lst[i]

    def build(self):
        nc = self.nc
        SEQ, NS, PAST, CB = self.SEQ, self.NS, self.PAST, self.CB
        di, do, dt_ = self.dram_in, self.dram_out, self.dram_tmp
        x_p = di("x_p", [SEQ, D]); x_s = di("x_s", [NS, D]); mem_p = di("mem_p", [NMEM, D])
        ca_k = di("ca_k", [NL, PAST, 512]); ca_v = di("ca_v", [NL, PAST, 512])
        cb_k = di("cb_k", [NL, PAST, 256]); cb_v = di("cb_v", [NL, PAST, 256])
        cc_k = di("cc_k", [NL, CB, 256]); cc_v = di("cc_v", [NL, CB, 256])
        cm_k = di("cm_k", [NL, NMEM, 256]); cm_v = di("cm_v", [NL, NMEM, 256])
        t5 = di("t5", [32, 4])
        gains = di("gains", [NL, 5, D])
        hgains = di("hgains", [NL, 6, HD])
        lvec = di("lvec", [NL, 4, HD])
        subln = di("subln", [NL, 128])
        crel = di("crel", [NL, 4, 257])
        W = {}
        wshapes = {"wg1": (D, DFF), "wu1": (D, DFF), "wd1": (DFF, D), "w_in": (D, 6144), "w_br_a": (512, D),
                   "w_br_b": (256, D), "w_br_c": (256, D), "w_out": (D, D), "x_wq": (D, 256), "x_wkv": (D, 512),
                   "x_wo": (256, D), "wg2": (D, DFF), "wu2": (D, DFF), "wd2": (DFF, D)}
        for n_, shp in wshapes.items():
            W[n_] = di(n_, [NL, shp[0], shp[1]])
        oh_a = di("oh_a", [2, 32, GA_W]); oh_c = di("oh_c", [2, 384, GC_W])
        mask_a = di("mask_a", [128, WA_STRIP]); mask_b = di("mask_b", [128, WB_STRIP]); mask_c = di("mask_c", [128, WC_STRIP])

        y_p = do("y_p", [SEQ, D]); y_s = do("y_s", [NS, D])
        o_akp = do("a_k_p", [NL, SEQ, 512]); o_avp = do("a_v_p", [NL, SEQ, 512])
        o_bkp = do("b_k_p", [NL, SEQ, 256]); o_bvp = do("b_v_p", [NL, SEQ, 256])
        CKP = 512
        o_ckp = do("c_k_p", [NL, CKP, 256]); o_cvp = do("c_v_p", [NL, CKP, 256])
        o_mkp = do("m_k_p", [NL, NMEM, 256]); o_mvp = do("m_v_p", [NL, NMEM, 256])
        o_aks = do("a_k_s", [NL, NS, 512]); o_avs = do("a_v_s", [NL, NS, 512])
        o_bks = do("b_k_s", [NL, NS, 256]); o_bvs = do("b_v_s", [NL, NS, 256])
        o_cks = do("c_k_s", [NL, CB, 256]); o_cvs = do("c_v_s", [NL, CB, 256])

        Wb = {}
        Wbuf = {}
        for n_, shp in wshapes.items():
            Wb[n_] = dt_("wb_" + n_, [NL, shp[0], shp[1]], BF16)
        xcur = dt_("xcur", [SEQ, D], F32)
        xcur_s = dt_("xcur_s", [NS, D], F32)
        ga_d = dt_("ga_d", [2, 4, GA_W], F32)
        gc_d = dt_("gc_d", [2, 4, GC_W], F32)
        NSLOT = SEQ // 512
        xkv = [dt_("xkv%d" % i, [2048, 512], BF16) for i in range(2)]
        xkvb = [Buf("xkv%d" % i) for i in range(2)]
        gout = [dt_("gout%d" % i, [4096, 512], BF16) for i in range(NSLOT)]
        gbuf = [Buf("gout%d" % i) for i in range(NSLOT)]

        def kvscr(tag, n):
            g = {}
            g["kta"] = dt_("kta_" + tag, [4, 128, n], BF16); g["va"] = dt_("va_" + tag, [n, 512], BF16)
            g["ktb"] = dt_("ktb_" + tag, [2, 128, n], BF16); g["vb"] = dt_("vb_" + tag, [n, 256], BF16)
            g["ktc"] = dt_("ktc_" + tag, [2, 128, n], BF16); g["vc"] = dt_("vc_" + tag, [n, 256], BF16)
            return g

        with contextlib.ExitStack() as st:
            self.st = st
            sb = self.sb
            S = self.S
            self.bank = [TT(st.enter_context(nc.psum_tensor("bank%d" % i, [128, 512], F32)), "bank%d" % i) for i in range(8)]
            bank = self.bank
            for b_ in bank:
                b_.b.x = True
            ident = sb("ident", [128, 128], BF16); onesb = sb("onesb", [128, 128], BF16)
            onesf = sb("onesf", [128, 128], F32); Jf = sb("Jf", [128, 128], F32)
            negU8 = sb("negU8", [128, 128], BF16); epsc = sb("epsc", [128, 1], F32)
            tmpc = sb("tmpc", [128, 128], F32)
            self.ident, self.onesb, self.onesf, self.negU8, self.epsc = ident, onesb, onesf, negU8, epsc
            self.memset(tmpc.t[:], 0.0, [tmpc.b])
            S.op("pool", lambda e: e.affine_select(out=tmpc.t[:], in_=tmpc.t[:], pattern=[[-1, 128]], compare_op=ALU.not_equal, fill=1.0, base=0, channel_multiplier=1), [tmpc.b], [tmpc.b])
            self.cp(ident.t[:], tmpc.t[:], [tmpc.b], [ident.b])
            identF = sb("identF", [128, 128], F32)
            self.cp(identF.t[:], tmpc.t[:], [tmpc.b], [identF.b])
            self.memset(Jf.t[:], 0.0, [Jf.b])
            S.op("pool", lambda e: e.affine_select(out=Jf.t[:], in_=Jf.t[:], pattern=[[1, 128]], compare_op=ALU.not_equal, fill=1.0, base=-127, channel_multiplier=1), [Jf.b], [Jf.b])
            self.memset(onesf.t[:], 1.0, [onesf.b])
            self.cp(onesb.t[:], onesf.t[:], [onesf.b], [onesb.b])
            tmpc2 = tmpc
            self.memset(tmpc2.t[:], -8.0, [tmpc2.b])
            S.op("pool", lambda e: e.affine_select(out=tmpc2.t[:], in_=tmpc2.t[:], pattern=[[-1, 128]], compare_op=ALU.is_ge, fill=0.0, base=0, channel_multiplier=1), [tmpc2.b], [tmpc2.b])
            self.cp(negU8.t[:], tmpc2.t[:], [tmpc2.b], [negU8.b])
            self.memset(epsc.t[:], EPS, [epsc.b])
            onec = sb("onec", [128, 1], F32)
            self.memset(onec.t[:], 1.0, [onec.b])

            def bc_ap(src, off, n):
                return bass.AP(tensor=src.tensor, offset=off, ap=[[0, 128], [1, n]])

            t5bc = sb("t5bc", [128, 128], F32)
            self.dma(t5bc.t[:], bc_ap(t5, 0, 128), (), [t5bc.b])
            gcol = sb("gcol", [128, NL * 5, 8], F32)
            NG = NL * 5 * 8
            self.dma(tmpc.t[:NG, :], gains.rearrange("l w (c p) -> (l w c) p", p=128), (), [tmpc.b])
            self.mm(bank[1].t[:, :NG], tmpc.t[:NG, :], identF.t[:NG, :NG], True, True, [tmpc.b, identF.b], [bank[1].b])
            self.cp(gcol.t[:].rearrange("p a b -> p (a b)"), bank[1].t[:, :NG], [bank[1].b], [gcol.b])
            hg = sb("hg", [128, NL * 6, HD], F32)
            self.dma(hg.t[:].rearrange("p a b -> p (a b)"), bc_ap(hgains, 0, NL * 6 * HD), (), [hg.b])
            lv = sb("lv", [128, NL * 4, HD], F32)
            self.dma(lv.t[:].rearrange("p a b -> p (a b)"), bc_ap(lvec, 0, NL * 4 * HD), (), [lv.b])
            subcol = sb("subcol", [128, NL], F32)
            self.dma(tmpc.t[:NL, :], subln[:, :], (), [tmpc.b])
            self.mm(bank[1].t[:, :NL], tmpc.t[:NL, :], identF.t[:NL, :NL], True, True, [tmpc.b, identF.b], [bank[1].b])
            self.cp(subcol.t[:], bank[1].t[:, :NL], [bank[1].b], [subcol.b])
            neglam = sb("neglam", [128, NL], F32)
            lt = sb("lt", [128, HD], F32); le = sb("le", [128, 4], F32)
            for l in range(NL):
                for i in range(2):
                    self.tt(lt.t[:], lv.t[:, l * 4 + 2 * i, :], lv.t[:, l * 4 + 2 * i + 1, :], ALU.mult, [lv.b], [lt.b])
                    S.op("dve", lambda e, i=i: e.tensor_reduce(out=le.t[:, i:i + 1], in_=lt.t[:], axis=AX.X, op=ALU.add), [lt.b], [le.b])
                self.act(le.t[:, 2:4], le.t[:, 0:2], AF.Exp, [le.b], [le.b])
                self.tt(neglam.t[:, l:l + 1], le.t[:, 3:4], le.t[:, 2:3], ALU.subtract, [le.b], [neglam.b])
                self.ts(neglam.t[:, l:l + 1], neglam.t[:, l:l + 1], -lambda_init(l), None, ALU.add, ALU.bypass, [neglam.b], [neglam.b])
                self.ts(subcol.t[:, l:l + 1], subcol.t[:, l:l + 1], 1.0 - lambda_init(l), None, ALU.mult, ALU.bypass, [subcol.b], [subcol.b])
            self.neglam, self.subcol, self.hg, self.gcol, self.t5bc = neglam, subcol, hg, gcol, t5bc

            xin = sb("xin", [128, 4, D], F32)
            hT = sb("hT", [128, 8, 512], BF16)
            hidT = sb("hidT", [128, 32, 512], BF16)
            self.rings = {
                "xn": [[sb("xn%d" % i, [128, D], BF16) for i in range(2)], 0],
                "w": [[sb("wslot%d" % i, [128, 8, 512], BF16) for i in range(4)], 0],
                "gt": [[sb("gt%d" % i, [128, 3, 512], BF16) for i in range(2)], 0],
                "tf": [[sb("tf%d" % i, [128, 512], F32) for i in range(4)], 0],
                "tb": [[sb("tb%d" % i, [128, 512], BF16) for i in range(5)], 0],
                "of": [[sb("of%d" % i, [128, 512], F32) for i in range(3)], 0],
                "ob": [[sb("ob%d" % i, [128, 4, 128], BF16) for i in range(2)], 0],
                "kt": [[sb("ktl%d" % i, [128, 1024], BF16) for i in range(2)], 0],
                "v": [[sb("vl%d" % i, [128, 8, 128], BF16) for i in range(2)], 0],
                "sc": [[bank[0], bank[1], bank[6], bank[7]], 0],
                "gu": [[(bank[0], bank[1]), (bank[2], bank[3]), (bank[4], bank[5])], 0],
            }
            QTA = [sb("QTA%d" % i, [128, 4, 512], BF16) for i in range(2)]
            QTB = [sb("QTB%d" % i, [128, 2, 512], BF16) for i in range(2)]
            QTC = [sb("QTC%d" % i, [128, 2, 512], BF16) for i in range(2)]
            QTX = QTC
            for q_ in QTA + QTB + QTC:
                self.memset(q_.t[:], 0.0, [q_.b])
            gates_d = {"p": [dt_("gates_p%d" % i, [4, 128, 3072], BF16) for i in range(2)], "s": [dt_("gates_s", [4, 128, 3072], BF16)]}
            gates_b = {"p": [Buf("gp0"), Buf("gp1")], "s": [Buf("gs")]}
            oaT = sb("oaT", [128, 4, 512], BF16); obT = sb("obT", [128, 2, 512], BF16)
            ocT = sb("ocT", [128, 2, 512], BF16); oxT = ocT
            Rt = sb("Rt", [128, 512], F32)
            mrgb = sb("mrgb", [128, D], BF16); junk = mrgb
            stripA = sb("stripA", [128, 4, WA_STRIP], BF16)
            stripC = sb("stripC", [128, 4, WC_STRIP], BF16)
            maskB = sb("maskB", [128, WB_STRIP], BF16)
            mkT = {g: sb("mkT_" + g, [128, 2, NMEM], BF16) for g in "ps"}
            mvb = {g: sb("mvb_" + g, [128, 2, 256], BF16) for g in "ps"}
            self.rings["hn"] = [[(sb("ssg%d" % i, [128, 8], F32), sb("sdg%d" % i, [128, 8], F32), sb("rsg%d" % i, [128, 8], F32)) for i in range(4)], 0]
            self.rings["rn"] = [[(sb("ss%d" % i, [128, 1], F32), sb("sd%d" % i, [128, 1], F32), sb("rs%d" % i, [128, 1], F32)) for i in range(4)], 0]
            self.xin, self.hT, self.hidT = xin, hT, hidT

            for n_, shp in wshapes.items():
                Wbuf[n_] = []
                rows = max(1, (1 << 20) // shp[1])
                for l in range(NL):
                    bl = []
                    for r0 in range(0, shp[0], rows):
                        r1 = min(shp[0], r0 + rows)
                        b_ = Buf("wb_%s_%d_%d" % (n_, l, r0))
                        self.dma(Wb[n_][l, r0:r1, :], W[n_][l, r0:r1, :], (), [b_], q="pool")
                        bl.append(b_)
                    Wbuf[n_].append(bl)

            stg_t = xin.t[:, 0:2, :].rearrange("p a b -> p (a b)")
            stg2_t = xin.t[:, 2:4, :].rearrange("p a b -> p (a b)")
            self.dma(stg_t[:, :WB_STRIP], mask_b[:, :], (), [xin.b])
            self.cp(maskB.t[:], stg_t[:, :WB_STRIP], [xin.b], [maskB.b])
            t5_sb = sb("t5_sb", [32, 4], F32)
            self.dma(t5_sb.t[:], t5[:, :], (), [t5_sb.b])
            gsb_t = stg2_t[:4, :GC_W]
            gab = Buf("ga_d")
            for v_ in range(2):
                self.dma(stg_t[:32, :GA_W], oh_a[v_], (), [xin.b])
                for c0 in range(0, GA_W, 512):
                    self.mm(bank[0].t[:4, :512], t5_sb.t[:, :], stg_t[:32, c0:c0 + 512], True, True, [t5_sb.b, xin.b], [bank[0].b])
                    self.cp(gsb_t[:, c0:c0 + 512], bank[0].t[:4, :512], [bank[0].b], [xin.b])
                self.dma(ga_d[v_], gsb_t[:, :GA_W], [xin.b], [gab])

            def build_strip(gd, gbuf_, gw, goff, jj_start, mask_d, mcol0, width, dst, dcol0):
                self.dma(stg2_t[:, :width], mask_d[:, mcol0:mcol0 + width], (), [xin.b])
                for h in range(4):
                    src = bass.AP(tensor=gd.tensor, offset=goff + h * gw + jj_start, ap=[[1, 128], [1, width]])
                    self.dma(stg_t[:, :width], src, [gbuf_], [xin.b])
                    for c0 in range(0, width, 512):
                        cw = min(512, width - c0)
                        self.mm(bank[0].t[:, :cw], Jf.t[:, :], stg_t[:, c0:c0 + cw], True, True, [Jf.b, xin.b], [bank[0].b])
                        self.tt(dst.t[:, h, dcol0 + c0:dcol0 + c0 + cw], bank[0].t[:, :cw], stg2_t[:, c0:c0 + cw], ALU.add, [bank[0].b, xin.b], [dst.b])

            build_strip(ga_d, gab, GA_W, 0, 0, mask_a, 0, WA_MAIN, stripA, 0)
            build_strip(ga_d, gab, GA_W, 4 * GA_W, 384, mask_a, WA_MAIN, WA_S, stripA, WA_MAIN)
            crT = sb("crT", [128, 3, 4], F32)
            cr_in = sb("cr_in", [4, 264], F32)
            gcb = Buf("gc_d")

            def layer_setup(l):
                self.memset(crT.t[:], 0.0, [crT.b])
                self.dma(cr_in.t[:4, :257], crel[l], (), [cr_in.b])
                for c in range(3):
                    n = min(128, 257 - c * 128)
                    self.mm(bank[1].t[:n, :4], cr_in.t[:4, c * 128:c * 128 + n], identF.t[:4, :4], True, True, [cr_in.b, identF.b], [bank[1].b])
                    self.cp(crT.t[:n, c, :], bank[1].t[:n, :4], [bank[1].b], [crT.b])
                ohv = stg_t[:, :1536].rearrange("p (c s) -> p c s", s=512)
                for v_ in range(2):
                    for c0 in range(0, GC_W, 512):
                        self.dma(ohv, oh_c[v_, :, c0:c0 + 512].rearrange("(c p) s -> p c s", p=128), (), [xin.b])
                        for c in range(3):
                            self.mm(bank[0].t[:4, :512], crT.t[:, c, :], ohv[:, c, :], c == 0, c == 2, [crT.b, xin.b], [bank[0].b])
                        self.cp(gsb_t[:, c0:c0 + 512], bank[0].t[:4, :512], [bank[0].b], [xin.b])
                    self.dma(gc_d[v_], gsb_t[:, :], [xin.b], [gcb])
                build_strip(gc_d, gcb, GC_W, 0, 0, mask_c, 0, WC_MAIN, stripC, 0)
                build_strip(gc_d, gcb, GC_W, 4 * GC_W, 384, mask_c, WC_MAIN, WC_S, stripC, WC_MAIN)

            def load_w(name, l, r0, c0, ncols, kc=8, pk=128):
                slot = self.ring("w")
                src = Wb[name][l, r0:r0 + kc * pk, c0:c0 + ncols].rearrange("(c p) n -> p c n", p=pk)
                self.dma(slot.t[:pk, :kc, :ncols], src, Wbuf[name][l], [slot.b])
                return slot

            def rms_T(src, np_, nsub, gi, dst):
                sc_ = []
                for s in range(nsub):
                    ss_, sd_, rs_ = self.ring("rn")
                    self.memset(ss_.t[:], 0.0, [ss_.b], eng="dve")
                    self.act(junk.t[:np_, :], src.t[:np_, s, :], AF.Square, [src.b], [junk.b, ss_.b], accum_out=ss_.t[:np_, 0:1])
                    sc_.append((ss_, sd_, rs_))
                for s in range(nsub):
                    ss_, sd_, rs_ = sc_[s]
                    self.act(sd_.t[:np_, :], ss_.t[:np_, :], AF.Sqrt, [ss_.b, epsc.b], [sd_.b], scale=1.0 / D, bias=epsc.t[:np_, :])
                    self.recip(rs_.t[:np_, :], sd_.t[:np_, :], [sd_.b], [rs_.b])
                    xn = self.ring("xn")
                    self.ts(xn.t[:np_, :], src.t[:np_, s, :], rs_.t[:np_, 0:1], None, ALU.mult, ALU.bypass, [src.b, rs_.b], [xn.b])
                    self.transpose_to(lambda c, xn=xn: xn.t[:np_, c * 128:(c + 1) * 128], [xn.b], 8, np_,
                                      dst.t[:, :, s * 128:s * 128 + np_], [dst.b],
                                      mul=gcol.t[:, gi, :].unsqueeze(2).to_broadcast([128, 8, np_]), mulb=[gcol.b])

            self.trp_i = 0
            self.acc_i = 0

            def transpose_to(src_fn, srcb, nblk, np_, dst_ap, dstb, mul=None, mulb=()):
                bk = bank[6] if self.trp_i % 2 == 0 else bank[7]
                self.trp_i += 1
                tv = bk.t[:, :].bitcast(BF16).rearrange("p (c n) -> p c n", n=128)
                for c in range(nblk):
                    S.op("pe", lambda e, c=c: e.transpose(out=tv[:, c, :np_], in_=src_fn(c), identity=ident.t[:np_, :np_]), list(srcb) + [ident.b], [bk.b])
                if mul is None:
                    self.act(dst_ap, tv[:, :nblk, :np_], AF.Copy, [bk.b], dstb)
                else:
                    self.tt(dst_ap, tv[:, :nblk, :np_], mul, ALU.mult, [bk.b] + list(mulb), dstb)
            self.transpose_to = transpose_to

            def transpose_split(src_fn, srcb, nblk, np_, q2, cs_):
                bk = bank[6] if self.trp_i % 2 == 0 else bank[7]
                self.trp_i += 1
                tv = bk.t[:, :].bitcast(BF16).rearrange("p (c n) -> p c n", n=128)
                for c in range(nblk):
                    S.op("pe", lambda e, c=c: e.transpose(out=tv[:, c, :np_], in_=src_fn(c), identity=ident.t[:np_, :np_]), list(srcb) + [ident.b], [bk.b])
                self.act(q2[0].t[0:64, :nblk, cs_], tv[0:64, :nblk, :np_], AF.Copy, [bk.b], [q2[0].b])
                self.cp(q2[1].t[64:128, :nblk, cs_], tv[64:128, :nblk, :np_], [bk.b], [q2[1].b])

            def linear_tm(name, l, KC, N, lhs_fn, lhsb, np_, nsub, evac, pieces=None):
                npieces = (N + 511) // 512
                accring = [bank[2], bank[3], bank[4], bank[5], bank[0], bank[1]]
                DEFER = 3
                pend_ev = []
                for p in (pieces if pieces is not None else range(npieces)):
                    ncols = min(512, N - p * 512)
                    if KC <= 8:
                        slot = load_w(name, l, 0, p * 512, ncols, KC)
                        for s in range(nsub):
                            acc = accring[self.acc_i % 6]
                            self.acc_i += 1
                            for k in range(KC):
                                self.mm(acc.t[:np_, :ncols], lhs_fn(k, s), slot.t[:, k, :ncols], k == 0, k == KC - 1,
                                        list(lhsb) + [slot.b], [acc.b])
                            pend_ev.append((p, s, acc, ncols))
                            if len(pend_ev) > DEFER:
                                evac(*pend_ev.pop(0))
                        continue
                    for kg in range(0, KC, 8):
                        kc = min(8, KC - kg)
                        slot = load_w(name, l, kg * 128, p * 512, ncols, kc)
                        for s in range(nsub):
                            acc = bank[2 + s]
                            for k in range(kc):
                                self.mm(acc.t[:np_, :ncols], lhs_fn(kg + k, s), slot.t[:, k, :ncols], kg + k == 0, kg + k == KC - 1,
                                        list(lhsb) + [slot.b], [acc.b])
                    for s in range(nsub):
                        evac(p, s, bank[2 + s], ncols)
                while pend_ev:
                    evac(*pend_ev.pop(0))

            def headnorm(pb_ap, pbb, np_, G, gi, out_ap, outb):
                tf = self.ring("tf")
                ssg, sdg, rsg = self.ring("hn")
                self.act(tf.t[:np_, :G * 64], pb_ap, AF.Square, pbb, [tf.b])
                S.op("dve", lambda e: e.tensor_reduce(out=ssg.t[:np_, :G], in_=tf.t[:np_, :G * 64].rearrange("p (g d) -> p g d", d=64), axis=AX.X, op=ALU.add), [tf.b], [ssg.b])
                self.act(sdg.t[:np_, :G], ssg.t[:np_, :G], AF.Sqrt, [ssg.b, epsc.b], [sdg.b], scale=1.0 / HD, bias=epsc.t[:np_, :])
                self.recip(rsg.t[:np_, :G], sdg.t[:np_, :G], [sdg.b], [rsg.b])
                tf2 = self.ring("tf")
                self.tt(tf2.t[:np_, :G * 64].rearrange("p (g d) -> p g d", d=64), pb_ap.rearrange("p (g d) -> p g d", d=64),
                        rsg.t[:np_, :G].unsqueeze(2).to_broadcast([np_, G, 64]), ALU.mult, list(pbb) + [rsg.b], [tf2.b])
                self.tt(out_ap.rearrange("p (g d) -> p g d", d=64), tf2.t[:np_, :G * 64].rearrange("p (g d) -> p g d", d=64),
                        hg.t[:np_, gi, :].unsqueeze(1).to_broadcast([np_, G, 64]), ALU.mult, [tf2.b, hg.b], outb)

            def ffn(l, which, G):
                np_, nsub, T = G.np, G.nsub, G.T
                sfx = "1" if which == 0 else "2"
                rms_T(xin, np_, nsub, l * 5 + (0 if which == 0 else 4), hT)
                for fg in range(8):
                    wgs = load_w("wg" + sfx, l, 0, fg * 512, 512)
                    wus = load_w("wu" + sfx, l, 0, fg * 512, 512)
                    for fc in range(4):
                        bg, bu = self.ring("gu")
                        for k in range(8):
                            self.mm(bg.t[:, :T], wgs.t[:, k, fc * 128:(fc + 1) * 128], hT.t[:, k, :T], k == 0, k == 7, [wgs.b, hT.b], [bg.b])
                        for k in range(8):
                            self.mm(bu.t[:, :T], wus.t[:, k, fc * 128:(fc + 1) * 128], hT.t[:, k, :T], k == 0, k == 7, [wus.b, hT.b], [bu.b])
                        tf = self.ring("tf")
                        self.act(tf.t[:, :T], bg.t[:, :T], AF.Silu, [bg.b], [tf.b])
                        self.tt(hidT.t[:, fg * 4 + fc, :T], tf.t[:, :T], bu.t[:, :T], ALU.mult, [tf.b, bu.b], [hidT.b])

                def evac(p, s, acc, ncols):
                    self.stt(xin.t[:np_, s, p * 512:(p + 1) * 512], acc.t[:np_, :512], 0.5, xin.t[:np_, s, p * 512:(p + 1) * 512],
                             ALU.mult, ALU.add, [acc.b, xin.b], [xin.b])
                linear_tm("wd" + sfx, l, 32, D, lambda k, s: hidT.t[:, k, s * 128:s * 128 + np_], [hidT.b], np_, nsub, evac)

            def proj(l, G, t):
                np_, nsub, T = G.np, G.nsub, G.T
                i0 = G.widx0(t)
                i0c = G.widx0c(t)
                scr_w = G.scrw(t)
                r0 = G.orow(t)
                rms_T(xin, np_, nsub, l * 5 + 1, hT)

                def out_f(dst, of, w):
                    self.dma(dst, of.t[:np_, :w], [of.b], (), q="pool")

                def kt_out(kname, tb, nb, idx0, s):
                    ob = self.ring("ob")
                    transpose_to(lambda c, tb=tb: tb.t[:np_, c * 128:(c + 1) * 128], [tb.b], nb, np_, ob.t[:, :nb, :np_], [ob.b])
                    self.dma(scr_w[kname][:, :, idx0 + s * 128:idx0 + s * 128 + np_].rearrange("h p s -> p h s"), ob.t[:, :nb, :np_], [ob.b], [G.wbuf(kname, t)], q="pool")

                def v_out(vname, tb, w, idx0, s):
                    if os.environ.get("KNOV") == "1":
                        return
                    self.dma(scr_w[vname][idx0 + s * 128:idx0 + s * 128 + np_, :], tb.t[:np_, :w], [tb.b], [G.wbuf(vname, t)], q="pool")

                def evac(p, s, acc, ncols):
                    pb = acc.t[:np_, :512]
                    rs_ = slice(r0 + s * 128, r0 + s * 128 + np_)
                    if p == 0:
                        tb = self.ring("tb")
                        headnorm(pb, [acc.b], np_, 8, l * 6 + 0, tb.t[:np_, :], [tb.b])
                        transpose_split(lambda c, tb=tb: tb.t[:np_, c * 128:(c + 1) * 128], [tb.b], 4, np_, QTA, slice(s * 128, s * 128 + np_))
                    elif p == 1:
                        of = self.ring("of")
                        headnorm(pb, [acc.b], np_, 8, l * 6 + 1, of.t[:np_, :], [of.b])
                        out_f(G.o_ak[l, rs_, :], of, 512)
                        tb = self.ring("tb")
                        self.cp(tb.t[:np_, :], of.t[:np_, :], [of.b], [tb.b])
                        kt_out("kta", tb, 4, i0, s)
                    elif p == 2:
                        of = self.ring("of")
                        self.act(of.t[:np_, :], pb, AF.Copy, [acc.b], [of.b])
                        out_f(G.o_av[l, rs_, :], of, 512)
                        tb = self.ring("tb")
                        self.cp(tb.t[:np_, :], pb, [acc.b], [tb.b])
                        v_out("va", tb, 512, i0, s)
                    elif p == 3:
                        tb = self.ring("tb")
                        self.cp(tb.t[:np_, :256], pb[:, 0:256], [acc.b], [tb.b])
                        transpose_split(lambda c, tb=tb: tb.t[:np_, c * 128:(c + 1) * 128], [tb.b], 2, np_, QTB, slice(s * 128, s * 128 + np_))
                        of = self.ring("of")
                        self.act(of.t[:np_, :256], pb[:, 256:512], AF.Copy, [acc.b], [of.b])
                        out_f(G.o_bk[l, rs_, :], of, 256)
                        tb2 = self.ring("tb")
                        self.cp(tb2.t[:np_, :256], pb[:, 256:512], [acc.b], [tb2.b])
                        kt_out("ktb", tb2, 2, i0, s)
                    elif p == 4:
                        of = self.ring("of")
                        self.act(of.t[:np_, :256], pb[:, 0:256], AF.Copy, [acc.b], [of.b])
                        out_f(G.o_bv[l, rs_, :], of, 256)
                        tb = self.ring("tb")
                        self.cp(tb.t[:np_, :256], pb[:, 0:256], [acc.b], [tb.b])
                        v_out("vb", tb, 256, i0, s)
                        tb2 = self.ring("tb")
                        headnorm(pb[:, 256:512], [acc.b], np_, 4, l * 6 + 2, tb2.t[:np_, :256], [tb2.b])
                        transpose_split(lambda c, tb2=tb2: tb2.t[:np_, c * 128:(c + 1) * 128], [tb2.b], 2, np_, QTC, slice(s * 128, s * 128 + np_))
                    elif p == 5:
                        of = self.ring("of")
                        headnorm(pb[:, 0:256], [acc.b], np_, 4, l * 6 + 3, of.t[:np_, 0:256], [of.b])
                        self.act(of.t[:np_, 256:512], pb[:, 256:512], AF.Copy, [acc.b], [of.b])
                        G.store_c(l, t, s, of)
                        tb = self.ring("tb")
                        self.cp(tb.t[:np_, :512], of.t[:np_, :512], [of.b], [tb.b])
                        kt_out("ktc", tb, 2, i0c, s)
                        tb3 = self.ring("tb")
                        self.cp(tb3.t[:np_, :256], pb[:, 256:512], [acc.b], [tb3.b])
                        v_out("vc", tb3, 256, i0c, s)
                    else:
                        tbg = self.ring("tb")
                        self.act(tbg.t[:np_, :], pb, AF.Sigmoid, [acc.b], [tbg.b])
                        gi_ = t % len(gates_d[G.name])
                        self.dma(gates_d[G.name][gi_][s, :np_, (p - 6) * 512:(p - 5) * 512], tbg.t[:np_, :], [tbg.b], [gates_b[G.name][gi_]], q="pool")
                linear_tm("w_in", l, 8, 6144, lambda k, s: hT.t[:, k, s * 128:s * 128 + np_], [hT.b], np_, nsub, evac,
                          pieces=[int(x) for x in os.environ["KPIECES"].split(",")] if "KPIECES" in os.environ else None)

            def key_chunks(lo, hi, csz=1024):
                out = []
                c0 = lo
                while c0 < hi:
                    n = min(csz, hi - c0)
                    out.append((c0, n))
                    c0 += n
                return out

            KBASE = {"kta": 0, "ktb": 1024, "ktc": 1536}
            VBASE = {"va": 512, "vb": 1280, "vc": 1792}

            def load_kv(G, kname, vname, hp_or_h, prow0, nrow, vcol0, dv, c0, n):
                kt = self.ring("kt")
                v = self.ring("v")
                if G.name == "p":
                    slot = c0 // 1024
                    assert c0 % 1024 == 0 and n == 1024
                    g, deps = gout[slot], [gbuf[slot]]
                    for r in range(2):
                        base = r * 2048
                        krow = base + KBASE[kname] + hp_or_h * 128 + prow0
                        self.dma(kt.t[prow0:prow0 + nrow, r * 512:(r + 1) * 512], g[krow:krow + nrow, :], deps, [kt.b])
                        if vname == "va":
                            vsrc = g[base + 512:base + 1024, vcol0:vcol0 + dv]
                        else:
                            vsrc = g[base + VBASE[vname]:base + VBASE[vname] + 256, :].rearrange("r (a c) -> (r a) c", c=256)[:, vcol0:vcol0 + dv]
                        self.dma(v.t[:, r * 4:(r + 1) * 4, :dv], vsrc.rearrange("(c p) f -> p c f", p=128), deps, [v.b])
                    return kt, v
                deps_k = G.kdeps(kname, c0, n)
                deps_v = G.kdeps(vname, c0, n)
                self.dma(kt.t[prow0:prow0 + nrow, :n], G.scr[kname][hp_or_h, prow0:prow0 + nrow, c0:c0 + n], deps_k, [kt.b])
                nfull = n // 128
                if nfull:
                    self.dma(v.t[:, :nfull, :dv], G.scr[vname][c0:c0 + nfull * 128, vcol0:vcol0 + dv].rearrange("(c p) f -> p c f", p=128), deps_v, [v.b])
                rem = n - nfull * 128
                if rem:
                    self.dma(v.t[:rem, nfull, :dv], G.scr[vname][c0 + nfull * 128:c0 + n, vcol0:vcol0 + dv], deps_v, [v.b])
                return kt, v

            def attn_A(l, G, t):
                T = G.T
                q0 = G.kidx0(t)
                hi = q0 + T
                LA = 2
                for h in range(4):
                    O = [bank[2], bank[4]]
                    Sm = [bank[3], bank[5]]
                    chunks = key_chunks(0, hi)
                    items = []
                    for ci, (c0, n) in enumerate(chunks):
                        nt = (n + 127) // 128
                        for i in range(nt):
                            for m in range(2):
                                items.append((ci, c0, n, i, m))
                    nit = len(items)
                    loaded = {}
                    pend = []

                    def stage1(j):
                        ci, c0, n, i, m = items[j]
                        if ci not in loaded:
                            loaded[ci] = load_kv(G, "kta", "va", h, 0, 128, h * 128, 128, c0, n)
                        kt, v = loaded[ci]
                        k0 = c0 + i * 128
                        nk = min(128, c0 + n - k0)
                        delta = k0 - q0
                        sc = self.ring("sc")
                        self.mm(sc.t[:nk, :T], kt.t[:, i * 128:i * 128 + nk], QTA[m].t[:, h, :T], True, True, [kt.b, QTA[m].b], [sc.b])
                        P = self.ring("tb")
                        if delta >= G.sthr:
                            j0 = G.jcol(384 - delta, WA_MAIN)
                            tf = self.ring("tf")
                            self.stt(tf.t[:nk, :T], sc.t[:nk, :T], 0.125, stripA.t[:nk, h, j0:j0 + T], ALU.mult, ALU.add, [sc.b, stripA.b], [tf.b])
                            self.act(P.t[:nk, :T], tf.t[:nk, :T], AF.Exp, [tf.b], [P.b])
                        else:
                            fb = self.far_bucket * 4 + h
                            self.act(P.t[:nk, :T], sc.t[:nk, :T], AF.Exp, [sc.b, t5bc.b], [P.b], scale=0.125, bias=t5bc.t[:nk, fb:fb + 1])
                        pend.append((P, v, i, nk, m, j < 2, j >= nit - 2))

                    def stage2(j):
                        P, v, i, nk, m, first, last = pend[j]
                        self.mm(O[m].t[:, :T], v.t[:nk, i, :128], P.t[:nk, :T], first, last, [v.b, P.b], [O[m].b])
                        self.mm(Sm[m].t[:, :T], onesb.t[:nk, :], P.t[:nk, :T], first, last, [onesb.b, P.b], [Sm[m].b])

                    for j in range(nit + LA):
                        if j < nit:
                            stage1(j)
                        if j >= LA:
                            stage2(j - LA)
                    r1 = self.ring("tf"); t1 = self.ring("tf"); r2 = self.ring("tf"); t2 = self.ring("tf")
                    self.act(r1.t[:, :T], Sm[0].t[:, :T], AF.Ln, [Sm[0].b], [r1.b])
                    self.act(r1.t[:, :T], r1.t[:, :T], AF.Exp, [r1.b], [r1.b], scale=-1.0)
                    self.tt(t1.t[:, :T], O[0].t[:, :T], r1.t[:, :T], ALU.mult, [O[0].b, r1.b], [t1.b])
                    self.act(r2.t[:, :T], Sm[1].t[:, :T], AF.Ln, [Sm[1].b], [r2.b])
                    self.act(r2.t[:, :T], r2.t[:, :T], AF.Exp, [r2.b], [r2.b], scale=-1.0)
                    self.tt(t2.t[:, :T], O[1].t[:, :T], r2.t[:, :T], ALU.mult, [O[1].b, r2.b], [t2.b])
                    self.stt(r1.t[:, :T], t2.t[:, :T], neglam.t[:, l:l + 1], t1.t[:, :T], ALU.mult, ALU.add, [t2.b, t1.b, neglam.b], [r1.b])
                    self.act(r2.t[:, :T], r1.t[:, :T], AF.Square, [r1.b], [r2.b])
                    self.mm(bank[2].t[:, :T], onesf.t[:, :], r2.t[:, :T], True, True, [onesf.b, r2.b], [bank[2].b])
                    self.act(t1.t[:, :T], bank[2].t[:, :T], AF.Ln, [bank[2].b, epsc.b], [t1.b], scale=1.0 / 128, bias=epsc.t[:, :])
                    self.act(t2.t[:, :T], t1.t[:, :T], AF.Exp, [t1.b], [t2.b], scale=-0.5)
                    self.tt(r2.t[:, :T], r1.t[:, :T], t2.t[:, :T], ALU.mult, [r1.b, t2.b], [r2.b])
                    self.ts(oaT.t[:, h, :T], r2.t[:, :T], subcol.t[:, l:l + 1], None, ALU.mult, ALU.bypass, [r2.b, subcol.b], [oaT.b])

            def attn_B(l, G, t):
                T = G.T
                q0 = G.kidx0(t)
                hi = q0 + T
                zring = [bank[0], bank[1], bank[6], bank[4]]
                cring = [bank[7], bank[5], bank[3]]
                for h in range(4):
                    hp, ho = h // 2, (h % 2) * 64
                    self.memset(Rt.t[:, :], 0.0, [Rt.b], eng="dve")
                    chunks = key_chunks(0, hi)[::-1]
                    items = []
                    for ci, (c0, n) in enumerate(chunks):
                        nt = (n + 127) // 128
                        for i in range(nt - 1, -1, -1):
                            items.append((ci, c0, n, i))
                    nit = len(items)
                    loaded = {}
                    st = {}

                    def s12(j):
                        ci, c0, n, i = items[j]
                        if ci not in loaded:
                            loaded[ci] = load_kv(G, "ktb", "vb", hp, 0, 128, hp * 128, 128, c0, n)
                        kt, v = loaded[ci]
                        k0 = c0 + i * 128
                        nk = min(128, c0 + n - k0)
                        delta = k0 - q0
                        msk = delta >= G.bthr
                        j0 = G.jcol(384 - delta, WB_MAIN) if msk else 0
                        z = zring[j % 4]
                        cs = cring[j % 3]
                        self.mm(z.t[:nk, :T], kt.t[:, i * 128:i * 128 + nk], QTB[h % 2].t[:, hp, :T], True, True, [kt.b, QTB[h % 2].b], [z.b])
                        e_ = self.ring("tf")
                        self.act(e_.t[:nk, :T], z.t[:nk, :T], AF.Exp, [z.b], [e_.b], scale=0.125)
                        sp = self.ring("tb")
                        self.act(sp.t[:nk, :T], e_.t[:nk, :T], AF.Ln, [e_.b, onec.b], [sp.b], bias=onec.t[:nk, :])
                        if msk:
                            self.tt(sp.t[:nk, :T], sp.t[:nk, :T], maskB.t[:nk, j0:j0 + T], ALU.mult, [sp.b, maskB.b], [sp.b])
                        st[j] = dict(v=v, i=i, nk=nk, msk=msk, j0=j0, z=z, cs=cs, sp=sp)

                    def s345(j):
                        d = st[j]
                        z, cs, sp, nk = d["z"], d["cs"], d["sp"], d["nk"]
                        S.op("pe", lambda e, z=z, sp=sp, nk=nk: e.matmul(z.t[:nk, :T], lhsT=negU8.t[:nk, :nk], rhs=sp.t[:nk, :T], start=False, stop=True, skip_group_check=True),
                             [negU8.b, sp.b], [z.b])
                        self.mm(cs.t[:, :T], onesb.t[:nk, :], sp.t[:nk, :T], True, True, [onesb.b, sp.b], [cs.b])
                        arg = self.ring("tf")
                        self.stt(arg.t[:nk, :T], z.t[:nk, :T], 0.125, Rt.t[:nk, :T], ALU.mult, ALU.subtract, [z.b, Rt.b], [arg.b])
                        self.tt(Rt.t[:, :T], cs.t[:, :T], Rt.t[:, :T], ALU.add, [cs.b, Rt.b], [Rt.b])
                        wb_ = self.ring("tb")
                        self.act(wb_.t[:nk, :T], arg.t[:nk, :T], AF.Exp, [arg.b], [wb_.b])
                        if d["msk"]:
                            self.tt(wb_.t[:nk, :T], wb_.t[:nk, :T], maskB.t[:nk, d["j0"]:d["j0"] + T], ALU.mult, [wb_.b, maskB.b], [wb_.b])
                        d["w"] = wb_

                    def s6(j):
                        d = st.pop(j)
                        self.mm(bank[2].t[:, :T], d["v"].t[:d["nk"], d["i"], :128], d["w"].t[:d["nk"], :T], j == 0, j == nit - 1, [d["v"].b, d["w"].b], [bank[2].b])

                    for j in range(nit + 2):
                        if j < nit:
                            s12(j)
                        if 1 <= j <= nit:
                            s345(j - 1)
                        if j >= 2:
                            s6(j - 2)
                    self.act(obT.t[ho:ho + 64, hp, :T], bank[2].t[ho:ho + 64, :T], AF.Copy, [bank[2].b], [obT.b])

            def attn_soft(G, T, QT, nkeys_tiles, dst, h, hp, ho, strip=None):
                n_ = len(nkeys_tiles)
                pend = []
                LA = 2

                def stage1(i):
                    kap, vap, nk, j0, deps = nkeys_tiles[i]
                    sc = self.ring("sc")
                    self.mm(sc.t[:nk, :T], kap, QT[h % 2].t[:, hp, :T], True, True, deps + [QT[h % 2].b], [sc.b])
                    P = self.ring("tb")
                    if j0 is not None:
                        tf = self.ring("tf")
                        self.stt(tf.t[:nk, :T], sc.t[:nk, :T], 0.125, strip.t[:nk, h, j0:j0 + T], ALU.mult, ALU.add, [sc.b, strip.b], [tf.b])
                        self.act(P.t[:nk, :T], tf.t[:nk, :T], AF.Exp, [tf.b], [P.b])
                    else:
                        self.act(P.t[:nk, :T], sc.t[:nk, :T], AF.Exp, [sc.b], [P.b], scale=0.125)
                    pend.append(P)

                def stage2(i):
                    kap, vap, nk, j0, deps = nkeys_tiles[i]
                    P = pend[i]
                    self.mm(bank[2].t[:, :T], vap, P.t[:nk, :T], i == 0, i == n_ - 1, deps + [P.b], [bank[2].b])
                    self.mm(bank[3].t[:, :T], onesb.t[:nk, :], P.t[:nk, :T], i == 0, i == n_ - 1, [onesb.b, P.b], [bank[3].b])

                for i in range(n_ + LA):
                    if i < n_:
                        stage1(i)
                    if i >= LA:
                        stage2(i - LA)
                r = self.ring("tf")
                self.act(r.t[ho:ho + 64, :T], bank[3].t[ho:ho + 64, :T], AF.Ln, [bank[3].b], [r.b])
                self.act(r.t[ho:ho + 64, :T], r.t[ho:ho + 64, :T], AF.Exp, [r.b], [r.b], scale=-1.0)
                self.tt(dst.t[ho:ho + 64, hp, :T], bank[2].t[ho:ho + 64, :T], r.t[ho:ho + 64, :T], ALU.mult, [bank[2].b, r.b], [dst.b])

            def attn_C(l, G, t):
                T = G.T
                q0 = G.kidx0c(t)
                hi = q0 + T
                lo = max(0, q0 - G.cback)
                for h in range(4):
                    hp, ho = h // 2, (h % 2) * 64
                    tl = []
                    for (c0, n) in key_chunks((lo // 1024) * 1024 if G.name == "p" else lo, hi, 1024 if G.name == "p" else 2048):
                        kt, v = load_kv(G, "ktc", "vc", hp, 0, 128, hp * 128, 128, c0, n)
                        for i in range((n + 127) // 128):
                            k0 = c0 + i * 128
                            if k0 < lo:
                                continue
                            nk = min(128, c0 + n - k0)
                            delta = k0 - q0
                            tl.append((kt.t[:, i * 128:i * 128 + nk], v.t[:nk, i, :128], nk, G.jcol(384 - delta, WC_MAIN), [kt.b, v.b]))
                    attn_soft(G, T, QTC, tl, ocT, h, hp, ho, strip=stripC)

            def attn_X(l, G, t):
                T = G.T
                mk_, mv_ = mkT[G.name], mvb[G.name]
                for h in range(4):
                    hp, ho = h // 2, (h % 2) * 64
                    tl = []
                    for i in range(2):
                        tl.append((mk_.t[:, hp, i * 128:(i + 1) * 128], mv_.t[:, i, hp * 128:(hp + 1) * 128], 128, None, [mk_.b, mv_.b]))
                    attn_soft(G, T, QTX, tl, oxT, h, hp, ho)

            def merge(l, G):
                np_, nsub, T = G.np, G.nsub, G.T
                gi_ = G.gidx
                mrgs_t = hidT.t[:, 0:8, :].rearrange("p (s a) c -> p s (a c)", s=4)
                for n in range(2):
                    cs_ = slice(n * 512, (n + 1) * 512)
                    wa = load_w("w_br_a", l, 0, n * 512, 512, kc=4)
                    wb_ = load_w("w_br_b", l, 0, n * 512, 512, kc=2)
                    wc = load_w("w_br_c", l, 0, n * 512, 512, kc=2)
                    for s in range(nsub):
                        sl = slice(s * 128, s * 128 + np_)
                        gt = self.ring("gt")
                        self.dma(gt.t[:np_, :, :], gates_d[G.name][gi_][s, :np_, :].rearrange("p (g c) -> p g c", g=3)[:, :, n * 512:(n + 1) * 512], [gates_b[G.name][gi_]], [gt.b])
                        ba, bb, bc = (bank[0], bank[1], bank[6]) if s % 2 == 0 else (bank[2], bank[3], bank[7])
                        for h in range(4):
                            self.mm(ba.t[:np_, :], oaT.t[:, h, sl], wa.t[:, h, :], h == 0, h == 3, [oaT.b, wa.b], [ba.b])
                        for h in range(2):
                            self.mm(bb.t[:np_, :], obT.t[:, h, sl], wb_.t[:, h, :], h == 0, h == 1, [obT.b, wb_.b], [bb.b])
                        for h in range(2):
                            self.mm(bc.t[:np_, :], ocT.t[:, h, sl], wc.t[:, h, :], h == 0, h == 1, [ocT.b, wc.b], [bc.b])
                        m1 = self.ring("tf"); m2 = self.ring("tf")
                        self.tt(m1.t[:np_, :], ba.t[:np_, :], gt.t[:np_, 0, :], ALU.mult, [ba.b, gt.b], [m1.b])
                        self.tt(m2.t[:np_, :], bb.t[:np_, :], gt.t[:np_, 1, :], ALU.mult, [bb.b, gt.b], [m2.b])
                        self.tt(m1.t[:np_, :], m1.t[:np_, :], m2.t[:np_, :], ALU.add, [m1.b, m2.b], [m1.b])
                        self.tt(m2.t[:np_, :], bc.t[:np_, :], gt.t[:np_, 2, :], ALU.mult, [bc.b, gt.b], [m2.b])
                        self.tt(mrgs_t[:np_, s, cs_], m1.t[:np_, :], m2.t[:np_, :], ALU.add, [m1.b, m2.b], [hidT.b])
                for s in range(nsub):
                    sl = slice(s * 128, s * 128 + np_)
                    transpose_to(lambda c, s=s: mrgs_t[:np_, s, c * 128:(c + 1) * 128], [hidT.b], 8, np_, hT.t[:, :, sl], [hT.b])

                def evac(p, s, acc, ncols):
                    self.tt(xin.t[:np_, s, p * 512:(p + 1) * 512], acc.t[:np_, :512], xin.t[:np_, s, p * 512:(p + 1) * 512], ALU.add, [acc.b, xin.b], [xin.b])
                linear_tm("w_out", l, 8, D, lambda k, s: hT.t[:, k, s * 128:s * 128 + np_], [hT.b], np_, nsub, evac)

            def cross(l, G, t):
                np_, nsub, T = G.np, G.nsub, G.T
                rms_T(xin, np_, nsub, l * 5 + 2, hT)

                def evq(p, s, acc, ncols):
                    tb = self.ring("tb")
                    headnorm(acc.t[:np_, :256], [acc.b], np_, 4, l * 6 + 4, tb.t[:np_, :256], [tb.b])
                    transpose_split(lambda c, tb=tb: tb.t[:np_, c * 128:(c + 1) * 128], [tb.b], 2, np_, QTX, slice(s * 128, s * 128 + np_))
                linear_tm("x_wq", l, 8, 256, lambda k, s: hT.t[:, k, s * 128:s * 128 + np_], [hT.b], np_, nsub, evq)
                attn_X(l, G, t)
                for n in range(2):
                    cs_ = slice(n * 512, (n + 1) * 512)
                    wo = load_w("x_wo", l, 0, n * 512, 512, kc=2)
                    for s in range(nsub):
                        sl = slice(s * 128, s * 128 + np_)
                        acc = bank[4 + (s % 2)]
                        for h in range(2):
                            self.mm(acc.t[:np_, :], oxT.t[:, h, sl], wo.t[:, h, :], h == 0, h == 1, [oxT.b, wo.b], [acc.b])
                        self.tt(xin.t[:np_, s, cs_], acc.t[:np_, :], xin.t[:np_, s, cs_], ALU.add, [acc.b, xin.b], [xin.b])

            def mem_kv_prompt(l):
                self.dma(xin.t[:, 0:2, :], mem_p.rearrange("(c p) d -> p c d", p=128), (), [xin.b])
                rms_T(xin, 128, 2, l * 5 + 3, hT)
                mk_, mv_ = mkT["p"], mvb["p"]

                def ev(p, s, acc, ncols):
                    of = self.ring("of")
                    headnorm(acc.t[:, 0:256], [acc.b], 128, 4, l * 6 + 5, of.t[:, 0:256], [of.b])
                    self.act(of.t[:, 256:512], acc.t[:, 256:512], AF.Copy, [acc.b], [of.b])
                    tb = self.ring("tb")
                    self.cp(tb.t[:, :256], of.t[:, 0:256], [of.b], [tb.b])
                    transpose_to(lambda c, tb=tb: tb.t[:, c * 128:(c + 1) * 128], [tb.b], 2, 128, mk_.t[:, :, s * 128:(s + 1) * 128], [mk_.b])
                    self.cp(mv_.t[:, s, :], acc.t[:, 256:512], [acc.b], [mv_.b])
                    self.dma(o_mkp[l, s * 128:(s + 1) * 128, :], of.t[:, 0:256], [of.b], (), q="pool")
                    self.dma(o_mvp[l, s * 128:(s + 1) * 128, :], of.t[:, 256:512], [of.b], (), q="pool")
                linear_tm("x_wkv", l, 8, 512, lambda k, s: hT.t[:, k, s * 128:(s + 1) * 128], [hT.b], 128, 2, ev)

            def prep_kT(src, nrow, F, dst_fn):
                for kt_ in range(nrow // 128):
                    of = self.ring("of")
                    self.dma(of.t[:, :F], src[kt_ * 128:(kt_ + 1) * 128, :], (), [of.b])
                    tb = self.ring("tb")
                    self.cp(tb.t[:, :F], of.t[:, :F], [of.b], [tb.b])
                    dst_fn(kt_, tb)

            def sample_prep(l, G):
                scr, sbf = G.scr, G.sb_
                self.dma(scr["va"][0:PAST, :], ca_v[l], (), [sbf["va"]["cache"][0]], q="pool")
                self.dma(scr["vb"][0:PAST, :], cb_v[l], (), [sbf["vb"]["cache"][0]], q="pool")
                self.dma(scr["vc"][0:CB, :], cc_v[l], (), [sbf["vc"]["cache"][0]], q="pool")
                for (src, nrow, F, kname) in ((ca_k[l], PAST, 512, "kta"), (cb_k[l], PAST, 256, "ktb"), (cc_k[l], CB, 256, "ktc")):
                    def dst_fn(kt_, tb, kname=kname, F=F):
                        ob = self.ring("ob")
                        nb = F // 128
                        transpose_to(lambda c, tb=tb: tb.t[:, c * 128:(c + 1) * 128], [tb.b], nb, 128, ob.t[:, :nb, :128], [ob.b])
                        self.dma(scr[kname][:, :, kt_ * 128:(kt_ + 1) * 128].rearrange("h p s -> p h s"), ob.t[:, :nb, :128], [ob.b], [sbf[kname]["cache"][kt_]], q="pool")
                    prep_kT(src, nrow, F, dst_fn)
                mk_, mv_ = mkT["s"], mvb["s"]

                def dst_m(kt_, tb):
                    transpose_to(lambda c, tb=tb: tb.t[:, c * 128:(c + 1) * 128], [tb.b], 2, 128, mk_.t[:, :, kt_ * 128:(kt_ + 1) * 128], [mk_.b])
                prep_kT(cm_k[l], NMEM, 256, dst_m)

                def dst_v(kt_, tb):
                    self.cp(mv_.t[:, kt_, :], tb.t[:, :256], [tb.b], [mv_.b])
                prep_kT(cm_v[l], NMEM, 256, dst_v)
                self.dma(o_cks[l, 0:CB - NS, :], cc_k[l, NS:CB, :], (), (), q="pool")
                self.dma(o_cvs[l, 0:CB - NS, :], cc_v[l, NS:CB, :], (), (), q="pool")

            NT = NSLOT
            Gp = Group()
            Gp.name, Gp.T, Gp.np, Gp.nsub = "p", 512, 128, 4

            def scrw_p(t):
                x_ = xkv[t % 2]
                return {"kta": x_[0:512, :].rearrange("(h p) s -> h p s", p=128), "va": x_[512:1024, :],
                        "ktb": x_[1024:1280, :].rearrange("(h p) s -> h p s", p=128),
                        "vb": x_[1280:1536, :].rearrange("r (a c) -> (r a) c", c=256),
                        "ktc": x_[1536:1792, :].rearrange("(h p) s -> h p s", p=128),
                        "vc": x_[1792:2048, :].rearrange("r (a c) -> (r a) c", c=256)}
            Gp.scrw = scrw_p
            Gp.wbuf = lambda name, t: xkvb[t % 2]
            Gp.widx0 = lambda t: 0
            Gp.widx0c = lambda t: 0
            Gp.kidx0 = lambda t: (2 * t + 1) * 512
            Gp.kidx0c = lambda t: (2 * t + 1) * 512
            Gp.orow = lambda t: t * 512
            Gp.sthr, Gp.bthr, Gp.cback = -640, -512, 1024
            Gp.jcol = lambda j0, wmain: j0
            Gp.o_ak, Gp.o_av, Gp.o_bk, Gp.o_bv = o_akp, o_avp, o_bkp, o_bvp

            def store_c_p(l, t, s, of):
                if t == NT - 1:
                    r = s * 128
                    self.dma(o_ckp[l, r:r + 128, :], of.t[:, 0:256], [of.b], (), q="pool")
                    self.dma(o_cvp[l, r:r + 128, :], of.t[:, 256:512], [of.b], (), q="pool")
            Gp.store_c = store_c_p

            groups_cc = [[b_, b_ + self.NPAIR] for b_ in range(self.NPAIR)]

            def exchange(l, t):
                src, dst = xkv[t % 2], gout[t]
                self.S.cc(lambda e: e.collective_compute("AllGather", ALU.bypass, replica_groups=groups_cc, ins=[src[:, :]], outs=[dst[:, :]]),
                          [xkvb[t % 2]], [gbuf[t]])

            Gs = Group()
            Gs.name, Gs.T, Gs.np, Gs.nsub = "s", NS, NS, 1
            Gs.scr = kvscr("s", PAST + NS)
            Gs.sb_ = {k_: {"cache": [Buf("%s_s_c%d" % (k_, i)) for i in range(PAST // 128)], 0: Buf(k_ + "_s_n")} for k_ in Gs.scr}
            Gs.scrw = lambda t: Gs.scr
            Gs.wbuf = lambda name, t: Gs.sb_[name][0]
            Gs.widx0 = lambda t: PAST
            Gs.widx0c = lambda t: CB
            Gs.kidx0 = lambda t: PAST
            Gs.kidx0c = lambda t: CB
            Gs.orow = lambda t: 0
            Gs.sthr, Gs.bthr, Gs.cback = -128, -127, 512
            Gs.jcol = lambda j0, wmain: wmain + (j0 - 384)
            Gs.o_ak, Gs.o_av, Gs.o_bk, Gs.o_bv = o_aks, o_avs, o_bks, o_bvs
            Gs.kdeps = lambda name, c0, n: list(Gs.sb_[name]["cache"]) + [Gs.sb_[name][0]]

            def store_c_s(l, t, s, of):
                self.dma(o_cks[l, CB - NS:CB, :], of.t[:NS, 0:256], [of.b], (), q="pool")
                self.dma(o_cvs[l, CB - NS:CB, :], of.t[:NS, 256:512], [of.b], (), q="pool")
            Gs.store_c = store_c_s

            def run_tile(l, G, t, xsrc, xdst, xsrc_deps, xdst_buf):
                np_, nsub, T = G.np, G.nsub, G.T
                self.dma(xin.t[:np_, :nsub, :], xsrc.rearrange("(c p) d -> p c d", p=np_), xsrc_deps, [xin.b])
                if STAGE >= 1:
                    ffn(l, 0, G)
                if STAGE >= 2:
                    proj(l, G, t)
                    if G.name == "p":
                        exchange(l, t)
                if STAGE >= 3:
                    attn_A(l, G, t)
                if STAGE >= 4:
                    attn_B(l, G, t)
                if STAGE >= 5:
                    attn_C(l, G, t)
                G.gidx = t % len(gates_d[G.name])
                if STAGE >= 6:
                    merge(l, G)
                if STAGE >= 7:
                    cross(l, G, t)
                if STAGE >= 8:
                    ffn(l, 1, G)
                self.dma(xdst.rearrange("(c p) d -> p c d", p=np_), xin.t[:np_, :nsub, :], [xin.b], xdst_buf, q="pool")

            xb_p = [Buf("xcur_%d" % t) for t in range(NT)]
            xb_s = Buf("xcur_s")
            for l in range(NL):
                layer_setup(l)
                mem_kv_prompt(l)
                for t in range(NT if "p" in KGROUP else 0):
                    src = x_p if l == 0 else xcur
                    dst = xcur if l < NL - 1 else y_p
                    run_tile(l, Gp, t, src[t * 512:(t + 1) * 512, :], dst[t * 512:(t + 1) * 512, :],
                             [xb_p[t]] if l > 0 else [], [xb_p[t]] if l < NL - 1 else [])
                if "s" not in KGROUP:
                    continue
                sample_prep(l, Gs)
                src = x_s if l == 0 else xcur_s
                dst = xcur_s if l < NL - 1 else y_s
                run_tile(l, Gs, 0, src[:, :], dst[:, :], [xb_s] if l > 0 else [], [xb_s] if l < NL - 1 else [])
            S.emit()
        return nc


_CACHE = {}


def _prep_inputs(inp, c, consts2, npair):
    b = c % npair
    hf = c // npair
    consts = consts2[1 - hf]
    f = lambda a: np.ascontiguousarray(a, dtype=np.float32)
    nsb = inp["x_sample"].shape[0]
    cs = c % nsb
    xb = inp["x_prompt"][b]
    SEQ = xb.shape[0]
    m = {
        "x_p": f(xb.reshape(SEQ // 1024, 2, 512, D)[:, hf].reshape(SEQ // 2, D)), "x_s": f(inp["x_sample"][cs]), "mem_p": f(inp["mem_prompt"][b]),
        "ca_k": f(inp["cache_a_k"][:, cs].reshape(NL, -1, 512)), "ca_v": f(inp["cache_a_v"][:, cs].reshape(NL, -1, 512)),
        "cb_k": f(inp["cache_b_k"][:, cs].reshape(NL, -1, 256)), "cb_v": f(inp["cache_b_v"][:, cs].reshape(NL, -1, 256)),
        "cc_k": f(inp["cache_c_k"][:, cs].reshape(NL, -1, 256)), "cc_v": f(inp["cache_c_v"][:, cs].reshape(NL, -1, 256)),
        "cm_k": f(inp["cache_mem_k"][:, cs].reshape(NL, -1, 256)), "cm_v": f(inp["cache_mem_v"][:, cs].reshape(NL, -1, 256)),
        "t5": f(inp["t5_bias"]),
        "gains": f(np.stack([inp["ffn1_norm"], inp["mix_norm"], inp["x_norm"], inp["mem_norm"], inp["ffn2_norm"]], axis=1)),
        "hgains": f(np.stack([inp["a_qnorm"], inp["a_knorm"], inp["c_qnorm"], inp["c_knorm"], inp["x_qnorm"], inp["x_knorm"]], axis=1)),
        "lvec": f(np.stack([inp["a_lq1"], inp["a_lk1"], inp["a_lq2"], inp["a_lk2"]], axis=1)),
        "subln": f(inp["a_subln"]), "crel": f(inp["c_rel_bias"]),
        "wg1": f(inp["ffn1_wg"]), "wu1": f(inp["ffn1_wu"]), "wd1": f(inp["ffn1_wd"]), "w_in": f(inp["w_in"]),
        "w_br_a": f(inp["w_br_a"]), "w_br_b": f(inp["w_br_b"]), "w_br_c": f(inp["w_br_c"]), "w_out": f(inp["w_out"]),
        "x_wq": f(inp["x_wq"]), "x_wkv": f(inp["x_wkv"]), "x_wo": f(inp["x_wo"]),
        "wg2": f(inp["ffn2_wg"]), "wu2": f(inp["ffn2_wu"]), "wd2": f(inp["ffn2_wd"]),
        "oh_a": consts["oh_a"], "oh_c": consts["oh_c"], "mask_a": consts["mask_a"], "mask_b": consts["mask_b"], "mask_c": consts["mask_c"],
    }
    return m


def kernel(**inp):
    inp = {k: np.asarray(v) for k, v in inp.items()}
    B, SEQ = inp["x_prompt"].shape[0], inp["x_prompt"].shape[1]
    NB, NS = inp["x_sample"].shape[0], inp["x_sample"].shape[1]
    PAST = inp["cache_a_k"].shape[2]
    CB = inp["cache_c_k"].shape[2]
    consts2 = [host_consts(0), host_consts(1)]
    npair = B
    ncores = 2 * npair
    key = (SEQ, NS, PAST, CB, npair)
    if key not in _CACHE:
        _CACHE[key] = K(SEQ // 2, NS, PAST, CB, consts2[0]["far_bucket"], NPAIR=npair).build()
    nc = _CACHE[key]
    in_maps = [_prep_inputs(inp, c, consts2, npair) for c in range(ncores)]
    res = run_bass_kernel_spmd(nc, in_maps, core_ids=list(range(ncores)))
    R = res.results
    g = lambda c, name: np.asarray(R[c][name], dtype=np.float32)

    def inter(name, lead):
        outs = []
        for b in range(B):
            a0, a1 = g(b, name), g(b + npair, name)
            sh = a0.shape
            if lead:
                a0 = a0.reshape(sh[0], -1, 512, sh[-1]); a1 = a1.reshape(sh[0], -1, 512, sh[-1])
                outs.append(np.stack([a0, a1], axis=2).reshape(sh[0], SEQ, sh[-1]))
            else:
                a0 = a0.reshape(-1, 512, sh[-1]); a1 = a1.reshape(-1, 512, sh[-1])
                outs.append(np.stack([a0, a1], axis=1).reshape(SEQ, sh[-1]))
        return np.stack(outs, axis=1 if lead else 0)

    stl = lambda name, cores: np.stack([g(c, name) for c in cores], axis=1)
    sc = [c % ncores for c in range(NB)]
    late = [b + npair for b in range(B)]
    early = list(range(B))
    y_p = inter("y_p", False)
    y_s = np.stack([g(c, "y_s") for c in sc])
    outs = [y_p, y_s,
            inter("a_k_p", True).reshape(NL, B, SEQ, 4, 128), inter("a_v_p", True).reshape(NL, B, SEQ, 4, 128),
            inter("b_k_p", True).reshape(NL, B, SEQ, 4, 64), inter("b_v_p", True).reshape(NL, B, SEQ, 4, 64),
            stl("c_k_p", late).reshape(NL, B, 512, 4, 64), stl("c_v_p", late).reshape(NL, B, 512, 4, 64),
            stl("m_k_p", early).reshape(NL, B, NMEM, 4, 64), stl("m_v_p", early).reshape(NL, B, NMEM, 4, 64),
            stl("a_k_s", sc).reshape(NL, NB, NS, 4, 128), stl("a_v_s", sc).reshape(NL, NB, NS, 4, 128),
            stl("b_k_s", sc).reshape(NL, NB, NS, 4, 64), stl("b_v_s", sc).reshape(NL, NB, NS, 4, 64),
            stl("c_k_s", sc).reshape(NL, NB, CB, 4, 64), stl("c_v_s", sc).reshape(NL, NB, CB, 4, 64)]
    return tuple(np.ascontiguousarray(o) for o in outs)
```

```python
import math
import contextlib
import numpy as np
import concourse.bass as bass
import concourse.mybir as mybir
from concourse.bass_utils import run_bass_kernel_spmd

F32 = mybir.dt.float32
BF16 = mybir.dt.bfloat16
ALU = mybir.AluOpType
AF = mybir.ActivationFunctionType
AX = mybir.AxisListType

D = 1024
DFF = 4096
HD = 64
NL = 2
NMEM = 256
CHUNK = 64
C_PREV = 8
REL_CLIP = 128
EPS = 1e-6
NEG = -30000.0
WA_MAIN, WA_S = 1536, 160
WB_MAIN, WB_S = 1408, 32
WC_MAIN, WC_S = 1920, 544
WA_STRIP = WA_MAIN + WA_S
WB_STRIP = WB_MAIN + WB_S
WC_STRIP = WC_MAIN + WC_S
GA_W = 2048
GC_W = 2048

import os
STAGE = int(os.environ.get("KSTAGE", "8"))
KGROUP = os.environ.get("KGROUP", "ps")
ENGS = ("pe", "act", "dve", "pool", "sp")
NDMASEM = 24


class Buf:
    __slots__ = ("w", "r", "name", "x")

    def __init__(self, name=""):
        self.w = None
        self.r = []
        self.name = name
        self.x = False


class TT:
    def __init__(self, t, name):
        self.t = t
        self.b = Buf(name)


class Sched:
    def __init__(self, nc):
        self.nc = nc
        self.acts = {e: [] for e in ENGS}
        self.ninst = {e: 0 for e in ENGS}
        self.flag = {e: set() for e in ENGS}
        self.seen = {e: {s: 0 for s in ENGS} for e in ENGS}
        self.dseen = {e: {} for e in ENGS}
        self.dma_n = {e: 0 for e in ENGS}
        self.dma_val = {}

    def _wait(self, e, tok):
        if tok is None:
            return
        if tok[0] == "e":
            _, s, idx = tok
            if self.seen[e][s] >= idx:
                return
            self.flag[s].add(idx)
            self.acts[e].append(("we", s, idx))
            self.seen[e][s] = idx
        else:
            _, q, slot, val = tok
            if self.dseen[e].get((q, slot), 0) >= val:
                return
            self.acts[e].append(("wd", q, slot, val))
            self.dseen[e][(q, slot)] = val

    def _deps(self, e, reads, writes):
        for b in reads:
            self._wait(e, b.w)
            if b.x:
                for t in b.r:
                    if not (t[0] == "e" and t[1] == e):
                        self._wait(e, t)
        for b in writes:
            for t in [b.w] + b.r:
                if t is not None and t[0] == "e" and t[1] == e and e == "pe":
                    continue
                self._wait(e, t)

    def _commit(self, tok, reads, writes):
        for b in reads:
            if tok[0] == "e":
                b.r = [t for t in b.r if not (t[0] == "e" and t[1] == tok[1])]
            else:
                b.r = [t for t in b.r if not (t[0] == "d" and t[1] == tok[1] and t[2] == tok[2])]
            b.r.append(tok)
        for b in writes:
            b.w = tok
            b.r = []

    def op(self, e, fn, reads=(), writes=()):
        self._deps(e, reads, writes)
        self.ninst[e] += 1
        idx = self.ninst[e]
        self.acts[e].append(("i", fn, idx))
        self._commit(("e", e, idx), reads, writes)

    def dma(self, q, out, in_, reads=(), writes=(), **kw):
        self._deps(q, reads, writes)
        n = self.dma_n[q]
        self.dma_n[q] += 1
        slot = n % NDMASEM
        prev = self.dma_val.get((q, slot), 0)
        if prev:
            self._wait(q, ("d", q, slot, prev))
        val = prev + 16
        self.dma_val[(q, slot)] = val
        self.acts[q].append(("d", out, in_, slot, kw))
        self._commit(("d", q, slot, val), reads, writes)

    def cc(self, fn, reads=(), writes=()):
        self._deps("pool", reads, writes)
        self.ncc = getattr(self, "ncc", 0) + 1
        self.acts["pool"].append(("c", fn))
        self._commit(("d", "cc", 0, self.ncc), reads, writes)

    def emit(self):
        nc = self.nc
        for (q, slot), val in sorted(self.dma_val.items()):
            self._wait("sp", ("d", q, slot, val))
        if getattr(self, "ncc", 0):
            self._wait("sp", ("d", "cc", 0, self.ncc))
        with contextlib.ExitStack() as st:
            esem = {e: st.enter_context(nc.semaphore("es_" + e)) for e in ENGS}
            dsem = {}
            for q in ENGS:
                for s in range(min(NDMASEM, self.dma_n[q])):
                    dsem[(q, s)] = st.enter_context(nc.semaphore("ds_%s_%d" % (q, s)))
            if getattr(self, "ncc", 0):
                dsem[("cc", 0)] = st.enter_context(nc.semaphore("ccsem"))
            block = st.enter_context(nc.Block())
            cum = {}
            for e in ENGS:
                cum[e] = {idx: i + 1 for i, idx in enumerate(sorted(self.flag[e]))}

            def run(e, eng):
                for a in self.acts[e]:
                    k = a[0]
                    if k == "we":
                        eng.wait_ge(esem[a[1]], cum[a[1]][a[2]])
                    elif k == "wd":
                        eng.wait_ge(dsem[(a[1], a[2])], a[3])
                    elif k == "c":
                        a[1](eng).then_inc(dsem[("cc", 0)], 1)
                    elif k == "i":
                        ins = a[1](eng)
                        if a[2] in cum[e]:
                            ins.then_inc(esem[e], 1)
                    else:
                        eng.dma_start(out=a[1], in_=a[2], **a[4]).then_inc(dsem[(e, a[3])], 16)

            @block.tensor
            def _(eng):
                run("pe", eng)

            @block.scalar
            def _(eng):
                run("act", eng)

            @block.vector
            def _(eng):
                run("dve", eng)

            @block.gpsimd
            def _(eng):
                run("pool", eng)

            @block.sync
            def _(eng):
                run("sp", eng)


def _t5_bucket_np(rel):
    import jax
    import jax.numpy as jnp
    with jax.default_device(jax.devices("cpu")[0]):
        rel = jnp.asarray(rel, dtype=jnp.int32)
        half = 16
        max_exact = 8
        n = jnp.abs(rel)
        nf = jnp.maximum(n, 1).astype(jnp.float32)
        large = max_exact + (jnp.log(nf / max_exact) / math.log(128 / max_exact) * (half - max_exact)).astype(jnp.int32)
        large = jnp.minimum(large, half - 1)
        out = jnp.where(rel > 0, half, 0) + jnp.where(n < max_exact, n, large)
        return np.asarray(out)


def lambda_init(layer):
    return 0.8 - 0.6 * math.exp(-0.3 * layer)


def host_consts(e=0):
    c = {}
    def oh_a_for(ee):
        s_ = np.arange(GA_W)
        bk = _t5_bucket_np(511 + 512 * ee - s_)
        oh = np.zeros((32, GA_W), np.float32)
        oh[bk, s_] = 1.0
        return oh
    def oh_c_for(ee):
        s_ = np.arange(GC_W)
        idx = np.clip(511 + 512 * ee - s_, -REL_CLIP, REL_CLIP) + REL_CLIP
        oh = np.zeros((384, GC_W), np.float32)
        oh[idx, s_] = 1.0
        return oh
    c["oh_a"] = np.stack([oh_a_for(e), oh_a_for(0)])
    c["oh_c"] = np.stack([oh_c_for(e), oh_c_for(0)])
    c["far_bucket"] = int(_t5_bucket_np(np.array([-4000]))[0])
    k = np.arange(128)[:, None]
    def masks(ee, jj):
        rel = k - jj + 384 + 512 * ee
        dch = k // 64 - jj // 64 + 6 + 8 * ee
        ma = np.where(dch <= 0, 0.0, NEG).astype(np.float32)
        mb = (rel < 0).astype(np.float32)
        mc = np.where((dch <= 0) & (dch >= -C_PREV), 0.0, NEG).astype(np.float32)
        return ma, mb, mc
    ma, _, _ = masks(e, np.arange(WA_MAIN)[None, :])
    _, mb, _ = masks(e, np.arange(WB_MAIN)[None, :])
    _, _, mc = masks(e, np.arange(WC_MAIN)[None, :])
    mas, _, _ = masks(0, np.arange(384, 384 + WA_S)[None, :])
    _, mbs, _ = masks(0, np.arange(384, 384 + WB_S)[None, :])
    _, _, mcs = masks(0, np.arange(384, 384 + WC_S)[None, :])
    c["mask_a"] = np.concatenate([ma, mas], axis=1)
    c["mask_b"] = np.concatenate([mb, mbs], axis=1)
    c["mask_c"] = np.concatenate([mc, mcs], axis=1)
    return c


class Group:
    pass


class K:
    def __init__(self, SEQ, NS, PAST, CB, far_bucket, NPAIR=4):
        self.SEQ, self.NS, self.PAST, self.CB = SEQ, NS, PAST, CB
        self.NPAIR = NPAIR
        self.far_bucket = far_bucket
        self.nc = bass.Bass("TRN2", target_bir_lowering=False)
        self.S = Sched(self.nc)
        self.din = {}
        self.dout = {}

    def dram_in(self, name, shape):
        self.din[name] = self.nc.dram_tensor(name, list(shape), F32, kind="ExternalInput").ap()
        return self.din[name]

    def dram_out(self, name, shape):
        self.dout[name] = self.nc.dram_tensor(name, list(shape), F32, kind="ExternalOutput").ap()
        return self.dout[name]

    def dram_tmp(self, name, shape, dt):
        return self.nc.dram_tensor(name, list(shape), dt).ap()

    def sb(self, name, shape, dt):
        return TT(self.st.enter_context(self.nc.sbuf_tensor(name, list(shape), dt)), name)

    def mm(self, out, lhsT, rhs, start, stop, reads, writes):
        self.S.op("pe", lambda e: e.matmul(out, lhsT=lhsT, rhs=rhs, start=start, stop=stop), reads, writes)

    def act(self, out, in_, func, reads, writes, **kw):
        self.S.op("act", lambda e: e.activation(out=out, in_=in_, func=func, **kw), reads, writes)

    def tt(self, out, in0, in1, op, reads, writes, eng="dve"):
        self.S.op(eng, lambda e: e.tensor_tensor(out=out, in0=in0, in1=in1, op=op), reads, writes)

    def ts(self, out, in0, s1, s2, op0, op1, reads, writes, eng="dve"):
        if s2 is None:
            self.S.op(eng, lambda e: e.tensor_scalar(out=out, in0=in0, scalar1=s1, scalar2=None, op0=op0), reads, writes)
        else:
            self.S.op(eng, lambda e: e.tensor_scalar(out=out, in0=in0, scalar1=s1, scalar2=s2, op0=op0, op1=op1), reads, writes)

    def stt(self, out, in0, scalar, in1, op0, op1, reads, writes):
        self.S.op("dve", lambda e: e.scalar_tensor_tensor(out=out, in0=in0, scalar=scalar, in1=in1, op0=op0, op1=op1), reads, writes)

    def cp(self, out, in_, reads, writes, eng="dve"):
        self.S.op(eng, lambda e: e.tensor_copy(out=out, in_=in_), reads, writes)

    def recip(self, out, in_, reads, writes):
        self.S.op("dve", lambda e: e.reciprocal(out=out, in_=in_), reads, writes)

    def memset(self, ap, val, writes, eng="pool"):
        self.S.op(eng, lambda e: e.memset(ap, val), (), writes)

    def dma(self, out, in_, reads=(), writes=(), q="sp", **kw):
        self.S.dma(q, out, in_, reads, writes, **kw)

    def ring(self, key):
        lst, i = self.rings[key]
        self.rings[key][1] = (i + 1) % len(lst)
        return lst[i]

    def build(self):
        nc = self.nc
        SEQ, NS, PAST, CB = self.SEQ, self.NS, self.PAST, self.CB
        di, do, dt_ = self.dram_in, self.dram_out, self.dram_tmp
        x_p = di("x_p", [SEQ, D]); x_s = di("x_s", [NS, D]); mem_p = di("mem_p", [NMEM, D])
        ca_k = di("ca_k", [NL, PAST, 512]); ca_v = di("ca_v", [NL, PAST, 512])
        cb_k = di("cb_k", [NL, PAST, 256]); cb_v = di("cb_v", [NL, PAST, 256])
        cc_k = di("cc_k", [NL, CB, 256]); cc_v = di("cc_v", [NL, CB, 256])
        cm_k = di("cm_k", [NL, NMEM, 256]); cm_v = di("cm_v", [NL, NMEM, 256])
        t5 = di("t5", [32, 4])
        gains = di("gains", [NL, 5, D])
        hgains = di("hgains", [NL, 6, HD])
        lvec = di("lvec", [NL, 4, HD])
        subln = di("subln", [NL, 128])
        crel = di("crel", [NL, 4, 257])
        W = {}
        wshapes = {"wg1": (D, DFF), "wu1": (D, DFF), "wd1": (DFF, D), "w_in": (D, 6144), "w_br_a": (512, D),
                   "w_br_b": (256, D), "w_br_c": (256, D), "w_out": (D, D), "x_wq": (D, 256), "x_wkv": (D, 512),
                   "x_wo": (256, D), "wg2": (D, DFF), "wu2": (D, DFF), "wd2": (DFF, D)}
        for n_, shp in wshapes.items():
            W[n_] = di(n_, [NL, shp[0], shp[1]])
        oh_a = di("oh_a", [2, 32, GA_W]); oh_c = di("oh_c", [2, 384, GC_W])
        mask_a = di("mask_a", [128, WA_STRIP]); mask_b = di("mask_b", [128, WB_STRIP]); mask_c = di("mask_c", [128, WC_STRIP])

        y_p = do("y_p", [SEQ, D]); y_s = do("y_s", [NS, D])
        o_akp = do("a_k_p", [NL, SEQ, 512]); o_avp = do("a_v_p", [NL, SEQ, 512])
        o_bkp = do("b_k_p", [NL, SEQ, 256]); o_bvp = do("b_v_p", [NL, SEQ, 256])
        CKP = 512
        o_ckp = do("c_k_p", [NL, CKP, 256]); o_cvp = do("c_v_p", [NL, CKP, 256])
        o_mkp = do("m_k_p", [NL, NMEM, 256]); o_mvp = do("m_v_p", [NL, NMEM, 256])
        o_aks = do("a_k_s", [NL, NS, 512]); o_avs = do("a_v_s", [NL, NS, 512])
        o_bks = do("b_k_s", [NL, NS, 256]); o_bvs = do("b_v_s", [NL, NS, 256])
        o_cks = do("c_k_s", [NL, CB, 256]); o_cvs = do("c_v_s", [NL, CB, 256])

        Wb = {}
        Wbuf = {}
        for n_, shp in wshapes.items():
            Wb[n_] = dt_("wb_" + n_, [NL, shp[0], shp[1]], BF16)
        xcur = dt_("xcur", [SEQ, D], F32)
        xcur_s = dt_("xcur_s", [NS, D], F32)
        ga_d = dt_("ga_d", [2, 4, GA_W], F32)
        gc_d = dt_("gc_d", [2, 4, GC_W], F32)
        NSLOT = SEQ // 512
        xkv = [dt_("xkv%d" % i, [2048, 512], BF16) for i in range(2)]
        xkvb = [Buf("xkv%d" % i) for i in range(2)]
        gout = [dt_("gout%d" % i, [4096, 512], BF16) for i in range(NSLOT)]
        gbuf = [Buf("gout%d" % i) for i in range(NSLOT)]

        def kvscr(tag, n):
            g = {}
            g["kta"] = dt_("kta_" + tag, [4, 128, n], BF16); g["va"] = dt_("va_" + tag, [n, 512], BF16)
            g["ktb"] = dt_("ktb_" + tag, [2, 128, n], BF16); g["vb"] = dt_("vb_" + tag, [n, 256], BF16)
            g["ktc"] = dt_("ktc_" + tag, [2, 128, n], BF16); g["vc"] = dt_("vc_" + tag, [n, 256], BF16)
            return g

        with contextlib.ExitStack() as st:
            self.st = st
            sb = self.sb
            S = self.S
            self.bank = [TT(st.enter_context(nc.psum_tensor("bank%d" % i, [128, 512], F32)), "bank%d" % i) for i in range(8)]
            bank = self.bank
            for b_ in bank:
                b_.b.x = True
            ident = sb("ident", [128, 128], BF16); onesb = sb("onesb", [128, 128], BF16)
            onesf = sb("onesf", [128, 128], F32); Jf = sb("Jf", [128, 128], F32)
            negU8 = sb("negU8", [128, 128], BF16); epsc = sb("epsc", [128, 1], F32)
            tmpc = sb("tmpc", [128, 128], F32)
            self.ident, self.onesb, self.onesf, self.negU8, self.epsc = ident, onesb, onesf, negU8, epsc
            self.memset(tmpc.t[:], 0.0, [tmpc.b])
            S.op("pool", lambda e: e.affine_select(out=tmpc.t[:], in_=tmpc.t[:], pattern=[[-1, 128]], compare_op=ALU.not_equal, fill=1.0, base=0, channel_multiplier=1), [tmpc.b], [tmpc.b])
            self.cp(ident.t[:], tmpc.t[:], [tmpc.b], [ident.b])
            identF = sb("identF", [128, 128], F32)
            self.cp(identF.t[:], tmpc.t[:], [tmpc.b], [identF.b])
            self.memset(Jf.t[:], 0.0, [Jf.b])
            S.op("pool", lambda e: e.affine_select(out=Jf.t[:], in_=Jf.t[:], pattern=[[1, 128]], compare_op=ALU.not_equal, fill=1.0, base=-127, channel_multiplier=1), [Jf.b], [Jf.b])
            self.memset(onesf.t[:], 1.0, [onesf.b])
            self.cp(onesb.t[:], onesf.t[:], [onesf.b], [onesb.b])
            tmpc2 = tmpc
            self.memset(tmpc2.t[:], -8.0, [tmpc2.b])
            S.op("pool", lambda e: e.affine_select(out=tmpc2.t[:], in_=tmpc2.t[:], pattern=[[-1, 128]], compare_op=ALU.is_ge, fill=0.0, base=0, channel_multiplier=1), [tmpc2.b], [tmpc2.b])
            self.cp(negU8.t[:], tmpc2.t[:], [tmpc2.b], [negU8.b])
            self.memset(epsc.t[:], EPS, [epsc.b])
            onec = sb("onec", [128, 1], F32)
            self.memset(onec.t[:], 1.0, [onec.b])

            def bc_ap(src, off, n):
                return bass.AP(tensor=src.tensor, offset=off, ap=[[0, 128], [1, n]])

            t5bc = sb("t5bc", [128, 128], F32)
            self.dma(t5bc.t[:], bc_ap(t5, 0, 128), (), [t5bc.b])
            gcol = sb("gcol", [128, NL * 5, 8], F32)
            NG = NL * 5 * 8
            self.dma(tmpc.t[:NG, :], gains.rearrange("l w (c p) -> (l w c) p", p=128), (), [tmpc.b])
            self.mm(bank[1].t[:, :NG], tmpc.t[:NG, :], identF.t[:NG, :NG], True, True, [tmpc.b, identF.b], [bank[1].b])
            self.cp(gcol.t[:].rearrange("p a b -> p (a b)"), bank[1].t[:, :NG], [bank[1].b], [gcol.b])
            hg = sb("hg", [128, NL * 6, HD], F32)
            self.dma(hg.t[:].rearrange("p a b -> p (a b)"), bc_ap(hgains, 0, NL * 6 * HD), (), [hg.b])
            lv = sb("lv", [128, NL * 4, HD], F32)
            self.dma(lv.t[:].rearrange("p a b -> p (a b)"), bc_ap(lvec, 0, NL * 4 * HD), (), [lv.b])
            subcol = sb("subcol", [128, NL], F32)
            self.dma(tmpc.t[:NL, :], subln[:, :], (), [tmpc.b])
            self.mm(bank[1].t[:, :NL], tmpc.t[:NL, :], identF.t[:NL, :NL], True, True, [tmpc.b, identF.b], [bank[1].b])
            self.cp(subcol.t[:], bank[1].t[:, :NL], [bank[1].b], [subcol.b])
            neglam = sb("neglam", [128, NL], F32)
            lt = sb("lt", [128, HD], F32); le = sb("le", [128, 4], F32)
            for l in range(NL):
                for i in range(2):
                    self.tt(lt.t[:], lv.t[:, l * 4 + 2 * i, :], lv.t[:, l * 4 + 2 * i + 1, :], ALU.mult, [lv.b], [lt.b])
                    S.op("dve", lambda e, i=i: e.tensor_reduce(out=le.t[:, i:i + 1], in_=lt.t[:], axis=AX.X, op=ALU.add), [lt.b], [le.b])
                self.act(le.t[:, 2:4], le.t[:, 0:2], AF.Exp, [le.b], [le.b])
                self.tt(neglam.t[:, l:l + 1], le.t[:, 3:4], le.t[:, 2:3], ALU.subtract, [le.b], [neglam.b])
                self.ts(neglam.t[:, l:l + 1], neglam.t[:, l:l + 1], -lambda_init(l), None, ALU.add, ALU.bypass, [neglam.b], [neglam.b])
                self.ts(subcol.t[:, l:l + 1], subcol.t[:, l:l + 1], 1.0 - lambda_init(l), None, ALU.mult, ALU.bypass, [subcol.b], [subcol.b])
            self.neglam, self.subcol, self.hg, self.gcol, self.t5bc = neglam, subcol, hg, gcol, t5bc

            xin = sb("xin", [128, 4, D], F32)
            hT = sb("hT", [128, 8, 512], BF16)
            hidT = sb("hidT", [128, 32, 512], BF16)
            self.rings = {
                "xn": [[sb("xn%d" % i, [128, D], BF16) for i in range(2)], 0],
                "w": [[sb("wslot%d" % i, [128, 8, 512], BF16) for i in range(4)], 0],
                "gt": [[sb("gt%d" % i, [128, 3, 512], BF16) for i in range(2)], 0],
                "tf": [[sb("tf%d" % i, [128, 512], F32) for i in range(4)], 0],
                "tb": [[sb("tb%d" % i, [128, 512], BF16) for i in range(5)], 0],
                "of": [[sb("of%d" % i, [128, 512], F32) for i in range(3)], 0],
                "ob": [[sb("ob%d" % i, [128, 4, 128], BF16) for i in range(2)], 0],
                "kt": [[sb("ktl%d" % i, [128, 1024], BF16) for i in range(2)], 0],
                "v": [[sb("vl%d" % i, [128, 8, 128], BF16) for i in range(2)], 0],
                "sc": [[bank[0], bank[1], bank[6], bank[7]], 0],
                "gu": [[(bank[0], bank[1]), (bank[2], bank[3]), (bank[4], bank[5])], 0],
            }
            QTA = [sb("QTA%d" % i, [128, 4, 512], BF16) for i in range(2)]
            QTB = [sb("QTB%d" % i, [128, 2, 512], BF16) for i in range(2)]
            QTC = [sb("QTC%d" % i, [128, 2, 512], BF16) for i in range(2)]
            QTX = QTC
            for q_ in QTA + QTB + QTC:
                self.memset(q_.t[:], 0.0, [q_.b])
            gates_d = {"p": [dt_("gates_p%d" % i, [4, 128, 3072], BF16) for i in range(2)], "s": [dt_("gates_s", [4, 128, 3072], BF16)]}
            gates_b = {"p": [Buf("gp0"), Buf("gp1")], "s": [Buf("gs")]}
            oaT = sb("oaT", [128, 4, 512], BF16); obT = sb("obT", [128, 2, 512], BF16)
            ocT = sb("ocT", [128, 2, 512], BF16); oxT = ocT
            Rt = sb("Rt", [128, 512], F32)
            mrgb = sb("mrgb", [128, D], BF16); junk = mrgb
            stripA = sb("stripA", [128, 4, WA_STRIP], BF16)
            stripC = sb("stripC", [128, 4, WC_STRIP], BF16)
            maskB = sb("maskB", [128, WB_STRIP], BF16)
            mkT = {g: sb("mkT_" + g, [128, 2, NMEM], BF16) for g in "ps"}
            mvb = {g: sb("mvb_" + g, [128, 2, 256], BF16) for g in "ps"}
            self.rings["hn"] = [[(sb("ssg%d" % i, [128, 8], F32), sb("sdg%d" % i, [128, 8], F32), sb("rsg%d" % i, [128, 8], F32)) for i in range(4)], 0]
            self.rings["rn"] = [[(sb("ss%d" % i, [128, 1], F32), sb("sd%d" % i, [128, 1], F32), sb("rs%d" % i, [128, 1], F32)) for i in range(4)], 0]
            self.xin, self.hT, self.hidT = xin, hT, hidT

            for n_, shp in wshapes.items():
                Wbuf[n_] = []
                rows = max(1, (1 << 20) // shp[1])
                for l in range(NL):
                    bl = []
                    for r0 in range(0, shp[0], rows):
                        r1 = min(shp[0], r0 + rows)
                        b_ = Buf("wb_%s_%d_%d" % (n_, l, r0))
                        self.dma(Wb[n_][l, r0:r1, :], W[n_][l, r0:r1, :], (), [b_], q="pool")
                        bl.append(b_)
                    Wbuf[n_].append(bl)

            stg_t = xin.t[:, 0:2, :].rearrange("p a b -> p (a b)")
            stg2_t = xin.t[:, 2:4, :].rearrange("p a b -> p (a b)")
            self.dma(stg_t[:, :WB_STRIP], mask_b[:, :], (), [xin.b])
            self.cp(maskB.t[:], stg_t[:, :WB_STRIP], [xin.b], [maskB.b])
            t5_sb = sb("t5_sb", [32, 4], F32)
            self.dma(t5_sb.t[:], t5[:, :], (), [t5_sb.b])
            gsb_t = stg2_t[:4, :GC_W]
            gab = Buf("ga_d")
            for v_ in range(2):
                self.dma(stg_t[:32, :GA_W], oh_a[v_], (), [xin.b])
                for c0 in range(0, GA_W, 512):
                    self.mm(bank[0].t[:4, :512], t5_sb.t[:, :], stg_t[:32, c0:c0 + 512], True, True, [t5_sb.b, xin.b], [bank[0].b])
                    self.cp(gsb_t[:, c0:c0 + 512], bank[0].t[:4, :512], [bank[0].b], [xin.b])
                self.dma(ga_d[v_], gsb_t[:, :GA_W], [xin.b], [gab])

            def build_strip(gd, gbuf_, gw, goff, jj_start, mask_d, mcol0, width, dst, dcol0):
                self.dma(stg2_t[:, :width], mask_d[:, mcol0:mcol0 + width], (), [xin.b])
                for h in range(4):
                    src = bass.AP(tensor=gd.tensor, offset=goff + h * gw + jj_start, ap=[[1, 128], [1, width]])
                    self.dma(stg_t[:, :width], src, [gbuf_], [xin.b])
                    for c0 in range(0, width, 512):
                        cw = min(512, width - c0)
                        self.mm(bank[0].t[:, :cw], Jf.t[:, :], stg_t[:, c0:c0 + cw], True, True, [Jf.b, xin.b], [bank[0].b])
                        self.tt(dst.t[:, h, dcol0 + c0:dcol0 + c0 + cw], bank[0].t[:, :cw], stg2_t[:, c0:c0 + cw], ALU.add, [bank[0].b, xin.b], [dst.b])

            build_strip(ga_d, gab, GA_W, 0, 0, mask_a, 0, WA_MAIN, stripA, 0)
            build_strip(ga_d, gab, GA_W, 4 * GA_W, 384, mask_a, WA_MAIN, WA_S, stripA, WA_MAIN)
            crT = sb("crT", [128, 3, 4], F32)
            cr_in = sb("cr_in", [4, 264], F32)
            gcb = Buf("gc_d")

            def layer_setup(l):
                self.memset(crT.t[:], 0.0, [crT.b])
                self.dma(cr_in.t[:4, :257], crel[l], (), [cr_in.b])
                for c in range(3):
                    n = min(128, 257 - c * 128)
                    self.mm(bank[1].t[:n, :4], cr_in.t[:4, c * 128:c * 128 + n], identF.t[:4, :4], True, True, [cr_in.b, identF.b], [bank[1].b])
                    self.cp(crT.t[:n, c, :], bank[1].t[:n, :4], [bank[1].b], [crT.b])
                ohv = stg_t[:, :1536].rearrange("p (c s) -> p c s", s=512)
                for v_ in range(2):
                    for c0 in range(0, GC_W, 512):
                        self.dma(ohv, oh_c[v_, :, c0:c0 + 512].rearrange("(c p) s -> p c s", p=128), (), [xin.b])
                        for c in range(3):
                            self.mm(bank[0].t[:4, :512], crT.t[:, c, :], ohv[:, c, :], c == 0, c == 2, [crT.b, xin.b], [bank[0].b])
                        self.cp(gsb_t[:, c0:c0 + 512], bank[0].t[:4, :512], [bank[0].b], [xin.b])
                    self.dma(gc_d[v_], gsb_t[:, :], [xin.b], [gcb])
                build_strip(gc_d, gcb, GC_W, 0, 0, mask_c, 0, WC_MAIN, stripC, 0)
                build_strip(gc_d, gcb, GC_W, 4 * GC_W, 384, mask_c, WC_MAIN, WC_S, stripC, WC_MAIN)

            def load_w(name, l, r0, c0, ncols, kc=8, pk=128):
                slot = self.ring("w")
                src = Wb[name][l, r0:r0 + kc * pk, c0:c0 + ncols].rearrange("(c p) n -> p c n", p=pk)
                self.dma(slot.t[:pk, :kc, :ncols], src, Wbuf[name][l], [slot.b])
                return slot

            def rms_T(src, np_, nsub, gi, dst):
                sc_ = []
                for s in range(nsub):
                    ss_, sd_, rs_ = self.ring("rn")
                    self.memset(ss_.t[:], 0.0, [ss_.b], eng="dve")
                    self.act(junk.t[:np_, :], src.t[:np_, s, :], AF.Square, [src.b], [junk.b, ss_.b], accum_out=ss_.t[:np_, 0:1])
                    sc_.append((ss_, sd_, rs_))
                for s in range(nsub):
                    ss_, sd_, rs_ = sc_[s]
                    self.act(sd_.t[:np_, :], ss_.t[:np_, :], AF.Sqrt, [ss_.b, epsc.b], [sd_.b], scale=1.0 / D, bias=epsc.t[:np_, :])
                    self.recip(rs_.t[:np_, :], sd_.t[:np_, :], [sd_.b], [rs_.b])
                    xn = self.ring("xn")
                    self.ts(xn.t[:np_, :], src.t[:np_, s, :], rs_.t[:np_, 0:1], None, ALU.mult, ALU.bypass, [src.b, rs_.b], [xn.b])
                    self.transpose_to(lambda c, xn=xn: xn.t[:np_, c * 128:(c + 1) * 128], [xn.b], 8, np_,
                                      dst.t[:, :, s * 128:s * 128 + np_], [dst.b],
                                      mul=gcol.t[:, gi, :].unsqueeze(2).to_broadcast([128, 8, np_]), mulb=[gcol.b])

            self.trp_i = 0
            self.acc_i = 0

            def transpose_to(src_fn, srcb, nblk, np_, dst_ap, dstb, mul=None, mulb=()):
                bk = bank[6] if self.trp_i % 2 == 0 else bank[7]
                self.trp_i += 1
                tv = bk.t[:, :].bitcast(BF16).rearrange("p (c n) -> p c n", n=128)
                for c in range(nblk):
                    S.op("pe", lambda e, c=c: e.transpose(out=tv[:, c, :np_], in_=src_fn(c), identity=ident.t[:np_, :np_]), list(srcb) + [ident.b], [bk.b])
                if mul is None:
                    self.act(dst_ap, tv[:, :nblk, :np_], AF.Copy, [bk.b], dstb)
                else:
                    self.tt(dst_ap, tv[:, :nblk, :np_], mul, ALU.mult, [bk.b] + list(mulb), dstb)
            self.transpose_to = transpose_to

            def transpose_split(src_fn, srcb, nblk, np_, q2, cs_):
                bk = bank[6] if self.trp_i % 2 == 0 else bank[7]
                self.trp_i += 1
                tv = bk.t[:, :].bitcast(BF16).rearrange("p (c n) -> p c n", n=128)
                for c in range(nblk):
                    S.op("pe", lambda e, c=c: e.transpose(out=tv[:, c, :np_], in_=src_fn(c), identity=ident.t[:np_, :np_]), list(srcb) + [ident.b], [bk.b])
                self.act(q2[0].t[0:64, :nblk, cs_], tv[0:64, :nblk, :np_], AF.Copy, [bk.b], [q2[0].b])
                self.cp(q2[1].t[64:128, :nblk, cs_], tv[64:128, :nblk, :np_], [bk.b], [q2[1].b])

            def linear_tm(name, l, KC, N, lhs_fn, lhsb, np_, nsub, evac, pieces=None):
                npieces = (N + 511) // 512
                accring = [bank[2], bank[3], bank[4], bank[5], bank[0], bank[1]]
                DEFER = 3
                pend_ev = []
                for p in (pieces if pieces is not None else range(npieces)):
                    ncols = min(512, N - p * 512)
                    if KC <= 8:
                        slot = load_w(name, l, 0, p * 512, ncols, KC)
                        for s in range(nsub):
                            acc = accring[self.acc_i % 6]
                            self.acc_i += 1
                            for k in range(KC):
                                self.mm(acc.t[:np_, :ncols], lhs_fn(k, s), slot.t[:, k, :ncols], k == 0, k == KC - 1,
                                        list(lhsb) + [slot.b], [acc.b])
                            pend_ev.append((p, s, acc, ncols))
                            if len(pend_ev) > DEFER:
                                evac(*pend_ev.pop(0))
                        continue
                    for kg in range(0, KC, 8):
                        kc = min(8, KC - kg)
                        slot = load_w(name, l, kg * 128, p * 512, ncols, kc)
                        for s in range(nsub):
                            acc = bank[2 + s]
                            for k in range(kc):
                                self.mm(acc.t[:np_, :ncols], lhs_fn(kg + k, s), slot.t[:, k, :ncols], kg + k == 0, kg + k == KC - 1,
                                        list(lhsb) + [slot.b], [acc.b])
                    for s in range(nsub):
                        evac(p, s, bank[2 + s], ncols)
                while pend_ev:
                    evac(*pend_ev.pop(0))

            def headnorm(pb_ap, pbb, np_, G, gi, out_ap, outb):
                tf = self.ring("tf")
                ssg, sdg, rsg = self.ring("hn")
                self.act(tf.t[:np_, :G * 64], pb_ap, AF.Square, pbb, [tf.b])
                S.op("dve", lambda e: e.tensor_reduce(out=ssg.t[:np_, :G], in_=tf.t[:np_, :G * 64].rearrange("p (g d) -> p g d", d=64), axis=AX.X, op=ALU.add), [tf.b], [ssg.b])
                self.act(sdg.t[:np_, :G], ssg.t[:np_, :G], AF.Sqrt, [ssg.b, epsc.b], [sdg.b], scale=1.0 / HD, bias=epsc.t[:np_, :])
                self.recip(rsg.t[:np_, :G], sdg.t[:np_, :G], [sdg.b], [rsg.b])
                tf2 = self.ring("tf")
                self.tt(tf2.t[:np_, :G * 64].rearrange("p (g d) -> p g d", d=64), pb_ap.rearrange("p (g d) -> p g d", d=64),
                        rsg.t[:np_, :G].unsqueeze(2).to_broadcast([np_, G, 64]), ALU.mult, list(pbb) + [rsg.b], [tf2.b])
                self.tt(out_ap.rearrange("p (g d) -> p g d", d=64), tf2.t[:np_, :G * 64].rearrange("p (g d) -> p g d", d=64),
                        hg.t[:np_, gi, :].unsqueeze(1).to_broadcast([np_, G, 64]), ALU.mult, [tf2.b, hg.b], outb)

            def ffn(l, which, G):
                np_, nsub, T = G.np, G.nsub, G.T
                sfx = "1" if which == 0 else "2"
                rms_T(xin, np_, nsub, l * 5 + (0 if which == 0 else 4), hT)
                for fg in range(8):
                    wgs = load_w("wg" + sfx, l, 0, fg * 512, 512)
                    wus = load_w("wu" + sfx, l, 0, fg * 512, 512)
                    for fc in range(4):
                        bg, bu = self.ring("gu")
                        for k in range(8):
                            self.mm(bg.t[:, :T], wgs.t[:, k, fc * 128:(fc + 1) * 128], hT.t[:, k, :T], k == 0, k == 7, [wgs.b, hT.b], [bg.b])
                        for k in range(8):
                            self.mm(bu.t[:, :T], wus.t[:, k, fc * 128:(fc + 1) * 128], hT.t[:, k, :T], k == 0, k == 7, [wus.b, hT.b], [bu.b])
                        tf = self.ring("tf")
                        self.act(tf.t[:, :T], bg.t[:, :T], AF.Silu, [bg.b], [tf.b])
                        self.tt(hidT.t[:, fg * 4 + fc, :T], tf.t[:, :T], bu.t[:, :T], ALU.mult, [tf.b, bu.b], [hidT.b])

                def evac(p, s, acc, ncols):
                    self.stt(xin.t[:np_, s, p * 512:(p + 1) * 512], acc.t[:np_, :512], 0.5, xin.t[:np_, s, p * 512:(p + 1) * 512],
                             ALU.mult, ALU.add, [acc.b, xin.b], [xin.b])
                linear_tm("wd" + sfx, l, 32, D, lambda k, s: hidT.t[:, k, s * 128:s * 128 + np_], [hidT.b], np_, nsub, evac)

            def proj(l, G, t):
                np_, nsub, T = G.np, G.nsub, G.T
                i0 = G.widx0(t)
                i0c = G.widx0c(t)
                scr_w = G.scrw(t)
                r0 = G.orow(t)
                rms_T(xin, np_, nsub, l * 5 + 1, hT)

                def out_f(dst, of, w):
                    self.dma(dst, of.t[:np_, :w], [of.b], (), q="pool")

                def kt_out(kname, tb, nb, idx0, s):
                    ob = self.ring("ob")
                    transpose_to(lambda c, tb=tb: tb.t[:np_, c * 128:(c + 1) * 128], [tb.b], nb, np_, ob.t[:, :nb, :np_], [ob.b])
                    self.dma(scr_w[kname][:, :, idx0 + s * 128:idx0 + s * 128 + np_].rearrange("h p s -> p h s"), ob.t[:, :nb, :np_], [ob.b], [G.wbuf(kname, t)], q="pool")

                def v_out(vname, tb, w, idx0, s):
                    if os.environ.get("KNOV") == "1":
                        return
                    self.dma(scr_w[vname][idx0 + s * 128:idx0 + s * 128 + np_, :], tb.t[:np_, :w], [tb.b], [G.wbuf(vname, t)], q="pool")

                def evac(p, s, acc, ncols):
                    pb = acc.t[:np_, :512]
                    rs_ = slice(r0 + s * 128, r0 + s * 128 + np_)
                    if p == 0:
                        tb = self.ring("tb")
                        headnorm(pb, [acc.b], np_, 8, l * 6 + 0, tb.t[:np_, :], [tb.b])
                        transpose_split(lambda c, tb=tb: tb.t[:np_, c * 128:(c + 1) * 128], [tb.b], 4, np_, QTA, slice(s * 128, s * 128 + np_))
                    elif p == 1:
                        of = self.ring("of")
                        headnorm(pb, [acc.b], np_, 8, l * 6 + 1, of.t[:np_, :], [of.b])
                        out_f(G.o_ak[l, rs_, :], of, 512)
                        tb = self.ring("tb")
                        self.cp(tb.t[:np_, :], of.t[:np_, :], [of.b], [tb.b])
                        kt_out("kta", tb, 4, i0, s)
                    elif p == 2:
                        of = self.ring("of")
                        self.act(of.t[:np_, :], pb, AF.Copy, [acc.b], [of.b])
                        out_f(G.o_av[l, rs_, :], of, 512)
                        tb = self.ring("tb")
                        self.cp(tb.t[:np_, :], pb, [acc.b], [tb.b])
                        v_out("va", tb, 512, i0, s)
                    elif p == 3:
                        tb = self.ring("tb")
                        self.cp(tb.t[:np_, :256], pb[:, 0:256], [acc.b], [tb.b])
                        transpose_split(lambda c, tb=tb: tb.t[:np_, c * 128:(c + 1) * 128], [tb.b], 2, np_, QTB, slice(s * 128, s * 128 + np_))
                        of = self.ring("of")
                        self.act(of.t[:np_, :256], pb[:, 256:512], AF.Copy, [acc.b], [of.b])
                        out_f(G.o_bk[l, rs_, :], of, 256)
                        tb2 = self.ring("tb")
                        self.cp(tb2.t[:np_, :256], pb[:, 256:512], [acc.b], [tb2.b])
                        kt_out("ktb", tb2, 2, i0, s)
                    elif p == 4:
                        of = self.ring("of")
                        self.act(of.t[:np_, :256], pb[:, 0:256], AF.Copy, [acc.b], [of.b])
                        out_f(G.o_bv[l, rs_, :], of, 256)
                        tb = self.ring("tb")
                        self.cp(tb.t[:np_, :256], pb[:, 0:256], [acc.b], [tb.b])
                        v_out("vb", tb, 256, i0, s)
                        tb2 = self.ring("tb")
                        headnorm(pb[:, 256:512], [acc.b], np_, 4, l * 6 + 2, tb2.t[:np_, :256], [tb2.b])
                        transpose_split(lambda c, tb2=tb2: tb2.t[:np_, c * 128:(c + 1) * 128], [tb2.b], 2, np_, QTC, slice(s * 128, s * 128 + np_))
                    elif p == 5:
                        of = self.ring("of")
                        headnorm(pb[:, 0:256], [acc.b], np_, 4, l * 6 + 3, of.t[:np_, 0:256], [of.b])
                        self.act(of.t[:np_, 256:512], pb[:, 256:512], AF.Copy, [acc.b], [of.b])
                        G.store_c(l, t, s, of)
                        tb = self.ring("tb")
                        self.cp(tb.t[:np_, :512], of.t[:np_, :512], [of.b], [tb.b])
                        kt_out("ktc", tb, 2, i0c, s)
                        tb3 = self.ring("tb")
                        self.cp(tb3.t[:np_, :256], pb[:, 256:512], [acc.b], [tb3.b])
                        v_out("vc", tb3, 256, i0c, s)
                    else:
                        tbg = self.ring("tb")
                        self.act(tbg.t[:np_, :], pb, AF.Sigmoid, [acc.b], [tbg.b])
                        gi_ = t % len(gates_d[G.name])
                        self.dma(gates_d[G.name][gi_][s, :np_, (p - 6) * 512:(p - 5) * 512], tbg.t[:np_, :], [tbg.b], [gates_b[G.name][gi_]], q="pool")
                linear_tm("w_in", l, 8, 6144, lambda k, s: hT.t[:, k, s * 128:s * 128 + np_], [hT.b], np_, nsub, evac,
                          pieces=[int(x) for x in os.environ["KPIECES"].split(",")] if "KPIECES" in os.environ else None)

            def key_chunks(lo, hi, csz=1024):
                out = []
                c0 = lo
                while c0 < hi:
                    n = min(csz, hi - c0)
                    out.append((c0, n))
                    c0 += n
                return out

            KBASE = {"kta": 0, "ktb": 1024, "ktc": 1536}
            VBASE = {"va": 512, "vb": 1280, "vc": 1792}

            def load_kv(G, kname, vname, hp_or_h, prow0, nrow, vcol0, dv, c0, n):
                kt = self.ring("kt")
                v = self.ring("v")
                if G.name == "p":
                    slot = c0 // 1024
                    assert c0 % 1024 == 0 and n == 1024
                    g, deps = gout[slot], [gbuf[slot]]
                    for r in range(2):
                        base = r * 2048
                        krow = base + KBASE[kname] + hp_or_h * 128 + prow0
                        self.dma(kt.t[prow0:prow0 + nrow, r * 512:(r + 1) * 512], g[krow:krow + nrow, :], deps, [kt.b])
                        if vname == "va":
                            vsrc = g[base + 512:base + 1024, vcol0:vcol0 + dv]
                        else:
                            vsrc = g[base + VBASE[vname]:base + VBASE[vname] + 256, :].rearrange("r (a c) -> (r a) c", c=256)[:, vcol0:vcol0 + dv]
                        self.dma(v.t[:, r * 4:(r + 1) * 4, :dv], vsrc.rearrange("(c p) f -> p c f", p=128), deps, [v.b])
                    return kt, v
                deps_k = G.kdeps(kname, c0, n)
                deps_v = G.kdeps(vname, c0, n)
                self.dma(kt.t[prow0:prow0 + nrow, :n], G.scr[kname][hp_or_h, prow0:prow0 + nrow, c0:c0 + n], deps_k, [kt.b])
                nfull = n // 128
                if nfull:
                    self.dma(v.t[:, :nfull, :dv], G.scr[vname][c0:c0 + nfull * 128, vcol0:vcol0 + dv].rearrange("(c p) f -> p c f", p=128), deps_v, [v.b])
                rem = n - nfull * 128
                if rem:
                    self.dma(v.t[:rem, nfull, :dv], G.scr[vname][c0 + nfull * 128:c0 + n, vcol0:vcol0 + dv], deps_v, [v.b])
                return kt, v

            def attn_A(l, G, t):
                T = G.T
                q0 = G.kidx0(t)
                hi = q0 + T
                LA = 2
                for h in range(4):
                    O = [bank[2], bank[4]]
                    Sm = [bank[3], bank[5]]
                    chunks = key_chunks(0, hi)
                    items = []
                    for ci, (c0, n) in enumerate(chunks):
                        nt = (n + 127) // 128
                        for i in range(nt):
                            for m in range(2):
                                items.append((ci, c0, n, i, m))
                    nit = len(items)
                    loaded = {}
                    pend = []

                    def stage1(j):
                        ci, c0, n, i, m = items[j]
                        if ci not in loaded:
                            loaded[ci] = load_kv(G, "kta", "va", h, 0, 128, h * 128, 128, c0, n)
                        kt, v = loaded[ci]
                        k0 = c0 + i * 128
                        nk = min(128, c0 + n - k0)
                        delta = k0 - q0
                        sc = self.ring("sc")
                        self.mm(sc.t[:nk, :T], kt.t[:, i * 128:i * 128 + nk], QTA[m].t[:, h, :T], True, True, [kt.b, QTA[m].b], [sc.b])
                        P = self.ring("tb")
                        if delta >= G.sthr:
                            j0 = G.jcol(384 - delta, WA_MAIN)
                            tf = self.ring("tf")
                            self.stt(tf.t[:nk, :T], sc.t[:nk, :T], 0.125, stripA.t[:nk, h, j0:j0 + T], ALU.mult, ALU.add, [sc.b, stripA.b], [tf.b])
                            self.act(P.t[:nk, :T], tf.t[:nk, :T], AF.Exp, [tf.b], [P.b])
                        else:
                            fb = self.far_bucket * 4 + h
                            self.act(P.t[:nk, :T], sc.t[:nk, :T], AF.Exp, [sc.b, t5bc.b], [P.b], scale=0.125, bias=t5bc.t[:nk, fb:fb + 1])
                        pend.append((P, v, i, nk, m, j < 2, j >= nit - 2))

                    def stage2(j):
                        P, v, i, nk, m, first, last = pend[j]
                        self.mm(O[m].t[:, :T], v.t[:nk, i, :128], P.t[:nk, :T], first, last, [v.b, P.b], [O[m].b])
                        self.mm(Sm[m].t[:, :T], onesb.t[:nk, :], P.t[:nk, :T], first, last, [onesb.b, P.b], [Sm[m].b])

                    for j in range(nit + LA):
                        if j < nit:
                            stage1(j)
                        if j >= LA:
                            stage2(j - LA)
                    r1 = self.ring("tf"); t1 = self.ring("tf"); r2 = self.ring("tf"); t2 = self.ring("tf")
                    self.act(r1.t[:, :T], Sm[0].t[:, :T], AF.Ln, [Sm[0].b], [r1.b])
                    self.act(r1.t[:, :T], r1.t[:, :T], AF.Exp, [r1.b], [r1.b], scale=-1.0)
                    self.tt(t1.t[:, :T], O[0].t[:, :T], r1.t[:, :T], ALU.mult, [O[0].b, r1.b], [t1.b])
                    self.act(r2.t[:, :T], Sm[1].t[:, :T], AF.Ln, [Sm[1].b], [r2.b])
                    self.act(r2.t[:, :T], r2.t[:, :T], AF.Exp, [r2.b], [r2.b], scale=-1.0)
                    self.tt(t2.t[:, :T], O[1].t[:, :T], r2.t[:, :T], ALU.mult, [O[1].b, r2.b], [t2.b])
                    self.stt(r1.t[:, :T], t2.t[:, :T], neglam.t[:, l:l + 1], t1.t[:, :T], ALU.mult, ALU.add, [t2.b, t1.b, neglam.b], [r1.b])
                    self.act(r2.t[:, :T], r1.t[:, :T], AF.Square, [r1.b], [r2.b])
                    self.mm(bank[2].t[:, :T], onesf.t[:, :], r2.t[:, :T], True, True, [onesf.b, r2.b], [bank[2].b])
                    self.act(t1.t[:, :T], bank[2].t[:, :T], AF.Ln, [bank[2].b, epsc.b], [t1.b], scale=1.0 / 128, bias=epsc.t[:, :])
                    self.act(t2.t[:, :T], t1.t[:, :T], AF.Exp, [t1.b], [t2.b], scale=-0.5)
                    self.tt(r2.t[:, :T], r1.t[:, :T], t2.t[:, :T], ALU.mult, [r1.b, t2.b], [r2.b])
                    self.ts(oaT.t[:, h, :T], r2.t[:, :T], subcol.t[:, l:l + 1], None, ALU.mult, ALU.bypass, [r2.b, subcol.b], [oaT.b])

            def attn_B(l, G, t):
                T = G.T
                q0 = G.kidx0(t)
                hi = q0 + T
                zring = [bank[0], bank[1], bank[6], bank[4]]
                cring = [bank[7], bank[5], bank[3]]
                for h in range(4):
                    hp, ho = h // 2, (h % 2) * 64
                    self.memset(Rt.t[:, :], 0.0, [Rt.b], eng="dve")
                    chunks = key_chunks(0, hi)[::-1]
                    items = []
                    for ci, (c0, n) in enumerate(chunks):
                        nt = (n + 127) // 128
                        for i in range(nt - 1, -1, -1):
                            items.append((ci, c0, n, i))
                    nit = len(items)
                    loaded = {}
                    st = {}

                    def s12(j):
                        ci, c0, n, i = items[j]
                        if ci not in loaded:
                            loaded[ci] = load_kv(G, "ktb", "vb", hp, 0, 128, hp * 128, 128, c0, n)
                        kt, v = loaded[ci]
                        k0 = c0 + i * 128
                        nk = min(128, c0 + n - k0)
                        delta = k0 - q0
                        msk = delta >= G.bthr
                        j0 = G.jcol(384 - delta, WB_MAIN) if msk else 0
                        z = zring[j % 4]
                        cs = cring[j % 3]
                        self.mm(z.t[:nk, :T], kt.t[:, i * 128:i * 128 + nk], QTB[h % 2].t[:, hp, :T], True, True, [kt.b, QTB[h % 2].b], [z.b])
                        e_ = self.ring("tf")
                        self.act(e_.t[:nk, :T], z.t[:nk, :T], AF.Exp, [z.b], [e_.b], scale=0.125)
                        sp = self.ring("tb")
                        self.act(sp.t[:nk, :T], e_.t[:nk, :T], AF.Ln, [e_.b, onec.b], [sp.b], bias=onec.t[:nk, :])
                        if msk:
                            self.tt(sp.t[:nk, :T], sp.t[:nk, :T], maskB.t[:nk, j0:j0 + T], ALU.mult, [sp.b, maskB.b], [sp.b])
                        st[j] = dict(v=v, i=i, nk=nk, msk=msk, j0=j0, z=z, cs=cs, sp=sp)

                    def s345(j):
                        d = st[j]
                        z, cs, sp, nk = d["z"], d["cs"], d["sp"], d["nk"]
                        S.op("pe", lambda e, z=z, sp=sp, nk=nk: e.matmul(z.t[:nk, :T], lhsT=negU8.t[:nk, :nk], rhs=sp.t[:nk, :T], start=False, stop=True, skip_group_check=True),
                             [negU8.b, sp.b], [z.b])
                        self.mm(cs.t[:, :T], onesb.t[:nk, :], sp.t[:nk, :T], True, True, [onesb.b, sp.b], [cs.b])
                        arg = self.ring("tf")
                        self.stt(arg.t[:nk, :T], z.t[:nk, :T], 0.125, Rt.t[:nk, :T], ALU.mult, ALU.subtract, [z.b, Rt.b], [arg.b])
                        self.tt(Rt.t[:, :T], cs.t[:, :T], Rt.t[:, :T], ALU.add, [cs.b, Rt.b], [Rt.b])
                        wb_ = self.ring("tb")
                        self.act(wb_.t[:nk, :T], arg.t[:nk, :T], AF.Exp, [arg.b], [wb_.b])
                        if d["msk"]:
                            self.tt(wb_.t[:nk, :T], wb_.t[:nk, :T], maskB.t[:nk, d["j0"]:d["j0"] + T], ALU.mult, [wb_.b, maskB.b], [wb_.b])
                        d["w"] = wb_

                    def s6(j):
                        d = st.pop(j)
                        self.mm(bank[2].t[:, :T], d["v"].t[:d["nk"], d["i"], :128], d["w"].t[:d["nk"], :T], j == 0, j == nit - 1, [d["v"].b, d["w"].b], [bank[2].b])

                    for j in range(nit + 2):
                        if j < nit:
                            s12(j)
                        if 1 <= j <= nit:
                            s345(j - 1)
                        if j >= 2:
                            s6(j - 2)
                    self.act(obT.t[ho:ho + 64, hp, :T], bank[2].t[ho:ho + 64, :T], AF.Copy, [bank[2].b], [obT.b])

            def attn_soft(G, T, QT, nkeys_tiles, dst, h, hp, ho, strip=None):
                n_ = len(nkeys_tiles)
                pend = []
                LA = 2

                def stage1(i):
                    kap, vap, nk, j0, deps = nkeys_tiles[i]
                    sc = self.ring("sc")
                    self.mm(sc.t[:nk, :T], kap, QT[h % 2].t[:, hp, :T], True, True, deps + [QT[h % 2].b], [sc.b])
                    P = self.ring("tb")
                    if j0 is not None:
                        tf = self.ring("tf")
                        self.stt(tf.t[:nk, :T], sc.t[:nk, :T], 0.125, strip.t[:nk, h, j0:j0 + T], ALU.mult, ALU.add, [sc.b, strip.b], [tf.b])
                        self.act(P.t[:nk, :T], tf.t[:nk, :T], AF.Exp, [tf.b], [P.b])
                    else:
                        self.act(P.t[:nk, :T], sc.t[:nk, :T], AF.Exp, [sc.b], [P.b], scale=0.125)
                    pend.append(P)

                def stage2(i):
                    kap, vap, nk, j0, deps = nkeys_tiles[i]
                    P = pend[i]
                    self.mm(bank[2].t[:, :T], vap, P.t[:nk, :T], i == 0, i == n_ - 1, deps + [P.b], [bank[2].b])
                    self.mm(bank[3].t[:, :T], onesb.t[:nk, :], P.t[:nk, :T], i == 0, i == n_ - 1, [onesb.b, P.b], [bank[3].b])

                for i in range(n_ + LA):
                    if i < n_:
                        stage1(i)
                    if i >= LA:
                        stage2(i - LA)
                r = self.ring("tf")
                self.act(r.t[ho:ho + 64, :T], bank[3].t[ho:ho + 64, :T], AF.Ln, [bank[3].b], [r.b])
                self.act(r.t[ho:ho + 64, :T], r.t[ho:ho + 64, :T], AF.Exp, [r.b], [r.b], scale=-1.0)
                self.tt(dst.t[ho:ho + 64, hp, :T], bank[2].t[ho:ho + 64, :T], r.t[ho:ho + 64, :T], ALU.mult, [bank[2].b, r.b], [dst.b])

            def attn_C(l, G, t):
                T = G.T
                q0 = G.kidx0c(t)
                hi = q0 + T
                lo = max(0, q0 - G.cback)
                for h in range(4):
                    hp, ho = h // 2, (h % 2) * 64
                    tl = []
                    for (c0, n) in key_chunks((lo // 1024) * 1024 if G.name == "p" else lo, hi, 1024 if G.name == "p" else 2048):
                        kt, v = load_kv(G, "ktc", "vc", hp, 0, 128, hp * 128, 128, c0, n)
                        for i in range((n + 127) // 128):
                            k0 = c0 + i * 128
                            if k0 < lo:
                                continue
                            nk = min(128, c0 + n - k0)
                            delta = k0 - q0
                            tl.append((kt.t[:, i * 128:i * 128 + nk], v.t[:nk, i, :128], nk, G.jcol(384 - delta, WC_MAIN), [kt.b, v.b]))
                    attn_soft(G, T, QTC, tl, ocT, h, hp, ho, strip=stripC)

            def attn_X(l, G, t):
                T = G.T
                mk_, mv_ = mkT[G.name], mvb[G.name]
                for h in range(4):
                    hp, ho = h // 2, (h % 2) * 64
                    tl = []
                    for i in range(2):
                        tl.append((mk_.t[:, hp, i * 128:(i + 1) * 128], mv_.t[:, i, hp * 128:(hp + 1) * 128], 128, None, [mk_.b, mv_.b]))
                    attn_soft(G, T, QTX, tl, oxT, h, hp, ho)

            def merge(l, G):
                np_, nsub, T = G.np, G.nsub, G.T
                gi_ = G.gidx
                mrgs_t = hidT.t[:, 0:8, :].rearrange("p (s a) c -> p s (a c)", s=4)
                for n in range(2):
                    cs_ = slice(n * 512, (n + 1) * 512)
                    wa = load_w("w_br_a", l, 0, n * 512, 512, kc=4)
                    wb_ = load_w("w_br_b", l, 0, n * 512, 512, kc=2)
                    wc = load_w("w_br_c", l, 0, n * 512, 512, kc=2)
                    for s in range(nsub):
                        sl = slice(s * 128, s * 128 + np_)
                        gt = self.ring("gt")
                        self.dma(gt.t[:np_, :, :], gates_d[G.name][gi_][s, :np_, :].rearrange("p (g c) -> p g c", g=3)[:, :, n * 512:(n + 1) * 512], [gates_b[G.name][gi_]], [gt.b])
                        ba, bb, bc = (bank[0], bank[1], bank[6]) if s % 2 == 0 else (bank[2], bank[3], bank[7])
                        for h in range(4):
                            self.mm(ba.t[:np_, :], oaT.t[:, h, sl], wa.t[:, h, :], h == 0, h == 3, [oaT.b, wa.b], [ba.b])
                        for h in range(2):
                            self.mm(bb.t[:np_, :], obT.t[:, h, sl], wb_.t[:, h, :], h == 0, h == 1, [obT.b, wb_.b], [bb.b])
                        for h in range(2):
                            self.mm(bc.t[:np_, :], ocT.t[:, h, sl], wc.t[:, h, :], h == 0, h == 1, [ocT.b, wc.b], [bc.b])
                        m1 = self.ring("tf"); m2 = self.ring("tf")
                        self.tt(m1.t[:np_, :], ba.t[:np_, :], gt.t[:np_, 0, :], ALU.mult, [ba.b, gt.b], [m1.b])
                        self.tt(m2.t[:np_, :], bb.t[:np_, :], gt.t[:np_, 1, :], ALU.mult, [bb.b, gt.b], [m2.b])
                        self.tt(m1.t[:np_, :], m1.t[:np_, :], m2.t[:np_, :], ALU.add, [m1.b, m2.b], [m1.b])
                        self.tt(m2.t[:np_, :], bc.t[:np_, :], gt.t[:np_, 2, :], ALU.mult, [bc.b, gt.b], [m2.b])
                        self.tt(mrgs_t[:np_, s, cs_], m1.t[:np_, :], m2.t[:np_, :], ALU.add, [m1.b, m2.b], [hidT.b])
                for s in range(nsub):
                    sl = slice(s * 128, s * 128 + np_)
                    transpose_to(lambda c, s=s: mrgs_t[:np_, s, c * 128:(c + 1) * 128], [hidT.b], 8, np_, hT.t[:, :, sl], [hT.b])

                def evac(p, s, acc, ncols):
                    self.tt(xin.t[:np_, s, p * 512:(p + 1) * 512], acc.t[:np_, :512], xin.t[:np_, s, p * 512:(p + 1) * 512], ALU.add, [acc.b, xin.b], [xin.b])
                linear_tm("w_out", l, 8, D, lambda k, s: hT.t[:, k, s * 128:s * 128 + np_], [hT.b], np_, nsub, evac)

            def cross(l, G, t):
                np_, nsub, T = G.np, G.nsub, G.T
                rms_T(xin, np_, nsub, l * 5 + 2, hT)

                def evq(p, s, acc, ncols):
                    tb = self.ring("tb")
                    headnorm(acc.t[:np_, :256], [acc.b], np_, 4, l * 6 + 4, tb.t[:np_, :256], [tb.b])
                    transpose_split(lambda c, tb=tb: tb.t[:np_, c * 128:(c + 1) * 128], [tb.b], 2, np_, QTX, slice(s * 128, s * 128 + np_))
                linear_tm("x_wq", l, 8, 256, lambda k, s: hT.t[:, k, s * 128:s * 128 + np_], [hT.b], np_, nsub, evq)
                attn_X(l, G, t)
                for n in range(2):
                    cs_ = slice(n * 512, (n + 1) * 512)
                    wo = load_w("x_wo", l, 0, n * 512, 512, kc=2)
                    for s in range(nsub):
                        sl = slice(s * 128, s * 128 + np_)
                        acc = bank[4 + (s % 2)]
                        for h in range(2):
                            self.mm(acc.t[:np_, :], oxT.t[:, h, sl], wo.t[:, h, :], h == 0, h == 1, [oxT.b, wo.b], [acc.b])
                        self.tt(xin.t[:np_, s, cs_], acc.t[:np_, :], xin.t[:np_, s, cs_], ALU.add, [acc.b, xin.b], [xin.b])

            def mem_kv_prompt(l):
                self.dma(xin.t[:, 0:2, :], mem_p.rearrange("(c p) d -> p c d", p=128), (), [xin.b])
                rms_T(xin, 128, 2, l * 5 + 3, hT)
                mk_, mv_ = mkT["p"], mvb["p"]

                def ev(p, s, acc, ncols):
                    of = self.ring("of")
                    headnorm(acc.t[:, 0:256], [acc.b], 128, 4, l * 6 + 5, of.t[:, 0:256], [of.b])
                    self.act(of.t[:, 256:512], acc.t[:, 256:512], AF.Copy, [acc.b], [of.b])
                    tb = self.ring("tb")
                    self.cp(tb.t[:, :256], of.t[:, 0:256], [of.b], [tb.b])
                    transpose_to(lambda c, tb=tb: tb.t[:, c * 128:(c + 1) * 128], [tb.b], 2, 128, mk_.t[:, :, s * 128:(s + 1) * 128], [mk_.b])
                    self.cp(mv_.t[:, s, :], acc.t[:, 256:512], [acc.b], [mv_.b])
                    self.dma(o_mkp[l, s * 128:(s + 1) * 128, :], of.t[:, 0:256], [of.b], (), q="pool")
                    self.dma(o_mvp[l, s * 128:(s + 1) * 128, :], of.t[:, 256:512], [of.b], (), q="pool")
                linear_tm("x_wkv", l, 8, 512, lambda k, s: hT.t[:, k, s * 128:(s + 1) * 128], [hT.b], 128, 2, ev)

            def prep_kT(src, nrow, F, dst_fn):
                for kt_ in range(nrow // 128):
                    of = self.ring("of")
                    self.dma(of.t[:, :F], src[kt_ * 128:(kt_ + 1) * 128, :], (), [of.b])
                    tb = self.ring("tb")
                    self.cp(tb.t[:, :F], of.t[:, :F], [of.b], [tb.b])
                    dst_fn(kt_, tb)

            def sample_prep(l, G):
                scr, sbf = G.scr, G.sb_
                self.dma(scr["va"][0:PAST, :], ca_v[l], (), [sbf["va"]["cache"][0]], q="pool")
                self.dma(scr["vb"][0:PAST, :], cb_v[l], (), [sbf["vb"]["cache"][0]], q="pool")
                self.dma(scr["vc"][0:CB, :], cc_v[l], (), [sbf["vc"]["cache"][0]], q="pool")
                for (src, nrow, F, kname) in ((ca_k[l], PAST, 512, "kta"), (cb_k[l], PAST, 256, "ktb"), (cc_k[l], CB, 256, "ktc")):
                    def dst_fn(kt_, tb, kname=kname, F=F):
                        ob = self.ring("ob")
                        nb = F // 128
                        transpose_to(lambda c, tb=tb: tb.t[:, c * 128:(c + 1) * 128], [tb.b], nb, 128, ob.t[:, :nb, :128], [ob.b])
                        self.dma(scr[kname][:, :, kt_ * 128:(kt_ + 1) * 128].rearrange("h p s -> p h s"), ob.t[:, :nb, :128], [ob.b], [sbf[kname]["cache"][kt_]], q="pool")
                    prep_kT(src, nrow, F, dst_fn)
                mk_, mv_ = mkT["s"], mvb["s"]

                def dst_m(kt_, tb):
                    transpose_to(lambda c, tb=tb: tb.t[:, c * 128:(c + 1) * 128], [tb.b], 2, 128, mk_.t[:, :, kt_ * 128:(kt_ + 1) * 128], [mk_.b])
                prep_kT(cm_k[l], NMEM, 256, dst_m)

                def dst_v(kt_, tb):
                    self.cp(mv_.t[:, kt_, :], tb.t[:, :256], [tb.b], [mv_.b])
                prep_kT(cm_v[l], NMEM, 256, dst_v)
                self.dma(o_cks[l, 0:CB - NS, :], cc_k[l, NS:CB, :], (), (), q="pool")
                self.dma(o_cvs[l, 0:CB - NS, :], cc_v[l, NS:CB, :], (), (), q="pool")

            NT = NSLOT
            Gp = Group()
            Gp.name, Gp.T, Gp.np, Gp.nsub = "p", 512, 128, 4

            def scrw_p(t):
                x_ = xkv[t % 2]
                return {"kta": x_[0:512, :].rearrange("(h p) s -> h p s", p=128), "va": x_[512:1024, :],
                        "ktb": x_[1024:1280, :].rearrange("(h p) s -> h p s", p=128),
                        "vb": x_[1280:1536, :].rearrange("r (a c) -> (r a) c", c=256),
                        "ktc": x_[1536:1792, :].rearrange("(h p) s -> h p s", p=128),
                        "vc": x_[1792:2048, :].rearrange("r (a c) -> (r a) c", c=256)}
            Gp.scrw = scrw_p
            Gp.wbuf = lambda name, t: xkvb[t % 2]
            Gp.widx0 = lambda t: 0
            Gp.widx0c = lambda t: 0
            Gp.kidx0 = lambda t: (2 * t + 1) * 512
            Gp.kidx0c = lambda t: (2 * t + 1) * 512
            Gp.orow = lambda t: t * 512
            Gp.sthr, Gp.bthr, Gp.cback = -640, -512, 1024
            Gp.jcol = lambda j0, wmain: j0
            Gp.o_ak, Gp.o_av, Gp.o_bk, Gp.o_bv = o_akp, o_avp, o_bkp, o_bvp

            def store_c_p(l, t, s, of):
                if t == NT - 1:
                    r = s * 128
                    self.dma(o_ckp[l, r:r + 128, :], of.t[:, 0:256], [of.b], (), q="pool")
                    self.dma(o_cvp[l, r:r + 128, :], of.t[:, 256:512], [of.b], (), q="pool")
            Gp.store_c = store_c_p

            groups_cc = [[b_, b_ + self.NPAIR] for b_ in range(self.NPAIR)]

            def exchange(l, t):
                src, dst = xkv[t % 2], gout[t]
                self.S.cc(lambda e: e.collective_compute("AllGather", ALU.bypass, replica_groups=groups_cc, ins=[src[:, :]], outs=[dst[:, :]]),
                          [xkvb[t % 2]], [gbuf[t]])

            Gs = Group()
            Gs.name, Gs.T, Gs.np, Gs.nsub = "s", NS, NS, 1
            Gs.scr = kvscr("s", PAST + NS)
            Gs.sb_ = {k_: {"cache": [Buf("%s_s_c%d" % (k_, i)) for i in range(PAST // 128)], 0: Buf(k_ + "_s_n")} for k_ in Gs.scr}
            Gs.scrw = lambda t: Gs.scr
            Gs.wbuf = lambda name, t: Gs.sb_[name][0]
            Gs.widx0 = lambda t: PAST
            Gs.widx0c = lambda t: CB
            Gs.kidx0 = lambda t: PAST
            Gs.kidx0c = lambda t: CB
            Gs.orow = lambda t: 0
            Gs.sthr, Gs.bthr, Gs.cback = -128, -127, 512
            Gs.jcol = lambda j0, wmain: wmain + (j0 - 384)
            Gs.o_ak, Gs.o_av, Gs.o_bk, Gs.o_bv = o_aks, o_avs, o_bks, o_bvs
            Gs.kdeps = lambda name, c0, n: list(Gs.sb_[name]["cache"]) + [Gs.sb_[name][0]]

            def store_c_s(l, t, s, of):
                self.dma(o_cks[l, CB - NS:CB, :], of.t[:NS, 0:256], [of.b], (), q="pool")
                self.dma(o_cvs[l, CB - NS:CB, :], of.t[:NS, 256:512], [of.b], (), q="pool")
            Gs.store_c = store_c_s

            def run_tile(l, G, t, xsrc, xdst, xsrc_deps, xdst_buf):
                np_, nsub, T = G.np, G.nsub, G.T
                self.dma(xin.t[:np_, :nsub, :], xsrc.rearrange("(c p) d -> p c d", p=np_), xsrc_deps, [xin.b])
                if STAGE >= 1:
                    ffn(l, 0, G)
                if STAGE >= 2:
                    proj(l, G, t)
                    if G.name == "p":
                        exchange(l, t)
                if STAGE >= 3:
                    attn_A(l, G, t)
                if STAGE >= 4:
                    attn_B(l, G, t)
                if STAGE >= 5:
                    attn_C(l, G, t)
                G.gidx = t % len(gates_d[G.name])
                if STAGE >= 6:
                    merge(l, G)
                if STAGE >= 7:
                    cross(l, G, t)
                if STAGE >= 8:
                    ffn(l, 1, G)
                self.dma(xdst.rearrange("(c p) d -> p c d", p=np_), xin.t[:np_, :nsub, :], [xin.b], xdst_buf, q="pool")

            xb_p = [Buf("xcur_%d" % t) for t in range(NT)]
            xb_s = Buf("xcur_s")
            for l in range(NL):
                layer_setup(l)
                mem_kv_prompt(l)
                for t in range(NT if "p" in KGROUP else 0):
                    src = x_p if l == 0 else xcur
                    dst = xcur if l < NL - 1 else y_p
                    run_tile(l, Gp, t, src[t * 512:(t + 1) * 512, :], dst[t * 512:(t + 1) * 512, :],
                             [xb_p[t]] if l > 0 else [], [xb_p[t]] if l < NL - 1 else [])
                if "s" not in KGROUP:
                    continue
                sample_prep(l, Gs)
                src = x_s if l == 0 else xcur_s
                dst = xcur_s if l < NL - 1 else y_s
                run_tile(l, Gs, 0, src[:, :], dst[:, :], [xb_s] if l > 0 else [], [xb_s] if l < NL - 1 else [])
            S.emit()
        return nc


_CACHE = {}


def _prep_inputs(inp, c, consts2, npair):
    b = c % npair
    hf = c // npair
    consts = consts2[1 - hf]
    f = lambda a: np.ascontiguousarray(a, dtype=np.float32)
    nsb = inp["x_sample"].shape[0]
    cs = c % nsb
    xb = inp["x_prompt"][b]
    SEQ = xb.shape[0]
    m = {
        "x_p": f(xb.reshape(SEQ // 1024, 2, 512, D)[:, hf].reshape(SEQ // 2, D)), "x_s": f(inp["x_sample"][cs]), "mem_p": f(inp["mem_prompt"][b]),
        "ca_k": f(inp["cache_a_k"][:, cs].reshape(NL, -1, 512)), "ca_v": f(inp["cache_a_v"][:, cs].reshape(NL, -1, 512)),
        "cb_k": f(inp["cache_b_k"][:, cs].reshape(NL, -1, 256)), "cb_v": f(inp["cache_b_v"][:, cs].reshape(NL, -1, 256)),
        "cc_k": f(inp["cache_c_k"][:, cs].reshape(NL, -1, 256)), "cc_v": f(inp["cache_c_v"][:, cs].reshape(NL, -1, 256)),
        "cm_k": f(inp["cache_mem_k"][:, cs].reshape(NL, -1, 256)), "cm_v": f(inp["cache_mem_v"][:, cs].reshape(NL, -1, 256)),
        "t5": f(inp["t5_bias"]),
        "gains": f(np.stack([inp["ffn1_norm"], inp["mix_norm"], inp["x_norm"], inp["mem_norm"], inp["ffn2_norm"]], axis=1)),
        "hgains": f(np.stack([inp["a_qnorm"], inp["a_knorm"], inp["c_qnorm"], inp["c_knorm"], inp["x_qnorm"], inp["x_knorm"]], axis=1)),
        "lvec": f(np.stack([inp["a_lq1"], inp["a_lk1"], inp["a_lq2"], inp["a_lk2"]], axis=1)),
        "subln": f(inp["a_subln"]), "crel": f(inp["c_rel_bias"]),
        "wg1": f(inp["ffn1_wg"]), "wu1": f(inp["ffn1_wu"]), "wd1": f(inp["ffn1_wd"]), "w_in": f(inp["w_in"]),
        "w_br_a": f(inp["w_br_a"]), "w_br_b": f(inp["w_br_b"]), "w_br_c": f(inp["w_br_c"]), "w_out": f(inp["w_out"]),
        "x_wq": f(inp["x_wq"]), "x_wkv": f(inp["x_wkv"]), "x_wo": f(inp["x_wo"]),
        "wg2": f(inp["ffn2_wg"]), "wu2": f(inp["ffn2_wu"]), "wd2": f(inp["ffn2_wd"]),
        "oh_a": consts["oh_a"], "oh_c": consts["oh_c"], "mask_a": consts["mask_a"], "mask_b": consts["mask_b"], "mask_c": consts["mask_c"],
    }
    return m


def kernel(**inp):
    inp = {k: np.asarray(v) for k, v in inp.items()}
    B, SEQ = inp["x_prompt"].shape[0], inp["x_prompt"].shape[1]
    NB, NS = inp["x_sample"].shape[0], inp["x_sample"].shape[1]
    PAST = inp["cache_a_k"].shape[2]
    CB = inp["cache_c_k"].shape[2]
    consts2 = [host_consts(0), host_consts(1)]
    npair = B
    ncores = 2 * npair
    key = (SEQ, NS, PAST, CB, npair)
    if key not in _CACHE:
        _CACHE[key] = K(SEQ // 2, NS, PAST, CB, consts2[0]["far_bucket"], NPAIR=npair).build()
    nc = _CACHE[key]
    in_maps = [_prep_inputs(inp, c, consts2, npair) for c in range(ncores)]
    res = run_bass_kernel_spmd(nc, in_maps, core_ids=list(range(ncores)))
    R = res.results
    g = lambda c, name: np.asarray(R[c][name], dtype=np.float32)

    def inter(name, lead):
        outs = []
        for b in range(B):
            a0, a1 = g(b, name), g(b + npair, name)
            sh = a0.shape
            if lead:
                a0 = a0.reshape(sh[0], -1, 512, sh[-1]); a1 = a1.reshape(sh[0], -1, 512, sh[-1])
                outs.append(np.stack([a0, a1], axis=2).reshape(sh[0], SEQ, sh[-1]))
            else:
                a0 = a0.reshape(-1, 512, sh[-1]); a1 = a1.reshape(-1, 512, sh[-1])
                outs.append(np.stack([a0, a1], axis=1).reshape(SEQ, sh[-1]))
        return np.stack(outs, axis=1 if lead else 0)

    stl = lambda name, cores: np.stack([g(c, name) for c in cores], axis=1)
    sc = [c % ncores for c in range(NB)]
    late = [b + npair for b in range(B)]
    early = list(range(B))
    y_p = inter("y_p", False)
    y_s = np.stack([g(c, "y_s") for c in sc])
    outs = [y_p, y_s,
            inter("a_k_p", True).reshape(NL, B, SEQ, 4, 128), inter("a_v_p", True).reshape(NL, B, SEQ, 4, 128),
            inter("b_k_p", True).reshape(NL, B, SEQ, 4, 64), inter("b_v_p", True).reshape(NL, B, SEQ, 4, 64),
            stl("c_k_p", late).reshape(NL, B, 512, 4, 64), stl("c_v_p", late).reshape(NL, B, 512, 4, 64),
            stl("m_k_p", early).reshape(NL, B, NMEM, 4, 64), stl("m_v_p", early).reshape(NL, B, NMEM, 4, 64),
            stl("a_k_s", sc).reshape(NL, NB, NS, 4, 128), stl("a_v_s", sc).reshape(NL, NB, NS, 4, 128),
            stl("b_k_s", sc).reshape(NL, NB, NS, 4, 64), stl("b_v_s", sc).reshape(NL, NB, NS, 4, 64),
            stl("c_k_s", sc).reshape(NL, NB, CB, 4, 64), stl("c_v_s", sc).reshape(NL, NB, CB, 4, 64)]
    return tuple(np.ascontiguousarray(o) for o in outs)
```
